# Optimizing a Trainium2 kernel written in Bass

```python
import math
import jax
import jax.numpy as jnp
from jax import lax
import numpy as np

D_MODEL = 1024
BATCH = 2
SEQ = 16384
DEPTH = 2

N_META = 16
CHUNK = 64
EPS = 1e-6

HG_W = D_MODEL // 4
HG_DK = 64
HG_DV = 64
HG_HEADS = HG_W // HG_DK
GDN_W = D_MODEL // 2
GDN_DK = 128
GDN_DV = 128
GDN_HEADS = GDN_W // GDN_DK
GDN_CONV = 4
S5_CH = D_MODEL // 4
S5_GROUP = 16
S5_NGROUPS = S5_CH // S5_GROUP
S5_STATE = 64
D_MIX = HG_W + GDN_W + S5_CH
D_FF = 4 * D_MODEL

IN_SPLITS = (HG_W, HG_W, HG_W, HG_W,
             GDN_W, GDN_W, GDN_W, GDN_W,
             GDN_HEADS, GDN_HEADS,
             S5_CH)
N_IN = sum(IN_SPLITS)

kernel_name = "hymba_style_hgrn2_gdn_s5_hybrid"


def split_points():
    pts, acc = [], 0
    for s in IN_SPLITS[:-1]:
        acc += s
        pts.append(acc)
    return pts


def rmsnorm(x, w):
    xf = x.astype(jnp.float32)
    y = xf * lax.rsqrt(jnp.mean(xf * xf, axis=-1, keepdims=True) + EPS)
    return (y * w.astype(jnp.float32)).astype(x.dtype)


def l2norm(x):
    return x * lax.rsqrt(jnp.sum(x * x, axis=-1, keepdims=True) + EPS)


def to_heads(t, n_heads):
    b, l, _ = t.shape
    return t.reshape(b, l, n_heads, -1).transpose(0, 2, 1, 3)


def from_heads(t):
    b, h, l, d = t.shape
    return t.transpose(0, 2, 1, 3).reshape(b, l, h * d)


def to_chunks(t, chunk):
    b, h, T = t.shape[:3]
    return t.reshape(b, h, T // chunk, chunk, *t.shape[3:])


def causal_depthwise_conv(x, w):
    k, c = w.shape
    return lax.conv_general_dilated(
        x, w[:, None, :].astype(x.dtype), window_strides=(1,), padding=[(k - 1, 0)],
        dimension_numbers=("NWC", "WIO", "NWC"), feature_group_count=c)


def run_meta_then_chunks(segment_fn, arrays, state0):
    head = [t[:, :, :N_META] for t in arrays]
    tail = [t[:, :, N_META:] for t in arrays]
    o_meta, state = segment_fn(*head, state0, N_META)
    o_tail, _ = segment_fn(*tail, state, CHUNK)
    return jnp.concatenate([o_meta, o_tail], axis=2)


def hgrn2_segment(q, k, v, log_f, state0, chunk):
    b, h, T, _ = q.shape
    xs = tuple(jnp.moveaxis(to_chunks(t, chunk), 2, 0) for t in (q, k, v, log_f))
    causal = jnp.tril(jnp.ones((chunk, chunk), dtype=bool))[:, :, None]

    def step(S, inp):
        qi, ki, vi, gi = inp
        cum = jnp.cumsum(gi, axis=2)
        diff = cum[:, :, :, None, :] - cum[:, :, None, :, :]
        decay = jnp.where(causal, jnp.exp(jnp.where(causal, diff, 0.0)), 0.0)
        scores = jnp.einsum("bhtd,bhsd,bhtsd->bhts", qi, ki, decay)
        o = (jnp.einsum("bhts,bhse->bhte", scores, vi)
             + jnp.einsum("bhtd,bhde->bhte", qi * jnp.exp(cum), S))
        last = cum[:, :, -1:, :]
        S = (S * jnp.exp(last)[:, :, 0, :, None]
             + jnp.einsum("bhsd,bhse->bhde", ki * jnp.exp(last - cum), vi))
        return S, o

    S, o = lax.scan(step, state0, xs)
    o = jnp.moveaxis(o, 0, 2).reshape(b, h, T, -1)
    return o, S


def hgrn2_mixer(q, f_logit, i, gate, lb, norm_w):
    f32 = jnp.float32
    bsz = q.shape[0]
    fl = f_logit.astype(f32)
    lbf = lb.astype(f32)
    log_f = jnp.logaddexp(jax.nn.log_sigmoid(fl), jnp.log(lbf) + jax.nn.log_sigmoid(-fl))
    k = (1.0 - lbf) * jax.nn.sigmoid(-fl)
    qh = to_heads(jax.nn.silu(q.astype(f32)), HG_HEADS)
    kh = to_heads(k, HG_HEADS)
    vh = to_heads(i.astype(f32), HG_HEADS)
    gh = to_heads(log_f, HG_HEADS)
    s0 = jnp.zeros((bsz, HG_HEADS, HG_DK, HG_DV), f32)
    o = run_meta_then_chunks(hgrn2_segment, [qh, kh, vh, gh], s0)
    o = rmsnorm(o, norm_w) * jax.nn.silu(to_heads(gate.astype(f32), HG_HEADS))
    return from_heads(o)


def gdn_segment(q, k, v, g, beta, state0, chunk):
    b, h, T, _ = q.shape
    qc, kc, vc = (to_chunks(t, chunk) for t in (q, k, v))
    gc, bc = to_chunks(g, chunk), to_chunks(beta, chunk)
    cum = jnp.cumsum(gc, axis=-1)
    incl = jnp.tril(jnp.ones((chunk, chunk), dtype=bool))
    strict = jnp.tril(jnp.ones((chunk, chunk), dtype=bool), k=-1)
    diff = cum[..., :, None] - cum[..., None, :]
    decay = jnp.where(incl, jnp.exp(jnp.where(incl, diff, 0.0)), 0.0)
    kb = kc * bc[..., None]
    m = jnp.where(strict, jnp.einsum("bhntd,bhnsd->bhnts", kb, kc) * decay, 0.0)
    eye = jnp.eye(chunk, dtype=m.dtype)
    tmat = lax.linalg.triangular_solve(eye + m, jnp.broadcast_to(eye, m.shape),
                                       left_side=True, lower=True, unit_diagonal=True)
    u = jnp.einsum("bhnts,bhnse->bhnte", tmat, vc * bc[..., None])
    w = jnp.einsum("bhnts,bhnsd->bhntd", tmat, kb * jnp.exp(cum)[..., None])
    a_intra = jnp.einsum("bhntd,bhnsd->bhnts", qc, kc) * decay
    q_dec = qc * jnp.exp(cum)[..., None]
    k_dec = kc * jnp.exp(cum[..., -1:] - cum)[..., None]
    chunk_decay = jnp.exp(cum[..., -1])
    xs = tuple(jnp.moveaxis(t, 2, 0) for t in (u, w, a_intra, q_dec, k_dec, chunk_decay))

    def step(S, inp):
        ui, wi, ai, qi, ki, di = inp
        v_new = ui - jnp.einsum("bhtd,bhde->bhte", wi, S)
        o = jnp.einsum("bhtd,bhde->bhte", qi, S) + jnp.einsum("bhts,bhse->bhte", ai, v_new)
        S = S * di[..., None, None] + jnp.einsum("bhtd,bhte->bhde", ki, v_new)
        return S, o

    S, o = lax.scan(step, state0, xs)
    o = jnp.moveaxis(o, 0, 2).reshape(b, h, T, -1)
    return o, S


def gdn_mixer(q, k, v, z, a, b, conv_w, a_log, dt_bias, norm_w):
    f32 = jnp.float32
    bsz = q.shape[0]
    qkv = jax.nn.silu(causal_depthwise_conv(jnp.concatenate([q, k, v], axis=-1), conv_w)).astype(f32)
    qc, kc, vc = jnp.split(qkv, 3, axis=-1)
    qh = l2norm(to_heads(qc, GDN_HEADS)) * (GDN_DK ** -0.5)
    kh = l2norm(to_heads(kc, GDN_HEADS))
    vh = to_heads(vc, GDN_HEADS)
    g = -jnp.exp(a_log.astype(f32)) * jax.nn.softplus(a.astype(f32) + dt_bias.astype(f32))
    beta = jax.nn.sigmoid(b.astype(f32))
    g = g.transpose(0, 2, 1)
    beta = beta.transpose(0, 2, 1)
    s0 = jnp.zeros((bsz, GDN_HEADS, GDN_DK, GDN_DV), f32)
    o = run_meta_then_chunks(gdn_segment, [qh, kh, vh, g, beta], s0)
    o = rmsnorm(o, norm_w) * jax.nn.silu(to_heads(z.astype(f32), GDN_HEADS))
    return from_heads(o)


def s5_combine(e1, e2):
    a1r, a1i, b1r, b1i = e1
    a2r, a2i, b2r, b2i = e2
    ar = a2r * a1r - a2i * a1i
    ai = a2r * a1i + a2i * a1r
    br = a2r * b1r - a2i * b1i + b2r
    bi = a2r * b1i + a2i * b1r + b2i
    return ar, ai, br, bi


def s5_mixer(u, lam_re, lam_im, log_step, b_re, b_im, c_re, c_im, d, glu_w, norm_w):
    f32 = jnp.float32
    bsz, l, _ = u.shape
    u_flat = u.astype(f32)
    ug = u_flat.reshape(bsz, l, S5_NGROUPS, S5_GROUP)
    lre = jnp.minimum(lam_re.astype(f32), -1e-4)
    lim = lam_im.astype(f32)
    dt = jnp.exp(log_step.astype(f32))[:, None]
    mag = jnp.exp(lre * dt)
    abar_re = mag * jnp.cos(lim * dt)
    abar_im = mag * jnp.sin(lim * dt)
    den = lre * lre + lim * lim
    nr = abar_re - 1.0
    ni = abar_im
    coef_re = ((nr * lre + ni * lim) / den)[..., None]
    coef_im = ((ni * lre - nr * lim) / den)[..., None]
    bbar_re = coef_re * b_re.astype(f32) - coef_im * b_im.astype(f32)
    bbar_im = coef_re * b_im.astype(f32) + coef_im * b_re.astype(f32)
    bu_re = jnp.einsum("blgc,gpc->blgp", ug, bbar_re)
    bu_im = jnp.einsum("blgc,gpc->blgp", ug, bbar_im)
    a_re = jnp.broadcast_to(abar_re, bu_re.shape)
    a_im = jnp.broadcast_to(abar_im, bu_im.shape)
    _, _, x_re, x_im = lax.associative_scan(s5_combine, (a_re, a_im, bu_re, bu_im), axis=1)
    y = (jnp.einsum("gcp,blgp->blgc", c_re.astype(f32), x_re)
         - jnp.einsum("gcp,blgp->blgc", c_im.astype(f32), x_im))
    y = y.reshape(bsz, l, S5_CH) + d.astype(f32) * u_flat
    zg = jax.nn.gelu(y)
    out = zg * jax.nn.sigmoid(zg @ glu_w.astype(f32))
    return rmsnorm(out, norm_w)


def setup_inputs(seed: int = 0) -> dict:
    key = jax.random.key(seed)
    ks = jax.random.split(key, 32)
    f32 = jnp.float32

    def nrm(k, shape, scale):
        return scale * jax.random.normal(k, shape, f32)

    def gain(k, shape):
        return 1.0 + 0.02 * jax.random.normal(k, shape, f32)

    x = nrm(ks[0], (BATCH, SEQ, D_MODEL), 1.0)
    meta = nrm(ks[1], (N_META, D_MODEL), 1.0)
    norm_mix_w = gain(ks[2], (DEPTH, D_MODEL))
    norm_mlp_w = gain(ks[3], (DEPTH, D_MODEL))
    w_in = nrm(ks[4], (DEPTH, D_MODEL, N_IN), D_MODEL ** -0.5)
    hgrn_lb_logits = nrm(ks[5], (DEPTH, HG_W), 0.1)
    hgrn_norm_w = gain(ks[6], (DEPTH, HG_DV))
    gdn_conv_w = nrm(ks[7], (DEPTH, GDN_CONV, 3 * GDN_W), GDN_CONV ** -0.5)
    gdn_a_log = jnp.log(jax.random.uniform(ks[8], (DEPTH, GDN_HEADS), f32, 1.0, 16.0))
    dt0 = jnp.exp(jax.random.uniform(ks[9], (DEPTH, GDN_HEADS), f32, math.log(1e-3), math.log(1e-1)))
    gdn_dt_bias = dt0 + jnp.log(-jnp.expm1(-dt0))
    gdn_norm_w = gain(ks[10], (DEPTH, GDN_DV))
    s5_lam_re = -0.5 + 0.01 * jax.random.normal(ks[11], (DEPTH, S5_NGROUPS, S5_STATE), f32)
    s5_lam_im = (jnp.pi * jnp.arange(S5_STATE, dtype=f32)[None, None, :]
                 + 0.01 * jax.random.normal(ks[12], (DEPTH, S5_NGROUPS, S5_STATE), f32))
    s5_log_step = jax.random.uniform(ks[13], (DEPTH, S5_NGROUPS), f32, math.log(1e-3), math.log(1e-1))
    s5_b_re = nrm(ks[14], (DEPTH, S5_NGROUPS, S5_STATE, S5_GROUP), (2 * S5_GROUP) ** -0.5)
    s5_b_im = nrm(ks[15], (DEPTH, S5_NGROUPS, S5_STATE, S5_GROUP), (2 * S5_GROUP) ** -0.5)
    s5_c_re = nrm(ks[16], (DEPTH, S5_NGROUPS, S5_GROUP, S5_STATE), (2 * S5_STATE) ** -0.5)
    s5_c_im = nrm(ks[17], (DEPTH, S5_NGROUPS, S5_GROUP, S5_STATE), (2 * S5_STATE) ** -0.5)
    s5_d = nrm(ks[18], (DEPTH, S5_CH), 1.0)
    s5_glu_w = nrm(ks[19], (DEPTH, S5_CH, S5_CH), S5_CH ** -0.5)
    s5_norm_w = gain(ks[20], (DEPTH, S5_CH))
    w_out = nrm(ks[21], (DEPTH, D_MIX, D_MODEL), D_MIX ** -0.5)
    mlp_w1 = nrm(ks[22], (DEPTH, D_MODEL, D_FF), D_MODEL ** -0.5)
    mlp_w2 = nrm(ks[23], (DEPTH, D_FF, D_MODEL), D_FF ** -0.5)
    final_norm_w = gain(ks[24], (D_MODEL,))
    return {
        "x": x, "meta": meta, "norm_mix_w": norm_mix_w, "norm_mlp_w": norm_mlp_w,
        "w_in": w_in, "hgrn_lb_logits": hgrn_lb_logits, "hgrn_norm_w": hgrn_norm_w,
        "gdn_conv_w": gdn_conv_w, "gdn_a_log": gdn_a_log, "gdn_dt_bias": gdn_dt_bias,
        "gdn_norm_w": gdn_norm_w, "s5_lam_re": s5_lam_re, "s5_lam_im": s5_lam_im,
        "s5_log_step": s5_log_step, "s5_b_re": s5_b_re, "s5_b_im": s5_b_im,
        "s5_c_re": s5_c_re, "s5_c_im": s5_c_im, "s5_d": s5_d, "s5_glu_w": s5_glu_w,
        "s5_norm_w": s5_norm_w, "w_out": w_out, "mlp_w1": mlp_w1, "mlp_w2": mlp_w2,
        "final_norm_w": final_norm_w,
    }


def reference(x, meta, norm_mix_w, norm_mlp_w, w_in, hgrn_lb_logits, hgrn_norm_w,
              gdn_conv_w, gdn_a_log, gdn_dt_bias, gdn_norm_w, s5_lam_re, s5_lam_im,
              s5_log_step, s5_b_re, s5_b_im, s5_c_re, s5_c_im, s5_d, s5_glu_w,
              s5_norm_w, w_out, mlp_w1, mlp_w2, final_norm_w):
    bsz = x.shape[0]
    meta_b = jnp.broadcast_to(meta[None].astype(x.dtype), (bsz, N_META, D_MODEL))
    h = jnp.concatenate([meta_b, x], axis=1)
    lb_table = jnp.cumsum(jax.nn.softmax(hgrn_lb_logits.astype(jnp.float32), axis=0), axis=0)
    lb_table = lb_table - lb_table[0]
    pts = split_points()
    for layer in range(DEPTH):
        hn = rmsnorm(h, norm_mix_w[layer])
        proj = hn @ w_in[layer]
        hq, hf, hi, hg, gq, gk, gv, gz, ga, gb, su = jnp.split(proj, pts, axis=-1)
        y_a = hgrn2_mixer(hq, hf, hi, hg, lb_table[layer], hgrn_norm_w[layer])
        y_b = gdn_mixer(gq, gk, gv, gz, ga, gb, gdn_conv_w[layer], gdn_a_log[layer],
                        gdn_dt_bias[layer], gdn_norm_w[layer])
        y_c = s5_mixer(su, s5_lam_re[layer], s5_lam_im[layer], s5_log_step[layer],
                       s5_b_re[layer], s5_b_im[layer], s5_c_re[layer], s5_c_im[layer],
                       s5_d[layer], s5_glu_w[layer], s5_norm_w[layer])
        mixed = jnp.concatenate([y_a, y_b, y_c], axis=-1).astype(h.dtype)
        h = h + mixed @ w_out[layer]
        hn = rmsnorm(h, norm_mlp_w[layer])
        h = h + jnp.square(jax.nn.relu(hn @ mlp_w1[layer])) @ mlp_w2[layer]
    return rmsnorm(h, final_norm_w)[:, N_META:]
```

```python
from contextlib import ExitStack
import numpy as np
import concourse.bass as bass
import concourse.mybir as mybir

F32 = mybir.dt.float32
BF16 = mybir.dt.bfloat16
ALU = mybir.AluOpType
AF = mybir.ActivationFunctionType
AX = mybir.AxisListType

EPOCH = 24000
NEPOCH = 10


class Sched:
    ENGS = ("pe", "act", "dve", "pool", "sp")

    def __init__(self, nc, st):
        self.nc = nc
        self.st = st
        self.q = {e: [] for e in self.ENGS}
        self.cnt = {e: 0 for e in self.ENGS}
        self.sems = {}
        self.waited = {}
        self.lastw = {}
        self.readers = {}
        self.dmacnt = {}
        self.latest = {}
        self.h = {"pe": nc.tensor, "act": nc.scalar, "dve": nc.vector,
                  "pool": nc.gpsimd, "sp": nc.sync}

    def _sem(self, key):
        s = self.sems.get(key)
        if s is None:
            s = self.st.enter_context(self.nc.semaphore("s_%s_%s" % key))
            self.sems[key] = s
        return s

    def _wait(self, e, clock):
        if clock is None:
            return
        key, val = clock
        if key[0] == "dma":
            val = max(val, self.dmacnt[key[1]])
        if key[0] == "pe" and e == "pe":
            return
        if self.waited.get((e, key), 0) >= val:
            return
        self.waited[(e, key)] = val
        sem = self._sem(key)
        h = self.h[e]
        self.q[e].append(lambda: h.wait_ge(sem, val))

    def _deps(self, e, reads, writes):
        for r in reads:
            self._wait(e, self.lastw.get(r))
        for w in writes:
            self._wait(e, self.lastw.get(w))
            for k, v in self.readers.get(w, {}).items():
                self._wait(e, (k, v))

    def _record(self, clock, reads, writes):
        key, val = clock
        for r in reads:
            d = self.readers.setdefault(r, {})
            if d.get(key, 0) < val:
                d[key] = val
        for w in writes:
            self.lastw[w] = clock
            self.readers[w] = {}
        if self.latest.get(key, 0) < val:
            self.latest[key] = val

    def barrier(self):
        for e in self.ENGS:
            for key, val in list(self.latest.items()):
                self._wait(e, (key, val))

    @staticmethod
    def _excl(reads, writes):
        ex = [r for r in reads if (isinstance(r, str) and (r.startswith("bank") or r == "tp")) or
              (isinstance(r, tuple) and r[0] in ("hid", "ops"))]
        if ex:
            reads = [r for r in reads if r not in ex]
            writes = list(writes) + ex
        return reads, writes

    def op(self, e, fn, reads=(), writes=()):
        reads, writes = self._excl(reads, writes)
        self._deps(e, reads, writes)
        i = self.cnt[e]
        self.cnt[e] += 1
        key = (e, i // EPOCH)
        val = i % EPOCH + 1
        sem = self._sem(key)
        h = self.h[e]
        self.q[e].append(lambda: fn(h).then_inc(sem, 1))
        self._record((key, val), reads, writes)

    def dma(self, stream, out, in_, reads=(), writes=(), q="sp"):
        self._deps(q, reads, writes)
        key = ("dma", stream)
        n = self.dmacnt.get(stream, 0) + 16
        self.dmacnt[stream] = n
        sem = self._sem(key)
        h = self.h[q]
        self.q[q].append(lambda: h.dma_start(out=out, in_=in_).then_inc(sem, 16))
        self._record((key, n), reads, writes)

    def final_wait(self, e, resources):
        for r in resources:
            self._wait(e, self.lastw.get(r))

    def run(self, block):
        q = self.q

        @block.tensor
        def _(eng):
            for f in q["pe"]:
                f()

        @block.scalar
        def _(eng):
            for f in q["act"]:
                f()

        @block.vector
        def _(eng):
            for f in q["dve"]:
                f()

        @block.gpsimd
        def _(eng):
            for f in q["pool"]:
                f()

        @block.sync
        def _(eng):
            for f in q["sp"]:
                f()


import sys, time, math
import ml_dtypes
from concourse.bass_utils import run_bass_kernel_spmd

EPS = 1e-6
TWO_PI = 2.0 * math.pi
GELU_C = 2.0 * math.sqrt(2.0 / math.pi)
NEG_BIG = 30000.0


def host_consts():
    c = {}
    c["identf"] = np.eye(128, dtype=np.float32)
    c["identb"] = np.eye(128).astype(ml_dtypes.bfloat16)
    s = np.arange(128)
    c["tri"] = (s[:, None] <= s[None, :]).astype(np.float32)
    c["gmask"] = (NEG_BIG * (s[None, :] > s[:, None])).astype(np.float32)
    c["strict"] = (s[None, :] < s[:, None]).astype(np.float32)
    c["iota"] = np.ascontiguousarray(np.broadcast_to(s.astype(np.float32), (128, 128)))
    m32 = (s[:32, None] <= s[None, :32]).astype(np.float32)
    c["mask32"] = np.ascontiguousarray(np.broadcast_to(m32[:, None, :], (32, 4, 32))).reshape(32, 128)
    rs = np.ones((64, 4, 128), np.float32)
    rs[:, :, 0::32] = 0.0
    c["hreset"] = rs.reshape(64, 512)
    c["gmask4"] = np.ascontiguousarray(np.tile(c["gmask"], (1, 4)))
    c["strict4"] = np.ascontiguousarray(np.tile(c["strict"], (1, 4)))
    return c


def host_layer_params(inp, l):
    p = {}
    bc = lambda v, n=128: np.ascontiguousarray(np.broadcast_to(np.asarray(v, np.float32).reshape(1, -1), (n, np.asarray(v).size)))
    p["nmix"] = bc(inp["norm_mix_w"][l])
    p["nmlp"] = bc(inp["norm_mlp_w"][l])
    p["nmixc"] = np.ascontiguousarray(np.asarray(inp["norm_mix_w"][l], np.float32).reshape(8, 128).T)
    p["w_in"] = np.ascontiguousarray(inp["w_in"][l])
    p["w_out"] = np.ascontiguousarray(inp["w_out"][l])
    p["w1"] = np.ascontiguousarray(inp["mlp_w1"][l])
    p["w2"] = np.ascontiguousarray(inp["mlp_w2"][l])
    p["lbl0"] = np.ascontiguousarray(inp["hgrn_lb_logits"][0].reshape(4, 64).T)
    p["lbl1"] = np.ascontiguousarray(inp["hgrn_lb_logits"][1].reshape(4, 64).T)
    p["hnw"] = np.ascontiguousarray(inp["hgrn_norm_w"][l].reshape(64, 1))
    p["convw"] = np.ascontiguousarray(inp["gdn_conv_w"][l].reshape(4, 12, 128).transpose(2, 1, 0))
    p["alog"] = bc(inp["gdn_a_log"][l])
    p["dtb"] = bc(inp["gdn_dt_bias"][l])
    p["gnw"] = np.ascontiguousarray(inp["gdn_norm_w"][l].reshape(128, 1))
    lre, lim, ls = inp["s5_lam_re"][l], inp["s5_lam_im"][l], inp["s5_log_step"][l]
    def pl(a):
        return np.ascontiguousarray(a.reshape(8, 128).T)
    p["lre_p"] = pl(lre); p["lim_p"] = pl(lim)
    p["ls_p"] = pl(np.broadcast_to(ls[:, None], (16, 64)))
    fl = lambda a: np.ascontiguousarray(np.broadcast_to(a.reshape(1, 1024), (128, 1024)))
    p["lre_f"] = fl(lre); p["lim_f"] = fl(lim)
    p["ls_f"] = fl(np.ascontiguousarray(np.broadcast_to(ls[:, None], (16, 64))))
    bre, bim = inp["s5_b_re"][l], inp["s5_b_im"][l]
    cre, cim = inp["s5_c_re"][l], inp["s5_c_im"][l]
    BTr = np.zeros((128, 8, 128), np.float32); BTi = np.zeros((128, 8, 128), np.float32)
    CTr = np.zeros((128, 8, 128), np.float32); CTi = np.zeros((128, 8, 128), np.float32)
    for g in range(16):
        j = g // 2; r0 = (g % 2) * 64; c0 = (16 * g) % 128
        BTr[c0:c0 + 16, j, r0:r0 + 64] = bre[g].T
        BTi[c0:c0 + 16, j, r0:r0 + 64] = bim[g].T
        CTr[r0:r0 + 64, j, c0:c0 + 16] = cre[g].T
        CTi[r0:r0 + 64, j, c0:c0 + 16] = cim[g].T
    p["BTr"] = BTr.reshape(128, 1024); p["BTi"] = BTi.reshape(128, 1024)
    p["CTr"] = CTr.reshape(128, 1024); p["CTi"] = CTi.reshape(128, 1024)
    p["s5d"] = np.ascontiguousarray(inp["s5_d"][l].reshape(2, 128).T)
    p["s5nw"] = np.ascontiguousarray(inp["s5_norm_w"][l].reshape(2, 128).T)
    p["glu"] = np.ascontiguousarray(inp["s5_glu_w"][l])
    return p


PSHAPES = {
    "nmix": [128, 1024], "nmixc": [128, 8], "nmlp": [128, 1024], "w_in": [1024, 3336], "w_out": [1024, 1024],
    "w1": [1024, 4096], "w2": [4096, 1024], "lbl0": [64, 4], "lbl1": [64, 4], "hnw": [64, 1],
    "convw": [128, 48], "alog": [128, 4], "dtb": [128, 4], "gnw": [128, 1],
    "lre_p": [128, 8], "lim_p": [128, 8], "ls_p": [128, 8],
    "lre_f": [128, 1024], "lim_f": [128, 1024], "ls_f": [128, 1024],
    "BTr": [128, 1024], "BTi": [128, 1024], "CTr": [128, 1024], "CTi": [128, 1024],
    "s5d": [128, 2], "s5nw": [128, 2], "glu": [256, 256],
}
CSHAPES = {"identf": ([128, 128], F32), "identb": ([128, 128], BF16), "tri": ([128, 128], F32),
           "gmask": ([128, 128], F32), "strict": ([128, 128], F32), "iota": ([128, 128], F32),
           "mask32": ([32, 128], F32), "hreset": ([64, 512], F32),
           "gmask4": ([128, 512], F32), "strict4": ([128, 512], F32)}


class Ctx:
    pass


def load_weight_bf16(S, name, w_dram, kts, N, dst, dst_off, stage, kt_per, col0=0, scale=None, np_=128):
    wv = w_dram.rearrange("(kt p) n -> p kt n", p=128)
    assert kt_per * N <= 1024
    cnt = getattr(S, "_wl", 0)
    for i0 in range(0, len(kts), kt_per):
        grp = kts[i0:i0 + kt_per]
        sl = cnt % 2
        cnt += 1
        sg = stage[sl][:, 0:len(grp) * N].rearrange("p (k n) -> p k n", k=len(grp))
        S.dma("wst%d" % sl, sg, wv[:, grp[0]:grp[0] + len(grp), col0:col0 + N], writes=[("wstage", sl)])
        for i, kt in enumerate(grp):
            eng = "dve" if (cnt + i) % 2 == 0 else "pool"
            if scale is None:
                S.op(eng, lambda h, sg=sg, i=i, kt=kt: h.tensor_copy(dst[:, kt - dst_off, 0:N], sg[:, i, :]),
                     reads=[("wstage", sl)], writes=[name])
            else:
                S.op(eng, lambda h, sg=sg, i=i, kt=kt: h.tensor_scalar(dst[:, kt - dst_off, 0:N], sg[:, i, :], scale[:, kt:kt + 1], None, ALU.mult),
                     reads=[("wstage", sl), "nwcol"], writes=[name])
    S._wl = cnt


def rmsnorm_T(S, C, h, hkey, nws, nwkey):
    S.op("act", lambda e: e.activation(C.junk[:], h[:], AF.Square, scale=1.0 / 32.0, accum_out=C.ss[:]),
         reads=[hkey], writes=["junk", "hn", "ss"])
    S.op("act", lambda e: e.activation(C.lnv[:], C.ss[:], AF.Ln, bias=C.epst[:, 0:1]), reads=["ss", "epst"], writes=["lnv"])
    S.op("act", lambda e: e.activation(C.rstd[:], C.lnv[:], AF.Exp, scale=-0.5), reads=["lnv"], writes=["rstd"])
    if nws is None:
        S.op("dve", lambda e: e.tensor_scalar(C.hn[:], h[:], C.rstd[:, 0:1], None, ALU.mult), reads=[hkey, "rstd"], writes=["hn"])
    else:
        S.op("dve", lambda e: e.scalar_tensor_tensor(C.hn[:], h[:], C.rstd[:, 0:1], nws[:], ALU.mult, ALU.mult),
             reads=[hkey, "rstd", nwkey], writes=["hn"])
    for kt in range(8):
        S.op("pe", lambda e, kt=kt: e.transpose(C.tp[:, kt, :], C.hn[:, kt * 128:(kt + 1) * 128], C.idb[:]),
             reads=["hn", "idb"], writes=["tp"])
    S.op("act", lambda e: e.copy(C.hnT[:], C.tp[:]), reads=["tp"], writes=["hnT"])


def sincos_tables(S, C, ang, angkey, sin_out, cos_out, okeys, kint, tmps, tkeys):
    kf, s2, s4 = tmps
    kk, k2, k4 = tkeys
    S.op("dve", lambda e: e.tensor_scalar(kint, ang, 1.0 / TWO_PI, None, ALU.mult), reads=[angkey], writes=["kint"])
    S.op("dve", lambda e: e.tensor_copy(kf, kint), reads=["kint"], writes=[kk])
    S.op("dve", lambda e: e.scalar_tensor_tensor(kf, kf, -TWO_PI, ang, ALU.mult, ALU.add), reads=[kk, angkey], writes=[kk])
    S.op("act", lambda e: e.activation(s4, kf, AF.Sin, scale=0.25), reads=[kk], writes=[k4])
    S.op("act", lambda e: e.activation(s2, kf, AF.Sin, scale=0.5), reads=[kk], writes=[k2])
    S.op("dve", lambda e: e.tensor_tensor(s4, s4, s4, ALU.mult), reads=[k4], writes=[k4])
    S.op("dve", lambda e: e.tensor_scalar(s4, s4, -2.0, 1.0, ALU.mult, ALU.add), reads=[k4], writes=[k4])
    S.op("dve", lambda e: e.scalar_tensor_tensor(sin_out, s2, 2.0, s4, ALU.mult, ALU.mult), reads=[k2, k4], writes=[okeys[0]])
    S.op("dve", lambda e: e.tensor_tensor(s2, s2, s2, ALU.mult), reads=[k2, okeys[0]], writes=[k2])
    S.op("dve", lambda e: e.tensor_scalar(cos_out, s2, -2.0, 1.0, ALU.mult, ALU.add), reads=[k2], writes=[okeys[1]])


def s5_setup(S, C, P):
    T = C.scr
    def ld(tile_ap, name, key):
        S.dma("ld_" + key, tile_ap, P[name], writes=[key])
    ld(C.lre_p[:], "lre_p", "lre_p"); ld(C.lim_p[:], "lim_p", "lim_p"); ld(C.ls_p[:], "ls_p", "ls_p")
    ld(C.s5d[:], "s5d", "s5d"); ld(C.s5nw[:], "s5nw", "s5nw")
    S.op("act", lambda e: e.activation(C.dtp[:], C.ls_p[:], AF.Exp), reads=["ls_p"], writes=["dtp"])
    S.op("dve", lambda e: e.tensor_scalar(C.lrep[:], C.lre_p[:], -1e-4, None, ALU.min), reads=["lre_p"], writes=["lrep"])
    S.op("dve", lambda e: e.tensor_tensor(C.magp[:], C.lrep[:], C.dtp[:], ALU.mult), reads=["lrep", "dtp"], writes=["magp"])
    S.op("act", lambda e: e.activation(C.magp[:], C.magp[:], AF.Exp), reads=["magp"], writes=["magp"])
    S.op("dve", lambda e: e.tensor_tensor(C.thp[:], C.lim_p[:], C.dtp[:], ALU.mult), reads=["lim_p", "dtp"], writes=["thp"])
    ang = T[0]
    for j in range(8):
        S.op("dve", lambda e, j=j: e.tensor_scalar(ang[:, j * 128:(j + 1) * 128], C.iota[:], C.thp[:, j:j + 1], None, ALU.mult),
             reads=["iota", "thp"], writes=["scr0"])
    sincos_tables(S, C, ang[:], "scr0", C.tabS[:].rearrange("p a b -> p (a b)"), C.tabC[:].rearrange("p a b -> p (a b)"),
                  ["tabS", "tabC"], C.kint[:], [T[1][:], T[2][:], T[3][:]], ["scr1", "scr2", "scr3"])
    S.op("dve", lambda e: e.tensor_scalar(C.a128[:], C.thp[:], 128.0, None, ALU.mult), reads=["thp"], writes=["a128"])
    sincos_tables(S, C, C.a128[:], "a128", C.se[:], C.ce[:], ["se", "ce"], C.kint[:, 0:8], [C.t128[:], C.c1[:], C.c2[:]], ["t128", "c1", "c2"])
    lre, lim, ls, t3, t4, t5, t6, t7 = T[2], T[3], T[4], T[5], T[6], T[7], T[8], T[9]
    ld(lre[:], "lre_f", "scr2"); ld(lim[:], "lim_f", "scr3"); ld(ls[:], "ls_f", "scr4")
    S.op("act", lambda e: e.activation(ls[:], ls[:], AF.Exp), reads=["scr4"], writes=["scr4"])
    S.op("dve", lambda e: e.tensor_scalar(lre[:], lre[:], -1e-4, None, ALU.min), reads=["scr2"], writes=["scr2"])
    S.op("dve", lambda e: e.tensor_tensor(t3[:], lre[:], ls[:], ALU.mult), reads=["scr2", "scr4"], writes=["scr5"])
    S.op("act", lambda e: e.activation(t3[:], t3[:], AF.Exp), reads=["scr5"], writes=["scr5"])
    S.op("dve", lambda e: e.tensor_tensor(t4[:], lim[:], ls[:], ALU.mult), reads=["scr3", "scr4"], writes=["scr6"])
    sincos_tables(S, C, t4[:], "scr6", t5[:], t6[:], ["scr7", "scr8"], C.kint[:], [t7[:], T[0][:], T[1][:]], ["scr9", "scr0", "scr1"])
    S.op("dve", lambda e: e.tensor_tensor(t5[:], t5[:], t3[:], ALU.mult), reads=["scr7", "scr5"], writes=["scr7"])
    S.op("dve", lambda e: e.tensor_tensor(t6[:], t6[:], t3[:], ALU.mult), reads=["scr8", "scr5"], writes=["scr8"])
    S.op("dve", lambda e: e.tensor_scalar(t6[:], t6[:], -1.0, None, ALU.add), reads=["scr8"], writes=["scr8"])
    S.op("dve", lambda e: e.tensor_tensor(t3[:], lre[:], lre[:], ALU.mult), reads=["scr2", "scr7", "scr8"], writes=["scr5"])
    S.op("dve", lambda e: e.tensor_tensor(t4[:], lim[:], lim[:], ALU.mult), reads=["scr3"], writes=["scr6"])
    S.op("dve", lambda e: e.tensor_tensor(t3[:], t3[:], t4[:], ALU.add), reads=["scr5", "scr6"], writes=["scr5"])
    S.op("dve", lambda e: e.reciprocal(t3[:], t3[:]), reads=["scr5"], writes=["scr5"])
    S.op("dve", lambda e: e.tensor_tensor(t4[:], t6[:], lre[:], ALU.mult), reads=["scr8", "scr2", "scr6"], writes=["scr6"])
    S.op("dve", lambda e: e.tensor_tensor(t7[:], t5[:], lim[:], ALU.mult), reads=["scr7", "scr3"], writes=["scr9"])
    S.op("dve", lambda e: e.tensor_tensor(t4[:], t4[:], t7[:], ALU.add), reads=["scr6", "scr9"], writes=["scr6"])
    S.op("dve", lambda e: e.tensor_tensor(t4[:], t4[:], t3[:], ALU.mult), reads=["scr6", "scr5"], writes=["scr6"])
    S.op("dve", lambda e: e.tensor_tensor(t7[:], t5[:], lre[:], ALU.mult), reads=["scr7", "scr2", "scr6"], writes=["scr9"])
    S.op("dve", lambda e: e.tensor_tensor(ls[:], t6[:], lim[:], ALU.mult), reads=["scr8", "scr3"], writes=["scr4"])
    S.op("dve", lambda e: e.tensor_tensor(t7[:], t7[:], ls[:], ALU.subtract), reads=["scr9", "scr4"], writes=["scr9"])
    S.op("dve", lambda e: e.tensor_tensor(t7[:], t7[:], t3[:], ALU.mult), reads=["scr9", "scr5"], writes=["scr9"])
    br, bi = lre, lim
    ld(br[:], "BTr", "scr2"); ld(bi[:], "BTi", "scr3")
    flat = lambda t: t[:].rearrange("p a b -> p (a b)")
    S.op("dve", lambda e: e.tensor_tensor(t5[:], t4[:], br[:], ALU.mult), reads=["scr6", "scr2", "scr7"], writes=["scr7"])
    S.op("dve", lambda e: e.tensor_tensor(t6[:], t7[:], bi[:], ALU.mult), reads=["scr9", "scr3", "scr8"], writes=["scr8"])
    S.op("dve", lambda e: e.tensor_tensor(flat(C.BTre), t5[:], t6[:], ALU.subtract), reads=["scr7", "scr8"], writes=["BTre"])
    S.op("dve", lambda e: e.tensor_tensor(t5[:], t4[:], bi[:], ALU.mult), reads=["scr6", "scr3", "BTre"], writes=["scr7"])
    S.op("dve", lambda e: e.tensor_tensor(t6[:], t7[:], br[:], ALU.mult), reads=["scr9", "scr2", "BTre"], writes=["scr8"])
    S.op("dve", lambda e: e.tensor_tensor(flat(C.BTim), t5[:], t6[:], ALU.add), reads=["scr7", "scr8"], writes=["BTim"])
    ld(t3[:], "CTr", "scr5"); ld(t4[:], "CTi", "scr6")
    S.op("dve", lambda e: e.tensor_copy(flat(C.CTre), t3[:]), reads=["scr5"], writes=["CTre"])
    S.op("dve", lambda e: e.tensor_scalar(flat(C.CTimn), t4[:], -1.0, None, ALU.mult), reads=["scr6"], writes=["CTimn"])
    gst = T[0][:, 0:512].rearrange("p (k n) -> p k n", k=2)
    S.dma("ld_glu", gst, P["glu"].rearrange("(k p) n -> p k n", p=128), reads=["tabS", "tabC"], writes=["scr0"])
    S.op("dve", lambda e: e.tensor_copy(C.glus[:], gst), reads=["scr0"], writes=["glus"])
    S.op("dve", lambda e: e.memset(C.zst[:], 0.0), writes=["zst"])


def s5_chunk(S, C, first):
    A = C.bank
    uP = A[0]
    for m in range(2):
        for kt in range(8):
            S.op("pe", lambda e, m=m, kt=kt: e.matmul(uP[:, m, :], C.wu[:, kt, m * 128:(m + 1) * 128], C.hnT[:, kt, :],
                                                     start=(kt == 0), stop=(kt == 7)),
                 reads=["wu", "hnT"], writes=["bank0"])
    S.op("act", lambda e: e.copy(C.uF[:], uP[:, 0:2, :]), reads=["bank0"], writes=["uF"])
    S.op("dve", lambda e: e.tensor_copy(C.uB[:], uP[:, 0:2, :]), reads=["bank0"], writes=["uB"])
    for j in range(8):
        S.op("pe", lambda e, j=j: e.matmul(A[2 + j // 4][:, j % 4, :], C.BTre[:, j, :], C.uB[:, j // 4, :], start=True, stop=True),
             reads=["BTre", "uB"], writes=["bank%d" % (2 + j // 4)])
        S.op("pe", lambda e, j=j: e.matmul(A[4 + j // 4][:, j % 4, :], C.BTim[:, j, :], C.uB[:, j // 4, :], start=True, stop=True),
             reads=["BTim", "uB"], writes=["bank%d" % (4 + j // 4)])
    f4 = lambda t, hf: t[:, hf * 4:(hf + 1) * 4, :].rearrange("p a b -> p (a b)")
    fb = lambda b: b[:].rearrange("p a b -> p (a b)")
    for hf in range(2):
        bre, bim = A[2 + hf], A[4 + hf]
        kre, kim = "bank%d" % (2 + hf), "bank%d" % (4 + hf)
        S.op("dve", lambda e, hf=hf, bre=bre: e.tensor_tensor(f4(C.wre, hf), fb(bre), f4(C.tabC, hf), ALU.mult), reads=[kre, "tabC"], writes=["wre"])
        S.op("dve", lambda e, hf=hf, bim=bim: e.tensor_tensor(f4(C.t2, hf), fb(bim), f4(C.tabS, hf), ALU.mult), reads=[kim, "tabS"], writes=["t2"])
        S.op("pool", lambda e, hf=hf: e.tensor_tensor(f4(C.wre, hf), f4(C.wre, hf), f4(C.t2, hf), ALU.add), reads=["wre", "t2"], writes=["wre"])
        S.op("dve", lambda e, hf=hf, bim=bim: e.tensor_tensor(f4(C.wim, hf), fb(bim), f4(C.tabC, hf), ALU.mult), reads=[kim, "tabC"], writes=["wim"])
        S.op("dve", lambda e, hf=hf, bre=bre: e.tensor_tensor(f4(C.t2, hf), fb(bre), f4(C.tabS, hf), ALU.mult), reads=[kre, "tabS", "wre"], writes=["t2"])
        S.op("pool", lambda e, hf=hf: e.tensor_tensor(f4(C.wim, hf), f4(C.wim, hf), f4(C.t2, hf), ALU.subtract), reads=["wim", "t2"], writes=["wim"])
    zl_re, zl_im = C.zst[:, 0, :], C.zst[:, 1, :]
    S.op("dve", lambda e: e.tensor_tensor(C.c1[:], C.ce[:], zl_re, ALU.mult), reads=["ce", "zst"], writes=["c1"])
    S.op("dve", lambda e: e.tensor_tensor(C.c2[:], C.se[:], zl_im, ALU.mult), reads=["se", "zst"], writes=["c2"])
    S.op("dve", lambda e: e.tensor_tensor(C.c1[:], C.c1[:], C.c2[:], ALU.subtract), reads=["c1", "c2"], writes=["c1"])
    S.op("dve", lambda e: e.tensor_tensor(C.c3[:], C.ce[:], zl_im, ALU.mult), reads=["ce", "zst"], writes=["c3"])
    S.op("dve", lambda e: e.tensor_tensor(C.c2[:], C.se[:], zl_re, ALU.mult), reads=["se", "zst", "c1"], writes=["c2"])
    S.op("dve", lambda e: e.tensor_tensor(C.c3[:], C.c3[:], C.c2[:], ALU.add), reads=["c3", "c2"], writes=["c3"])
    for j in range(8):
        dj = C.magp[:, j:j + 1].to_broadcast([128, 128])
        S.op("dve", lambda e, j=j, dj=dj: e.tensor_tensor_scan(C.wre[:, j, :], dj, C.wre[:, j, :], C.c1[:, j:j + 1], ALU.mult, ALU.add),
             reads=["magp", "wre", "c1"], writes=["wre"])
        S.op("dve", lambda e, j=j, dj=dj: e.tensor_tensor_scan(C.wim[:, j, :], dj, C.wim[:, j, :], C.c3[:, j:j + 1], ALU.mult, ALU.add),
             reads=["magp", "wim", "c3"], writes=["wim"])
    S.op("pool", lambda e: e.tensor_copy(C.zst[:, 0, :], C.wre[:, :, 127]), reads=["wre"], writes=["zst"])
    S.op("pool", lambda e: e.tensor_copy(C.zst[:, 1, :], C.wim[:, :, 127]), reads=["wim"], writes=["zst"])
    fl = lambda t: t[:].rearrange("p a b -> p (a b)")
    tB = C.s5tmp[:]
    S.op("dve", lambda e: e.tensor_tensor(fl(C.t2), fl(C.wre), fl(C.tabC), ALU.mult), reads=["wre", "tabC"], writes=["t2"])
    S.op("pool", lambda e: e.tensor_tensor(tB, fl(C.wim), fl(C.tabS), ALU.mult), reads=["wim", "tabS"], writes=["s5tmp"])
    S.op("dve", lambda e: e.tensor_tensor(fl(C.xre), fl(C.t2), tB, ALU.subtract), reads=["t2", "s5tmp"], writes=["xre"])
    S.op("pool", lambda e: e.tensor_tensor(tB, fl(C.wre), fl(C.tabS), ALU.mult), reads=["wre", "tabS", "xre"], writes=["s5tmp"])
    S.op("dve", lambda e: e.tensor_tensor(fl(C.t2), fl(C.wim), fl(C.tabC), ALU.mult), reads=["wim", "tabC", "xre"], writes=["t2"])
    S.op("pool", lambda e: e.tensor_tensor(fl(C.xim), tB, fl(C.t2), ALU.add), reads=["t2", "s5tmp"], writes=["xim"])
    yP = A[0][:, 2:4, :]
    for m in range(2):
        for i, j in enumerate(range(4 * m, 4 * m + 4)):
            S.op("pe", lambda e, m=m, j=j, i=i: e.matmul(yP[:, m, :], C.CTre[:, j, :], C.xre[:, j, :], start=(i == 0), stop=False),
                 reads=["CTre", "xre"], writes=["bank0"])
            S.op("pe", lambda e, m=m, j=j, i=i: e.matmul(yP[:, m, :], C.CTimn[:, j, :], C.xim[:, j, :], start=False, stop=(i == 3)),
                 reads=["CTimn", "xim"], writes=["bank0"])
    f2 = lambda t: t[:].rearrange("p a b -> p (a b)")
    for m in range(2):
        S.op("dve", lambda e, m=m: e.scalar_tensor_tensor(C.yv[:, m, :], C.uF[:, m, :], C.s5d[:, m:m + 1], yP[:, m, :], ALU.mult, ALU.add),
             reads=["uF", "s5d", "bank0"], writes=["yv"])
    S.op("dve", lambda e: e.tensor_tensor(f2(C.g1), f2(C.yv), f2(C.yv), ALU.mult), reads=["yv"], writes=["g1"])
    S.op("dve", lambda e: e.tensor_scalar(f2(C.g1), f2(C.g1), 0.044715, 1.0, ALU.mult, ALU.add), reads=["g1"], writes=["g1"])
    S.op("dve", lambda e: e.tensor_tensor(f2(C.g1), f2(C.g1), f2(C.yv), ALU.mult), reads=["g1", "yv"], writes=["g1"])
    S.op("act", lambda e: e.activation(f2(C.g1), f2(C.g1), AF.Exp, scale=-GELU_C), reads=["g1"], writes=["g1"])
    S.op("dve", lambda e: e.tensor_scalar(f2(C.g1), f2(C.g1), 1.0, None, ALU.add), reads=["g1"], writes=["g1"])
    S.op("dve", lambda e: e.reciprocal(f2(C.g1), f2(C.g1)), reads=["g1"], writes=["g1"])
    S.op("dve", lambda e: e.tensor_tensor(f2(C.zg), f2(C.yv), f2(C.g1), ALU.mult), reads=["g1", "yv"], writes=["zg"])
    S.op("dve", lambda e: e.tensor_copy(f2(C.zgb), f2(C.zg)), reads=["zg"], writes=["zgb"])
    gP = A[1]
    for m2 in range(2):
        for k in range(2):
            S.op("pe", lambda e, m2=m2, k=k: e.matmul(gP[:, m2, :], C.glus[:, k, m2 * 128:(m2 + 1) * 128], C.zgb[:, k, :],
                                                     start=(k == 0), stop=(k == 1)),
                 reads=["glus", "zgb"], writes=["bank1"])
    S.op("act", lambda e: e.activation(f2(C.g1), gP[:, 0:2, :].rearrange("p a b -> p (a b)"), AF.Exp, scale=-1.0), reads=["bank1"], writes=["g1"])
    S.op("dve", lambda e: e.tensor_scalar(f2(C.g1), f2(C.g1), 1.0, None, ALU.add), reads=["g1"], writes=["g1"])
    S.op("dve", lambda e: e.reciprocal(f2(C.g1), f2(C.g1)), reads=["g1"], writes=["g1"])
    S.op("dve", lambda e: e.tensor_tensor(f2(C.o5), f2(C.zg), f2(C.g1), ALU.mult), reads=["g1", "zg"], writes=["o5"])
    S.op("dve", lambda e: e.tensor_tensor(f2(C.zgb), f2(C.o5), f2(C.o5), ALU.mult), reads=["o5"], writes=["zgb"])
    sP = A[1][:, 2:4, :]
    for m in range(2):
        S.op("pe", lambda e, m=m: e.matmul(sP[:, 0, :], C.ones256b[:], C.zgb[:, m, :], start=(m == 0), stop=(m == 1)),
             reads=["ones256b", "zgb"], writes=["bank1"])
    S.op("act", lambda e: e.activation(C.rs5[:], sP[:, 0, :], AF.Ln, bias=C.epst[:, 0:1]), reads=["bank1", "epst"], writes=["rs5"])
    S.op("act", lambda e: e.activation(C.rs5[:], C.rs5[:], AF.Exp, scale=-0.5), reads=["rs5"], writes=["rs5"])
    for m in range(2):
        S.op("dve", lambda e, m=m: e.scalar_tensor_tensor(C.mixT[:, 6 + m, :], C.o5[:, m, :], C.s5nw[:, m:m + 1], C.rs5[:], ALU.mult, ALU.mult),
             reads=["o5", "s5nw", "rs5"], writes=["mixT"])


def hgrn_setup(S, C, P, layer):
    S.dma("ld_hnw", C.hnw[:], P["hnw"], writes=["hnw"])
    S.op("dve", lambda e: e.memset(C.ones64[:], 1.0 / 64.0), writes=["ones64"])
    S.op("dve", lambda e: e.memset(C.Sst[:], 0.0), writes=["Sst"])
    if layer == 0:
        S.op("dve", lambda e: e.memset(C.oml[:], 1.0), writes=["oml"])
    else:
        S.dma("ld_lbl0", C.lbl0[:], P["lbl0"], writes=["lbl0"])
        S.dma("ld_lbl1", C.lbl1[:], P["lbl1"], writes=["lbl1"])
        S.op("dve", lambda e: e.tensor_tensor(C.oml[:], C.lbl0[:], C.lbl1[:], ALU.subtract), reads=["lbl0", "lbl1"], writes=["oml"])
        S.op("act", lambda e: e.activation(C.oml[:], C.oml[:], AF.Exp), reads=["oml"], writes=["oml"])
        S.op("dve", lambda e: e.tensor_scalar(C.oml[:], C.oml[:], 1.0, None, ALU.add), reads=["oml"], writes=["oml"])
        S.op("dve", lambda e: e.reciprocal(C.oml[:], C.oml[:]), reads=["oml"], writes=["oml"])
        S.op("dve", lambda e: e.tensor_scalar(C.oml[:], C.oml[:], -1.0, 1.0, ALU.mult, ALU.add), reads=["oml"], writes=["oml"])


def hgrn_chunk(S, C, first):
    A = C.bank
    H = lambda t: t[0:64]
    fl = lambda t: t[:].rearrange("p a b -> p (a b)")
    fb = lambda b: b[0:64].rearrange("p a b -> p (a b)")
    for g in range(4):
        for h in range(4):
            c0 = g * 256 + h * 64
            for kt in range(8):
                S.op("pe", lambda e, g=g, h=h, kt=kt, c0=c0: e.matmul(A[g][0:64, h, :], C.wH[:, kt, c0:c0 + 64], C.hnT[:, kt, :],
                                                                     start=(kt == 0), stop=(kt == 7)),
                     reads=["wH", "hnT"], writes=["bank%d" % g])
    S.op("act", lambda e: e.activation(fl(C.he1), fb(A[1]), AF.Exp, scale=-1.0), reads=["bank1"], writes=["he1"])
    S.op("dve", lambda e: e.tensor_scalar(fl(C.hden), fl(C.he1), 1.0, None, ALU.add), reads=["he1"], writes=["hden"])
    S.op("dve", lambda e: e.reciprocal(fl(C.hden), fl(C.hden)), reads=["hden"], writes=["hden"])
    S.op("dve", lambda e: e.tensor_tensor(fl(C.he1), fl(C.he1), fl(C.hden), ALU.mult), reads=["he1", "hden"], writes=["he1"])
    S.op("dve", lambda e: e.tensor_tensor(C.hk[:], C.he1[:], C.oml[:].unsqueeze(2).to_broadcast([64, 4, 128]), ALU.mult),
         reads=["he1", "oml"], writes=["hk"])
    S.op("dve", lambda e: e.tensor_scalar(fl(C.hden), fl(C.hk), -1.0, 1.0, ALU.mult, ALU.add), reads=["hk"], writes=["hden"])
    S.op("act", lambda e: e.activation(fl(C.hden), fl(C.hden), AF.Ln), reads=["hden"], writes=["hden"])
    S.op("dve", lambda e: e.tensor_tensor_scan(fl(C.hc), C.hreset[:], fl(C.hden), 0.0, ALU.mult, ALU.add), reads=["hreset", "hden"], writes=["hc"])
    S.op("act", lambda e: e.activation(fl(C.hec), fl(C.hc), AF.Exp), reads=["hc"], writes=["hec"])
    S.op("act", lambda e: e.activation(fl(C.hof), fl(C.hc), AF.Exp, scale=-1.0), reads=["hc"], writes=["hof"])
    S.op("act", lambda e: e.activation(fl(C.he1), fb(A[0]), AF.Exp, scale=-1.0), reads=["bank0"], writes=["he1"])
    S.op("dve", lambda e: e.tensor_scalar(fl(C.he1), fl(C.he1), 1.0, None, ALU.add), reads=["he1"], writes=["he1"])
    S.op("dve", lambda e: e.reciprocal(fl(C.he1), fl(C.he1)), reads=["he1"], writes=["he1"])
    S.op("dve", lambda e: e.tensor_tensor(fl(C.hqt), fb(A[0]), fl(C.he1), ALU.mult), reads=["bank0", "he1"], writes=["hqt"])
    S.op("dve", lambda e: e.tensor_tensor(fl(C.hqt), fl(C.hqt), fl(C.hec), ALU.mult), reads=["hqt", "hec"], writes=["hqt"])
    S.op("dve", lambda e: e.tensor_tensor(fl(C.hkt), fl(C.hk), fl(C.hof), ALU.mult), reads=["hk", "hof"], writes=["hkt"])
    r4 = lambda t: t[:].rearrange("p h (n c) -> p h n c", c=32)
    S.op("dve", lambda e: e.tensor_tensor(r4(C.hkh), r4(C.hkt), r4(C.hec)[:, :, :, 31:32].to_broadcast([64, 4, 4, 32]), ALU.mult),
         reads=["hkt", "hec"], writes=["hkh"])
    S.op("act", lambda e: e.copy(fl(C.hv), fb(A[2])), reads=["bank2"], writes=["hv"])
    S.op("act", lambda e: e.activation(fl(C.he1), fb(A[3]), AF.Exp, scale=-1.0), reads=["bank3"], writes=["he1"])
    S.op("dve", lambda e: e.tensor_scalar(fl(C.he1), fl(C.he1), 1.0, None, ALU.add), reads=["he1"], writes=["he1"])
    S.op("dve", lambda e: e.reciprocal(fl(C.he1), fl(C.he1)), reads=["he1"], writes=["he1"])
    S.op("dve", lambda e: e.tensor_tensor(fl(C.hgs), fb(A[3]), fl(C.he1), ALU.mult), reads=["bank3", "he1"], writes=["hc"])
    scP = A[4][0:32, 0, :].rearrange("p (h t) -> p h t", h=4)
    trP = A[5][0:32]
    oP = A[2]
    dSP = A[3][0:64, 0:2, :].rearrange("p a (b c) -> p (a b) c", c=64)
    for n in range(4):
        sl = slice(32 * n, 32 * n + 32)
        for h in range(4):
            S.op("pe", lambda e, h=h, sl=sl: e.matmul(scP[:, h, :], C.hkt[:, h, sl], C.hqt[:, h, sl], start=True, stop=True),
                 reads=["hkt", "hqt"], writes=["bank4"])
        S.op("dve", lambda e: e.tensor_tensor(C.hscm[:], scP, C.mask32[:].rearrange("p (h t) -> p h t", h=4), ALU.mult),
             reads=["bank4", "mask32"], writes=["hscm"])
        for h in range(4):
            S.op("pe", lambda e, h=h, sl=sl: e.transpose(trP[:, h, 0:64], C.hkh[:, h, sl], C.identf[0:64, 0:64]),
                 reads=["hkh", "identf"], writes=["bank5"])
            S.op("pe", lambda e, h=h, sl=sl: e.transpose(trP[:, h, 64:128], C.hv[:, h, sl], C.identf[0:64, 0:64]),
                 reads=["hv", "identf"], writes=["bank5"])
        S.op("act", lambda e: e.copy(C.hkvT[:], trP), reads=["bank5"], writes=["hkvT"])
        for h in range(4):
            S.op("pe", lambda e, h=h, sl=sl: e.matmul(oP[0:64, h, sl], C.hkvT[:, h, 64:128], C.hscm[:, h, :], start=True, stop=False),
                 reads=["hkvT", "hscm"], writes=["bank2"])
            S.op("pe", lambda e, h=h, sl=sl: e.matmul(oP[0:64, h, sl], C.Sst[:, h, :], C.hqt[:, h, sl], start=False, stop=True),
                 reads=["Sst", "hqt"], writes=["bank2"])
        for h in range(4):
            S.op("pe", lambda e, h=h: e.matmul(dSP[:, h, :], C.hkvT[:, h, 0:64], C.hkvT[:, h, 64:128], start=True, stop=True),
                 reads=["hkvT"], writes=["bank3"])
        S.op("dve", lambda e, n=n: e.tensor_tensor(C.Sst[:], C.Sst[:], C.hec[:, :, 32 * n + 31:32 * n + 32].to_broadcast([64, 4, 64]), ALU.mult),
             reads=["Sst", "hec"], writes=["Sst"])
        S.op("dve", lambda e: e.tensor_tensor(C.Sst[:], C.Sst[:], dSP, ALU.add), reads=["Sst", "bank3"], writes=["Sst"])
    S.op("act", lambda e: e.copy(fl(C.hof), fb(oP)), reads=["bank2"], writes=["hof"])
    S.op("dve", lambda e: e.tensor_tensor(fl(C.he1), fl(C.hof), fl(C.hof), ALU.mult), reads=["hof"], writes=["he1"])
    msP = fb(A[4])
    S.op("pe", lambda e: e.matmul(msP, C.ones64[:], fl(C.he1), start=True, stop=True), reads=["ones64", "he1"], writes=["bank4"])
    S.op("act", lambda e: e.activation(fl(C.hden), msP, AF.Ln, bias=C.epst[0:64, 0:1]), reads=["bank4", "epst"], writes=["hden"])
    S.op("act", lambda e: e.activation(fl(C.hden), fl(C.hden), AF.Exp, scale=-0.5), reads=["hden"], writes=["hden"])
    S.op("dve", lambda e: e.scalar_tensor_tensor(fl(C.hof), fl(C.hof), C.hnw[:, 0:1], fl(C.hden), ALU.mult, ALU.mult),
         reads=["hof", "hnw", "hden"], writes=["hof"])
    S.op("dve", lambda e: e.tensor_tensor(fl(C.mixH), fl(C.hof), fl(C.hgs), ALU.mult), reads=["hof", "hc"], writes=["mixH"])


def gdn_setup(S, C, P):
    S.dma("ld_convw", C.convw[:].rearrange("p a b -> p (a b)"), P["convw"], writes=["convw"])
    S.dma("ld_alog", C.aneg[:], P["alog"], writes=["aneg"])
    S.dma("ld_dtb", C.dtb[:], P["dtb"], writes=["dtb"])
    S.dma("ld_gnw", C.gnw[:], P["gnw"], writes=["gnw"])
    S.op("act", lambda e: e.activation(C.aneg[:], C.aneg[:], AF.Exp), reads=["aneg"], writes=["aneg"])
    S.op("dve", lambda e: e.tensor_scalar(C.aneg[:], C.aneg[:], -1.0, None, ALU.mult), reads=["aneg"], writes=["aneg"])
    S.op("dve", lambda e: e.memset(C.ones128f[:], 1.0 / 128.0), writes=["ones128f"])
    S.op("dve", lambda e: e.memset(C.gx[:], 0.0), writes=["gx"])
    S.op("dve", lambda e: e.memset(C.gS[:], 0.0), writes=["gS"])


def gdn_chunk(S, C, first):
    A = C.bank
    f3 = lambda t: t[:].rearrange("p a b -> p (a b)")
    bc4 = lambda small: small.unsqueeze(2).to_broadcast([128, 4, 128])
    for g in range(4):
        for h in range(4):
            c0 = g * 512 + h * 128
            for kt in range(8):
                S.op("pe", lambda e, g=g, h=h, kt=kt, c0=c0: e.matmul(A[g][:, h, :], C.wG[:, kt, c0:c0 + 128], C.hnT[:, kt, :],
                                                                     start=(kt == 0), stop=(kt == 7)),
                     reads=["wG", "hnT"], writes=["bank%d" % g])
    abP = A[4][:, 0, 0:8]
    for kt in range(8):
        S.op("pe", lambda e, kt=kt: e.matmul(abP, C.hnT[:, kt, :], C.wab[:, kt, :], start=(kt == 0), stop=(kt == 7)),
             reads=["hnT", "wab"], writes=["bank4"])
    for g in range(3):
        S.op("act", lambda e, g=g: e.copy(C.gx[:, 4 * g:4 * g + 4, 3:131], A[g][:]), reads=["bank%d" % g], writes=["gx"])
    S.op("act", lambda e: e.activation(f3(C.gzs), f3(A[3]), AF.Exp, scale=-1.0), reads=["bank3"], writes=["gzs"])
    S.op("dve", lambda e: e.tensor_scalar(f3(C.gzs), f3(C.gzs), 1.0, None, ALU.add), reads=["gzs"], writes=["gzs"])
    S.op("dve", lambda e: e.reciprocal(f3(C.gzs), f3(C.gzs)), reads=["gzs"], writes=["gzs"])
    S.op("dve", lambda e: e.tensor_tensor(f3(C.gzs), f3(A[3]), f3(C.gzs), ALU.mult), reads=["bank3", "gzs"], writes=["gzs"])
    S.op("dve", lambda e: e.tensor_tensor(C.ga[:], A[4][:, 0, 0:4], C.dtb[:], ALU.add), reads=["bank4", "dtb"], writes=["ga"])
    S.op("act", lambda e: e.activation(C.ga[:], C.ga[:], AF.Exp), reads=["ga"], writes=["ga"])
    S.op("dve", lambda e: e.tensor_scalar(C.ga[:], C.ga[:], 1.0, None, ALU.add), reads=["ga"], writes=["ga"])
    S.op("act", lambda e: e.activation(C.ga[:], C.ga[:], AF.Ln), reads=["ga"], writes=["ga"])
    S.op("dve", lambda e: e.tensor_tensor(C.gg[:], C.ga[:], C.aneg[:], ALU.mult), reads=["ga", "aneg"], writes=["gg"])
    S.op("act", lambda e: e.activation(C.gbeta[:], A[4][:, 0, 4:8], AF.Exp, scale=-1.0), reads=["bank4"], writes=["gbeta"])
    S.op("dve", lambda e: e.tensor_scalar(C.gbeta[:], C.gbeta[:], 1.0, None, ALU.add), reads=["gbeta"], writes=["gbeta"])
    S.op("dve", lambda e: e.reciprocal(C.gbeta[:], C.gbeta[:]), reads=["gbeta"], writes=["gbeta"])
    cP = A[4][:, 1, 0:4]
    clP = A[4][:, 2, 0:4]
    S.op("pe", lambda e: e.matmul(cP, C.tri[:], C.gg[:], start=True, stop=True), reads=["tri", "gg"], writes=["bank4"])
    S.op("pe", lambda e: e.matmul(clP, C.onesf[:], C.gg[:], start=True, stop=True), reads=["onesf", "gg"], writes=["bank4"])
    S.op("act", lambda e: e.copy(C.gc[:], cP), reads=["bank4"], writes=["gc"])
    S.op("act", lambda e: e.activation(C.gec[:], cP, AF.Exp), reads=["bank4"], writes=["gec"])
    S.op("act", lambda e: e.activation(C.gecl[:], clP, AF.Exp), reads=["bank4"], writes=["gecl"])
    S.op("dve", lambda e: e.tensor_tensor(C.gedl[:], clP, C.gc[:], ALU.subtract), reads=["bank4", "gc"], writes=["gedl"])
    S.op("act", lambda e: e.activation(C.gedl[:], C.gedl[:], AF.Exp), reads=["gedl"], writes=["gedl"])
    S.op("dve", lambda e: e.tensor_tensor(C.gbec[:], C.gbeta[:], C.gec[:], ALU.mult), reads=["gbeta", "gec"], writes=["gbec"])
    S.op("dve", lambda e: e.tensor_scalar(C.gnb[:], C.gbeta[:], -1.0, None, ALU.mult), reads=["gbeta"], writes=["gnb"])
    cw = lambda j: C.convw[:, :, j:j + 1].to_broadcast([128, 12, 128])
    S.op("pool", lambda e: e.tensor_tensor(C.gcv[:], C.gx[:, :, 0:128], cw(0), ALU.mult), reads=["gx", "convw"], writes=["gcv"])
    for j in range(1, 4):
        S.op("pool", lambda e, j=j: e.tensor_tensor(C.gtmp[:], C.gx[:, :, j:j + 128], cw(j), ALU.mult), reads=["gx", "convw"], writes=["gtmp"])
        S.op("dve", lambda e: e.tensor_tensor(f3(C.gcv), f3(C.gcv), f3(C.gtmp), ALU.add), reads=["gcv", "gtmp"], writes=["gcv"])
    S.op("pool", lambda e: e.tensor_copy(C.gx[:, :, 0:3], C.gx[:, :, 128:131]), reads=["gx"], writes=["gx"])
    S.op("act", lambda e: e.activation(f3(C.gtmp), f3(C.gcv), AF.Exp, scale=-1.0), reads=["gcv"], writes=["gtmp"])
    S.op("dve", lambda e: e.tensor_scalar(f3(C.gtmp), f3(C.gtmp), 1.0, None, ALU.add), reads=["gtmp"], writes=["gtmp"])
    S.op("dve", lambda e: e.reciprocal(f3(C.gtmp), f3(C.gtmp)), reads=["gtmp"], writes=["gtmp"])
    S.op("dve", lambda e: e.tensor_tensor(f3(C.gcv), f3(C.gcv), f3(C.gtmp), ALU.mult), reads=["gcv", "gtmp"], writes=["gcv"])
    qk = C.gcv[:, 0:8, :]
    S.op("dve", lambda e: e.tensor_tensor(C.gtmp[:, 0:8, :], qk, qk, ALU.mult), reads=["gcv"], writes=["gtmp"])
    for i in range(2):
        S.op("pe", lambda e, i=i: e.matmul(f3(A[i]), C.onesf[:], C.gtmp[:, 4 * i:4 * i + 4, :].rearrange("p a b -> p (a b)"), start=True, stop=True),
             reads=["onesf", "gtmp"], writes=["bank%d" % i])
    for i in range(2):
        S.op("act", lambda e, i=i: e.activation(C.gtmp[:, 8 + 2 * i:10 + 2 * i, :].rearrange("p a b -> p (a b)")[:, 0:256], f3(A[i])[:, 0:256], AF.Ln, bias=C.epst[:, 0:1]),
             reads=["bank%d" % i, "epst"], writes=["gtmp"]) if False else None
    S.op("act", lambda e: e.activation(C.gtmp[:, 8:12, :].rearrange("p a b -> p (a b)"), f3(A[0]), AF.Ln, bias=C.epst[:, 0:1]), reads=["bank0", "epst"], writes=["gtmp"])
    S.op("act", lambda e: e.activation(C.gtmp[:, 8:12, :].rearrange("p a b -> p (a b)"), C.gtmp[:, 8:12, :].rearrange("p a b -> p (a b)"), AF.Exp, scale=-0.5), reads=["gtmp"], writes=["gtmp"])
    S.op("dve", lambda e: e.scalar_tensor_tensor(C.gcv[:, 0:4, :], C.gcv[:, 0:4, :], 128.0 ** -0.5, C.gtmp[:, 8:12, :], ALU.mult, ALU.mult),
         reads=["gcv", "gtmp"], writes=["gcv"])
    S.op("act", lambda e: e.activation(C.gtmp[:, 8:12, :].rearrange("p a b -> p (a b)"), f3(A[1]), AF.Ln, bias=C.epst[:, 0:1]), reads=["bank1", "epst", "gcv"], writes=["gtmp"])
    S.op("act", lambda e: e.activation(C.gtmp[:, 8:12, :].rearrange("p a b -> p (a b)"), C.gtmp[:, 8:12, :].rearrange("p a b -> p (a b)"), AF.Exp, scale=-0.5), reads=["gtmp"], writes=["gtmp"])
    S.op("dve", lambda e: e.tensor_tensor(C.gcv[:, 4:8, :], C.gcv[:, 4:8, :], C.gtmp[:, 8:12, :], ALU.mult), reads=["gcv", "gtmp"], writes=["gcv"])
    qn = lambda h: C.gcv[:, h, :]
    kn = lambda h: C.gcv[:, 4 + h, :]
    vf = lambda h: C.gcv[:, 8 + h, :]
    S.op("dve", lambda e: e.tensor_tensor(C.gdiag[:], C.identf[:].unsqueeze(1).to_broadcast([128, 4, 128]), bc4(C.gc[:]), ALU.mult),
         reads=["identf", "gc"], writes=["gdiag"])
    S.op("pe", lambda e: e.matmul(f3(A[2]), C.onesf[:], f3(C.gdiag), start=True, stop=True), reads=["onesf", "gdiag"], writes=["bank2"])
    S.op("pe", lambda e: e.matmul(f3(A[3]), C.onesf[:], f3(C.gdiag), start=True, stop=False), reads=["onesf", "gdiag"], writes=["bank3"])
    S.op("pe", lambda e: e.matmul(f3(A[3]), C.identf[:], C.gmask4[:], start=False, stop=True), reads=["identf", "gmask4"], writes=["bank3"])
    S.op("act", lambda e: e.activation(f3(C.gecb), f3(A[2]), AF.Exp), reads=["bank2"], writes=["gecb"])
    for h in range(4):
        S.op("act", lambda e, h=h: e.activation(C.gD[:, h, :], A[3][:, h, :], AF.Exp, scale=-1.0, bias=C.gc[:, h:h + 1]),
             reads=["bank3", "gc"], writes=["gD"])
    S.op("dve", lambda e: e.tensor_tensor(f3(C.gDst), f3(C.gD), C.strict4[:], ALU.mult), reads=["gD", "strict4"], writes=["gDst"])
    S.op("dve", lambda e: e.tensor_tensor(f3(C.gecb), C.gcv[:, 0:4, :].rearrange("p a b -> p (a b)"), f3(C.gecb), ALU.mult),
         reads=["gcv", "gecb"], writes=["gecb"])
    for h in range(4):
        S.op("pe", lambda e, h=h: e.matmul(A[5][:, h, :], kn(h), kn(h), start=True, stop=True), reads=["gcv"], writes=["bank5"])
        S.op("pe", lambda e, h=h: e.matmul(A[6][:, h, :], qn(h), kn(h), start=True, stop=True), reads=["gcv"], writes=["bank6"])
    S.op("dve", lambda e: e.tensor_tensor(f3(C.gdiag), f3(A[5]), f3(C.gDst), ALU.mult), reads=["bank5", "gDst"], writes=["gdiag"])
    S.op("dve", lambda e: e.tensor_tensor(C.gXTa[:], C.gdiag[:], bc4(C.gnb[:]), ALU.mult), reads=["gdiag", "gnb"], writes=["gXTa"])
    S.op("dve", lambda e: e.tensor_tensor(f3(C.gAm), f3(A[6]), f3(C.gD), ALU.mult), reads=["bank6", "gD"], writes=["gAm"])
    for h in range(4):
        S.op("pe", lambda e, h=h: e.transpose(A[0][:, h, :], C.gXTa[:, h, :], C.identf[:]), reads=["gXTa", "identf"], writes=["bank0"])
        S.op("pe", lambda e, h=h: e.transpose(A[1][:, h, :], C.gAm[:, h, :], C.identf[:]), reads=["gAm", "identf"], writes=["bank1"])
    S.op("act", lambda e: e.copy(f3(C.gXa), f3(A[0])), reads=["bank0"], writes=["gXa"])
    S.op("dve", lambda e: e.tensor_tensor(C.gR[:], A[0][:], C.identf[:].unsqueeze(1).to_broadcast([128, 4, 128]), ALU.add),
         reads=["bank0", "identf"], writes=["gR"])
    S.op("act", lambda e: e.copy(f3(C.gAT), f3(A[1])), reads=["bank1"], writes=["gAT"])
    X, XT = (C.gXa, "gXa"), (C.gXTa, "gXTa")
    Xn, XTn = (C.gXb, "gXb"), (C.gXTb, "gXTb")
    for j in range(1, 7):
        for h in range(4):
            S.op("pe", lambda e, h=h, X=X, XT=XT: e.matmul(A[5][:, h, :], X[0][:, h, :], XT[0][:, h, :], start=True, stop=True),
                 reads=[X[1], XT[1]], writes=["bank5"])
        S.op("act", lambda e, XTn=XTn: e.copy(f3(XTn[0]), f3(A[5])), reads=["bank5"], writes=[XTn[1]])
        if j < 6:
            for h in range(4):
                S.op("pe", lambda e, h=h, X=X, XT=XT: e.matmul(A[6][:, h, :], XT[0][:, h, :], X[0][:, h, :], start=True, stop=True),
                     reads=[X[1], XT[1]], writes=["bank6"])
            S.op("act", lambda e, Xn=Xn: e.copy(f3(Xn[0]), f3(A[6])), reads=["bank6"], writes=[Xn[1]])
        for h in range(4):
            S.op("pe", lambda e, h=h, XTn=XTn: e.matmul(A[0][:, h, :], XTn[0][:, h, :], C.gR[:, h, :], start=True, stop=True),
                 reads=[XTn[1], "gR"], writes=["bank0"])
        S.op("dve", lambda e: e.tensor_tensor(f3(C.gR), f3(C.gR), f3(A[0]), ALU.add), reads=["gR", "bank0"], writes=["gR"])
        X, XT, Xn, XTn = Xn, XTn, X, XT
    for h in range(4):
        S.op("pe", lambda e, h=h: e.transpose(A[1][:, h, :], kn(h), C.identf[:]), reads=["gcv", "identf"], writes=["bank1"])
        S.op("pe", lambda e, h=h: e.transpose(A[2][:, h, :], vf(h), C.identf[:]), reads=["gcv", "identf"], writes=["bank2"])
    S.op("dve", lambda e: e.tensor_tensor(C.gkbe[:], A[1][:], bc4(C.gbec[:]), ALU.mult), reads=["bank1", "gbec"], writes=["gXb"])
    S.op("dve", lambda e: e.tensor_tensor(C.gkdec[:], A[1][:], bc4(C.gedl[:]), ALU.mult), reads=["bank1", "gedl"], writes=["gXTa"])
    S.op("dve", lambda e: e.tensor_tensor(C.gvb[:], A[2][:], bc4(C.gbeta[:]), ALU.mult), reads=["bank2", "gbeta"], writes=["gXTb"])
    for h in range(4):
        S.op("pe", lambda e, h=h: e.matmul(A[3][:, h, :], C.gkbe[:, h, :], C.gR[:, h, :], start=True, stop=True), reads=["gXb", "gR"], writes=["bank3"])
        S.op("pe", lambda e, h=h: e.matmul(A[5][:, h, :], C.gR[:, h, :], C.gvb[:, h, :], start=True, stop=True), reads=["gXTb", "gR"], writes=["bank5"])
    S.op("act", lambda e: e.copy(f3(C.gwT), f3(A[3])), reads=["bank3"], writes=["gD"])
    S.op("act", lambda e: e.copy(f3(C.gu), f3(A[5])), reads=["bank5"], writes=["gAm"])
    for h in range(4):
        S.op("pe", lambda e, h=h: e.matmul(A[6][:, h, :], C.gwT[:, h, :], C.gS[:, h, :], start=True, stop=True), reads=["gD", "gS"], writes=["bank6"])
    S.op("dve", lambda e: e.tensor_tensor(f3(C.gvn), f3(C.gu), f3(A[6]), ALU.subtract), reads=["gAm", "bank6"], writes=["gDst"])
    for h in range(4):
        S.op("pe", lambda e, h=h: e.matmul(A[0][:, h, :], C.gS[:, h, :], C.gecb[:, h, :], start=True, stop=False), reads=["gS", "gecb"], writes=["bank0"])
        S.op("pe", lambda e, h=h: e.matmul(A[0][:, h, :], C.gvn[:, h, :], C.gAT[:, h, :], start=False, stop=True), reads=["gDst", "gAT"], writes=["bank0"])
    for h in range(4):
        S.op("pe", lambda e, h=h: e.matmul(A[1][:, h, :], C.gkdec[:, h, :], C.gvn[:, h, :], start=True, stop=True), reads=["gXTa", "gDst"], writes=["bank1"])
    S.op("dve", lambda e: e.tensor_tensor(C.gS[:], C.gS[:], bc4(C.gecl[:]), ALU.mult), reads=["gS", "gecl"], writes=["gS"])
    S.op("dve", lambda e: e.tensor_tensor(f3(C.gS), f3(C.gS), f3(A[1]), ALU.add), reads=["gS", "bank1"], writes=["gS"])
    S.op("act", lambda e: e.copy(f3(C.gXa), f3(A[0])), reads=["bank0"], writes=["gXa"])
    S.op("dve", lambda e: e.tensor_tensor(f3(C.gXb), f3(C.gXa), f3(C.gXa), ALU.mult), reads=["gXa"], writes=["gXb"])
    S.op("pe", lambda e: e.matmul(f3(A[2]), C.ones128f[:], f3(C.gXb), start=True, stop=True), reads=["ones128f", "gXb"], writes=["bank2"])
    S.op("act", lambda e: e.activation(f3(C.gXb), f3(A[2]), AF.Ln, bias=C.epst[:, 0:1]), reads=["bank2", "epst"], writes=["gXb"])
    S.op("act", lambda e: e.activation(f3(C.gXb), f3(C.gXb), AF.Exp, scale=-0.5), reads=["gXb"], writes=["gXb"])
    S.op("dve", lambda e: e.scalar_tensor_tensor(f3(C.gXa), f3(C.gXa), C.gnw[:, 0:1], f3(C.gXb), ALU.mult, ALU.mult), reads=["gXa", "gnw", "gXb"], writes=["gXa"])
    S.op("dve", lambda e: e.tensor_tensor(C.mixT[:, 2:6, :], C.gXa[:], C.gzs[:], ALU.mult), reads=["gXa", "gzs"], writes=["mixT"])


def alloc_mixer(nc, st, S):
    C = Ctx()
    def sb(name, shape, dt=F32):
        t = st.enter_context(nc.sbuf_tensor(name, shape, dt)); setattr(C, name, t); return t
    def ps(name, shape, dt=F32):
        return st.enter_context(nc.psum_tensor(name, shape, dt))
    for k, (shp, dt) in CSHAPES.items():
        sb(k, shp, dt)
    C.idb = C.identb
    sb("epst", [128, 1]); sb("ones256b", [128, 128], BF16); sb("onesf", [128, 128])
    sb("wH", [128, 8, 1024], BF16); sb("wG", [128, 8, 2048], BF16); sb("wab", [128, 8, 8], BF16); sb("wu", [128, 8, 256], BF16)
    sb("wout", [128, 6, 1024], BF16); sb("woutH", [64, 4, 1024], BF16)
    sb("nwcol", [128, 8])
    C.hs = [sb("h0", [128, 1024])]
    sb("ss", [128, 1]); sb("lnv", [128, 1]); sb("rstd", [128, 1])
    sb("hn", [128, 1024], BF16); sb("hnT", [128, 8, 128], BF16)
    C.junk = C.hn
    sb("mixT", [128, 8, 128], BF16); sb("mixH", [64, 4, 128], BF16)
    for n in ("tabC", "tabS", "t2", "wre", "wim"):
        sb(n, [128, 8, 128])
    sb("s5tmp", [128, 1024])
    C.kint = C.s5tmp[:].bitcast(mybir.dt.int32)
    for n in ("BTre", "BTim", "CTre", "CTimn", "xre", "xim"):
        sb(n, [128, 8, 128], BF16)
    for n in ("lre_p", "lim_p", "ls_p", "dtp", "lrep", "magp", "thp", "a128", "t128", "se", "ce", "c1", "c2", "c3"):
        sb(n, [128, 8])
    sb("zst", [128, 2, 8]); sb("s5d", [128, 2]); sb("s5nw", [128, 2]); sb("glus", [128, 2, 256], BF16)
    sb("uF", [128, 2, 128]); sb("uB", [128, 2, 128], BF16)
    for n in ("yv", "g1", "zg", "o5"):
        sb(n, [128, 2, 128])
    sb("zgb", [128, 2, 128], BF16); sb("rs5", [128, 128])
    for n in ("he1", "hden", "hk", "hc", "hec", "hqt", "hkt", "hkh", "hv", "hof"):
        sb(n, [64, 4, 128])
    C.hgs = C.hc
    sb("hscm", [32, 4, 32]); sb("hkvT", [32, 4, 128]); sb("Sst", [64, 4, 64]); sb("ones64", [64, 64])
    sb("oml", [64, 4]); sb("lbl0", [64, 4]); sb("lbl1", [64, 4]); sb("hnw", [64, 1])
    sb("gx", [128, 12, 131]); sb("gcv", [128, 12, 128]); sb("gtmp", [128, 12, 128])
    for n in ("gzs", "gD", "gDst", "gAm", "gdiag", "gXa", "gXb", "gXTa", "gXTb", "gR", "gecb", "gS", "gAT"):
        sb(n, [128, 4, 128])
    C.gwT = C.gD; C.gvn = C.gDst; C.gu = C.gAm
    C.gkbe = C.gXb; C.gkdec = C.gXTa; C.gvb = C.gXTb
    V = lambda a: type("V", (), {"__getitem__": lambda self, k, a=a: a[k]})()
    C.stg0 = V(C.gtmp[:, 0:8, :].rearrange("p a b -> p (a b)"))
    C.stg1 = V(C.gcv[:, 0:8, :].rearrange("p a b -> p (a b)"))
    C.ho = V(C.gtmp[:, 0:8, :].rearrange("p a b -> p (a b)"))
    for n in ("ga", "gg", "gbeta", "gc", "gec", "gecl", "gedl", "gbec", "gnb", "aneg", "dtb"):
        sb(n, [128, 4])
    sb("convw", [128, 12, 4]); sb("gnw", [128, 1]); sb("ones128f", [128, 128])
    wGf = C.wG[:].rearrange("p a b -> p (a b)").bitcast(F32)
    scr = [wGf[:, i * 1024:(i + 1) * 1024] for i in range(8)]
    scr += [C.t2[:].rearrange("p a b -> p (a b)"), C.wre[:].rearrange("p a b -> p (a b)")]
    C.scr = [type("V", (), {"__getitem__": lambda self, k, a=a: a[k]})() for a in scr]
    C.tp = ps("tp", [128, 8, 128], BF16)
    C.bank = [ps("bank%d" % i, [128, 4, 128]) for i in range(7)]
    return C


def declare_params(nc, suffix=""):
    P = {}
    for k, shp in PSHAPES.items():
        P[k] = nc.dram_tensor("p_" + k + suffix, shp, F32, kind="ExternalInput").ap()
    return P


def mixer_setup(S, C, P, Cst, layer):
    for k in CSHAPES:
        S.dma("ld_c_" + k, getattr(C, k)[:], Cst[k], writes=[k])
    S.op("dve", lambda e: e.memset(C.epst[:], EPS), writes=["epst"])
    S.op("dve", lambda e: e.memset(C.onesf[:], 1.0), writes=["onesf"])
    S.op("dve", lambda e: e.memset(C.ones256b[:], 1.0 / 256.0), writes=["ones256b"])
    S.op("dve", lambda e: e.memset(C.mixT[:], 0.0), writes=["mixT"])
    S.op("dve", lambda e: e.memset(C.mixH[:], 0.0), writes=["mixH"])
    s5_setup(S, C, P)
    S.barrier()
    S.dma("ld_nwcol", C.nwcol[:], P["nmixc"], writes=["nwcol"])
    stage = [C.stg0, C.stg1]
    K8 = list(range(8))
    load_weight_bf16(S, "wH", P["w_in"], K8, 1024, C.wH, 0, stage, 1, 0, C.nwcol)
    load_weight_bf16(S, "wG", P["w_in"], K8, 1024, C.wG[:, :, 0:1024], 0, stage, 1, 1024, C.nwcol)
    load_weight_bf16(S, "wG", P["w_in"], K8, 1024, C.wG[:, :, 1024:2048], 0, stage, 1, 2048, C.nwcol)
    load_weight_bf16(S, "wab", P["w_in"], K8, 8, C.wab, 0, stage, 8, 3072, C.nwcol)
    load_weight_bf16(S, "wu", P["w_in"], K8, 256, C.wu, 0, stage, 4, 3080, C.nwcol)
    load_weight_bf16(S, "wout", P["w_out"], list(range(2, 8)), 1024, C.wout, 2, stage, 1, 0)
    whv = P["w_out"][0:256, :].rearrange("(h p) n -> p h n", p=64)
    for i in range(4):
        sg = stage[i % 2][0:64, :]
        S.dma("wst%d" % (i % 2), sg, whv[:, i, :], writes=[("wstage", i % 2)])
        S.op("dve", lambda h, sg=sg, i=i: h.tensor_copy(C.woutH[:, i, :], sg), reads=[("wstage", i % 2)], writes=["woutH"])
    hgrn_setup(S, C, P, layer)
    if hasattr(C, "gdn_ready"):
        gdn_setup(S, C, P)
    S.barrier()


def out_proj_store(S, C, h, hkey, b, ydst, ykey):
    oP = [C.bank[4], C.bank[5]]
    for half in range(2):
        ob = oP[half][:].rearrange("p a b -> p (a b)")
        n = 0
        for hd in range(4):
            S.op("pe", lambda e, hd=hd, half=half, ob=ob: e.matmul(ob, C.mixH[:, hd, :], C.woutH[:, hd, half * 512:(half + 1) * 512],
                                                                  start=(hd == 0), stop=False),
                 reads=["mixH", "woutH"], writes=["bank%d" % (4 + half)])
        for ft in range(2, 8):
            S.op("pe", lambda e, ft=ft, half=half, ob=ob: e.matmul(ob, C.mixT[:, ft, :], C.wout[:, ft - 2, half * 512:(half + 1) * 512],
                                                                  start=False, stop=(ft == 7)),
                 reads=["mixT", "wout"], writes=["bank%d" % (4 + half)])
        S.op("dve", lambda e, half=half, ob=ob: e.tensor_tensor(C.ho[:, half * 512:(half + 1) * 512], ob, h[:, half * 512:(half + 1) * 512], ALU.add),
             reads=["bank%d" % (4 + half), hkey], writes=["gtmp"])
    S.dma("st0", ydst, C.ho[:], reads=["gtmp"], writes=[ykey])


def build_mixer(NT, parts=("s5",), debug=True, layer=0):
    nc = bass.Bass("TRN2", target_bir_lowering=False)
    x = nc.dram_tensor("x", [NT * 128, 1024], F32, kind="ExternalInput").ap()
    y = nc.dram_tensor("y", [NT * 128, 1024], F32, kind="ExternalOutput").ap()
    P = declare_params(nc)
    Cst = {k: nc.dram_tensor("c_" + k, shp, dt, kind="ExternalInput").ap() for k, (shp, dt) in CSHAPES.items()}
    dbg = nc.dram_tensor("dbg", [NT, 128, 1024], BF16, kind="ExternalOutput").ap() if debug else None
    dbgH = nc.dram_tensor("dbgH", [NT, 64, 512], BF16, kind="ExternalOutput").ap() if debug else None
    with ExitStack() as st:
        S = Sched(nc, st)
        C = alloc_mixer(nc, st, S)
        if "gdn" in parts:
            C.gdn_ready = True
        mixer_setup(S, C, P, Cst, layer)
        ykeys = []
        for c in range(NT):
            b = 0
            h = C.hs[b]
            S.dma("hl%d" % b, h[:], x[c * 128:(c + 1) * 128, :], writes=[("h", b)])
            rmsnorm_T(S, C, h, ("h", b), None, None)
            if "hgrn" in parts:
                hgrn_chunk(S, C, c == 0)
            if "gdn" in parts:
                gdn_chunk(S, C, c == 0)
            if "s5" in parts:
                s5_chunk(S, C, c == 0)
            if debug:
                S.dma("dbg", dbg[c], C.mixT[:].rearrange("p a b -> p (a b)"), reads=["mixT"], writes=[("dbg", c)])
                S.dma("dbgH", dbgH[c], C.mixH[:].rearrange("p a b -> p (a b)"), reads=["mixH"], writes=[("dbgH", c)])
                ykeys += [("dbg", c), ("dbgH", c)]
            out_proj_store(S, C, h, ("h", b), b, y[c * 128:(c + 1) * 128, :], ("y", c))
            ykeys.append(("y", c))
        S.final_wait("sp", ykeys)
        block = st.enter_context(nc.Block())
        S.run(block)
    return nc


def make_inmap(inp, l, xin):
    m = {"x": xin}
    for k, v in host_layer_params(inp, l).items():
        m["p_" + k] = v.reshape(PSHAPES[k])
    for k, v in host_consts().items():
        m["c_" + k] = v
    return m


def build_mlp(NT, final):
    nc = bass.Bass("TRN2", target_bir_lowering=False)
    x = nc.dram_tensor("x", [NT * 128, 1024], F32, kind="ExternalInput").ap()
    nw = nc.dram_tensor("p_nmlp", [128, 1024], F32, kind="ExternalInput").ap()
    fwd = nc.dram_tensor("p_finalw", [128, 1024], F32, kind="ExternalInput").ap()
    w1 = nc.dram_tensor("p_w1", [1024, 4096], F32, kind="ExternalInput").ap()
    w2 = nc.dram_tensor("p_w2", [4096, 1024], F32, kind="ExternalInput").ap()
    identb = nc.dram_tensor("c_identb", [128, 128], BF16, kind="ExternalInput").ap()
    y = nc.dram_tensor("y", [NT * 128, 1024], F32, kind="ExternalOutput").ap()
    with ExitStack() as st:
        C = Ctx()
        def sb(name, shape, dt=F32):
            t = st.enter_context(nc.sbuf_tensor(name, shape, dt)); setattr(C, name, t); return t
        def ps(name, shape, dt=F32):
            return st.enter_context(nc.psum_tensor(name, shape, dt))
        S = Sched(nc, st)
        sb("w1s", [128, 8, 4096], BF16); sb("w2s", [128, 32, 1024], BF16)
        stage = [sb("stg0", [128, 1024]), sb("stg1", [128, 1024])]
        sb("nws", [128, 1024]); sb("fws", [128, 1024]); sb("idb", [128, 128], BF16)
        C.hs = [sb("h0", [128, 1024]), sb("h1", [128, 1024])]
        sb("junk", [128, 1024]); sb("ss", [128, 1]); sb("lnv", [128, 1]); sb("rstd", [128, 1]); sb("epst", [128, 1])
        sb("hn", [128, 1024], BF16); sb("hnT", [128, 8, 128], BF16)
        rr = [sb("rr0", [128, 512]), sb("rr1", [128, 512])]
        sb("aT", [128, 32, 128], BF16)
        ho = [sb("ho0", [128, 1024]), sb("ho1", [128, 1024])]
        C.tp = ps("tp", [128, 8, 128], BF16)
        hid = [ps("hid%d" % i, [128, 4, 128]) for i in range(2)]
        ops_ = [ps("ops%d" % i, [128, 512]) for i in range(2)]
        S.op("dve", lambda e: e.memset(C.epst[:], EPS), writes=["epst"])
        S.dma("c0", C.nws[:], nw, writes=["nws"])
        S.dma("c2", C.fws[:], fwd, writes=["fws"])
        S.dma("c1", C.idb[:], identb, writes=["idb"])
        for cq in range(4):
            load_weight_bf16(S, "w1s", w1, list(range(8)), 1024, C.w1s[:, :, cq * 1024:(cq + 1) * 1024], 0, stage, 1, cq * 1024)
        load_weight_bf16(S, "w2s", w2, list(range(32)), 1024, C.w2s, 0, stage, 1, 0)
        ykeys = []
        for t in range(NT):
            b = t % 2
            h = C.hs[b]
            S.dma("hl%d" % b, h[:], x[t * 128:(t + 1) * 128, :], writes=[("h", b)])
            rmsnorm_T(S, C, h, ("h", b), C.nws, "nws")
            for g in range(8):
                hb = g % 2
                for mi in range(4):
                    m = g * 4 + mi
                    for kt in range(8):
                        S.op("pe", lambda e, m=m, mi=mi, kt=kt, hb=hb: e.matmul(
                            hid[hb][:, mi, :], C.w1s[:, kt, m * 128:(m + 1) * 128], C.hnT[:, kt, :],
                            start=(kt == 0), stop=(kt == 7)), reads=["w1s", "hnT"], writes=[("hid", hb)])
                S.op("act", lambda e, hb=hb: e.activation(rr[hb][:], hid[hb][:].rearrange("p a b -> p (a b)"), AF.Relu),
                     reads=[("hid", hb)], writes=[("rr", hb)])
                S.op("dve", lambda e, hb=hb, g=g: e.tensor_tensor(
                    C.aT[:, g * 4:(g + 1) * 4, :].rearrange("p a b -> p (a b)"), rr[hb][:], rr[hb][:], ALU.mult),
                    reads=[("rr", hb)], writes=["aT"])
            for half in range(2):
                for kt in range(32):
                    S.op("pe", lambda e, half=half, kt=kt: e.matmul(
                        ops_[half][:], C.aT[:, kt, :], C.w2s[:, kt, half * 512:(half + 1) * 512],
                        start=(kt == 0), stop=(kt == 31)), reads=["aT", "w2s"], writes=[("ops", half)])
                S.op("dve", lambda e, half=half, b=b, h=h: e.tensor_tensor(
                    ho[b][:, half * 512:(half + 1) * 512], ops_[half][:], h[:, half * 512:(half + 1) * 512], ALU.add),
                    reads=[("ops", half), ("h", b)], writes=[("ho", b)])
            if final:
                hb_ = ho[b]
                S.op("act", lambda e, hb_=hb_: e.activation(C.junk[:], hb_[:], AF.Square, scale=1.0 / 32.0, accum_out=C.ss[:]),
                     reads=[("ho", b)], writes=["junk", "ss"])
                S.op("act", lambda e: e.activation(C.lnv[:], C.ss[:], AF.Ln, bias=C.epst[:, 0:1]), reads=["ss", "epst"], writes=["lnv"])
                S.op("act", lambda e: e.activation(C.rstd[:], C.lnv[:], AF.Exp, scale=-0.5), reads=["lnv"], writes=["rstd"])
                S.op("dve", lambda e, hb_=hb_: e.scalar_tensor_tensor(hb_[:], hb_[:], C.rstd[:, 0:1], C.fws[:], ALU.mult, ALU.mult),
                     reads=[("ho", b), "rstd", "fws"], writes=[("ho", b)])
            S.dma("st%d" % b, y[t * 128:(t + 1) * 128, :], ho[b][:], reads=[("ho", b)], writes=[("y", t)])
            ykeys.append(("y", t))
        S.final_wait("sp", ykeys)
        block = st.enter_context(nc.Block())
        S.run(block)
    return nc


NT_FULL = 129
N_CORES = 8
MIXER_PARTS = ("hgrn", "gdn", "s5")


def kernel(**inputs):
    inp = {k: np.asarray(v) for k, v in inputs.items()}
    B = inp["x"].shape[0]
    NT = NT_FULL
    consts = host_consts()
    hcur = []
    for c in range(N_CORES):
        if c < B:
            hcur.append(np.concatenate([np.zeros((112, 1024), np.float32), inp["meta"].astype(np.float32), inp["x"][c]], 0))
        else:
            hcur.append(np.zeros((NT * 128, 1024), np.float32))
    nc_mix = {l: build_mixer(NT, MIXER_PARTS, debug=False, layer=l) for l in range(2)}
    nc_mlp = {False: build_mlp(NT, False), True: build_mlp(NT, True)}
    depth = inp["w_in"].shape[0]
    finalw = np.ascontiguousarray(np.broadcast_to(inp["final_norm_w"].reshape(1, 1024), (128, 1024)))
    for l in range(depth):
        lp = host_layer_params(inp, l)
        base = {"p_" + k: v.reshape(PSHAPES[k]) for k, v in lp.items()}
        for k, v in consts.items():
            base["c_" + k] = v
        ims = [dict(base, x=hcur[c]) for c in range(N_CORES)]
        res = run_bass_kernel_spmd(nc_mix[min(l, 1)], ims, core_ids=list(range(N_CORES)))
        hcur = [res.results[c]["y"] for c in range(N_CORES)]
        last = (l == depth - 1)
        mb = {"p_nmlp": lp["nmlp"], "p_finalw": finalw, "p_w1": lp["w1"], "p_w2": lp["w2"], "c_identb": consts["identb"]}
        ims = [dict(mb, x=hcur[c]) for c in range(N_CORES)]
        res = run_bass_kernel_spmd(nc_mlp[last], ims, core_ids=list(range(N_CORES)))
        hcur = [res.results[c]["y"] for c in range(N_CORES)]
    out = np.stack([hcur[c][128:] for c in range(B)], 0).astype(np.float32)
    return out
```

```python
from contextlib import ExitStack
import numpy as np
import concourse.bass as bass
import concourse.mybir as mybir

F32 = mybir.dt.float32
BF16 = mybir.dt.bfloat16
ALU = mybir.AluOpType
AF = mybir.ActivationFunctionType
AX = mybir.AxisListType

EPOCH = 24000
NEPOCH = 10


class Sched:
    ENGS = ("pe", "act", "dve", "pool", "sp")

    def __init__(self, nc, st):
        self.nc = nc
        self.st = st
        self.q = {e: [] for e in self.ENGS}
        self.cnt = {e: 0 for e in self.ENGS}
        self.sems = {}
        self.waited = {}
        self.lastw = {}
        self.readers = {}
        self.dmacnt = {}
        self.latest = {}
        self.h = {"pe": nc.tensor, "act": nc.scalar, "dve": nc.vector,
                  "pool": nc.gpsimd, "sp": nc.sync}

    def _sem(self, key):
        s = self.sems.get(key)
        if s is None:
            s = self.st.enter_context(self.nc.semaphore("s_%s_%s" % key))
            self.sems[key] = s
        return s

    def _wait(self, e, clock):
        if clock is None:
            return
        key, val = clock
        if key[0] == "dma":
            val = max(val, self.dmacnt[key[1]])
        if key[0] == "pe" and e == "pe":
            return
        if self.waited.get((e, key), 0) >= val:
            return
        self.waited[(e, key)] = val
        sem = self._sem(key)
        h = self.h[e]
        self.q[e].append(lambda: h.wait_ge(sem, val))

    def _deps(self, e, reads, writes):
        for r in reads:
            self._wait(e, self.lastw.get(r))
        for w in writes:
            self._wait(e, self.lastw.get(w))
            for k, v in self.readers.get(w, {}).items():
                self._wait(e, (k, v))

    def _record(self, clock, reads, writes):
        key, val = clock
        for r in reads:
            d = self.readers.setdefault(r, {})
            if d.get(key, 0) < val:
                d[key] = val
        for w in writes:
            self.lastw[w] = clock
            self.readers[w] = {}
        if self.latest.get(key, 0) < val:
            self.latest[key] = val

    def barrier(self):
        for e in self.ENGS:
            for key, val in list(self.latest.items()):
                self._wait(e, (key, val))

    @staticmethod
    def _excl(reads, writes):
        ex = [r for r in reads if (isinstance(r, str) and (r.startswith("bank") or r == "tp")) or
              (isinstance(r, tuple) and r[0] in ("hid", "ops"))]
        if ex:
            reads = [r for r in reads if r not in ex]
            writes = list(writes) + ex
        return reads, writes

    def op(self, e, fn, reads=(), writes=()):
        reads, writes = self._excl(reads, writes)
        self._deps(e, reads, writes)
        i = self.cnt[e]
        self.cnt[e] += 1
        key = (e, i // EPOCH)
        val = i % EPOCH + 1
        sem = self._sem(key)
        h = self.h[e]
        self.q[e].append(lambda: fn(h).then_inc(sem, 1))
        self._record((key, val), reads, writes)

    def dma(self, stream, out, in_, reads=(), writes=(), q="sp"):
        self._deps(q, reads, writes)
        key = ("dma", stream)
        n = self.dmacnt.get(stream, 0) + 16
        self.dmacnt[stream] = n
        sem = self._sem(key)
        h = self.h[q]
        self.q[q].append(lambda: h.dma_start(out=out, in_=in_).then_inc(sem, 16))
        self._record((key, n), reads, writes)

    def final_wait(self, e, resources):
        for r in resources:
            self._wait(e, self.lastw.get(r))

    def run(self, block):
        q = self.q
        self.q = {e: [] for e in self.ENGS}

        @block.tensor
        def _(eng):
            for f in q["pe"]:
                f()

        @block.scalar
        def _(eng):
            for f in q["act"]:
                f()

        @block.vector
        def _(eng):
            for f in q["dve"]:
                f()

        @block.gpsimd
        def _(eng):
            for f in q["pool"]:
                f()

        @block.sync
        def _(eng):
            for f in q["sp"]:
                f()


import sys, time, math
import ml_dtypes
from concourse.bass_utils import run_bass_kernel_spmd

EPS = 1e-6
TWO_PI = 2.0 * math.pi
GELU_C = 2.0 * math.sqrt(2.0 / math.pi)
NEG_BIG = 30000.0


def host_consts():
    c = {}
    c["identf"] = np.eye(128, dtype=np.float32)
    c["identb"] = np.eye(128).astype(ml_dtypes.bfloat16)
    s = np.arange(128)
    c["tri"] = (s[:, None] <= s[None, :]).astype(np.float32)
    c["gmask"] = (NEG_BIG * (s[None, :] > s[:, None])).astype(np.float32)
    c["strict"] = (s[None, :] < s[:, None]).astype(np.float32)
    c["iota"] = np.ascontiguousarray(np.broadcast_to(s.astype(np.float32), (128, 128)))
    m32 = (s[:32, None] <= s[None, :32]).astype(np.float32)
    c["mask32"] = np.ascontiguousarray(np.broadcast_to(m32[:, None, :], (32, 4, 32))).reshape(32, 128)
    rs = np.ones((64, 4, 128), np.float32)
    rs[:, :, 0::32] = 0.0
    c["hreset"] = rs.reshape(64, 512)
    c["gmask4"] = np.ascontiguousarray(np.tile(c["gmask"], (1, 4)))
    c["strict4"] = np.ascontiguousarray(np.tile(c["strict"], (1, 4)))
    return c


def host_layer_params(inp, l):
    p = {}
    bc = lambda v, n=128: np.ascontiguousarray(np.broadcast_to(np.asarray(v, np.float32).reshape(1, -1), (n, np.asarray(v).size)))
    p["nmix"] = bc(inp["norm_mix_w"][l])
    p["nmlp"] = bc(inp["norm_mlp_w"][l])
    p["nmixc"] = np.ascontiguousarray(np.asarray(inp["norm_mix_w"][l], np.float32).reshape(8, 128).T)
    p["w_in"] = np.ascontiguousarray(inp["w_in"][l])
    p["w_out"] = np.ascontiguousarray(inp["w_out"][l])
    p["w1"] = np.ascontiguousarray(inp["mlp_w1"][l])
    p["w2"] = np.ascontiguousarray(inp["mlp_w2"][l])
    p["lbl0"] = np.ascontiguousarray(inp["hgrn_lb_logits"][0].reshape(4, 64).T)
    p["lbl1"] = np.ascontiguousarray(inp["hgrn_lb_logits"][1].reshape(4, 64).T)
    p["hnw"] = np.ascontiguousarray(inp["hgrn_norm_w"][l].reshape(64, 1))
    p["convw"] = np.ascontiguousarray(inp["gdn_conv_w"][l].reshape(4, 12, 128).transpose(2, 1, 0))
    p["alog"] = bc(inp["gdn_a_log"][l])
    p["dtb"] = bc(inp["gdn_dt_bias"][l])
    p["gnw"] = np.ascontiguousarray(inp["gdn_norm_w"][l].reshape(128, 1))
    lre, lim, ls = inp["s5_lam_re"][l], inp["s5_lam_im"][l], inp["s5_log_step"][l]
    def pl(a):
        return np.ascontiguousarray(a.reshape(8, 128).T)
    p["lre_p"] = pl(lre); p["lim_p"] = pl(lim)
    p["ls_p"] = pl(np.broadcast_to(ls[:, None], (16, 64)))
    fl = lambda a: np.ascontiguousarray(np.broadcast_to(a.reshape(1, 1024), (128, 1024)))
    p["lre_f"] = fl(lre); p["lim_f"] = fl(lim)
    p["ls_f"] = fl(np.ascontiguousarray(np.broadcast_to(ls[:, None], (16, 64))))
    bre, bim = inp["s5_b_re"][l], inp["s5_b_im"][l]
    cre, cim = inp["s5_c_re"][l], inp["s5_c_im"][l]
    BTr = np.zeros((128, 8, 128), np.float32); BTi = np.zeros((128, 8, 128), np.float32)
    CTr = np.zeros((128, 8, 128), np.float32); CTi = np.zeros((128, 8, 128), np.float32)
    for g in range(16):
        j = g // 2; r0 = (g % 2) * 64; c0 = (16 * g) % 128
        BTr[c0:c0 + 16, j, r0:r0 + 64] = bre[g].T
        BTi[c0:c0 + 16, j, r0:r0 + 64] = bim[g].T
        CTr[r0:r0 + 64, j, c0:c0 + 16] = cre[g].T
        CTi[r0:r0 + 64, j, c0:c0 + 16] = cim[g].T
    p["BTr"] = BTr.reshape(128, 1024); p["BTi"] = BTi.reshape(128, 1024)
    p["CTr"] = CTr.reshape(128, 1024); p["CTi"] = CTi.reshape(128, 1024)
    p["s5d"] = np.ascontiguousarray(inp["s5_d"][l].reshape(2, 128).T)
    p["s5nw"] = np.ascontiguousarray(inp["s5_norm_w"][l].reshape(2, 128).T)
    p["glu"] = np.ascontiguousarray(inp["s5_glu_w"][l])
    return p


PSHAPES = {
    "nmix": [128, 1024], "nmixc": [128, 8], "nmlp": [128, 1024], "w_in": [1024, 3336], "w_out": [1024, 1024],
    "w1": [1024, 4096], "w2": [4096, 1024], "lbl0": [64, 4], "lbl1": [64, 4], "hnw": [64, 1],
    "convw": [128, 48], "alog": [128, 4], "dtb": [128, 4], "gnw": [128, 1],
    "lre_p": [128, 8], "lim_p": [128, 8], "ls_p": [128, 8],
    "lre_f": [128, 1024], "lim_f": [128, 1024], "ls_f": [128, 1024],
    "BTr": [128, 1024], "BTi": [128, 1024], "CTr": [128, 1024], "CTi": [128, 1024],
    "s5d": [128, 2], "s5nw": [128, 2], "glu": [256, 256],
}
CSHAPES = {"identf": ([128, 128], F32), "identb": ([128, 128], BF16), "tri": ([128, 128], F32),
           "gmask": ([128, 128], F32), "strict": ([128, 128], F32), "iota": ([128, 128], F32),
           "mask32": ([32, 128], F32), "hreset": ([64, 512], F32),
           "gmask4": ([128, 512], F32), "strict4": ([128, 512], F32)}


class Ctx:
    pass


def load_weight_bf16(S, name, w_dram, kts, N, dst, dst_off, stage, kt_per, col0=0, scale=None, np_=128):
    wv = w_dram.rearrange("(kt p) n -> p kt n", p=128)
    assert kt_per * N <= 1024
    cnt = getattr(S, "_wl", 0)
    for i0 in range(0, len(kts), kt_per):
        grp = kts[i0:i0 + kt_per]
        sl = cnt % 2
        cnt += 1
        sg = stage[sl][:, 0:len(grp) * N].rearrange("p (k n) -> p k n", k=len(grp))
        S.dma("wst%d" % sl, sg, wv[:, grp[0]:grp[0] + len(grp), col0:col0 + N], writes=[("wstage", sl)])
        for i, kt in enumerate(grp):
            eng = "dve" if (cnt + i) % 2 == 0 else "pool"
            if scale is None:
                S.op(eng, lambda h, sg=sg, i=i, kt=kt: h.tensor_copy(dst[:, kt - dst_off, 0:N], sg[:, i, :]),
                     reads=[("wstage", sl)], writes=[name])
            else:
                S.op(eng, lambda h, sg=sg, i=i, kt=kt: h.tensor_scalar(dst[:, kt - dst_off, 0:N], sg[:, i, :], scale[:, kt:kt + 1], None, ALU.mult),
                     reads=[("wstage", sl), "nwcol"], writes=[name])
    S._wl = cnt


def rmsnorm_T(S, C, h, hkey, nws, nwkey):
    S.op("act", lambda e: e.activation(C.junk[:], h[:], AF.Square, scale=1.0 / 32.0, accum_out=C.ss[:]),
         reads=[hkey], writes=["junk", "hn", "ss"])
    S.op("act", lambda e: e.activation(C.lnv[:], C.ss[:], AF.Ln, bias=C.epst[:, 0:1]), reads=["ss", "epst"], writes=["lnv"])
    S.op("act", lambda e: e.activation(C.rstd[:], C.lnv[:], AF.Exp, scale=-0.5), reads=["lnv"], writes=["rstd"])
    if nws is None:
        S.op("dve", lambda e: e.tensor_scalar(C.hn[:], h[:], C.rstd[:, 0:1], None, ALU.mult), reads=[hkey, "rstd"], writes=["hn"])
    else:
        S.op("dve", lambda e: e.scalar_tensor_tensor(C.hn[:], h[:], C.rstd[:, 0:1], nws[:], ALU.mult, ALU.mult),
             reads=[hkey, "rstd", nwkey], writes=["hn"])
    for kt in range(8):
        S.op("pe", lambda e, kt=kt: e.transpose(C.tp[:, kt, :], C.hn[:, kt * 128:(kt + 1) * 128], C.idb[:]),
             reads=["hn", "idb"], writes=["tp"])
    S.op("act", lambda e: e.copy(C.hnT[:], C.tp[:]), reads=["tp"], writes=["hnT"])


def sincos_tables(S, C, ang, angkey, sin_out, cos_out, okeys, kint, tmps, tkeys):
    kf, s2, s4 = tmps
    kk, k2, k4 = tkeys
    S.op("dve", lambda e: e.tensor_scalar(kint, ang, 1.0 / TWO_PI, None, ALU.mult), reads=[angkey], writes=["kint"])
    S.op("dve", lambda e: e.tensor_copy(kf, kint), reads=["kint"], writes=[kk])
    S.op("dve", lambda e: e.scalar_tensor_tensor(kf, kf, -TWO_PI, ang, ALU.mult, ALU.add), reads=[kk, angkey], writes=[kk])
    S.op("act", lambda e: e.activation(s4, kf, AF.Sin, scale=0.25), reads=[kk], writes=[k4])
    S.op("act", lambda e: e.activation(s2, kf, AF.Sin, scale=0.5), reads=[kk], writes=[k2])
    S.op("dve", lambda e: e.tensor_tensor(s4, s4, s4, ALU.mult), reads=[k4], writes=[k4])
    S.op("dve", lambda e: e.tensor_scalar(s4, s4, -2.0, 1.0, ALU.mult, ALU.add), reads=[k4], writes=[k4])
    S.op("dve", lambda e: e.scalar_tensor_tensor(sin_out, s2, 2.0, s4, ALU.mult, ALU.mult), reads=[k2, k4], writes=[okeys[0]])
    S.op("dve", lambda e: e.tensor_tensor(s2, s2, s2, ALU.mult), reads=[k2, okeys[0]], writes=[k2])
    S.op("dve", lambda e: e.tensor_scalar(cos_out, s2, -2.0, 1.0, ALU.mult, ALU.add), reads=[k2], writes=[okeys[1]])


def s5_setup(S, C, P):
    T = C.scr
    def ld(tile_ap, name, key):
        S.dma("ld_" + key, tile_ap, P[name], writes=[key])
    ld(C.lre_p[:], "lre_p", "lre_p"); ld(C.lim_p[:], "lim_p", "lim_p"); ld(C.ls_p[:], "ls_p", "ls_p")
    ld(C.s5d[:], "s5d", "s5d"); ld(C.s5nw[:], "s5nw", "s5nw")
    S.op("act", lambda e: e.activation(C.dtp[:], C.ls_p[:], AF.Exp), reads=["ls_p"], writes=["dtp"])
    S.op("dve", lambda e: e.tensor_scalar(C.lrep[:], C.lre_p[:], -1e-4, None, ALU.min), reads=["lre_p"], writes=["lrep"])
    S.op("dve", lambda e: e.tensor_tensor(C.magp[:], C.lrep[:], C.dtp[:], ALU.mult), reads=["lrep", "dtp"], writes=["magp"])
    S.op("act", lambda e: e.activation(C.magp[:], C.magp[:], AF.Exp), reads=["magp"], writes=["magp"])
    S.op("dve", lambda e: e.tensor_tensor(C.thp[:], C.lim_p[:], C.dtp[:], ALU.mult), reads=["lim_p", "dtp"], writes=["thp"])
    ang = T[0]
    for j in range(8):
        S.op("dve", lambda e, j=j: e.tensor_scalar(ang[:, j * 128:(j + 1) * 128], C.iota[:], C.thp[:, j:j + 1], None, ALU.mult),
             reads=["iota", "thp"], writes=["scr0"])
    sincos_tables(S, C, ang[:], "scr0", C.tabS[:].rearrange("p a b -> p (a b)"), C.tabC[:].rearrange("p a b -> p (a b)"),
                  ["tabS", "tabC"], C.kint[:], [T[1][:], T[2][:], T[3][:]], ["scr1", "scr2", "scr3"])
    S.op("dve", lambda e: e.tensor_scalar(C.a128[:], C.thp[:], 128.0, None, ALU.mult), reads=["thp"], writes=["a128"])
    sincos_tables(S, C, C.a128[:], "a128", C.se[:], C.ce[:], ["se", "ce"], C.kint[:, 0:8], [C.t128[:], C.c1[:], C.c2[:]], ["t128", "c1", "c2"])
    lre, lim, ls, t3, t4, t5, t6, t7 = T[2], T[3], T[4], T[5], T[6], T[7], T[8], T[9]
    ld(lre[:], "lre_f", "scr2"); ld(lim[:], "lim_f", "scr3"); ld(ls[:], "ls_f", "scr4")
    S.op("act", lambda e: e.activation(ls[:], ls[:], AF.Exp), reads=["scr4"], writes=["scr4"])
    S.op("dve", lambda e: e.tensor_scalar(lre[:], lre[:], -1e-4, None, ALU.min), reads=["scr2"], writes=["scr2"])
    S.op("dve", lambda e: e.tensor_tensor(t3[:], lre[:], ls[:], ALU.mult), reads=["scr2", "scr4"], writes=["scr5"])
    S.op("act", lambda e: e.activation(t3[:], t3[:], AF.Exp), reads=["scr5"], writes=["scr5"])
    S.op("dve", lambda e: e.tensor_tensor(t4[:], lim[:], ls[:], ALU.mult), reads=["scr3", "scr4"], writes=["scr6"])
    sincos_tables(S, C, t4[:], "scr6", t5[:], t6[:], ["scr7", "scr8"], C.kint[:], [t7[:], T[0][:], T[1][:]], ["scr9", "scr0", "scr1"])
    S.op("dve", lambda e: e.tensor_tensor(t5[:], t5[:], t3[:], ALU.mult), reads=["scr7", "scr5"], writes=["scr7"])
    S.op("dve", lambda e: e.tensor_tensor(t6[:], t6[:], t3[:], ALU.mult), reads=["scr8", "scr5"], writes=["scr8"])
    S.op("dve", lambda e: e.tensor_scalar(t6[:], t6[:], -1.0, None, ALU.add), reads=["scr8"], writes=["scr8"])
    S.op("dve", lambda e: e.tensor_tensor(t3[:], lre[:], lre[:], ALU.mult), reads=["scr2", "scr7", "scr8"], writes=["scr5"])
    S.op("dve", lambda e: e.tensor_tensor(t4[:], lim[:], lim[:], ALU.mult), reads=["scr3"], writes=["scr6"])
    S.op("dve", lambda e: e.tensor_tensor(t3[:], t3[:], t4[:], ALU.add), reads=["scr5", "scr6"], writes=["scr5"])
    S.op("dve", lambda e: e.reciprocal(t3[:], t3[:]), reads=["scr5"], writes=["scr5"])
    S.op("dve", lambda e: e.tensor_tensor(t4[:], t6[:], lre[:], ALU.mult), reads=["scr8", "scr2", "scr6"], writes=["scr6"])
    S.op("dve", lambda e: e.tensor_tensor(t7[:], t5[:], lim[:], ALU.mult), reads=["scr7", "scr3"], writes=["scr9"])
    S.op("dve", lambda e: e.tensor_tensor(t4[:], t4[:], t7[:], ALU.add), reads=["scr6", "scr9"], writes=["scr6"])
    S.op("dve", lambda e: e.tensor_tensor(t4[:], t4[:], t3[:], ALU.mult), reads=["scr6", "scr5"], writes=["scr6"])
    S.op("dve", lambda e: e.tensor_tensor(t7[:], t5[:], lre[:], ALU.mult), reads=["scr7", "scr2", "scr6"], writes=["scr9"])
    S.op("dve", lambda e: e.tensor_tensor(ls[:], t6[:], lim[:], ALU.mult), reads=["scr8", "scr3"], writes=["scr4"])
    S.op("dve", lambda e: e.tensor_tensor(t7[:], t7[:], ls[:], ALU.subtract), reads=["scr9", "scr4"], writes=["scr9"])
    S.op("dve", lambda e: e.tensor_tensor(t7[:], t7[:], t3[:], ALU.mult), reads=["scr9", "scr5"], writes=["scr9"])
    br, bi = lre, lim
    ld(br[:], "BTr", "scr2"); ld(bi[:], "BTi", "scr3")
    flat = lambda t: t[:].rearrange("p a b -> p (a b)")
    S.op("dve", lambda e: e.tensor_tensor(t5[:], t4[:], br[:], ALU.mult), reads=["scr6", "scr2", "scr7"], writes=["scr7"])
    S.op("dve", lambda e: e.tensor_tensor(t6[:], t7[:], bi[:], ALU.mult), reads=["scr9", "scr3", "scr8"], writes=["scr8"])
    S.op("dve", lambda e: e.tensor_tensor(flat(C.BTre), t5[:], t6[:], ALU.subtract), reads=["scr7", "scr8"], writes=["BTre"])
    S.op("dve", lambda e: e.tensor_tensor(t5[:], t4[:], bi[:], ALU.mult), reads=["scr6", "scr3", "BTre"], writes=["scr7"])
    S.op("dve", lambda e: e.tensor_tensor(t6[:], t7[:], br[:], ALU.mult), reads=["scr9", "scr2", "BTre"], writes=["scr8"])
    S.op("dve", lambda e: e.tensor_tensor(flat(C.BTim), t5[:], t6[:], ALU.add), reads=["scr7", "scr8"], writes=["BTim"])
    ld(t3[:], "CTr", "scr5"); ld(t4[:], "CTi", "scr6")
    S.op("dve", lambda e: e.tensor_copy(flat(C.CTre), t3[:]), reads=["scr5"], writes=["CTre"])
    S.op("dve", lambda e: e.tensor_scalar(flat(C.CTimn), t4[:], -1.0, None, ALU.mult), reads=["scr6"], writes=["CTimn"])
    gst = T[0][:, 0:512].rearrange("p (k n) -> p k n", k=2)
    S.dma("ld_glu", gst, P["glu"].rearrange("(k p) n -> p k n", p=128), reads=["tabS", "tabC"], writes=["scr0"])
    S.op("dve", lambda e: e.tensor_copy(C.glus[:], gst), reads=["scr0"], writes=["glus"])
    S.op("dve", lambda e: e.memset(C.zst[:], 0.0), writes=["zst"])


def s5_chunk(S, C, first):
    A = C.bank
    uP = A[0]
    for m in range(2):
        for kt in range(8):
            S.op("pe", lambda e, m=m, kt=kt: e.matmul(uP[:, m, :], C.wu[:, kt, m * 128:(m + 1) * 128], C.hnT[:, kt, :],
                                                     start=(kt == 0), stop=(kt == 7)),
                 reads=["wu", "hnT"], writes=["bank0"])
    S.op("act", lambda e: e.copy(C.uF[:], uP[:, 0:2, :]), reads=["bank0"], writes=["uF"])
    S.op("dve", lambda e: e.tensor_copy(C.uB[:], uP[:, 0:2, :]), reads=["bank0"], writes=["uB"])
    for j in range(8):
        S.op("pe", lambda e, j=j: e.matmul(A[2 + j // 4][:, j % 4, :], C.BTre[:, j, :], C.uB[:, j // 4, :], start=True, stop=True),
             reads=["BTre", "uB"], writes=["bank%d" % (2 + j // 4)])
        S.op("pe", lambda e, j=j: e.matmul(A[4 + j // 4][:, j % 4, :], C.BTim[:, j, :], C.uB[:, j // 4, :], start=True, stop=True),
             reads=["BTim", "uB"], writes=["bank%d" % (4 + j // 4)])
    f4 = lambda t, hf: t[:, hf * 4:(hf + 1) * 4, :].rearrange("p a b -> p (a b)")
    fb = lambda b: b[:].rearrange("p a b -> p (a b)")
    for hf in range(2):
        bre, bim = A[2 + hf], A[4 + hf]
        kre, kim = "bank%d" % (2 + hf), "bank%d" % (4 + hf)
        S.op("dve", lambda e, hf=hf, bre=bre: e.tensor_tensor(f4(C.wre, hf), fb(bre), f4(C.tabC, hf), ALU.mult), reads=[kre, "tabC"], writes=["wre"])
        S.op("dve", lambda e, hf=hf, bim=bim: e.tensor_tensor(f4(C.t2, hf), fb(bim), f4(C.tabS, hf), ALU.mult), reads=[kim, "tabS"], writes=["t2"])
        S.op("pool", lambda e, hf=hf: e.tensor_tensor(f4(C.wre, hf), f4(C.wre, hf), f4(C.t2, hf), ALU.add), reads=["wre", "t2"], writes=["wre"])
        S.op("dve", lambda e, hf=hf, bim=bim: e.tensor_tensor(f4(C.wim, hf), fb(bim), f4(C.tabC, hf), ALU.mult), reads=[kim, "tabC"], writes=["wim"])
        S.op("dve", lambda e, hf=hf, bre=bre: e.tensor_tensor(f4(C.t2, hf), fb(bre), f4(C.tabS, hf), ALU.mult), reads=[kre, "tabS", "wre"], writes=["t2"])
        S.op("pool", lambda e, hf=hf: e.tensor_tensor(f4(C.wim, hf), f4(C.wim, hf), f4(C.t2, hf), ALU.subtract), reads=["wim", "t2"], writes=["wim"])
    zl_re, zl_im = C.zst[:, 0, :], C.zst[:, 1, :]
    S.op("dve", lambda e: e.tensor_tensor(C.c1[:], C.ce[:], zl_re, ALU.mult), reads=["ce", "zst"], writes=["c1"])
    S.op("dve", lambda e: e.tensor_tensor(C.c2[:], C.se[:], zl_im, ALU.mult), reads=["se", "zst"], writes=["c2"])
    S.op("dve", lambda e: e.tensor_tensor(C.c1[:], C.c1[:], C.c2[:], ALU.subtract), reads=["c1", "c2"], writes=["c1"])
    S.op("dve", lambda e: e.tensor_tensor(C.c3[:], C.ce[:], zl_im, ALU.mult), reads=["ce", "zst"], writes=["c3"])
    S.op("dve", lambda e: e.tensor_tensor(C.c2[:], C.se[:], zl_re, ALU.mult), reads=["se", "zst", "c1"], writes=["c2"])
    S.op("dve", lambda e: e.tensor_tensor(C.c3[:], C.c3[:], C.c2[:], ALU.add), reads=["c3", "c2"], writes=["c3"])
    for j in range(8):
        dj = C.magp[:, j:j + 1].to_broadcast([128, 128])
        S.op("dve", lambda e, j=j, dj=dj: e.tensor_tensor_scan(C.wre[:, j, :], dj, C.wre[:, j, :], C.c1[:, j:j + 1], ALU.mult, ALU.add),
             reads=["magp", "wre", "c1"], writes=["wre"])
        S.op("dve", lambda e, j=j, dj=dj: e.tensor_tensor_scan(C.wim[:, j, :], dj, C.wim[:, j, :], C.c3[:, j:j + 1], ALU.mult, ALU.add),
             reads=["magp", "wim", "c3"], writes=["wim"])
    S.op("pool", lambda e: e.tensor_copy(C.zst[:, 0, :], C.wre[:, :, 127]), reads=["wre"], writes=["zst"])
    S.op("pool", lambda e: e.tensor_copy(C.zst[:, 1, :], C.wim[:, :, 127]), reads=["wim"], writes=["zst"])
    fl = lambda t: t[:].rearrange("p a b -> p (a b)")
    tB = C.s5tmp[:]
    S.op("dve", lambda e: e.tensor_tensor(fl(C.t2), fl(C.wre), fl(C.tabC), ALU.mult), reads=["wre", "tabC"], writes=["t2"])
    S.op("pool", lambda e: e.tensor_tensor(tB, fl(C.wim), fl(C.tabS), ALU.mult), reads=["wim", "tabS"], writes=["s5tmp"])
    S.op("dve", lambda e: e.tensor_tensor(fl(C.xre), fl(C.t2), tB, ALU.subtract), reads=["t2", "s5tmp"], writes=["xre"])
    S.op("pool", lambda e: e.tensor_tensor(tB, fl(C.wre), fl(C.tabS), ALU.mult), reads=["wre", "tabS", "xre"], writes=["s5tmp"])
    S.op("dve", lambda e: e.tensor_tensor(fl(C.t2), fl(C.wim), fl(C.tabC), ALU.mult), reads=["wim", "tabC", "xre"], writes=["t2"])
    S.op("pool", lambda e: e.tensor_tensor(fl(C.xim), tB, fl(C.t2), ALU.add), reads=["t2", "s5tmp"], writes=["xim"])
    yP = A[0][:, 2:4, :]
    for m in range(2):
        for i, j in enumerate(range(4 * m, 4 * m + 4)):
            S.op("pe", lambda e, m=m, j=j, i=i: e.matmul(yP[:, m, :], C.CTre[:, j, :], C.xre[:, j, :], start=(i == 0), stop=False),
                 reads=["CTre", "xre"], writes=["bank0"])
            S.op("pe", lambda e, m=m, j=j, i=i: e.matmul(yP[:, m, :], C.CTimn[:, j, :], C.xim[:, j, :], start=False, stop=(i == 3)),
                 reads=["CTimn", "xim"], writes=["bank0"])
    f2 = lambda t: t[:].rearrange("p a b -> p (a b)")
    for m in range(2):
        S.op("dve", lambda e, m=m: e.scalar_tensor_tensor(C.yv[:, m, :], C.uF[:, m, :], C.s5d[:, m:m + 1], yP[:, m, :], ALU.mult, ALU.add),
             reads=["uF", "s5d", "bank0"], writes=["yv"])
    S.op("dve", lambda e: e.tensor_tensor(f2(C.g1), f2(C.yv), f2(C.yv), ALU.mult), reads=["yv"], writes=["g1"])
    S.op("dve", lambda e: e.tensor_scalar(f2(C.g1), f2(C.g1), 0.044715, 1.0, ALU.mult, ALU.add), reads=["g1"], writes=["g1"])
    S.op("dve", lambda e: e.tensor_tensor(f2(C.g1), f2(C.g1), f2(C.yv), ALU.mult), reads=["g1", "yv"], writes=["g1"])
    S.op("act", lambda e: e.activation(f2(C.g1), f2(C.g1), AF.Exp, scale=-GELU_C), reads=["g1"], writes=["g1"])
    S.op("dve", lambda e: e.tensor_scalar(f2(C.g1), f2(C.g1), 1.0, None, ALU.add), reads=["g1"], writes=["g1"])
    S.op("dve", lambda e: e.reciprocal(f2(C.g1), f2(C.g1)), reads=["g1"], writes=["g1"])
    S.op("dve", lambda e: e.tensor_tensor(f2(C.zg), f2(C.yv), f2(C.g1), ALU.mult), reads=["g1", "yv"], writes=["zg"])
    S.op("dve", lambda e: e.tensor_copy(f2(C.zgb), f2(C.zg)), reads=["zg"], writes=["zgb"])
    gP = A[1]
    for m2 in range(2):
        for k in range(2):
            S.op("pe", lambda e, m2=m2, k=k: e.matmul(gP[:, m2, :], C.glus[:, k, m2 * 128:(m2 + 1) * 128], C.zgb[:, k, :],
                                                     start=(k == 0), stop=(k == 1)),
                 reads=["glus", "zgb"], writes=["bank1"])
    S.op("act", lambda e: e.activation(f2(C.g1), gP[:, 0:2, :].rearrange("p a b -> p (a b)"), AF.Exp, scale=-1.0), reads=["bank1"], writes=["g1"])
    S.op("dve", lambda e: e.tensor_scalar(f2(C.g1), f2(C.g1), 1.0, None, ALU.add), reads=["g1"], writes=["g1"])
    S.op("dve", lambda e: e.reciprocal(f2(C.g1), f2(C.g1)), reads=["g1"], writes=["g1"])
    S.op("dve", lambda e: e.tensor_tensor(f2(C.o5), f2(C.zg), f2(C.g1), ALU.mult), reads=["g1", "zg"], writes=["o5"])
    S.op("dve", lambda e: e.tensor_tensor(f2(C.zgb), f2(C.o5), f2(C.o5), ALU.mult), reads=["o5"], writes=["zgb"])
    sP = A[1][:, 2:4, :]
    for m in range(2):
        S.op("pe", lambda e, m=m: e.matmul(sP[:, 0, :], C.ones256b[:], C.zgb[:, m, :], start=(m == 0), stop=(m == 1)),
             reads=["ones256b", "zgb"], writes=["bank1"])
    S.op("act", lambda e: e.activation(C.rs5[:], sP[:, 0, :], AF.Ln, bias=C.epst[:, 0:1]), reads=["bank1", "epst"], writes=["rs5"])
    S.op("act", lambda e: e.activation(C.rs5[:], C.rs5[:], AF.Exp, scale=-0.5), reads=["rs5"], writes=["rs5"])
    for m in range(2):
        S.op("dve", lambda e, m=m: e.scalar_tensor_tensor(C.mixT[:, 6 + m, :], C.o5[:, m, :], C.s5nw[:, m:m + 1], C.rs5[:], ALU.mult, ALU.mult),
             reads=["o5", "s5nw", "rs5"], writes=["mixT"])


def hgrn_setup(S, C, P, layer):
    S.dma("ld_hnw", C.hnw[:], P["hnw"], writes=["hnw"])
    S.op("dve", lambda e: e.memset(C.ones64[:], 1.0 / 64.0), writes=["ones64"])
    S.op("dve", lambda e: e.memset(C.Sst[:], 0.0), writes=["Sst"])
    if layer == 0:
        S.op("dve", lambda e: e.memset(C.oml[:], 1.0), writes=["oml"])
    else:
        S.dma("ld_lbl0", C.lbl0[:], P["lbl0"], writes=["lbl0"])
        S.dma("ld_lbl1", C.lbl1[:], P["lbl1"], writes=["lbl1"])
        S.op("dve", lambda e: e.tensor_tensor(C.oml[:], C.lbl0[:], C.lbl1[:], ALU.subtract), reads=["lbl0", "lbl1"], writes=["oml"])
        S.op("act", lambda e: e.activation(C.oml[:], C.oml[:], AF.Exp), reads=["oml"], writes=["oml"])
        S.op("dve", lambda e: e.tensor_scalar(C.oml[:], C.oml[:], 1.0, None, ALU.add), reads=["oml"], writes=["oml"])
        S.op("dve", lambda e: e.reciprocal(C.oml[:], C.oml[:]), reads=["oml"], writes=["oml"])
        S.op("dve", lambda e: e.tensor_scalar(C.oml[:], C.oml[:], -1.0, 1.0, ALU.mult, ALU.add), reads=["oml"], writes=["oml"])


def hgrn_chunk(S, C, first):
    A = C.bank
    H = lambda t: t[0:64]
    fl = lambda t: t[:].rearrange("p a b -> p (a b)")
    fb = lambda b: b[0:64].rearrange("p a b -> p (a b)")
    for g in range(4):
        for h in range(4):
            c0 = g * 256 + h * 64
            for kt in range(8):
                S.op("pe", lambda e, g=g, h=h, kt=kt, c0=c0: e.matmul(A[g][0:64, h, :], C.wH[:, kt, c0:c0 + 64], C.hnT[:, kt, :],
                                                                     start=(kt == 0), stop=(kt == 7)),
                     reads=["wH", "hnT"], writes=["bank%d" % g])
    S.op("act", lambda e: e.activation(fl(C.he1), fb(A[1]), AF.Exp, scale=-1.0), reads=["bank1"], writes=["he1"])
    S.op("dve", lambda e: e.tensor_scalar(fl(C.hden), fl(C.he1), 1.0, None, ALU.add), reads=["he1"], writes=["hden"])
    S.op("dve", lambda e: e.reciprocal(fl(C.hden), fl(C.hden)), reads=["hden"], writes=["hden"])
    S.op("dve", lambda e: e.tensor_tensor(fl(C.he1), fl(C.he1), fl(C.hden), ALU.mult), reads=["he1", "hden"], writes=["he1"])
    S.op("dve", lambda e: e.tensor_tensor(C.hk[:], C.he1[:], C.oml[:].unsqueeze(2).to_broadcast([64, 4, 128]), ALU.mult),
         reads=["he1", "oml"], writes=["hk"])
    S.op("dve", lambda e: e.tensor_scalar(fl(C.hden), fl(C.hk), -1.0, 1.0, ALU.mult, ALU.add), reads=["hk"], writes=["hden"])
    S.op("act", lambda e: e.activation(fl(C.hden), fl(C.hden), AF.Ln), reads=["hden"], writes=["hden"])
    S.op("dve", lambda e: e.tensor_tensor_scan(fl(C.hc), C.hreset[:], fl(C.hden), 0.0, ALU.mult, ALU.add), reads=["hreset", "hden"], writes=["hc"])
    S.op("act", lambda e: e.activation(fl(C.hec), fl(C.hc), AF.Exp), reads=["hc"], writes=["hec"])
    S.op("act", lambda e: e.activation(fl(C.hof), fl(C.hc), AF.Exp, scale=-1.0), reads=["hc"], writes=["hof"])
    S.op("act", lambda e: e.activation(fl(C.he1), fb(A[0]), AF.Exp, scale=-1.0), reads=["bank0"], writes=["he1"])
    S.op("dve", lambda e: e.tensor_scalar(fl(C.he1), fl(C.he1), 1.0, None, ALU.add), reads=["he1"], writes=["he1"])
    S.op("dve", lambda e: e.reciprocal(fl(C.he1), fl(C.he1)), reads=["he1"], writes=["he1"])
    S.op("dve", lambda e: e.tensor_tensor(fl(C.hqt), fb(A[0]), fl(C.he1), ALU.mult), reads=["bank0", "he1"], writes=["hqt"])
    S.op("dve", lambda e: e.tensor_tensor(fl(C.hqt), fl(C.hqt), fl(C.hec), ALU.mult), reads=["hqt", "hec"], writes=["hqt"])
    S.op("dve", lambda e: e.tensor_tensor(fl(C.hkt), fl(C.hk), fl(C.hof), ALU.mult), reads=["hk", "hof"], writes=["hkt"])
    r4 = lambda t: t[:].rearrange("p h (n c) -> p h n c", c=32)
    S.op("dve", lambda e: e.tensor_tensor(r4(C.hkh), r4(C.hkt), r4(C.hec)[:, :, :, 31:32].to_broadcast([64, 4, 4, 32]), ALU.mult),
         reads=["hkt", "hec"], writes=["hkh"])
    S.op("act", lambda e: e.copy(fl(C.hv), fb(A[2])), reads=["bank2"], writes=["hv"])
    S.op("act", lambda e: e.activation(fl(C.he1), fb(A[3]), AF.Exp, scale=-1.0), reads=["bank3"], writes=["he1"])
    S.op("dve", lambda e: e.tensor_scalar(fl(C.he1), fl(C.he1), 1.0, None, ALU.add), reads=["he1"], writes=["he1"])
    S.op("dve", lambda e: e.reciprocal(fl(C.he1), fl(C.he1)), reads=["he1"], writes=["he1"])
    S.op("dve", lambda e: e.tensor_tensor(fl(C.hgs), fb(A[3]), fl(C.he1), ALU.mult), reads=["bank3", "he1"], writes=["hc"])
    scP = A[4][0:32, 0, :].rearrange("p (h t) -> p h t", h=4)
    trP = A[5][0:32]
    oP = A[2]
    dSP = A[3][0:64, 0:2, :].rearrange("p a (b c) -> p (a b) c", c=64)
    for n in range(4):
        sl = slice(32 * n, 32 * n + 32)
        for h in range(4):
            S.op("pe", lambda e, h=h, sl=sl: e.matmul(scP[:, h, :], C.hkt[:, h, sl], C.hqt[:, h, sl], start=True, stop=True),
                 reads=["hkt", "hqt"], writes=["bank4"])
        S.op("dve", lambda e: e.tensor_tensor(C.hscm[:], scP, C.mask32[:].rearrange("p (h t) -> p h t", h=4), ALU.mult),
             reads=["bank4", "mask32"], writes=["hscm"])
        for h in range(4):
            S.op("pe", lambda e, h=h, sl=sl: e.transpose(trP[:, h, 0:64], C.hkh[:, h, sl], C.identf[0:64, 0:64]),
                 reads=["hkh", "identf"], writes=["bank5"])
            S.op("pe", lambda e, h=h, sl=sl: e.transpose(trP[:, h, 64:128], C.hv[:, h, sl], C.identf[0:64, 0:64]),
                 reads=["hv", "identf"], writes=["bank5"])
        S.op("act", lambda e: e.copy(C.hkvT[:], trP), reads=["bank5"], writes=["hkvT"])
        for h in range(4):
            S.op("pe", lambda e, h=h, sl=sl: e.matmul(oP[0:64, h, sl], C.hkvT[:, h, 64:128], C.hscm[:, h, :], start=True, stop=False),
                 reads=["hkvT", "hscm"], writes=["bank2"])
            S.op("pe", lambda e, h=h, sl=sl: e.matmul(oP[0:64, h, sl], C.Sst[:, h, :], C.hqt[:, h, sl], start=False, stop=True),
                 reads=["Sst", "hqt"], writes=["bank2"])
        for h in range(4):
            S.op("pe", lambda e, h=h: e.matmul(dSP[:, h, :], C.hkvT[:, h, 0:64], C.hkvT[:, h, 64:128], start=True, stop=True),
                 reads=["hkvT"], writes=["bank3"])
        S.op("dve", lambda e, n=n: e.tensor_tensor(C.Sst[:], C.Sst[:], C.hec[:, :, 32 * n + 31:32 * n + 32].to_broadcast([64, 4, 64]), ALU.mult),
             reads=["Sst", "hec"], writes=["Sst"])
        S.op("dve", lambda e: e.tensor_tensor(C.Sst[:], C.Sst[:], dSP, ALU.add), reads=["Sst", "bank3"], writes=["Sst"])
    S.op("act", lambda e: e.copy(fl(C.hof), fb(oP)), reads=["bank2"], writes=["hof"])
    S.op("dve", lambda e: e.tensor_tensor(fl(C.he1), fl(C.hof), fl(C.hof), ALU.mult), reads=["hof"], writes=["he1"])
    msP = fb(A[4])
    S.op("pe", lambda e: e.matmul(msP, C.ones64[:], fl(C.he1), start=True, stop=True), reads=["ones64", "he1"], writes=["bank4"])
    S.op("act", lambda e: e.activation(fl(C.hden), msP, AF.Ln, bias=C.epst[0:64, 0:1]), reads=["bank4", "epst"], writes=["hden"])
    S.op("act", lambda e: e.activation(fl(C.hden), fl(C.hden), AF.Exp, scale=-0.5), reads=["hden"], writes=["hden"])
    S.op("dve", lambda e: e.scalar_tensor_tensor(fl(C.hof), fl(C.hof), C.hnw[:, 0:1], fl(C.hden), ALU.mult, ALU.mult),
         reads=["hof", "hnw", "hden"], writes=["hof"])
    S.op("dve", lambda e: e.tensor_tensor(fl(C.mixH), fl(C.hof), fl(C.hgs), ALU.mult), reads=["hof", "hc"], writes=["mixH"])


def gdn_setup(S, C, P):
    S.dma("ld_convw", C.convw[:].rearrange("p a b -> p (a b)"), P["convw"], writes=["convw"])
    S.dma("ld_alog", C.aneg[:], P["alog"], writes=["aneg"])
    S.dma("ld_dtb", C.dtb[:], P["dtb"], writes=["dtb"])
    S.dma("ld_gnw", C.gnw[:], P["gnw"], writes=["gnw"])
    S.op("act", lambda e: e.activation(C.aneg[:], C.aneg[:], AF.Exp), reads=["aneg"], writes=["aneg"])
    S.op("dve", lambda e: e.tensor_scalar(C.aneg[:], C.aneg[:], -1.0, None, ALU.mult), reads=["aneg"], writes=["aneg"])
    S.op("dve", lambda e: e.memset(C.ones128f[:], 1.0 / 128.0), writes=["ones128f"])
    S.op("dve", lambda e: e.memset(C.gx[:], 0.0), writes=["gx"])
    S.op("dve", lambda e: e.memset(C.gS[:], 0.0), writes=["gS"])


def gdn_chunk(S, C, first):
    A = C.bank
    f3 = lambda t: t[:].rearrange("p a b -> p (a b)")
    bc4 = lambda small: small.unsqueeze(2).to_broadcast([128, 4, 128])
    for g in range(4):
        for h in range(4):
            c0 = g * 512 + h * 128
            for kt in range(8):
                S.op("pe", lambda e, g=g, h=h, kt=kt, c0=c0: e.matmul(A[g][:, h, :], C.wG[:, kt, c0:c0 + 128], C.hnT[:, kt, :],
                                                                     start=(kt == 0), stop=(kt == 7)),
                     reads=["wG", "hnT"], writes=["bank%d" % g])
    abP = A[4][:, 0, 0:8]
    for kt in range(8):
        S.op("pe", lambda e, kt=kt: e.matmul(abP, C.hnT[:, kt, :], C.wab[:, kt, :], start=(kt == 0), stop=(kt == 7)),
             reads=["hnT", "wab"], writes=["bank4"])
    for g in range(3):
        S.op("act", lambda e, g=g: e.copy(C.gx[:, 4 * g:4 * g + 4, 3:131], A[g][:]), reads=["bank%d" % g], writes=["gx"])
    S.op("act", lambda e: e.activation(f3(C.gzs), f3(A[3]), AF.Exp, scale=-1.0), reads=["bank3"], writes=["gzs"])
    S.op("dve", lambda e: e.tensor_scalar(f3(C.gzs), f3(C.gzs), 1.0, None, ALU.add), reads=["gzs"], writes=["gzs"])
    S.op("dve", lambda e: e.reciprocal(f3(C.gzs), f3(C.gzs)), reads=["gzs"], writes=["gzs"])
    S.op("dve", lambda e: e.tensor_tensor(f3(C.gzs), f3(A[3]), f3(C.gzs), ALU.mult), reads=["bank3", "gzs"], writes=["gzs"])
    S.op("dve", lambda e: e.tensor_tensor(C.ga[:], A[4][:, 0, 0:4], C.dtb[:], ALU.add), reads=["bank4", "dtb"], writes=["ga"])
    S.op("act", lambda e: e.activation(C.ga[:], C.ga[:], AF.Exp), reads=["ga"], writes=["ga"])
    S.op("dve", lambda e: e.tensor_scalar(C.ga[:], C.ga[:], 1.0, None, ALU.add), reads=["ga"], writes=["ga"])
    S.op("act", lambda e: e.activation(C.ga[:], C.ga[:], AF.Ln), reads=["ga"], writes=["ga"])
    S.op("dve", lambda e: e.tensor_tensor(C.gg[:], C.ga[:], C.aneg[:], ALU.mult), reads=["ga", "aneg"], writes=["gg"])
    S.op("act", lambda e: e.activation(C.gbeta[:], A[4][:, 0, 4:8], AF.Exp, scale=-1.0), reads=["bank4"], writes=["gbeta"])
    S.op("dve", lambda e: e.tensor_scalar(C.gbeta[:], C.gbeta[:], 1.0, None, ALU.add), reads=["gbeta"], writes=["gbeta"])
    S.op("dve", lambda e: e.reciprocal(C.gbeta[:], C.gbeta[:]), reads=["gbeta"], writes=["gbeta"])
    cP = A[4][:, 1, 0:4]
    clP = A[4][:, 2, 0:4]
    S.op("pe", lambda e: e.matmul(cP, C.tri[:], C.gg[:], start=True, stop=True), reads=["tri", "gg"], writes=["bank4"])
    S.op("pe", lambda e: e.matmul(clP, C.onesf[:], C.gg[:], start=True, stop=True), reads=["onesf", "gg"], writes=["bank4"])
    S.op("act", lambda e: e.copy(C.gc[:], cP), reads=["bank4"], writes=["gc"])
    S.op("act", lambda e: e.activation(C.gec[:], cP, AF.Exp), reads=["bank4"], writes=["gec"])
    S.op("act", lambda e: e.activation(C.gecl[:], clP, AF.Exp), reads=["bank4"], writes=["gecl"])
    S.op("dve", lambda e: e.tensor_tensor(C.gedl[:], clP, C.gc[:], ALU.subtract), reads=["bank4", "gc"], writes=["gedl"])
    S.op("act", lambda e: e.activation(C.gedl[:], C.gedl[:], AF.Exp), reads=["gedl"], writes=["gedl"])
    S.op("dve", lambda e: e.tensor_tensor(C.gbec[:], C.gbeta[:], C.gec[:], ALU.mult), reads=["gbeta", "gec"], writes=["gbec"])
    S.op("dve", lambda e: e.tensor_scalar(C.gnb[:], C.gbeta[:], -1.0, None, ALU.mult), reads=["gbeta"], writes=["gnb"])
    cw = lambda j: C.convw[:, :, j:j + 1].to_broadcast([128, 12, 128])
    S.op("pool", lambda e: e.tensor_tensor(C.gcv[:], C.gx[:, :, 0:128], cw(0), ALU.mult), reads=["gx", "convw"], writes=["gcv"])
    for j in range(1, 4):
        S.op("pool", lambda e, j=j: e.tensor_tensor(C.gtmp[:], C.gx[:, :, j:j + 128], cw(j), ALU.mult), reads=["gx", "convw"], writes=["gtmp"])
        S.op("dve", lambda e: e.tensor_tensor(f3(C.gcv), f3(C.gcv), f3(C.gtmp), ALU.add), reads=["gcv", "gtmp"], writes=["gcv"])
    S.op("pool", lambda e: e.tensor_copy(C.gx[:, :, 0:3], C.gx[:, :, 128:131]), reads=["gx"], writes=["gx"])
    S.op("act", lambda e: e.activation(f3(C.gtmp), f3(C.gcv), AF.Exp, scale=-1.0), reads=["gcv"], writes=["gtmp"])
    S.op("dve", lambda e: e.tensor_scalar(f3(C.gtmp), f3(C.gtmp), 1.0, None, ALU.add), reads=["gtmp"], writes=["gtmp"])
    S.op("dve", lambda e: e.reciprocal(f3(C.gtmp), f3(C.gtmp)), reads=["gtmp"], writes=["gtmp"])
    S.op("dve", lambda e: e.tensor_tensor(f3(C.gcv), f3(C.gcv), f3(C.gtmp), ALU.mult), reads=["gcv", "gtmp"], writes=["gcv"])
    qk = C.gcv[:, 0:8, :]
    S.op("dve", lambda e: e.tensor_tensor(C.gtmp[:, 0:8, :], qk, qk, ALU.mult), reads=["gcv"], writes=["gtmp"])
    for i in range(2):
        S.op("pe", lambda e, i=i: e.matmul(f3(A[i]), C.onesf[:], C.gtmp[:, 4 * i:4 * i + 4, :].rearrange("p a b -> p (a b)"), start=True, stop=True),
             reads=["onesf", "gtmp"], writes=["bank%d" % i])
    for i in range(2):
        S.op("act", lambda e, i=i: e.activation(C.gtmp[:, 8 + 2 * i:10 + 2 * i, :].rearrange("p a b -> p (a b)")[:, 0:256], f3(A[i])[:, 0:256], AF.Ln, bias=C.epst[:, 0:1]),
             reads=["bank%d" % i, "epst"], writes=["gtmp"]) if False else None
    S.op("act", lambda e: e.activation(C.gtmp[:, 8:12, :].rearrange("p a b -> p (a b)"), f3(A[0]), AF.Ln, bias=C.epst[:, 0:1]), reads=["bank0", "epst"], writes=["gtmp"])
    S.op("act", lambda e: e.activation(C.gtmp[:, 8:12, :].rearrange("p a b -> p (a b)"), C.gtmp[:, 8:12, :].rearrange("p a b -> p (a b)"), AF.Exp, scale=-0.5), reads=["gtmp"], writes=["gtmp"])
    S.op("dve", lambda e: e.scalar_tensor_tensor(C.gcv[:, 0:4, :], C.gcv[:, 0:4, :], 128.0 ** -0.5, C.gtmp[:, 8:12, :], ALU.mult, ALU.mult),
         reads=["gcv", "gtmp"], writes=["gcv"])
    S.op("act", lambda e: e.activation(C.gtmp[:, 8:12, :].rearrange("p a b -> p (a b)"), f3(A[1]), AF.Ln, bias=C.epst[:, 0:1]), reads=["bank1", "epst", "gcv"], writes=["gtmp"])
    S.op("act", lambda e: e.activation(C.gtmp[:, 8:12, :].rearrange("p a b -> p (a b)"), C.gtmp[:, 8:12, :].rearrange("p a b -> p (a b)"), AF.Exp, scale=-0.5), reads=["gtmp"], writes=["gtmp"])
    S.op("dve", lambda e: e.tensor_tensor(C.gcv[:, 4:8, :], C.gcv[:, 4:8, :], C.gtmp[:, 8:12, :], ALU.mult), reads=["gcv", "gtmp"], writes=["gcv"])
    qn = lambda h: C.gcv[:, h, :]
    kn = lambda h: C.gcv[:, 4 + h, :]
    vf = lambda h: C.gcv[:, 8 + h, :]
    S.op("dve", lambda e: e.tensor_tensor(C.gdiag[:], C.identf[:].unsqueeze(1).to_broadcast([128, 4, 128]), bc4(C.gc[:]), ALU.mult),
         reads=["identf", "gc"], writes=["gdiag"])
    S.op("pe", lambda e: e.matmul(f3(A[2]), C.onesf[:], f3(C.gdiag), start=True, stop=True), reads=["onesf", "gdiag"], writes=["bank2"])
    S.op("pe", lambda e: e.matmul(f3(A[3]), C.onesf[:], f3(C.gdiag), start=True, stop=False), reads=["onesf", "gdiag"], writes=["bank3"])
    S.op("pe", lambda e: e.matmul(f3(A[3]), C.identf[:], C.gmask4[:], start=False, stop=True), reads=["identf", "gmask4"], writes=["bank3"])
    S.op("act", lambda e: e.activation(f3(C.gecb), f3(A[2]), AF.Exp), reads=["bank2"], writes=["gecb"])
    for h in range(4):
        S.op("act", lambda e, h=h: e.activation(C.gD[:, h, :], A[3][:, h, :], AF.Exp, scale=-1.0, bias=C.gc[:, h:h + 1]),
             reads=["bank3", "gc"], writes=["gD"])
    S.op("dve", lambda e: e.tensor_tensor(f3(C.gDst), f3(C.gD), C.strict4[:], ALU.mult), reads=["gD", "strict4"], writes=["gDst"])
    S.op("dve", lambda e: e.tensor_tensor(f3(C.gecb), C.gcv[:, 0:4, :].rearrange("p a b -> p (a b)"), f3(C.gecb), ALU.mult),
         reads=["gcv", "gecb"], writes=["gecb"])
    for h in range(4):
        S.op("pe", lambda e, h=h: e.matmul(A[5][:, h, :], kn(h), kn(h), start=True, stop=True), reads=["gcv"], writes=["bank5"])
        S.op("pe", lambda e, h=h: e.matmul(A[6][:, h, :], qn(h), kn(h), start=True, stop=True), reads=["gcv"], writes=["bank6"])
    S.op("dve", lambda e: e.tensor_tensor(f3(C.gdiag), f3(A[5]), f3(C.gDst), ALU.mult), reads=["bank5", "gDst"], writes=["gdiag"])
    S.op("dve", lambda e: e.tensor_tensor(C.gXTa[:], C.gdiag[:], bc4(C.gnb[:]), ALU.mult), reads=["gdiag", "gnb"], writes=["gXTa"])
    S.op("dve", lambda e: e.tensor_tensor(f3(C.gAm), f3(A[6]), f3(C.gD), ALU.mult), reads=["bank6", "gD"], writes=["gAm"])
    for h in range(4):
        S.op("pe", lambda e, h=h: e.transpose(A[0][:, h, :], C.gXTa[:, h, :], C.identf[:]), reads=["gXTa", "identf"], writes=["bank0"])
        S.op("pe", lambda e, h=h: e.transpose(A[1][:, h, :], C.gAm[:, h, :], C.identf[:]), reads=["gAm", "identf"], writes=["bank1"])
    S.op("act", lambda e: e.copy(f3(C.gXa), f3(A[0])), reads=["bank0"], writes=["gXa"])
    S.op("dve", lambda e: e.tensor_tensor(C.gR[:], A[0][:], C.identf[:].unsqueeze(1).to_broadcast([128, 4, 128]), ALU.add),
         reads=["bank0", "identf"], writes=["gR"])
    S.op("act", lambda e: e.copy(f3(C.gAT), f3(A[1])), reads=["bank1"], writes=["gAT"])
    X, XT = (C.gXa, "gXa"), (C.gXTa, "gXTa")
    Xn, XTn = (C.gXb, "gXb"), (C.gXTb, "gXTb")
    for j in range(1, 7):
        for h in range(4):
            S.op("pe", lambda e, h=h, X=X, XT=XT: e.matmul(A[5][:, h, :], X[0][:, h, :], XT[0][:, h, :], start=True, stop=True),
                 reads=[X[1], XT[1]], writes=["bank5"])
        S.op("act", lambda e, XTn=XTn: e.copy(f3(XTn[0]), f3(A[5])), reads=["bank5"], writes=[XTn[1]])
        if j < 6:
            for h in range(4):
                S.op("pe", lambda e, h=h, X=X, XT=XT: e.matmul(A[6][:, h, :], XT[0][:, h, :], X[0][:, h, :], start=True, stop=True),
                     reads=[X[1], XT[1]], writes=["bank6"])
            S.op("act", lambda e, Xn=Xn: e.copy(f3(Xn[0]), f3(A[6])), reads=["bank6"], writes=[Xn[1]])
        for h in range(4):
            S.op("pe", lambda e, h=h, XTn=XTn: e.matmul(A[0][:, h, :], XTn[0][:, h, :], C.gR[:, h, :], start=True, stop=True),
                 reads=[XTn[1], "gR"], writes=["bank0"])
        S.op("dve", lambda e: e.tensor_tensor(f3(C.gR), f3(C.gR), f3(A[0]), ALU.add), reads=["gR", "bank0"], writes=["gR"])
        X, XT, Xn, XTn = Xn, XTn, X, XT
    for h in range(4):
        S.op("pe", lambda e, h=h: e.transpose(A[1][:, h, :], kn(h), C.identf[:]), reads=["gcv", "identf"], writes=["bank1"])
        S.op("pe", lambda e, h=h: e.transpose(A[2][:, h, :], vf(h), C.identf[:]), reads=["gcv", "identf"], writes=["bank2"])
    S.op("dve", lambda e: e.tensor_tensor(C.gkbe[:], A[1][:], bc4(C.gbec[:]), ALU.mult), reads=["bank1", "gbec"], writes=["gXb"])
    S.op("dve", lambda e: e.tensor_tensor(C.gkdec[:], A[1][:], bc4(C.gedl[:]), ALU.mult), reads=["bank1", "gedl"], writes=["gXTa"])
    S.op("dve", lambda e: e.tensor_tensor(C.gvb[:], A[2][:], bc4(C.gbeta[:]), ALU.mult), reads=["bank2", "gbeta"], writes=["gXTb"])
    for h in range(4):
        S.op("pe", lambda e, h=h: e.matmul(A[3][:, h, :], C.gkbe[:, h, :], C.gR[:, h, :], start=True, stop=True), reads=["gXb", "gR"], writes=["bank3"])
        S.op("pe", lambda e, h=h: e.matmul(A[5][:, h, :], C.gR[:, h, :], C.gvb[:, h, :], start=True, stop=True), reads=["gXTb", "gR"], writes=["bank5"])
    S.op("act", lambda e: e.copy(f3(C.gwT), f3(A[3])), reads=["bank3"], writes=["gD"])
    S.op("act", lambda e: e.copy(f3(C.gu), f3(A[5])), reads=["bank5"], writes=["gAm"])
    for h in range(4):
        S.op("pe", lambda e, h=h: e.matmul(A[6][:, h, :], C.gwT[:, h, :], C.gS[:, h, :], start=True, stop=True), reads=["gD", "gS"], writes=["bank6"])
    S.op("dve", lambda e: e.tensor_tensor(f3(C.gvn), f3(C.gu), f3(A[6]), ALU.subtract), reads=["gAm", "bank6"], writes=["gDst"])
    for h in range(4):
        S.op("pe", lambda e, h=h: e.matmul(A[0][:, h, :], C.gS[:, h, :], C.gecb[:, h, :], start=True, stop=False), reads=["gS", "gecb"], writes=["bank0"])
        S.op("pe", lambda e, h=h: e.matmul(A[0][:, h, :], C.gvn[:, h, :], C.gAT[:, h, :], start=False, stop=True), reads=["gDst", "gAT"], writes=["bank0"])
    for h in range(4):
        S.op("pe", lambda e, h=h: e.matmul(A[1][:, h, :], C.gkdec[:, h, :], C.gvn[:, h, :], start=True, stop=True), reads=["gXTa", "gDst"], writes=["bank1"])
    S.op("dve", lambda e: e.tensor_tensor(C.gS[:], C.gS[:], bc4(C.gecl[:]), ALU.mult), reads=["gS", "gecl"], writes=["gS"])
    S.op("dve", lambda e: e.tensor_tensor(f3(C.gS), f3(C.gS), f3(A[1]), ALU.add), reads=["gS", "bank1"], writes=["gS"])
    S.op("act", lambda e: e.copy(f3(C.gXa), f3(A[0])), reads=["bank0"], writes=["gXa"])
    S.op("dve", lambda e: e.tensor_tensor(f3(C.gXb), f3(C.gXa), f3(C.gXa), ALU.mult), reads=["gXa"], writes=["gXb"])
    S.op("pe", lambda e: e.matmul(f3(A[2]), C.ones128f[:], f3(C.gXb), start=True, stop=True), reads=["ones128f", "gXb"], writes=["bank2"])
    S.op("act", lambda e: e.activation(f3(C.gXb), f3(A[2]), AF.Ln, bias=C.epst[:, 0:1]), reads=["bank2", "epst"], writes=["gXb"])
    S.op("act", lambda e: e.activation(f3(C.gXb), f3(C.gXb), AF.Exp, scale=-0.5), reads=["gXb"], writes=["gXb"])
    S.op("dve", lambda e: e.scalar_tensor_tensor(f3(C.gXa), f3(C.gXa), C.gnw[:, 0:1], f3(C.gXb), ALU.mult, ALU.mult), reads=["gXa", "gnw", "gXb"], writes=["gXa"])
    S.op("dve", lambda e: e.tensor_tensor(C.mixT[:, 2:6, :], C.gXa[:], C.gzs[:], ALU.mult), reads=["gXa", "gzs"], writes=["mixT"])


def alloc_mixer(nc, st, S, pfx=""):
    C = Ctx()
    def sb(name, shape, dt=F32):
        t = st.enter_context(nc.sbuf_tensor(pfx + name, shape, dt)); setattr(C, name, t); return t
    def ps(name, shape, dt=F32):
        return st.enter_context(nc.psum_tensor(pfx + name, shape, dt))
    for k, (shp, dt) in CSHAPES.items():
        sb(k, shp, dt)
    C.idb = C.identb
    sb("epst", [128, 1]); sb("ones256b", [128, 128], BF16); sb("onesf", [128, 128])
    sb("wH", [128, 8, 1024], BF16); sb("wG", [128, 8, 2048], BF16); sb("wab", [128, 8, 8], BF16); sb("wu", [128, 8, 256], BF16)
    sb("wout", [128, 6, 1024], BF16); sb("woutH", [64, 4, 1024], BF16)
    sb("nwcol", [128, 8])
    C.hs = [sb("h0", [128, 1024])]
    sb("ss", [128, 1]); sb("lnv", [128, 1]); sb("rstd", [128, 1])
    sb("hn", [128, 1024], BF16); sb("hnT", [128, 8, 128], BF16)
    C.junk = C.hn
    sb("mixT", [128, 8, 128], BF16); sb("mixH", [64, 4, 128], BF16)
    for n in ("tabC", "tabS", "t2", "wre", "wim"):
        sb(n, [128, 8, 128])
    sb("s5tmp", [128, 1024])
    C.kint = C.s5tmp[:].bitcast(mybir.dt.int32)
    for n in ("BTre", "BTim", "CTre", "CTimn", "xre", "xim"):
        sb(n, [128, 8, 128], BF16)
    for n in ("lre_p", "lim_p", "ls_p", "dtp", "lrep", "magp", "thp", "a128", "t128", "se", "ce", "c1", "c2", "c3"):
        sb(n, [128, 8])
    sb("zst", [128, 2, 8]); sb("s5d", [128, 2]); sb("s5nw", [128, 2]); sb("glus", [128, 2, 256], BF16)
    sb("uF", [128, 2, 128]); sb("uB", [128, 2, 128], BF16)
    for n in ("yv", "g1", "zg", "o5"):
        sb(n, [128, 2, 128])
    sb("zgb", [128, 2, 128], BF16); sb("rs5", [128, 128])
    for n in ("he1", "hden", "hk", "hc", "hec", "hqt", "hkt", "hkh", "hv", "hof"):
        sb(n, [64, 4, 128])
    C.hgs = C.hc
    sb("hscm", [32, 4, 32]); sb("hkvT", [32, 4, 128]); sb("Sst", [64, 4, 64]); sb("ones64", [64, 64])
    sb("oml", [64, 4]); sb("lbl0", [64, 4]); sb("lbl1", [64, 4]); sb("hnw", [64, 1])
    sb("gx", [128, 12, 131]); sb("gcv", [128, 12, 128]); sb("gtmp", [128, 12, 128])
    for n in ("gzs", "gD", "gDst", "gAm", "gdiag", "gXa", "gXb", "gXTa", "gXTb", "gR", "gecb", "gS", "gAT"):
        sb(n, [128, 4, 128])
    C.gwT = C.gD; C.gvn = C.gDst; C.gu = C.gAm
    C.gkbe = C.gXb; C.gkdec = C.gXTa; C.gvb = C.gXTb
    V = lambda a: type("V", (), {"__getitem__": lambda self, k, a=a: a[k]})()
    C.stg0 = V(C.gtmp[:, 0:8, :].rearrange("p a b -> p (a b)"))
    C.stg1 = V(C.gcv[:, 0:8, :].rearrange("p a b -> p (a b)"))
    C.ho = V(C.gtmp[:, 0:8, :].rearrange("p a b -> p (a b)"))
    for n in ("ga", "gg", "gbeta", "gc", "gec", "gecl", "gedl", "gbec", "gnb", "aneg", "dtb"):
        sb(n, [128, 4])
    sb("convw", [128, 12, 4]); sb("gnw", [128, 1]); sb("ones128f", [128, 128])
    wGf = C.wG[:].rearrange("p a b -> p (a b)").bitcast(F32)
    scr = [wGf[:, i * 1024:(i + 1) * 1024] for i in range(8)]
    scr += [C.t2[:].rearrange("p a b -> p (a b)"), C.wre[:].rearrange("p a b -> p (a b)")]
    C.scr = [type("V", (), {"__getitem__": lambda self, k, a=a: a[k]})() for a in scr]
    C.tp = ps("tp", [128, 8, 128], BF16)
    C.bank = [ps("bank%d" % i, [128, 4, 128]) for i in range(7)]
    return C


def declare_params(nc, suffix=""):
    P = {}
    for k, shp in PSHAPES.items():
        P[k] = nc.dram_tensor("p_" + k + suffix, shp, F32, kind="ExternalInput").ap()
    return P


def mixer_setup(S, C, P, Cst, layer):
    for k in CSHAPES:
        S.dma("ld_c_" + k, getattr(C, k)[:], Cst[k], writes=[k])
    S.op("dve", lambda e: e.memset(C.epst[:], EPS), writes=["epst"])
    S.op("dve", lambda e: e.memset(C.onesf[:], 1.0), writes=["onesf"])
    S.op("dve", lambda e: e.memset(C.ones256b[:], 1.0 / 256.0), writes=["ones256b"])
    S.op("dve", lambda e: e.memset(C.mixT[:], 0.0), writes=["mixT"])
    S.op("dve", lambda e: e.memset(C.mixH[:], 0.0), writes=["mixH"])
    s5_setup(S, C, P)
    S.barrier()
    S.dma("ld_nwcol", C.nwcol[:], P["nmixc"], writes=["nwcol"])
    stage = [C.stg0, C.stg1]
    K8 = list(range(8))
    load_weight_bf16(S, "wH", P["w_in"], K8, 1024, C.wH, 0, stage, 1, 0, C.nwcol)
    load_weight_bf16(S, "wG", P["w_in"], K8, 1024, C.wG[:, :, 0:1024], 0, stage, 1, 1024, C.nwcol)
    load_weight_bf16(S, "wG", P["w_in"], K8, 1024, C.wG[:, :, 1024:2048], 0, stage, 1, 2048, C.nwcol)
    load_weight_bf16(S, "wab", P["w_in"], K8, 8, C.wab, 0, stage, 8, 3072, C.nwcol)
    load_weight_bf16(S, "wu", P["w_in"], K8, 256, C.wu, 0, stage, 4, 3080, C.nwcol)
    load_weight_bf16(S, "wout", P["w_out"], list(range(2, 8)), 1024, C.wout, 2, stage, 1, 0)
    whv = P["w_out"][0:256, :].rearrange("(h p) n -> p h n", p=64)
    for i in range(4):
        sg = stage[i % 2][0:64, :]
        S.dma("wst%d" % (i % 2), sg, whv[:, i, :], writes=[("wstage", i % 2)])
        S.op("dve", lambda h, sg=sg, i=i: h.tensor_copy(C.woutH[:, i, :], sg), reads=[("wstage", i % 2)], writes=["woutH"])
    hgrn_setup(S, C, P, layer)
    if hasattr(C, "gdn_ready"):
        gdn_setup(S, C, P)
    S.barrier()


def out_proj_store(S, C, h, hkey, b, ydst, ykey):
    oP = [C.bank[4], C.bank[5]]
    for half in range(2):
        ob = oP[half][:].rearrange("p a b -> p (a b)")
        n = 0
        for hd in range(4):
            S.op("pe", lambda e, hd=hd, half=half, ob=ob: e.matmul(ob, C.mixH[:, hd, :], C.woutH[:, hd, half * 512:(half + 1) * 512],
                                                                  start=(hd == 0), stop=False),
                 reads=["mixH", "woutH"], writes=["bank%d" % (4 + half)])
        for ft in range(2, 8):
            S.op("pe", lambda e, ft=ft, half=half, ob=ob: e.matmul(ob, C.mixT[:, ft, :], C.wout[:, ft - 2, half * 512:(half + 1) * 512],
                                                                  start=False, stop=(ft == 7)),
                 reads=["mixT", "wout"], writes=["bank%d" % (4 + half)])
        S.op("dve", lambda e, half=half, ob=ob: e.tensor_tensor(C.ho[:, half * 512:(half + 1) * 512], ob, h[:, half * 512:(half + 1) * 512], ALU.add),
             reads=["bank%d" % (4 + half), hkey], writes=["gtmp"])
    S.dma("st0", ydst, C.ho[:], reads=["gtmp"], writes=[ykey])


def build_mixer(NT, parts=("s5",), debug=True, layer=0):
    nc = bass.Bass("TRN2", target_bir_lowering=False)
    x = nc.dram_tensor("x", [NT * 128, 1024], F32, kind="ExternalInput").ap()
    y = nc.dram_tensor("y", [NT * 128, 1024], F32, kind="ExternalOutput").ap()
    P = declare_params(nc)
    Cst = {k: nc.dram_tensor("c_" + k, shp, dt, kind="ExternalInput").ap() for k, (shp, dt) in CSHAPES.items()}
    dbg = nc.dram_tensor("dbg", [NT, 128, 1024], BF16, kind="ExternalOutput").ap() if debug else None
    dbgH = nc.dram_tensor("dbgH", [NT, 64, 512], BF16, kind="ExternalOutput").ap() if debug else None
    with ExitStack() as st:
        S = Sched(nc, st)
        C = alloc_mixer(nc, st, S)
        if "gdn" in parts:
            C.gdn_ready = True
        mixer_setup(S, C, P, Cst, layer)
        ykeys = []
        for c in range(NT):
            b = 0
            h = C.hs[b]
            S.dma("hl%d" % b, h[:], x[c * 128:(c + 1) * 128, :], writes=[("h", b)])
            rmsnorm_T(S, C, h, ("h", b), None, None)
            if "hgrn" in parts:
                hgrn_chunk(S, C, c == 0)
            if "gdn" in parts:
                gdn_chunk(S, C, c == 0)
            if "s5" in parts:
                s5_chunk(S, C, c == 0)
            if debug:
                S.dma("dbg", dbg[c], C.mixT[:].rearrange("p a b -> p (a b)"), reads=["mixT"], writes=[("dbg", c)])
                S.dma("dbgH", dbgH[c], C.mixH[:].rearrange("p a b -> p (a b)"), reads=["mixH"], writes=[("dbgH", c)])
                ykeys += [("dbg", c), ("dbgH", c)]
            out_proj_store(S, C, h, ("h", b), b, y[c * 128:(c + 1) * 128, :], ("y", c))
            ykeys.append(("y", c))
        S.final_wait("sp", ykeys)
        block = st.enter_context(nc.Block())
        S.run(block)
    return nc


def make_inmap(inp, l, xin):
    m = {"x": xin}
    for k, v in host_layer_params(inp, l).items():
        m["p_" + k] = v.reshape(PSHAPES[k])
    for k, v in host_consts().items():
        m["c_" + k] = v
    return m


def build_mlp(NT, final):
    nc = bass.Bass("TRN2", target_bir_lowering=False)
    x = nc.dram_tensor("x", [NT * 128, 1024], F32, kind="ExternalInput").ap()
    nw = nc.dram_tensor("p_nmlp", [128, 1024], F32, kind="ExternalInput").ap()
    fwd = nc.dram_tensor("p_finalw", [128, 1024], F32, kind="ExternalInput").ap()
    w1 = nc.dram_tensor("p_w1", [1024, 4096], F32, kind="ExternalInput").ap()
    w2 = nc.dram_tensor("p_w2", [4096, 1024], F32, kind="ExternalInput").ap()
    identb = nc.dram_tensor("c_identb", [128, 128], BF16, kind="ExternalInput").ap()
    y = nc.dram_tensor("y", [NT * 128, 1024], F32, kind="ExternalOutput").ap()
    with ExitStack() as st:
        C = Ctx()
        def sb(name, shape, dt=F32):
            t = st.enter_context(nc.sbuf_tensor(name, shape, dt)); setattr(C, name, t); return t
        def ps(name, shape, dt=F32):
            return st.enter_context(nc.psum_tensor(name, shape, dt))
        S = Sched(nc, st)
        sb("w1s", [128, 8, 4096], BF16); sb("w2s", [128, 32, 1024], BF16)
        stage = [sb("stg0", [128, 1024]), sb("stg1", [128, 1024])]
        sb("nws", [128, 1024]); sb("fws", [128, 1024]); sb("idb", [128, 128], BF16)
        C.hs = [sb("h0", [128, 1024]), sb("h1", [128, 1024])]
        sb("junk", [128, 1024]); sb("ss", [128, 1]); sb("lnv", [128, 1]); sb("rstd", [128, 1]); sb("epst", [128, 1])
        sb("hn", [128, 1024], BF16); sb("hnT", [128, 8, 128], BF16)
        rr = [sb("rr0", [128, 512]), sb("rr1", [128, 512])]
        sb("aT", [128, 32, 128], BF16)
        ho = [sb("ho0", [128, 1024]), sb("ho1", [128, 1024])]
        C.tp = ps("tp", [128, 8, 128], BF16)
        hid = [ps("hid%d" % i, [128, 4, 128]) for i in range(2)]
        ops_ = [ps("ops%d" % i, [128, 512]) for i in range(2)]
        S.op("dve", lambda e: e.memset(C.epst[:], EPS), writes=["epst"])
        S.dma("c0", C.nws[:], nw, writes=["nws"])
        S.dma("c2", C.fws[:], fwd, writes=["fws"])
        S.dma("c1", C.idb[:], identb, writes=["idb"])
        for cq in range(4):
            load_weight_bf16(S, "w1s", w1, list(range(8)), 1024, C.w1s[:, :, cq * 1024:(cq + 1) * 1024], 0, stage, 1, cq * 1024)
        load_weight_bf16(S, "w2s", w2, list(range(32)), 1024, C.w2s, 0, stage, 1, 0)
        ykeys = []
        for t in range(NT):
            b = t % 2
            h = C.hs[b]
            S.dma("hl%d" % b, h[:], x[t * 128:(t + 1) * 128, :], writes=[("h", b)])
            rmsnorm_T(S, C, h, ("h", b), C.nws, "nws")
            for g in range(8):
                hb = g % 2
                for mi in range(4):
                    m = g * 4 + mi
                    for kt in range(8):
                        S.op("pe", lambda e, m=m, mi=mi, kt=kt, hb=hb: e.matmul(
                            hid[hb][:, mi, :], C.w1s[:, kt, m * 128:(m + 1) * 128], C.hnT[:, kt, :],
                            start=(kt == 0), stop=(kt == 7)), reads=["w1s", "hnT"], writes=[("hid", hb)])
                S.op("act", lambda e, hb=hb: e.activation(rr[hb][:], hid[hb][:].rearrange("p a b -> p (a b)"), AF.Relu),
                     reads=[("hid", hb)], writes=[("rr", hb)])
                S.op("dve", lambda e, hb=hb, g=g: e.tensor_tensor(
                    C.aT[:, g * 4:(g + 1) * 4, :].rearrange("p a b -> p (a b)"), rr[hb][:], rr[hb][:], ALU.mult),
                    reads=[("rr", hb)], writes=["aT"])
            for half in range(2):
                for kt in range(32):
                    S.op("pe", lambda e, half=half, kt=kt: e.matmul(
                        ops_[half][:], C.aT[:, kt, :], C.w2s[:, kt, half * 512:(half + 1) * 512],
                        start=(kt == 0), stop=(kt == 31)), reads=["aT", "w2s"], writes=[("ops", half)])
                S.op("dve", lambda e, half=half, b=b, h=h: e.tensor_tensor(
                    ho[b][:, half * 512:(half + 1) * 512], ops_[half][:], h[:, half * 512:(half + 1) * 512], ALU.add),
                    reads=[("ops", half), ("h", b)], writes=[("ho", b)])
            if final:
                hb_ = ho[b]
                S.op("act", lambda e, hb_=hb_: e.activation(C.junk[:], hb_[:], AF.Square, scale=1.0 / 32.0, accum_out=C.ss[:]),
                     reads=[("ho", b)], writes=["junk", "ss"])
                S.op("act", lambda e: e.activation(C.lnv[:], C.ss[:], AF.Ln, bias=C.epst[:, 0:1]), reads=["ss", "epst"], writes=["lnv"])
                S.op("act", lambda e: e.activation(C.rstd[:], C.lnv[:], AF.Exp, scale=-0.5), reads=["lnv"], writes=["rstd"])
                S.op("dve", lambda e, hb_=hb_: e.scalar_tensor_tensor(hb_[:], hb_[:], C.rstd[:, 0:1], C.fws[:], ALU.mult, ALU.mult),
                     reads=[("ho", b), "rstd", "fws"], writes=[("ho", b)])
            S.dma("st%d" % b, y[t * 128:(t + 1) * 128, :], ho[b][:], reads=[("ho", b)], writes=[("y", t)])
            ykeys.append(("y", t))
        S.final_wait("sp", ykeys)
        block = st.enter_context(nc.Block())
        S.run(block)
    return nc


NT_FULL = 129
N_CORES = 8
MIXER_PARTS = ("hgrn", "gdn", "s5")


def kernel_unfused(**inputs):
    inp = {k: np.asarray(v) for k, v in inputs.items()}
    B = inp["x"].shape[0]
    NT = NT_FULL
    consts = host_consts()
    hcur = []
    for c in range(N_CORES):
        if c < B:
            hcur.append(np.concatenate([np.zeros((112, 1024), np.float32), inp["meta"].astype(np.float32), inp["x"][c]], 0))
        else:
            hcur.append(np.zeros((NT * 128, 1024), np.float32))
    nc_mix = {l: build_mixer(NT, MIXER_PARTS, debug=False, layer=l) for l in range(2)}
    nc_mlp = {False: build_mlp(NT, False), True: build_mlp(NT, True)}
    depth = inp["w_in"].shape[0]
    finalw = np.ascontiguousarray(np.broadcast_to(inp["final_norm_w"].reshape(1, 1024), (128, 1024)))
    for l in range(depth):
        lp = host_layer_params(inp, l)
        base = {"p_" + k: v.reshape(PSHAPES[k]) for k, v in lp.items()}
        for k, v in consts.items():
            base["c_" + k] = v
        ims = [dict(base, x=hcur[c]) for c in range(N_CORES)]
        res = run_bass_kernel_spmd(nc_mix[min(l, 1)], ims, core_ids=list(range(N_CORES)))
        hcur = [res.results[c]["y"] for c in range(N_CORES)]
        last = (l == depth - 1)
        mb = {"p_nmlp": lp["nmlp"], "p_finalw": finalw, "p_w1": lp["w1"], "p_w2": lp["w2"], "c_identb": consts["identb"]}
        ims = [dict(mb, x=hcur[c]) for c in range(N_CORES)]
        res = run_bass_kernel_spmd(nc_mlp[last], ims, core_ids=list(range(N_CORES)))
        hcur = [res.results[c]["y"] for c in range(N_CORES)]
    out = np.stack([hcur[c][128:] for c in range(B)], 0).astype(np.float32)
    return out


def emit_mixer_pass(nc, S, x, y, P, Cst, NT, layer, pfx):
    with ExitStack() as st:
        C = alloc_mixer(nc, st, S, pfx)
        C.gdn_ready = True
        S.barrier()
        mixer_setup(S, C, P, Cst, layer)
        ykeys = []
        for c in range(NT):
            h = C.hs[0]
            S.dma("hl0", h[:], x[c * 128:(c + 1) * 128, :], writes=[("h", 0)])
            rmsnorm_T(S, C, h, ("h", 0), None, None)
            hgrn_chunk(S, C, c == 0)
            gdn_chunk(S, C, c == 0)
            s5_chunk(S, C, c == 0)
            out_proj_store(S, C, h, ("h", 0), 0, y[c * 128:(c + 1) * 128, :], (pfx + "y", c))
            ykeys.append((pfx + "y", c))
        S.final_wait("sp", ykeys)
        S.barrier()
        block = st.enter_context(nc.Block())
        S.run(block)


def emit_mlp_pass(nc, S, x, y, P, finalw, identb, NT, final, pfx, skip_tiles=0):
    nw, w1, w2 = P["nmlp"], P["w1"], P["w2"]
    with ExitStack() as st:
        C = Ctx()
        def sb(name, shape, dt=F32):
            t = st.enter_context(nc.sbuf_tensor(pfx + name, shape, dt)); setattr(C, name, t); return t
        def ps(name, shape, dt=F32):
            return st.enter_context(nc.psum_tensor(pfx + name, shape, dt))
        S.barrier()
        sb("w1s", [128, 8, 4096], BF16); sb("w2s", [128, 32, 1024], BF16)
        stage = [sb("stg0", [128, 1024]), sb("stg1", [128, 1024])]
        sb("nws", [128, 1024]); sb("fws", [128, 1024]); sb("idb", [128, 128], BF16)
        C.hs = [sb("h0", [128, 1024]), sb("h1", [128, 1024])]
        sb("junk", [128, 1024]); sb("ss", [128, 1]); sb("lnv", [128, 1]); sb("rstd", [128, 1]); sb("epst", [128, 1])
        sb("hn", [128, 1024], BF16); sb("hnT", [128, 8, 128], BF16)
        rr = [sb("rr0", [128, 512]), sb("rr1", [128, 512])]
        sb("aT", [128, 32, 128], BF16)
        ho = [sb("ho0", [128, 1024]), sb("ho1", [128, 1024])]
        C.tp = ps("tp", [128, 8, 128], BF16)
        hid = [ps("hid%d" % i, [128, 4, 128]) for i in range(2)]
        ops_ = [ps("ops%d" % i, [128, 512]) for i in range(2)]
        S.op("dve", lambda e: e.memset(C.epst[:], EPS), writes=["epst"])
        S.dma("c0", C.nws[:], nw, writes=["nws"])
        S.dma("c2", C.fws[:], finalw, writes=["fws"])
        S.dma("c1", C.idb[:], identb, writes=["idb"])
        for cq in range(4):
            load_weight_bf16(S, "w1s", w1, list(range(8)), 1024, C.w1s[:, :, cq * 1024:(cq + 1) * 1024], 0, stage, 1, cq * 1024)
        load_weight_bf16(S, "w2s", w2, list(range(32)), 1024, C.w2s, 0, stage, 1, 0)
        ykeys = []
        for t in range(skip_tiles, NT):
            b = t % 2
            h = C.hs[b]
            S.dma("hl%d" % b, h[:], x[t * 128:(t + 1) * 128, :], writes=[("h", b)])
            rmsnorm_T(S, C, h, ("h", b), C.nws, "nws")
            for g in range(8):
                hb = g % 2
                for mi in range(4):
                    m = g * 4 + mi
                    for kt in range(8):
                        S.op("pe", lambda e, m=m, mi=mi, kt=kt, hb=hb: e.matmul(
                            hid[hb][:, mi, :], C.w1s[:, kt, m * 128:(m + 1) * 128], C.hnT[:, kt, :],
                            start=(kt == 0), stop=(kt == 7)), reads=["w1s", "hnT"], writes=[("hid", hb)])
                S.op("act", lambda e, hb=hb: e.activation(rr[hb][:], hid[hb][:].rearrange("p a b -> p (a b)"), AF.Relu),
                     reads=[("hid", hb)], writes=[("rr", hb)])
                S.op("dve", lambda e, hb=hb, g=g: e.tensor_tensor(
                    C.aT[:, g * 4:(g + 1) * 4, :].rearrange("p a b -> p (a b)"), rr[hb][:], rr[hb][:], ALU.mult),
                    reads=[("rr", hb)], writes=["aT"])
            for half in range(2):
                for kt in range(32):
                    S.op("pe", lambda e, half=half, kt=kt: e.matmul(
                        ops_[half][:], C.aT[:, kt, :], C.w2s[:, kt, half * 512:(half + 1) * 512],
                        start=(kt == 0), stop=(kt == 31)), reads=["aT", "w2s"], writes=[("ops", half)])
                S.op("dve", lambda e, half=half, b=b, h=h: e.tensor_tensor(
                    ho[b][:, half * 512:(half + 1) * 512], ops_[half][:], h[:, half * 512:(half + 1) * 512], ALU.add),
                    reads=[("ops", half), ("h", b)], writes=[("ho", b)])
            if final:
                hb_ = ho[b]
                S.op("act", lambda e, hb_=hb_: e.activation(C.junk[:], hb_[:], AF.Square, scale=1.0 / 32.0, accum_out=C.ss[:]),
                     reads=[("ho", b)], writes=["junk", "ss"])
                S.op("act", lambda e: e.activation(C.lnv[:], C.ss[:], AF.Ln, bias=C.epst[:, 0:1]), reads=["ss", "epst"], writes=["lnv"])
                S.op("act", lambda e: e.activation(C.rstd[:], C.lnv[:], AF.Exp, scale=-0.5), reads=["lnv"], writes=["rstd"])
                S.op("dve", lambda e, hb_=hb_: e.scalar_tensor_tensor(hb_[:], hb_[:], C.rstd[:, 0:1], C.fws[:], ALU.mult, ALU.mult),
                     reads=[("ho", b), "rstd", "fws"], writes=[("ho", b)])
            t2 = t - skip_tiles
            S.dma("st%d" % b, y[t2 * 128:(t2 + 1) * 128, :], ho[b][:], reads=[("ho", b)], writes=[(pfx + "y", t)])
            ykeys.append((pfx + "y", t))
        S.final_wait("sp", ykeys)
        S.barrier()
        block = st.enter_context(nc.Block())
        S.run(block)


def build_fused(NT, depth=2):
    nc = bass.Bass("TRN2", target_bir_lowering=False)
    x = nc.dram_tensor("x", [NT * 128, 1024], F32, kind="ExternalInput").ap()
    y = nc.dram_tensor("y", [(NT - 1) * 128, 1024], F32, kind="ExternalOutput").ap()
    hA = nc.dram_tensor("scr_hA", [NT * 128, 1024], F32, kind="Internal").ap()
    hB = nc.dram_tensor("scr_hB", [NT * 128, 1024], F32, kind="Internal").ap()
    finalw = nc.dram_tensor("p_finalw", [128, 1024], F32, kind="ExternalInput").ap()
    Ps = [declare_params(nc, "_l%d" % l) for l in range(depth)]
    Cst = {k: nc.dram_tensor("c_" + k, shp, dt, kind="ExternalInput").ap() for k, (shp, dt) in CSHAPES.items()}
    with ExitStack() as st:
        S = Sched(nc, st)
        src = x
        for l in range(depth):
            emit_mixer_pass(nc, S, src, hA, Ps[l], Cst, NT, min(l, 1), "m%d_" % l)
            last = (l == depth - 1)
            emit_mlp_pass(nc, S, hA, y if last else hB, Ps[l], finalw, Cst["identb"], NT, last, "f%d_" % l,
                          skip_tiles=1 if last else 0)
            src = hB
    return nc


def kernel_fused(**inputs):
    inp = {k: np.asarray(v) for k, v in inputs.items()}
    B = inp["x"].shape[0]
    NT = NT_FULL
    depth = inp["w_in"].shape[0]
    base = {}
    for l in range(depth):
        for k, v in host_layer_params(inp, l).items():
            base["p_" + k + "_l%d" % l] = v.reshape(PSHAPES[k])
    for k, v in host_consts().items():
        base["c_" + k] = v
    base["p_finalw"] = np.ascontiguousarray(np.broadcast_to(inp["final_norm_w"].reshape(1, 1024).astype(np.float32), (128, 1024)))
    ims = []
    for c in range(N_CORES):
        if c < B:
            xin = np.concatenate([np.zeros((112, 1024), np.float32), inp["meta"].astype(np.float32), inp["x"][c, :(NT - 1) * 128]], 0)
        else:
            xin = np.zeros((NT * 128, 1024), np.float32)
        ims.append(dict(base, x=xin))
    nc = build_fused(NT, depth)
    res = run_bass_kernel_spmd(nc, ims, core_ids=list(range(N_CORES)))
    return np.stack([res.results[c]["y"] for c in range(B)], 0).astype(np.float32)


kernel = kernel_fused
```

```python
from contextlib import ExitStack
import numpy as np
import concourse.bass as bass
import concourse.mybir as mybir

F32 = mybir.dt.float32
BF16 = mybir.dt.bfloat16
ALU = mybir.AluOpType
AF = mybir.ActivationFunctionType
AX = mybir.AxisListType

EPOCH = 24000
NEPOCH = 10


class Sched:
    ENGS = ("pe", "act", "dve", "pool", "sp")

    def __init__(self, nc, st):
        self.nc = nc
        self.st = st
        self.q = {e: [] for e in self.ENGS}
        self.cnt = {e: 0 for e in self.ENGS}
        self.sems = {}
        self.waited = {}
        self.lastw = {}
        self.readers = {}
        self.dmacnt = {}
        self.latest = {}
        self.h = {"pe": nc.tensor, "act": nc.scalar, "dve": nc.vector,
                  "pool": nc.gpsimd, "sp": nc.sync}

    def _sem(self, key):
        s = self.sems.get(key)
        if s is None:
            s = self.st.enter_context(self.nc.semaphore("s_%s_%s" % key))
            self.sems[key] = s
        return s

    def _wait(self, e, clock):
        if clock is None:
            return
        key, val = clock
        if key[0] == "dma":
            val = max(val, self.dmacnt[key[1]])
        if key[0] == "pe" and e == "pe":
            return
        if self.waited.get((e, key), 0) >= val:
            return
        self.waited[(e, key)] = val
        sem = self._sem(key)
        h = self.h[e]
        self.q[e].append(lambda: h.wait_ge(sem, val))

    def _deps(self, e, reads, writes):
        for r in reads:
            self._wait(e, self.lastw.get(r))
        for w in writes:
            self._wait(e, self.lastw.get(w))
            for k, v in self.readers.get(w, {}).items():
                self._wait(e, (k, v))

    def _record(self, clock, reads, writes):
        key, val = clock
        for r in reads:
            d = self.readers.setdefault(r, {})
            if d.get(key, 0) < val:
                d[key] = val
        for w in writes:
            self.lastw[w] = clock
            self.readers[w] = {}
        if self.latest.get(key, 0) < val:
            self.latest[key] = val

    def barrier(self):
        for e in self.ENGS:
            for key, val in list(self.latest.items()):
                self._wait(e, (key, val))

    @staticmethod
    def _excl(reads, writes):
        ex = [r for r in reads if (isinstance(r, str) and (r.startswith("bank") or r == "tp")) or
              (isinstance(r, tuple) and r[0] in ("hid", "ops"))]
        if ex:
            reads = [r for r in reads if r not in ex]
            writes = list(writes) + ex
        return reads, writes

    def op(self, e, fn, reads=(), writes=()):
        reads, writes = self._excl(reads, writes)
        self._deps(e, reads, writes)
        i = self.cnt[e]
        self.cnt[e] += 1
        key = (e, i // EPOCH)
        val = i % EPOCH + 1
        sem = self._sem(key)
        h = self.h[e]
        self.q[e].append(lambda: fn(h).then_inc(sem, 1))
        self._record((key, val), reads, writes)

    def dma(self, stream, out, in_, reads=(), writes=(), q="sp"):
        self._deps(q, reads, writes)
        key = ("dma", stream)
        n = self.dmacnt.get(stream, 0) + 16
        self.dmacnt[stream] = n
        sem = self._sem(key)
        h = self.h[q]
        self.q[q].append(lambda: h.dma_start(out=out, in_=in_).then_inc(sem, 16))
        self._record((key, n), reads, writes)

    def final_wait(self, e, resources):
        for r in resources:
            self._wait(e, self.lastw.get(r))

    def run(self, block):
        q = self.q
        self.q = {e: [] for e in self.ENGS}

        @block.tensor
        def _(eng):
            for f in q["pe"]:
                f()

        @block.scalar
        def _(eng):
            for f in q["act"]:
                f()

        @block.vector
        def _(eng):
            for f in q["dve"]:
                f()

        @block.gpsimd
        def _(eng):
            for f in q["pool"]:
                f()

        @block.sync
        def _(eng):
            for f in q["sp"]:
                f()


import sys, time, math
import ml_dtypes
from concourse.bass_utils import run_bass_kernel_spmd

EPS = 1e-6
TWO_PI = 2.0 * math.pi
GELU_C = 2.0 * math.sqrt(2.0 / math.pi)
NEG_BIG = 30000.0


def host_consts():
    c = {}
    c["identf"] = np.eye(128, dtype=np.float32)
    c["identb"] = np.eye(128).astype(ml_dtypes.bfloat16)
    s = np.arange(128)
    c["tri"] = (s[:, None] <= s[None, :]).astype(np.float32)
    c["gmask"] = (NEG_BIG * (s[None, :] > s[:, None])).astype(np.float32)
    c["strict"] = (s[None, :] < s[:, None]).astype(np.float32)
    c["iota"] = np.ascontiguousarray(np.broadcast_to(s.astype(np.float32), (128, 128)))
    m32 = (s[:32, None] <= s[None, :32]).astype(np.float32)
    c["mask32"] = np.ascontiguousarray(np.broadcast_to(m32[:, None, :], (32, 4, 32))).reshape(32, 128)
    rs = np.ones((64, 4, 128), np.float32)
    rs[:, :, 0::32] = 0.0
    c["hreset"] = rs.reshape(64, 512)
    c["gmask4"] = np.ascontiguousarray(np.tile(c["gmask"], (1, 4)))
    c["strict4"] = np.ascontiguousarray(np.tile(c["strict"], (1, 4)))
    return c


def host_layer_params(inp, l):
    p = {}
    bc = lambda v, n=128: np.ascontiguousarray(np.broadcast_to(np.asarray(v, np.float32).reshape(1, -1), (n, np.asarray(v).size)))
    p["nmix"] = bc(inp["norm_mix_w"][l])
    p["nmlp"] = bc(inp["norm_mlp_w"][l])
    p["nmixc"] = np.ascontiguousarray(np.asarray(inp["norm_mix_w"][l], np.float32).reshape(8, 128).T)
    p["w_in"] = np.ascontiguousarray(inp["w_in"][l])
    p["w_out"] = np.ascontiguousarray(inp["w_out"][l])
    p["w1"] = np.ascontiguousarray(inp["mlp_w1"][l])
    p["w2"] = np.ascontiguousarray(inp["mlp_w2"][l])
    p["lbl0"] = np.ascontiguousarray(inp["hgrn_lb_logits"][0].reshape(4, 64).T)
    p["lbl1"] = np.ascontiguousarray(inp["hgrn_lb_logits"][1].reshape(4, 64).T)
    p["hnw"] = np.ascontiguousarray(inp["hgrn_norm_w"][l].reshape(64, 1))
    p["convw"] = np.ascontiguousarray(inp["gdn_conv_w"][l].reshape(4, 12, 128).transpose(2, 1, 0))
    p["alog"] = bc(inp["gdn_a_log"][l])
    p["dtb"] = bc(inp["gdn_dt_bias"][l])
    p["gnw"] = np.ascontiguousarray(inp["gdn_norm_w"][l].reshape(128, 1))
    lre, lim, ls = inp["s5_lam_re"][l], inp["s5_lam_im"][l], inp["s5_log_step"][l]
    def pl(a):
        return np.ascontiguousarray(a.reshape(8, 128).T)
    p["lre_p"] = pl(lre); p["lim_p"] = pl(lim)
    p["ls_p"] = pl(np.broadcast_to(ls[:, None], (16, 64)))
    fl = lambda a: np.ascontiguousarray(np.broadcast_to(a.reshape(1, 1024), (128, 1024)))
    p["lre_f"] = fl(lre); p["lim_f"] = fl(lim)
    p["ls_f"] = fl(np.ascontiguousarray(np.broadcast_to(ls[:, None], (16, 64))))
    bre, bim = inp["s5_b_re"][l], inp["s5_b_im"][l]
    cre, cim = inp["s5_c_re"][l], inp["s5_c_im"][l]
    BTr = np.zeros((128, 8, 128), np.float32); BTi = np.zeros((128, 8, 128), np.float32)
    CTr = np.zeros((128, 8, 128), np.float32); CTi = np.zeros((128, 8, 128), np.float32)
    for g in range(16):
        j = g // 2; r0 = (g % 2) * 64; c0 = (16 * g) % 128
        BTr[c0:c0 + 16, j, r0:r0 + 64] = bre[g].T
        BTi[c0:c0 + 16, j, r0:r0 + 64] = bim[g].T
        CTr[r0:r0 + 64, j, c0:c0 + 16] = cre[g].T
        CTi[r0:r0 + 64, j, c0:c0 + 16] = cim[g].T
    p["BTr"] = BTr.reshape(128, 1024); p["BTi"] = BTi.reshape(128, 1024)
    p["CTr"] = CTr.reshape(128, 1024); p["CTi"] = CTi.reshape(128, 1024)
    p["s5d"] = np.ascontiguousarray(inp["s5_d"][l].reshape(2, 128).T)
    p["s5nw"] = np.ascontiguousarray(inp["s5_norm_w"][l].reshape(2, 128).T)
    p["glu"] = np.ascontiguousarray(inp["s5_glu_w"][l])
    return p


PSHAPES = {
    "nmix": [128, 1024], "nmixc": [128, 8], "nmlp": [128, 1024], "w_in": [1024, 3336], "w_out": [1024, 1024],
    "w1": [1024, 4096], "w2": [4096, 1024], "lbl0": [64, 4], "lbl1": [64, 4], "hnw": [64, 1],
    "convw": [128, 48], "alog": [128, 4], "dtb": [128, 4], "gnw": [128, 1],
    "lre_p": [128, 8], "lim_p": [128, 8], "ls_p": [128, 8],
    "lre_f": [128, 1024], "lim_f": [128, 1024], "ls_f": [128, 1024],
    "BTr": [128, 1024], "BTi": [128, 1024], "CTr": [128, 1024], "CTi": [128, 1024],
    "s5d": [128, 2], "s5nw": [128, 2], "glu": [256, 256],
}
CSHAPES = {"identf": ([128, 128], F32), "identb": ([128, 128], BF16), "tri": ([128, 128], F32),
           "gmask": ([128, 128], F32), "strict": ([128, 128], F32), "iota": ([128, 128], F32),
           "mask32": ([32, 128], F32), "hreset": ([64, 512], F32),
           "gmask4": ([128, 512], F32), "strict4": ([128, 512], F32)}


class Ctx:
    pass


def load_weight_bf16(S, name, w_dram, kts, N, dst, dst_off, stage, kt_per, col0=0, scale=None, np_=128):
    wv = w_dram.rearrange("(kt p) n -> p kt n", p=128)
    assert kt_per * N <= 1024
    cnt = getattr(S, "_wl", 0)
    for i0 in range(0, len(kts), kt_per):
        grp = kts[i0:i0 + kt_per]
        sl = cnt % 2
        cnt += 1
        sg = stage[sl][:, 0:len(grp) * N].rearrange("p (k n) -> p k n", k=len(grp))
        S.dma("wst%d" % sl, sg, wv[:, grp[0]:grp[0] + len(grp), col0:col0 + N], writes=[("wstage", sl)])
        for i, kt in enumerate(grp):
            eng = "dve" if (cnt + i) % 2 == 0 else "pool"
            if scale is None:
                S.op(eng, lambda h, sg=sg, i=i, kt=kt: h.tensor_copy(dst[:, kt - dst_off, 0:N], sg[:, i, :]),
                     reads=[("wstage", sl)], writes=[name])
            else:
                S.op(eng, lambda h, sg=sg, i=i, kt=kt: h.tensor_scalar(dst[:, kt - dst_off, 0:N], sg[:, i, :], scale[:, kt:kt + 1], None, ALU.mult),
                     reads=[("wstage", sl), "nwcol"], writes=[name])
    S._wl = cnt


def rmsnorm_T(S, C, h, hkey, nws, nwkey):
    S.op("act", lambda e: e.activation(C.junk[:], h[:], AF.Square, scale=1.0 / 32.0, accum_out=C.ss[:]),
         reads=[hkey], writes=["junk", "hn", "ss"])
    S.op("act", lambda e: e.activation(C.lnv[:], C.ss[:], AF.Ln, bias=C.epst[:, 0:1]), reads=["ss", "epst"], writes=["lnv"])
    S.op("act", lambda e: e.activation(C.rstd[:], C.lnv[:], AF.Exp, scale=-0.5), reads=["lnv"], writes=["rstd"])
    if nws is None:
        S.op("dve", lambda e: e.tensor_scalar(C.hn[:], h[:], C.rstd[:, 0:1], None, ALU.mult), reads=[hkey, "rstd"], writes=["hn"])
    else:
        S.op("dve", lambda e: e.scalar_tensor_tensor(C.hn[:], h[:], C.rstd[:, 0:1], nws[:], ALU.mult, ALU.mult),
             reads=[hkey, "rstd", nwkey], writes=["hn"])
    for kt in range(8):
        S.op("pe", lambda e, kt=kt: e.transpose(C.tp[:, kt, :], C.hn[:, kt * 128:(kt + 1) * 128], C.idb[:]),
             reads=["hn", "idb"], writes=["tp"])
    S.op("act", lambda e: e.copy(C.hnT[:], C.tp[:]), reads=["tp"], writes=["hnT"])


def sincos_tables(S, C, ang, angkey, sin_out, cos_out, okeys, kint, tmps, tkeys):
    kf, s2, s4 = tmps
    kk, k2, k4 = tkeys
    S.op("dve", lambda e: e.tensor_scalar(kint, ang, 1.0 / TWO_PI, None, ALU.mult), reads=[angkey], writes=["kint"])
    S.op("dve", lambda e: e.tensor_copy(kf, kint), reads=["kint"], writes=[kk])
    S.op("dve", lambda e: e.scalar_tensor_tensor(kf, kf, -TWO_PI, ang, ALU.mult, ALU.add), reads=[kk, angkey], writes=[kk])
    S.op("act", lambda e: e.activation(s4, kf, AF.Sin, scale=0.25), reads=[kk], writes=[k4])
    S.op("act", lambda e: e.activation(s2, kf, AF.Sin, scale=0.5), reads=[kk], writes=[k2])
    S.op("dve", lambda e: e.tensor_tensor(s4, s4, s4, ALU.mult), reads=[k4], writes=[k4])
    S.op("dve", lambda e: e.tensor_scalar(s4, s4, -2.0, 1.0, ALU.mult, ALU.add), reads=[k4], writes=[k4])
    S.op("dve", lambda e: e.scalar_tensor_tensor(sin_out, s2, 2.0, s4, ALU.mult, ALU.mult), reads=[k2, k4], writes=[okeys[0]])
    S.op("dve", lambda e: e.tensor_tensor(s2, s2, s2, ALU.mult), reads=[k2, okeys[0]], writes=[k2])
    S.op("dve", lambda e: e.tensor_scalar(cos_out, s2, -2.0, 1.0, ALU.mult, ALU.add), reads=[k2], writes=[okeys[1]])


def s5_setup(S, C, P):
    T = C.scr
    def ld(tile_ap, name, key):
        S.dma("ld_" + key, tile_ap, P[name], writes=[key])
    ld(C.lre_p[:], "lre_p", "lre_p"); ld(C.lim_p[:], "lim_p", "lim_p"); ld(C.ls_p[:], "ls_p", "ls_p")
    ld(C.s5d[:], "s5d", "s5d"); ld(C.s5nw[:], "s5nw", "s5nw")
    S.op("act", lambda e: e.activation(C.dtp[:], C.ls_p[:], AF.Exp), reads=["ls_p"], writes=["dtp"])
    S.op("dve", lambda e: e.tensor_scalar(C.lrep[:], C.lre_p[:], -1e-4, None, ALU.min), reads=["lre_p"], writes=["lrep"])
    S.op("dve", lambda e: e.tensor_tensor(C.magp[:], C.lrep[:], C.dtp[:], ALU.mult), reads=["lrep", "dtp"], writes=["magp"])
    S.op("act", lambda e: e.activation(C.magp[:], C.magp[:], AF.Exp), reads=["magp"], writes=["magp"])
    S.op("dve", lambda e: e.tensor_tensor(C.thp[:], C.lim_p[:], C.dtp[:], ALU.mult), reads=["lim_p", "dtp"], writes=["thp"])
    ang = T[0]
    for j in range(8):
        S.op("dve", lambda e, j=j: e.tensor_scalar(ang[:, j * 128:(j + 1) * 128], C.iota[:], C.thp[:, j:j + 1], None, ALU.mult),
             reads=["iota", "thp"], writes=["scr0"])
    sincos_tables(S, C, ang[:], "scr0", C.tabS[:].rearrange("p a b -> p (a b)"), C.tabC[:].rearrange("p a b -> p (a b)"),
                  ["tabS", "tabC"], C.kint[:], [T[1][:], T[2][:], T[3][:]], ["scr1", "scr2", "scr3"])
    S.op("dve", lambda e: e.tensor_scalar(C.a128[:], C.thp[:], 128.0, None, ALU.mult), reads=["thp"], writes=["a128"])
    sincos_tables(S, C, C.a128[:], "a128", C.se[:], C.ce[:], ["se", "ce"], C.kint[:, 0:8], [C.t128[:], C.c1[:], C.c2[:]], ["t128", "c1", "c2"])
    lre, lim, ls, t3, t4, t5, t6, t7 = T[2], T[3], T[4], T[5], T[6], T[7], T[8], T[9]
    ld(lre[:], "lre_f", "scr2"); ld(lim[:], "lim_f", "scr3"); ld(ls[:], "ls_f", "scr4")
    S.op("act", lambda e: e.activation(ls[:], ls[:], AF.Exp), reads=["scr4"], writes=["scr4"])
    S.op("dve", lambda e: e.tensor_scalar(lre[:], lre[:], -1e-4, None, ALU.min), reads=["scr2"], writes=["scr2"])
    S.op("dve", lambda e: e.tensor_tensor(t3[:], lre[:], ls[:], ALU.mult), reads=["scr2", "scr4"], writes=["scr5"])
    S.op("act", lambda e: e.activation(t3[:], t3[:], AF.Exp), reads=["scr5"], writes=["scr5"])
    S.op("dve", lambda e: e.tensor_tensor(t4[:], lim[:], ls[:], ALU.mult), reads=["scr3", "scr4"], writes=["scr6"])
    sincos_tables(S, C, t4[:], "scr6", t5[:], t6[:], ["scr7", "scr8"], C.kint[:], [t7[:], T[0][:], T[1][:]], ["scr9", "scr0", "scr1"])
    S.op("dve", lambda e: e.tensor_tensor(t5[:], t5[:], t3[:], ALU.mult), reads=["scr7", "scr5"], writes=["scr7"])
    S.op("dve", lambda e: e.tensor_tensor(t6[:], t6[:], t3[:], ALU.mult), reads=["scr8", "scr5"], writes=["scr8"])
    S.op("dve", lambda e: e.tensor_scalar(t6[:], t6[:], -1.0, None, ALU.add), reads=["scr8"], writes=["scr8"])
    S.op("dve", lambda e: e.tensor_tensor(t3[:], lre[:], lre[:], ALU.mult), reads=["scr2", "scr7", "scr8"], writes=["scr5"])
    S.op("dve", lambda e: e.tensor_tensor(t4[:], lim[:], lim[:], ALU.mult), reads=["scr3"], writes=["scr6"])
    S.op("dve", lambda e: e.tensor_tensor(t3[:], t3[:], t4[:], ALU.add), reads=["scr5", "scr6"], writes=["scr5"])
    S.op("dve", lambda e: e.reciprocal(t3[:], t3[:]), reads=["scr5"], writes=["scr5"])
    S.op("dve", lambda e: e.tensor_tensor(t4[:], t6[:], lre[:], ALU.mult), reads=["scr8", "scr2", "scr6"], writes=["scr6"])
    S.op("dve", lambda e: e.tensor_tensor(t7[:], t5[:], lim[:], ALU.mult), reads=["scr7", "scr3"], writes=["scr9"])
    S.op("dve", lambda e: e.tensor_tensor(t4[:], t4[:], t7[:], ALU.add), reads=["scr6", "scr9"], writes=["scr6"])
    S.op("dve", lambda e: e.tensor_tensor(t4[:], t4[:], t3[:], ALU.mult), reads=["scr6", "scr5"], writes=["scr6"])
    S.op("dve", lambda e: e.tensor_tensor(t7[:], t5[:], lre[:], ALU.mult), reads=["scr7", "scr2", "scr6"], writes=["scr9"])
    S.op("dve", lambda e: e.tensor_tensor(ls[:], t6[:], lim[:], ALU.mult), reads=["scr8", "scr3"], writes=["scr4"])
    S.op("dve", lambda e: e.tensor_tensor(t7[:], t7[:], ls[:], ALU.subtract), reads=["scr9", "scr4"], writes=["scr9"])
    S.op("dve", lambda e: e.tensor_tensor(t7[:], t7[:], t3[:], ALU.mult), reads=["scr9", "scr5"], writes=["scr9"])
    br, bi = lre, lim
    ld(br[:], "BTr", "scr2"); ld(bi[:], "BTi", "scr3")
    flat = lambda t: t[:].rearrange("p a b -> p (a b)")
    S.op("dve", lambda e: e.tensor_tensor(t5[:], t4[:], br[:], ALU.mult), reads=["scr6", "scr2", "scr7"], writes=["scr7"])
    S.op("dve", lambda e: e.tensor_tensor(t6[:], t7[:], bi[:], ALU.mult), reads=["scr9", "scr3", "scr8"], writes=["scr8"])
    S.op("dve", lambda e: e.tensor_tensor(flat(C.BTre), t5[:], t6[:], ALU.subtract), reads=["scr7", "scr8"], writes=["BTre"])
    S.op("dve", lambda e: e.tensor_tensor(t5[:], t4[:], bi[:], ALU.mult), reads=["scr6", "scr3", "BTre"], writes=["scr7"])
    S.op("dve", lambda e: e.tensor_tensor(t6[:], t7[:], br[:], ALU.mult), reads=["scr9", "scr2", "BTre"], writes=["scr8"])
    S.op("dve", lambda e: e.tensor_tensor(flat(C.BTim), t5[:], t6[:], ALU.add), reads=["scr7", "scr8"], writes=["BTim"])
    ld(t3[:], "CTr", "scr5"); ld(t4[:], "CTi", "scr6")
    S.op("dve", lambda e: e.tensor_copy(flat(C.CTre), t3[:]), reads=["scr5"], writes=["CTre"])
    S.op("dve", lambda e: e.tensor_scalar(flat(C.CTimn), t4[:], -1.0, None, ALU.mult), reads=["scr6"], writes=["CTimn"])
    gst = T[0][:, 0:512].rearrange("p (k n) -> p k n", k=2)
    S.dma("ld_glu", gst, P["glu"].rearrange("(k p) n -> p k n", p=128), reads=["tabS", "tabC"], writes=["scr0"])
    S.op("dve", lambda e: e.tensor_copy(C.glus[:], gst), reads=["scr0"], writes=["glus"])
    S.op("dve", lambda e: e.memset(C.zst[:], 0.0), writes=["zst"])


def s5_chunk(S, C, first):
    A = C.bank
    uP = A[0]
    for m in range(2):
        for kt in range(8):
            S.op("pe", lambda e, m=m, kt=kt: e.matmul(uP[:, m, :], C.wu[:, kt, m * 128:(m + 1) * 128], C.hnT[:, kt, :],
                                                     start=(kt == 0), stop=(kt == 7)),
                 reads=["wu", "hnT"], writes=["bank0"])
    S.op("act", lambda e: e.copy(C.uF[:], uP[:, 0:2, :]), reads=["bank0"], writes=["uF"])
    S.op("dve", lambda e: e.tensor_copy(C.uB[:], uP[:, 0:2, :]), reads=["bank0"], writes=["uB"])
    for j in range(8):
        S.op("pe", lambda e, j=j: e.matmul(A[2 + j // 4][:, j % 4, :], C.BTre[:, j, :], C.uB[:, j // 4, :], start=True, stop=True),
             reads=["BTre", "uB"], writes=["bank%d" % (2 + j // 4)])
        S.op("pe", lambda e, j=j: e.matmul(A[4 + j // 4][:, j % 4, :], C.BTim[:, j, :], C.uB[:, j // 4, :], start=True, stop=True),
             reads=["BTim", "uB"], writes=["bank%d" % (4 + j // 4)])
    f4 = lambda t, hf: t[:, hf * 4:(hf + 1) * 4, :].rearrange("p a b -> p (a b)")
    fb = lambda b: b[:].rearrange("p a b -> p (a b)")
    for hf in range(2):
        bre, bim = A[2 + hf], A[4 + hf]
        kre, kim = "bank%d" % (2 + hf), "bank%d" % (4 + hf)
        S.op("dve", lambda e, hf=hf, bre=bre: e.tensor_tensor(f4(C.wre, hf), fb(bre), f4(C.tabC, hf), ALU.mult), reads=[kre, "tabC"], writes=["wre"])
        S.op("dve", lambda e, hf=hf, bim=bim: e.tensor_tensor(f4(C.t2, hf), fb(bim), f4(C.tabS, hf), ALU.mult), reads=[kim, "tabS"], writes=["t2"])
        S.op("pool", lambda e, hf=hf: e.tensor_tensor(f4(C.wre, hf), f4(C.wre, hf), f4(C.t2, hf), ALU.add), reads=["wre", "t2"], writes=["wre"])
        S.op("dve", lambda e, hf=hf, bim=bim: e.tensor_tensor(f4(C.wim, hf), fb(bim), f4(C.tabC, hf), ALU.mult), reads=[kim, "tabC"], writes=["wim"])
        S.op("dve", lambda e, hf=hf, bre=bre: e.tensor_tensor(f4(C.t2, hf), fb(bre), f4(C.tabS, hf), ALU.mult), reads=[kre, "tabS", "wre"], writes=["t2"])
        S.op("pool", lambda e, hf=hf: e.tensor_tensor(f4(C.wim, hf), f4(C.wim, hf), f4(C.t2, hf), ALU.subtract), reads=["wim", "t2"], writes=["wim"])
    zl_re, zl_im = C.zst[:, 0, :], C.zst[:, 1, :]
    S.op("dve", lambda e: e.tensor_tensor(C.c1[:], C.ce[:], zl_re, ALU.mult), reads=["ce", "zst"], writes=["c1"])
    S.op("dve", lambda e: e.tensor_tensor(C.c2[:], C.se[:], zl_im, ALU.mult), reads=["se", "zst"], writes=["c2"])
    S.op("dve", lambda e: e.tensor_tensor(C.c1[:], C.c1[:], C.c2[:], ALU.subtract), reads=["c1", "c2"], writes=["c1"])
    S.op("dve", lambda e: e.tensor_tensor(C.c3[:], C.ce[:], zl_im, ALU.mult), reads=["ce", "zst"], writes=["c3"])
    S.op("dve", lambda e: e.tensor_tensor(C.c2[:], C.se[:], zl_re, ALU.mult), reads=["se", "zst", "c1"], writes=["c2"])
    S.op("dve", lambda e: e.tensor_tensor(C.c3[:], C.c3[:], C.c2[:], ALU.add), reads=["c3", "c2"], writes=["c3"])
    for j in range(8):
        dj = C.magp[:, j:j + 1].to_broadcast([128, 128])
        S.op("dve", lambda e, j=j, dj=dj: e.tensor_tensor_scan(C.wre[:, j, :], dj, C.wre[:, j, :], C.c1[:, j:j + 1], ALU.mult, ALU.add),
             reads=["magp", "wre", "c1"], writes=["wre"])
        S.op("dve", lambda e, j=j, dj=dj: e.tensor_tensor_scan(C.wim[:, j, :], dj, C.wim[:, j, :], C.c3[:, j:j + 1], ALU.mult, ALU.add),
             reads=["magp", "wim", "c3"], writes=["wim"])
    S.op("pool", lambda e: e.tensor_copy(C.zst[:, 0, :], C.wre[:, :, 127]), reads=["wre"], writes=["zst"])
    S.op("pool", lambda e: e.tensor_copy(C.zst[:, 1, :], C.wim[:, :, 127]), reads=["wim"], writes=["zst"])
    fl = lambda t: t[:].rearrange("p a b -> p (a b)")
    tB = C.s5tmp[:]
    S.op("dve", lambda e: e.tensor_tensor(fl(C.t2), fl(C.wre), fl(C.tabC), ALU.mult), reads=["wre", "tabC"], writes=["t2"])
    S.op("pool", lambda e: e.tensor_tensor(tB, fl(C.wim), fl(C.tabS), ALU.mult), reads=["wim", "tabS"], writes=["s5tmp"])
    S.op("dve", lambda e: e.tensor_tensor(fl(C.xre), fl(C.t2), tB, ALU.subtract), reads=["t2", "s5tmp"], writes=["xre"])
    S.op("pool", lambda e: e.tensor_tensor(tB, fl(C.wre), fl(C.tabS), ALU.mult), reads=["wre", "tabS", "xre"], writes=["s5tmp"])
    S.op("dve", lambda e: e.tensor_tensor(fl(C.t2), fl(C.wim), fl(C.tabC), ALU.mult), reads=["wim", "tabC", "xre"], writes=["t2"])
    S.op("pool", lambda e: e.tensor_tensor(fl(C.xim), tB, fl(C.t2), ALU.add), reads=["t2", "s5tmp"], writes=["xim"])
    yP = A[0][:, 2:4, :]
    for m in range(2):
        for i, j in enumerate(range(4 * m, 4 * m + 4)):
            S.op("pe", lambda e, m=m, j=j, i=i: e.matmul(yP[:, m, :], C.CTre[:, j, :], C.xre[:, j, :], start=(i == 0), stop=False),
                 reads=["CTre", "xre"], writes=["bank0"])
            S.op("pe", lambda e, m=m, j=j, i=i: e.matmul(yP[:, m, :], C.CTimn[:, j, :], C.xim[:, j, :], start=False, stop=(i == 3)),
                 reads=["CTimn", "xim"], writes=["bank0"])
    f2 = lambda t: t[:].rearrange("p a b -> p (a b)")
    for m in range(2):
        S.op("dve", lambda e, m=m: e.scalar_tensor_tensor(C.yv[:, m, :], C.uF[:, m, :], C.s5d[:, m:m + 1], yP[:, m, :], ALU.mult, ALU.add),
             reads=["uF", "s5d", "bank0"], writes=["yv"])
    S.op("dve", lambda e: e.tensor_tensor(f2(C.g1), f2(C.yv), f2(C.yv), ALU.mult), reads=["yv"], writes=["g1"])
    S.op("dve", lambda e: e.tensor_scalar(f2(C.g1), f2(C.g1), 0.044715, 1.0, ALU.mult, ALU.add), reads=["g1"], writes=["g1"])
    S.op("dve", lambda e: e.tensor_tensor(f2(C.g1), f2(C.g1), f2(C.yv), ALU.mult), reads=["g1", "yv"], writes=["g1"])
    S.op("dve", lambda e: e.tensor_scalar(f2(C.g1), f2(C.g1), -GELU_C, 40.0, ALU.mult, ALU.min), reads=["g1"], writes=["g1"])
    S.op("act", lambda e: e.activation(f2(C.g1), f2(C.g1), AF.Exp), reads=["g1"], writes=["g1"])
    S.op("act", lambda e: e.activation(f2(C.g1), f2(C.g1), AF.Ln, bias=C.onet[:, 0:1]), reads=["g1", "onet"], writes=["g1"])
    S.op("act", lambda e: e.activation(f2(C.g1), f2(C.g1), AF.Exp, scale=-1.0), reads=["g1"], writes=["g1"])
    S.op("dve", lambda e: e.tensor_tensor(f2(C.zg), f2(C.yv), f2(C.g1), ALU.mult), reads=["g1", "yv"], writes=["zg"])
    S.op("dve", lambda e: e.tensor_copy(f2(C.zgb), f2(C.zg)), reads=["zg"], writes=["zgb"])
    gP = A[1]
    for m2 in range(2):
        for k in range(2):
            S.op("pe", lambda e, m2=m2, k=k: e.matmul(gP[:, m2, :], C.glus[:, k, m2 * 128:(m2 + 1) * 128], C.zgb[:, k, :],
                                                     start=(k == 0), stop=(k == 1)),
                 reads=["glus", "zgb"], writes=["bank1"])
    S.op("act", lambda e: e.activation(f2(C.g1), gP[:, 0:2, :].rearrange("p a b -> p (a b)"), AF.Exp, scale=-1.0), reads=["bank1"], writes=["g1"])
    S.op("act", lambda e: e.activation(f2(C.g1), f2(C.g1), AF.Ln, bias=C.onet[:, 0:1]), reads=["g1", "onet"], writes=["g1"])
    S.op("act", lambda e: e.activation(f2(C.g1), f2(C.g1), AF.Exp, scale=-1.0), reads=["g1"], writes=["g1"])
    S.op("dve", lambda e: e.tensor_tensor(f2(C.o5), f2(C.zg), f2(C.g1), ALU.mult), reads=["g1", "zg"], writes=["o5"])
    S.op("dve", lambda e: e.tensor_tensor(f2(C.zgb), f2(C.o5), f2(C.o5), ALU.mult), reads=["o5"], writes=["zgb"])
    sP = A[1][:, 2:4, :]
    for m in range(2):
        S.op("pe", lambda e, m=m: e.matmul(sP[:, 0, :], C.ones256b[:], C.zgb[:, m, :], start=(m == 0), stop=(m == 1)),
             reads=["ones256b", "zgb"], writes=["bank1"])
    S.op("act", lambda e: e.activation(C.rs5[:], sP[:, 0, :], AF.Ln, bias=C.epst[:, 0:1]), reads=["bank1", "epst"], writes=["rs5"])
    S.op("act", lambda e: e.activation(C.rs5[:], C.rs5[:], AF.Exp, scale=-0.5), reads=["rs5"], writes=["rs5"])
    for m in range(2):
        S.op("dve", lambda e, m=m: e.scalar_tensor_tensor(C.mixT[:, 6 + m, :], C.o5[:, m, :], C.s5nw[:, m:m + 1], C.rs5[:], ALU.mult, ALU.mult),
             reads=["o5", "s5nw", "rs5"], writes=["mixT"])


def hgrn_setup(S, C, P, layer):
    S.dma("ld_hnw", C.hnw[:], P["hnw"], writes=["hnw"])
    S.op("dve", lambda e: e.memset(C.ones64[:], 1.0 / 64.0), writes=["ones64"])
    S.op("dve", lambda e: e.memset(C.Sst[:], 0.0), writes=["Sst"])
    if layer == 0:
        S.op("dve", lambda e: e.memset(C.oml[:], 1.0), writes=["oml"])
    else:
        S.dma("ld_lbl0", C.lbl0[:], P["lbl0"], writes=["lbl0"])
        S.dma("ld_lbl1", C.lbl1[:], P["lbl1"], writes=["lbl1"])
        S.op("dve", lambda e: e.tensor_tensor(C.oml[:], C.lbl0[:], C.lbl1[:], ALU.subtract), reads=["lbl0", "lbl1"], writes=["oml"])
        S.op("act", lambda e: e.activation(C.oml[:], C.oml[:], AF.Exp), reads=["oml"], writes=["oml"])
        S.op("dve", lambda e: e.tensor_scalar(C.oml[:], C.oml[:], 1.0, None, ALU.add), reads=["oml"], writes=["oml"])
        S.op("dve", lambda e: e.reciprocal(C.oml[:], C.oml[:]), reads=["oml"], writes=["oml"])
        S.op("dve", lambda e: e.tensor_scalar(C.oml[:], C.oml[:], -1.0, 1.0, ALU.mult, ALU.add), reads=["oml"], writes=["oml"])


def hgrn_chunk(S, C, first):
    A = C.bank
    H = lambda t: t[0:64]
    fl = lambda t: t[:].rearrange("p a b -> p (a b)")
    fb = lambda b: b[0:64].rearrange("p a b -> p (a b)")
    for g in range(4):
        for h in range(4):
            c0 = g * 256 + h * 64
            for kt in range(8):
                S.op("pe", lambda e, g=g, h=h, kt=kt, c0=c0: e.matmul(A[g][0:64, h, :], C.wH[:, kt, c0:c0 + 64], C.hnT[:, kt, :],
                                                                     start=(kt == 0), stop=(kt == 7)),
                     reads=["wH", "hnT"], writes=["bank%d" % g])
    S.op("act", lambda e: e.activation(fl(C.he1), fb(A[1]), AF.Exp), reads=["bank1"], writes=["he1"])
    S.op("act", lambda e: e.activation(fl(C.he1), fl(C.he1), AF.Ln, bias=C.onet[0:64, 0:1]), reads=["he1", "onet"], writes=["he1"])
    S.op("act", lambda e: e.activation(fl(C.he1), fl(C.he1), AF.Exp, scale=-1.0), reads=["he1"], writes=["he1"])
    S.op("dve", lambda e: e.tensor_tensor(C.hk[:], C.he1[:], C.oml[:].unsqueeze(2).to_broadcast([64, 4, 128]), ALU.mult),
         reads=["he1", "oml"], writes=["hk"])
    S.op("dve", lambda e: e.tensor_scalar(fl(C.hden), fl(C.hk), -1.0, 1.0, ALU.mult, ALU.add), reads=["hk"], writes=["hden"])
    S.op("act", lambda e: e.activation(fl(C.hden), fl(C.hden), AF.Ln), reads=["hden"], writes=["hden"])
    S.op("dve", lambda e: e.tensor_tensor_scan(fl(C.hc), C.hreset[:], fl(C.hden), 0.0, ALU.mult, ALU.add), reads=["hreset", "hden"], writes=["hc"])
    S.op("act", lambda e: e.activation(fl(C.hec), fl(C.hc), AF.Exp), reads=["hc"], writes=["hec"])
    S.op("act", lambda e: e.activation(fl(C.hof), fl(C.hc), AF.Exp, scale=-1.0), reads=["hc"], writes=["hof"])
    S.op("act", lambda e: e.activation(fl(C.he1), fb(A[0]), AF.Exp, scale=-1.0), reads=["bank0"], writes=["he1"])
    S.op("act", lambda e: e.activation(fl(C.he1), fl(C.he1), AF.Ln, bias=C.onet[0:64, 0:1]), reads=["he1", "onet"], writes=["he1"])
    S.op("act", lambda e: e.activation(fl(C.he1), fl(C.he1), AF.Exp, scale=-1.0), reads=["he1"], writes=["he1"])
    S.op("dve", lambda e: e.tensor_tensor(fl(C.hqt), fb(A[0]), fl(C.he1), ALU.mult), reads=["bank0", "he1"], writes=["hqt"])
    S.op("dve", lambda e: e.tensor_tensor(fl(C.hqt), fl(C.hqt), fl(C.hec), ALU.mult), reads=["hqt", "hec"], writes=["hqt"])
    S.op("dve", lambda e: e.tensor_tensor(fl(C.hkt), fl(C.hk), fl(C.hof), ALU.mult), reads=["hk", "hof"], writes=["hkt"])
    r4 = lambda t: t[:].rearrange("p h (n c) -> p h n c", c=32)
    S.op("dve", lambda e: e.tensor_tensor(r4(C.hkh), r4(C.hkt), r4(C.hec)[:, :, :, 31:32].to_broadcast([64, 4, 4, 32]), ALU.mult),
         reads=["hkt", "hec"], writes=["hkh"])
    S.op("act", lambda e: e.copy(fl(C.hv), fb(A[2])), reads=["bank2"], writes=["hv"])
    S.op("act", lambda e: e.activation(fl(C.he1), fb(A[3]), AF.Exp, scale=-1.0), reads=["bank3"], writes=["he1"])
    S.op("act", lambda e: e.activation(fl(C.he1), fl(C.he1), AF.Ln, bias=C.onet[0:64, 0:1]), reads=["he1", "onet"], writes=["he1"])
    S.op("act", lambda e: e.activation(fl(C.he1), fl(C.he1), AF.Exp, scale=-1.0), reads=["he1"], writes=["he1"])
    S.op("dve", lambda e: e.tensor_tensor(fl(C.hgs), fb(A[3]), fl(C.he1), ALU.mult), reads=["bank3", "he1"], writes=["hc"])
    scP = A[4][0:32, 0, :].rearrange("p (h t) -> p h t", h=4)
    trP = A[5][0:32]
    oP = A[2]
    dSP = A[3][0:64, 0:2, :].rearrange("p a (b c) -> p (a b) c", c=64)
    for n in range(4):
        sl = slice(32 * n, 32 * n + 32)
        for h in range(4):
            S.op("pe", lambda e, h=h, sl=sl: e.matmul(scP[:, h, :], C.hkt[:, h, sl], C.hqt[:, h, sl], start=True, stop=True),
                 reads=["hkt", "hqt"], writes=["bank4"])
        S.op("dve", lambda e: e.tensor_tensor(C.hscm[:], scP, C.mask32[:].rearrange("p (h t) -> p h t", h=4), ALU.mult),
             reads=["bank4", "mask32"], writes=["hscm"])
        for h in range(4):
            S.op("pe", lambda e, h=h, sl=sl: e.transpose(trP[:, h, 0:64], C.hkh[:, h, sl], C.identf[0:64, 0:64]),
                 reads=["hkh", "identf"], writes=["bank5"])
            S.op("pe", lambda e, h=h, sl=sl: e.transpose(trP[:, h, 64:128], C.hv[:, h, sl], C.identf[0:64, 0:64]),
                 reads=["hv", "identf"], writes=["bank5"])
        S.op("act", lambda e: e.copy(C.hkvT[:], trP), reads=["bank5"], writes=["hkvT"])
        for h in range(4):
            S.op("pe", lambda e, h=h, sl=sl: e.matmul(oP[0:64, h, sl], C.hkvT[:, h, 64:128], C.hscm[:, h, :], start=True, stop=False),
                 reads=["hkvT", "hscm"], writes=["bank2"])
            S.op("pe", lambda e, h=h, sl=sl: e.matmul(oP[0:64, h, sl], C.Sst[:, h, :], C.hqt[:, h, sl], start=False, stop=True),
                 reads=["Sst", "hqt"], writes=["bank2"])
        for h in range(4):
            S.op("pe", lambda e, h=h: e.matmul(dSP[:, h, :], C.hkvT[:, h, 0:64], C.hkvT[:, h, 64:128], start=True, stop=True),
                 reads=["hkvT"], writes=["bank3"])
        S.op("dve", lambda e, n=n: e.tensor_tensor(C.Sst[:], C.Sst[:], C.hec[:, :, 32 * n + 31:32 * n + 32].to_broadcast([64, 4, 64]), ALU.mult),
             reads=["Sst", "hec"], writes=["Sst"])
        S.op("dve", lambda e: e.tensor_tensor(C.Sst[:], C.Sst[:], dSP, ALU.add), reads=["Sst", "bank3"], writes=["Sst"])
    S.op("act", lambda e: e.copy(fl(C.hof), fb(oP)), reads=["bank2"], writes=["hof"])
    S.op("dve", lambda e: e.tensor_tensor(fl(C.he1), fl(C.hof), fl(C.hof), ALU.mult), reads=["hof"], writes=["he1"])
    msP = fb(A[4])
    S.op("pe", lambda e: e.matmul(msP, C.ones64[:], fl(C.he1), start=True, stop=True), reads=["ones64", "he1"], writes=["bank4"])
    S.op("act", lambda e: e.activation(fl(C.hden), msP, AF.Ln, bias=C.epst[0:64, 0:1]), reads=["bank4", "epst"], writes=["hden"])
    S.op("act", lambda e: e.activation(fl(C.hden), fl(C.hden), AF.Exp, scale=-0.5), reads=["hden"], writes=["hden"])
    S.op("dve", lambda e: e.scalar_tensor_tensor(fl(C.hof), fl(C.hof), C.hnw[:, 0:1], fl(C.hden), ALU.mult, ALU.mult),
         reads=["hof", "hnw", "hden"], writes=["hof"])
    S.op("dve", lambda e: e.tensor_tensor(fl(C.mixH), fl(C.hof), fl(C.hgs), ALU.mult), reads=["hof", "hc"], writes=["mixH"])


def gdn_setup(S, C, P):
    S.dma("ld_convw", C.convw[:].rearrange("p a b -> p (a b)"), P["convw"], writes=["convw"])
    S.dma("ld_alog", C.aneg[:], P["alog"], writes=["aneg"])
    S.dma("ld_dtb", C.dtb[:], P["dtb"], writes=["dtb"])
    S.dma("ld_gnw", C.gnw[:], P["gnw"], writes=["gnw"])
    S.op("act", lambda e: e.activation(C.aneg[:], C.aneg[:], AF.Exp), reads=["aneg"], writes=["aneg"])
    S.op("dve", lambda e: e.tensor_scalar(C.aneg[:], C.aneg[:], -1.0, None, ALU.mult), reads=["aneg"], writes=["aneg"])
    S.op("dve", lambda e: e.memset(C.ones128f[:], 1.0 / 128.0), writes=["ones128f"])
    S.op("dve", lambda e: e.memset(C.gx[:], 0.0), writes=["gx"])
    S.op("dve", lambda e: e.memset(C.gS[:], 0.0), writes=["gS"])


def gdn_chunk(S, C, first):
    A = C.bank
    f3 = lambda t: t[:].rearrange("p a b -> p (a b)")
    bc4 = lambda small: small.unsqueeze(2).to_broadcast([128, 4, 128])
    for g in range(4):
        for h in range(4):
            c0 = g * 512 + h * 128
            for kt in range(8):
                S.op("pe", lambda e, g=g, h=h, kt=kt, c0=c0: e.matmul(A[g][:, h, :], C.wG[:, kt, c0:c0 + 128], C.hnT[:, kt, :],
                                                                     start=(kt == 0), stop=(kt == 7)),
                     reads=["wG", "hnT"], writes=["bank%d" % g])
    abP = A[4][:, 0, 0:8]
    for kt in range(8):
        S.op("pe", lambda e, kt=kt: e.matmul(abP, C.hnT[:, kt, :], C.wab[:, kt, :], start=(kt == 0), stop=(kt == 7)),
             reads=["hnT", "wab"], writes=["bank4"])
    for g in range(3):
        S.op("act", lambda e, g=g: e.copy(C.gx[:, 4 * g:4 * g + 4, 3:131], A[g][:]), reads=["bank%d" % g], writes=["gx"])
    S.op("act", lambda e: e.activation(f3(C.gzs), f3(A[3]), AF.Exp, scale=-1.0), reads=["bank3"], writes=["gzs"])
    S.op("act", lambda e: e.activation(f3(C.gzs), f3(C.gzs), AF.Ln, bias=C.onet[:, 0:1]), reads=["gzs", "onet"], writes=["gzs"])
    S.op("act", lambda e: e.activation(f3(C.gzs), f3(C.gzs), AF.Exp, scale=-1.0), reads=["gzs"], writes=["gzs"])
    S.op("dve", lambda e: e.tensor_tensor(f3(C.gzs), f3(A[3]), f3(C.gzs), ALU.mult), reads=["bank3", "gzs"], writes=["gzs"])
    S.op("dve", lambda e: e.tensor_tensor(C.ga[:], A[4][:, 0, 0:4], C.dtb[:], ALU.add), reads=["bank4", "dtb"], writes=["ga"])
    S.op("act", lambda e: e.activation(C.ga[:], C.ga[:], AF.Exp), reads=["ga"], writes=["ga"])
    S.op("act", lambda e: e.activation(C.ga[:], C.ga[:], AF.Ln, bias=C.onet[:, 0:1]), reads=["ga", "onet"], writes=["ga"])
    S.op("dve", lambda e: e.tensor_tensor(C.gg[:], C.ga[:], C.aneg[:], ALU.mult), reads=["ga", "aneg"], writes=["gg"])
    S.op("act", lambda e: e.activation(C.gbeta[:], A[4][:, 0, 4:8], AF.Exp, scale=-1.0), reads=["bank4"], writes=["gbeta"])
    S.op("act", lambda e: e.activation(C.gbeta[:], C.gbeta[:], AF.Ln, bias=C.onet[:, 0:1]), reads=["gbeta", "onet"], writes=["gbeta"])
    S.op("act", lambda e: e.activation(C.gbeta[:], C.gbeta[:], AF.Exp, scale=-1.0), reads=["gbeta"], writes=["gbeta"])
    cP = A[4][:, 1, 0:4]
    clP = A[4][:, 2, 0:4]
    S.op("pe", lambda e: e.matmul(cP, C.tri[:], C.gg[:], start=True, stop=True), reads=["tri", "gg"], writes=["bank4"])
    S.op("pe", lambda e: e.matmul(clP, C.onesf[:], C.gg[:], start=True, stop=True), reads=["onesf", "gg"], writes=["bank4"])
    S.op("act", lambda e: e.copy(C.gc[:], cP), reads=["bank4"], writes=["gc"])
    S.op("act", lambda e: e.activation(C.gec[:], cP, AF.Exp), reads=["bank4"], writes=["gec"])
    S.op("act", lambda e: e.activation(C.gecl[:], clP, AF.Exp), reads=["bank4"], writes=["gecl"])
    S.op("dve", lambda e: e.tensor_tensor(C.gedl[:], clP, C.gc[:], ALU.subtract), reads=["bank4", "gc"], writes=["gedl"])
    S.op("act", lambda e: e.activation(C.gedl[:], C.gedl[:], AF.Exp), reads=["gedl"], writes=["gedl"])
    S.op("dve", lambda e: e.tensor_tensor(C.gbec[:], C.gbeta[:], C.gec[:], ALU.mult), reads=["gbeta", "gec"], writes=["gbec"])
    S.op("dve", lambda e: e.tensor_scalar(C.gnb[:], C.gbeta[:], -1.0, None, ALU.mult), reads=["gbeta"], writes=["gnb"])
    cw = lambda j: C.convw[:, :, j:j + 1].to_broadcast([128, 12, 128])
    S.op("pool", lambda e: e.tensor_tensor(C.gtmp[:], C.gx[:, :, 3:131], cw(3), ALU.mult), reads=["gx", "convw"], writes=["gtmp"])
    S.op("dve", lambda e: e.tensor_tensor(C.gcv[:], C.gx[:, :, 0:128], cw(0), ALU.mult), reads=["gx", "convw"], writes=["gcv"])
    for j in range(1, 3):
        S.op("dve", lambda e, j=j: e.tensor_tensor(C.gcv2[:], C.gx[:, :, j:j + 128], cw(j), ALU.mult), reads=["gx", "convw"], writes=["gXa", "gXb", "gXTa"])
        S.op("dve", lambda e: e.tensor_tensor(f3(C.gcv), f3(C.gcv), f3(C.gcv2), ALU.add), reads=["gcv", "gXa", "gXb", "gXTa"], writes=["gcv"])
    S.op("dve", lambda e: e.tensor_tensor(f3(C.gcv), f3(C.gcv), f3(C.gtmp), ALU.add), reads=["gcv", "gtmp"], writes=["gcv"])
    S.op("pool", lambda e: e.tensor_copy(C.gx[:, :, 0:3], C.gx[:, :, 128:131]), reads=["gx"], writes=["gx"])
    S.op("act", lambda e: e.activation(f3(C.gtmp), f3(C.gcv), AF.Exp, scale=-1.0), reads=["gcv"], writes=["gtmp"])
    S.op("act", lambda e: e.activation(f3(C.gtmp), f3(C.gtmp), AF.Ln, bias=C.onet[:, 0:1]), reads=["gtmp", "onet"], writes=["gtmp"])
    S.op("act", lambda e: e.activation(f3(C.gtmp), f3(C.gtmp), AF.Exp, scale=-1.0), reads=["gtmp"], writes=["gtmp"])
    S.op("dve", lambda e: e.tensor_tensor(f3(C.gcv), f3(C.gcv), f3(C.gtmp), ALU.mult), reads=["gcv", "gtmp"], writes=["gcv"])
    qk = C.gcv[:, 0:8, :]
    S.op("dve", lambda e: e.tensor_tensor(C.gtmp[:, 0:8, :], qk, qk, ALU.mult), reads=["gcv"], writes=["gtmp"])
    for i in range(2):
        S.op("pe", lambda e, i=i: e.matmul(f3(A[i]), C.onesf[:], C.gtmp[:, 4 * i:4 * i + 4, :].rearrange("p a b -> p (a b)"), start=True, stop=True),
             reads=["onesf", "gtmp"], writes=["bank%d" % i])
    for i in range(2):
        S.op("act", lambda e, i=i: e.activation(C.gtmp[:, 8 + 2 * i:10 + 2 * i, :].rearrange("p a b -> p (a b)")[:, 0:256], f3(A[i])[:, 0:256], AF.Ln, bias=C.epst[:, 0:1]),
             reads=["bank%d" % i, "epst"], writes=["gtmp"]) if False else None
    S.op("act", lambda e: e.activation(C.gtmp[:, 8:12, :].rearrange("p a b -> p (a b)"), f3(A[0]), AF.Ln, bias=C.epst[:, 0:1]), reads=["bank0", "epst"], writes=["gtmp"])
    S.op("act", lambda e: e.activation(C.gtmp[:, 8:12, :].rearrange("p a b -> p (a b)"), C.gtmp[:, 8:12, :].rearrange("p a b -> p (a b)"), AF.Exp, scale=-0.5), reads=["gtmp"], writes=["gtmp"])
    S.op("dve", lambda e: e.scalar_tensor_tensor(C.gcv[:, 0:4, :], C.gcv[:, 0:4, :], 128.0 ** -0.5, C.gtmp[:, 8:12, :], ALU.mult, ALU.mult),
         reads=["gcv", "gtmp"], writes=["gcv"])
    S.op("act", lambda e: e.activation(C.gtmp[:, 8:12, :].rearrange("p a b -> p (a b)"), f3(A[1]), AF.Ln, bias=C.epst[:, 0:1]), reads=["bank1", "epst", "gcv"], writes=["gtmp"])
    S.op("act", lambda e: e.activation(C.gtmp[:, 8:12, :].rearrange("p a b -> p (a b)"), C.gtmp[:, 8:12, :].rearrange("p a b -> p (a b)"), AF.Exp, scale=-0.5), reads=["gtmp"], writes=["gtmp"])
    S.op("dve", lambda e: e.tensor_tensor(C.gcv[:, 4:8, :], C.gcv[:, 4:8, :], C.gtmp[:, 8:12, :], ALU.mult), reads=["gcv", "gtmp"], writes=["gcv"])
    qn = lambda h: C.gcv[:, h, :]
    kn = lambda h: C.gcv[:, 4 + h, :]
    vf = lambda h: C.gcv[:, 8 + h, :]
    S.op("dve", lambda e: e.tensor_tensor(C.gdiag[:], C.identf[:].unsqueeze(1).to_broadcast([128, 4, 128]), bc4(C.gc[:]), ALU.mult),
         reads=["identf", "gc"], writes=["gdiag"])
    S.op("pe", lambda e: e.matmul(f3(A[2]), C.onesf[:], f3(C.gdiag), start=True, stop=True), reads=["onesf", "gdiag"], writes=["bank2"])
    S.op("pe", lambda e: e.matmul(f3(A[3]), C.onesf[:], f3(C.gdiag), start=True, stop=False), reads=["onesf", "gdiag"], writes=["bank3"])
    S.op("pe", lambda e: e.matmul(f3(A[3]), C.identf[:], C.gmask4[:], start=False, stop=True), reads=["identf", "gmask4"], writes=["bank3"])
    S.op("act", lambda e: e.activation(f3(C.gecb), f3(A[2]), AF.Exp), reads=["bank2"], writes=["gecb"])
    for h in range(4):
        S.op("act", lambda e, h=h: e.activation(C.gD[:, h, :], A[3][:, h, :], AF.Exp, scale=-1.0, bias=C.gc[:, h:h + 1]),
             reads=["bank3", "gc"], writes=["gD"])
    S.op("dve", lambda e: e.tensor_tensor(f3(C.gDst), f3(C.gD), C.strict4[:], ALU.mult), reads=["gD", "strict4"], writes=["gDst"])
    S.op("dve", lambda e: e.tensor_tensor(f3(C.gecb), C.gcv[:, 0:4, :].rearrange("p a b -> p (a b)"), f3(C.gecb), ALU.mult),
         reads=["gcv", "gecb"], writes=["gecb"])
    for h in range(4):
        S.op("pe", lambda e, h=h: e.matmul(A[5][:, h, :], kn(h), kn(h), start=True, stop=True), reads=["gcv"], writes=["bank5"])
        S.op("pe", lambda e, h=h: e.matmul(A[6][:, h, :], qn(h), kn(h), start=True, stop=True), reads=["gcv"], writes=["bank6"])
    S.op("dve", lambda e: e.tensor_tensor(f3(C.gdiag), f3(A[5]), f3(C.gDst), ALU.mult), reads=["bank5", "gDst"], writes=["gdiag"])
    S.op("dve", lambda e: e.tensor_tensor(C.gXTa[:], C.gdiag[:], bc4(C.gnb[:]), ALU.mult), reads=["gdiag", "gnb"], writes=["gXTa"])
    S.op("dve", lambda e: e.tensor_tensor(f3(C.gAm), f3(A[6]), f3(C.gD), ALU.mult), reads=["bank6", "gD"], writes=["gAm"])
    for h in range(4):
        S.op("pe", lambda e, h=h: e.transpose(A[0][:, h, :], C.gXTa[:, h, :], C.identf[:]), reads=["gXTa", "identf"], writes=["bank0"])
        S.op("pe", lambda e, h=h: e.transpose(A[1][:, h, :], C.gAm[:, h, :], C.identf[:]), reads=["gAm", "identf"], writes=["bank1"])
    S.op("act", lambda e: e.copy(f3(C.gXa), f3(A[0])), reads=["bank0"], writes=["gXa"])
    S.op("dve", lambda e: e.tensor_tensor(C.gR[:], A[0][:], C.identf[:].unsqueeze(1).to_broadcast([128, 4, 128]), ALU.add),
         reads=["bank0", "identf"], writes=["gR"])
    S.op("act", lambda e: e.copy(f3(C.gAT), f3(A[1])), reads=["bank1"], writes=["gAT"])
    X, XT = (C.gXa, "gXa"), (C.gXTa, "gXTa")
    Xn, XTn = (C.gXb, "gXb"), (C.gXTb, "gXTb")
    for j in range(1, 7):
        for h in range(4):
            S.op("pe", lambda e, h=h, X=X, XT=XT: e.matmul(A[5][:, h, :], X[0][:, h, :], XT[0][:, h, :], start=True, stop=True),
                 reads=[X[1], XT[1]], writes=["bank5"])
        S.op("act", lambda e, XTn=XTn: e.copy(f3(XTn[0]), f3(A[5])), reads=["bank5"], writes=[XTn[1]])
        if j < 6:
            for h in range(4):
                S.op("pe", lambda e, h=h, X=X, XT=XT: e.matmul(A[6][:, h, :], XT[0][:, h, :], X[0][:, h, :], start=True, stop=True),
                     reads=[X[1], XT[1]], writes=["bank6"])
            S.op("act", lambda e, Xn=Xn: e.copy(f3(Xn[0]), f3(A[6])), reads=["bank6"], writes=[Xn[1]])
        for h in range(4):
            S.op("pe", lambda e, h=h, XTn=XTn: e.matmul(A[0][:, h, :], XTn[0][:, h, :], C.gR[:, h, :], start=True, stop=True),
                 reads=[XTn[1], "gR"], writes=["bank0"])
        S.op("dve", lambda e: e.tensor_tensor(f3(C.gR), f3(C.gR), f3(A[0]), ALU.add), reads=["gR", "bank0"], writes=["gR"])
        X, XT, Xn, XTn = Xn, XTn, X, XT
    for h in range(4):
        S.op("pe", lambda e, h=h: e.transpose(A[1][:, h, :], kn(h), C.identf[:]), reads=["gcv", "identf"], writes=["bank1"])
        S.op("pe", lambda e, h=h: e.transpose(A[2][:, h, :], vf(h), C.identf[:]), reads=["gcv", "identf"], writes=["bank2"])
    S.op("dve", lambda e: e.tensor_tensor(C.gkbe[:], A[1][:], bc4(C.gbec[:]), ALU.mult), reads=["bank1", "gbec"], writes=["gXb"])
    S.op("dve", lambda e: e.tensor_tensor(C.gkdec[:], A[1][:], bc4(C.gedl[:]), ALU.mult), reads=["bank1", "gedl"], writes=["gXTa"])
    S.op("dve", lambda e: e.tensor_tensor(C.gvb[:], A[2][:], bc4(C.gbeta[:]), ALU.mult), reads=["bank2", "gbeta"], writes=["gXTb"])
    for h in range(4):
        S.op("pe", lambda e, h=h: e.matmul(A[3][:, h, :], C.gkbe[:, h, :], C.gR[:, h, :], start=True, stop=True), reads=["gXb", "gR"], writes=["bank3"])
        S.op("pe", lambda e, h=h: e.matmul(A[5][:, h, :], C.gR[:, h, :], C.gvb[:, h, :], start=True, stop=True), reads=["gXTb", "gR"], writes=["bank5"])
    S.op("act", lambda e: e.copy(f3(C.gwT), f3(A[3])), reads=["bank3"], writes=["gD"])
    S.op("act", lambda e: e.copy(f3(C.gu), f3(A[5])), reads=["bank5"], writes=["gAm"])
    for h in range(4):
        S.op("pe", lambda e, h=h: e.matmul(A[6][:, h, :], C.gwT[:, h, :], C.gS[:, h, :], start=True, stop=True), reads=["gD", "gS"], writes=["bank6"])
    S.op("dve", lambda e: e.tensor_tensor(f3(C.gvn), f3(C.gu), f3(A[6]), ALU.subtract), reads=["gAm", "bank6"], writes=["gDst"])
    for h in range(4):
        S.op("pe", lambda e, h=h: e.matmul(A[0][:, h, :], C.gS[:, h, :], C.gecb[:, h, :], start=True, stop=False), reads=["gS", "gecb"], writes=["bank0"])
        S.op("pe", lambda e, h=h: e.matmul(A[0][:, h, :], C.gvn[:, h, :], C.gAT[:, h, :], start=False, stop=True), reads=["gDst", "gAT"], writes=["bank0"])
    for h in range(4):
        S.op("pe", lambda e, h=h: e.matmul(A[1][:, h, :], C.gkdec[:, h, :], C.gvn[:, h, :], start=True, stop=True), reads=["gXTa", "gDst"], writes=["bank1"])
    S.op("dve", lambda e: e.tensor_tensor(C.gS[:], C.gS[:], bc4(C.gecl[:]), ALU.mult), reads=["gS", "gecl"], writes=["gS"])
    S.op("dve", lambda e: e.tensor_tensor(f3(C.gS), f3(C.gS), f3(A[1]), ALU.add), reads=["gS", "bank1"], writes=["gS"])
    S.op("act", lambda e: e.copy(f3(C.gXa), f3(A[0])), reads=["bank0"], writes=["gXa"])
    S.op("dve", lambda e: e.tensor_tensor(f3(C.gXb), f3(C.gXa), f3(C.gXa), ALU.mult), reads=["gXa"], writes=["gXb"])
    S.op("pe", lambda e: e.matmul(f3(A[2]), C.ones128f[:], f3(C.gXb), start=True, stop=True), reads=["ones128f", "gXb"], writes=["bank2"])
    S.op("act", lambda e: e.activation(f3(C.gXb), f3(A[2]), AF.Ln, bias=C.epst[:, 0:1]), reads=["bank2", "epst"], writes=["gXb"])
    S.op("act", lambda e: e.activation(f3(C.gXb), f3(C.gXb), AF.Exp, scale=-0.5), reads=["gXb"], writes=["gXb"])
    S.op("dve", lambda e: e.scalar_tensor_tensor(f3(C.gXa), f3(C.gXa), C.gnw[:, 0:1], f3(C.gXb), ALU.mult, ALU.mult), reads=["gXa", "gnw", "gXb"], writes=["gXa"])
    S.op("dve", lambda e: e.tensor_tensor(C.mixT[:, 2:6, :], C.gXa[:], C.gzs[:], ALU.mult), reads=["gXa", "gzs"], writes=["mixT"])


def alloc_mixer(nc, st, S, pfx=""):
    C = Ctx()
    def sb(name, shape, dt=F32):
        t = st.enter_context(nc.sbuf_tensor(pfx + name, shape, dt)); setattr(C, name, t); return t
    def ps(name, shape, dt=F32):
        return st.enter_context(nc.psum_tensor(pfx + name, shape, dt))
    for k, (shp, dt) in CSHAPES.items():
        sb(k, shp, dt)
    C.idb = C.identb
    sb("epst", [128, 1]); sb("onet", [128, 1]); sb("ones256b", [128, 128], BF16); sb("onesf", [128, 128])
    sb("wH", [128, 8, 1024], BF16); sb("wG", [128, 8, 2048], BF16); sb("wab", [128, 8, 8], BF16); sb("wu", [128, 8, 256], BF16)
    sb("wout", [128, 6, 1024], BF16); sb("woutH", [64, 4, 1024], BF16)
    sb("nwcol", [128, 8])
    C.hs = [sb("h0", [128, 1024])]
    sb("ss", [128, 1]); sb("lnv", [128, 1]); sb("rstd", [128, 1])
    sb("hn", [128, 1024], BF16); sb("hnT", [128, 8, 128], BF16)
    C.junk = C.hn
    sb("mixT", [128, 8, 128], BF16); sb("mixH", [64, 4, 128], BF16)
    for n in ("tabC", "tabS", "t2", "wre", "wim"):
        sb(n, [128, 8, 128])
    sb("s5tmp", [128, 1024])
    C.kint = C.s5tmp[:].bitcast(mybir.dt.int32)
    for n in ("BTre", "BTim", "CTre", "CTimn", "xre", "xim"):
        sb(n, [128, 8, 128], BF16)
    for n in ("lre_p", "lim_p", "ls_p", "dtp", "lrep", "magp", "thp", "a128", "t128", "se", "ce", "c1", "c2", "c3"):
        sb(n, [128, 8])
    sb("zst", [128, 2, 8]); sb("s5d", [128, 2]); sb("s5nw", [128, 2]); sb("glus", [128, 2, 256], BF16)
    sb("uF", [128, 2, 128]); sb("uB", [128, 2, 128], BF16)
    for n in ("yv", "g1", "zg", "o5"):
        sb(n, [128, 2, 128])
    sb("zgb", [128, 2, 128], BF16); sb("rs5", [128, 128])
    for n in ("he1", "hden", "hk", "hc", "hec", "hqt", "hkt", "hkh", "hv", "hof"):
        sb(n, [64, 4, 128])
    C.hgs = C.hc
    sb("hscm", [32, 4, 32]); sb("hkvT", [32, 4, 128]); sb("Sst", [64, 4, 64]); sb("ones64", [64, 64])
    sb("oml", [64, 4]); sb("lbl0", [64, 4]); sb("lbl1", [64, 4]); sb("hnw", [64, 1])
    sb("gx", [128, 12, 131]); sb("gcv", [128, 12, 128]); sb("gtmp", [128, 12, 128])
    for n in ("gzs", "gD", "gDst", "gAm", "gdiag", "gXTb", "gR", "gecb", "gS", "gAT"):
        sb(n, [128, 4, 128])
    sb("gXblk", [128, 12, 128])
    C.gXa = C.gXblk[:, 0:4, :]; C.gXb = C.gXblk[:, 4:8, :]; C.gXTa = C.gXblk[:, 8:12, :]
    C.gcv2 = C.gXblk
    C.gwT = C.gD; C.gvn = C.gDst; C.gu = C.gAm
    C.gkbe = C.gXb; C.gkdec = C.gXTa; C.gvb = C.gXTb
    V = lambda a: type("V", (), {"__getitem__": lambda self, k, a=a: a[k]})()
    C.stg0 = V(C.gtmp[:, 0:8, :].rearrange("p a b -> p (a b)"))
    C.stg1 = V(C.gcv[:, 0:8, :].rearrange("p a b -> p (a b)"))
    C.ho = V(C.gtmp[:, 0:8, :].rearrange("p a b -> p (a b)"))
    for n in ("ga", "gg", "gbeta", "gc", "gec", "gecl", "gedl", "gbec", "gnb", "aneg", "dtb"):
        sb(n, [128, 4])
    sb("convw", [128, 12, 4]); sb("gnw", [128, 1]); sb("ones128f", [128, 128])
    wGf = C.wG[:].rearrange("p a b -> p (a b)").bitcast(F32)
    scr = [wGf[:, i * 1024:(i + 1) * 1024] for i in range(8)]
    scr += [C.t2[:].rearrange("p a b -> p (a b)"), C.wre[:].rearrange("p a b -> p (a b)")]
    C.scr = [type("V", (), {"__getitem__": lambda self, k, a=a: a[k]})() for a in scr]
    C.tp = ps("tp", [128, 8, 128], BF16)
    C.bank = [ps("bank%d" % i, [128, 4, 128]) for i in range(7)]
    return C


def declare_params(nc, suffix=""):
    P = {}
    for k, shp in PSHAPES.items():
        P[k] = nc.dram_tensor("p_" + k + suffix, shp, F32, kind="ExternalInput").ap()
    return P


def mixer_setup(S, C, P, Cst, layer):
    for k in CSHAPES:
        S.dma("ld_c_" + k, getattr(C, k)[:], Cst[k], writes=[k])
    S.op("dve", lambda e: e.memset(C.epst[:], EPS), writes=["epst"])
    S.op("dve", lambda e: e.memset(C.onesf[:], 1.0), writes=["onesf"])
    S.op("dve", lambda e: e.memset(C.onet[:], 1.0), writes=["onet"])
    S.op("dve", lambda e: e.memset(C.ones256b[:], 1.0 / 256.0), writes=["ones256b"])
    S.op("dve", lambda e: e.memset(C.mixT[:], 0.0), writes=["mixT"])
    S.op("dve", lambda e: e.memset(C.mixH[:], 0.0), writes=["mixH"])
    s5_setup(S, C, P)
    S.barrier()
    S.dma("ld_nwcol", C.nwcol[:], P["nmixc"], writes=["nwcol"])
    stage = [C.stg0, C.stg1]
    K8 = list(range(8))
    load_weight_bf16(S, "wH", P["w_in"], K8, 1024, C.wH, 0, stage, 1, 0, C.nwcol)
    load_weight_bf16(S, "wG", P["w_in"], K8, 1024, C.wG[:, :, 0:1024], 0, stage, 1, 1024, C.nwcol)
    load_weight_bf16(S, "wG", P["w_in"], K8, 1024, C.wG[:, :, 1024:2048], 0, stage, 1, 2048, C.nwcol)
    load_weight_bf16(S, "wab", P["w_in"], K8, 8, C.wab, 0, stage, 8, 3072, C.nwcol)
    load_weight_bf16(S, "wu", P["w_in"], K8, 256, C.wu, 0, stage, 4, 3080, C.nwcol)
    load_weight_bf16(S, "wout", P["w_out"], list(range(2, 8)), 1024, C.wout, 2, stage, 1, 0)
    whv = P["w_out"][0:256, :].rearrange("(h p) n -> p h n", p=64)
    for i in range(4):
        sg = stage[i % 2][0:64, :]
        S.dma("wst%d" % (i % 2), sg, whv[:, i, :], writes=[("wstage", i % 2)])
        S.op("dve", lambda h, sg=sg, i=i: h.tensor_copy(C.woutH[:, i, :], sg), reads=[("wstage", i % 2)], writes=["woutH"])
    hgrn_setup(S, C, P, layer)
    if hasattr(C, "gdn_ready"):
        gdn_setup(S, C, P)
    S.barrier()


def out_proj_store(S, C, h, hkey, b, ydst, ykey):
    oP = [C.bank[4], C.bank[5]]
    for half in range(2):
        ob = oP[half][:].rearrange("p a b -> p (a b)")
        n = 0
        for hd in range(4):
            S.op("pe", lambda e, hd=hd, half=half, ob=ob: e.matmul(ob, C.mixH[:, hd, :], C.woutH[:, hd, half * 512:(half + 1) * 512],
                                                                  start=(hd == 0), stop=False),
                 reads=["mixH", "woutH"], writes=["bank%d" % (4 + half)])
        for ft in range(2, 8):
            S.op("pe", lambda e, ft=ft, half=half, ob=ob: e.matmul(ob, C.mixT[:, ft, :], C.wout[:, ft - 2, half * 512:(half + 1) * 512],
                                                                  start=False, stop=(ft == 7)),
                 reads=["mixT", "wout"], writes=["bank%d" % (4 + half)])
        S.op("dve", lambda e, half=half, ob=ob: e.tensor_tensor(C.ho[:, half * 512:(half + 1) * 512], ob, h[:, half * 512:(half + 1) * 512], ALU.add),
             reads=["bank%d" % (4 + half), hkey], writes=["gtmp"])
    S.dma("st0", ydst, C.ho[:], reads=["gtmp"], writes=[ykey])


def build_mixer(NT, parts=("s5",), debug=True, layer=0):
    nc = bass.Bass("TRN2", target_bir_lowering=False)
    x = nc.dram_tensor("x", [NT * 128, 1024], F32, kind="ExternalInput").ap()
    y = nc.dram_tensor("y", [NT * 128, 1024], F32, kind="ExternalOutput").ap()
    P = declare_params(nc)
    Cst = {k: nc.dram_tensor("c_" + k, shp, dt, kind="ExternalInput").ap() for k, (shp, dt) in CSHAPES.items()}
    dbg = nc.dram_tensor("dbg", [NT, 128, 1024], BF16, kind="ExternalOutput").ap() if debug else None
    dbgH = nc.dram_tensor("dbgH", [NT, 64, 512], BF16, kind="ExternalOutput").ap() if debug else None
    with ExitStack() as st:
        S = Sched(nc, st)
        C = alloc_mixer(nc, st, S)
        if "gdn" in parts:
            C.gdn_ready = True
        mixer_setup(S, C, P, Cst, layer)
        ykeys = []
        for c in range(NT):
            b = 0
            h = C.hs[b]
            S.dma("hl%d" % b, h[:], x[c * 128:(c + 1) * 128, :], writes=[("h", b)])
            rmsnorm_T(S, C, h, ("h", b), None, None)
            if "hgrn" in parts:
                hgrn_chunk(S, C, c == 0)
            if "gdn" in parts:
                gdn_chunk(S, C, c == 0)
            if "s5" in parts:
                s5_chunk(S, C, c == 0)
            if debug:
                S.dma("dbg", dbg[c], C.mixT[:].rearrange("p a b -> p (a b)"), reads=["mixT"], writes=[("dbg", c)])
                S.dma("dbgH", dbgH[c], C.mixH[:].rearrange("p a b -> p (a b)"), reads=["mixH"], writes=[("dbgH", c)])
                ykeys += [("dbg", c), ("dbgH", c)]
            out_proj_store(S, C, h, ("h", b), b, y[c * 128:(c + 1) * 128, :], ("y", c))
            ykeys.append(("y", c))
        S.final_wait("sp", ykeys)
        block = st.enter_context(nc.Block())
        S.run(block)
    return nc


def make_inmap(inp, l, xin):
    m = {"x": xin}
    for k, v in host_layer_params(inp, l).items():
        m["p_" + k] = v.reshape(PSHAPES[k])
    for k, v in host_consts().items():
        m["c_" + k] = v
    return m


def build_mlp(NT, final):
    nc = bass.Bass("TRN2", target_bir_lowering=False)
    x = nc.dram_tensor("x", [NT * 128, 1024], F32, kind="ExternalInput").ap()
    nw = nc.dram_tensor("p_nmlp", [128, 1024], F32, kind="ExternalInput").ap()
    fwd = nc.dram_tensor("p_finalw", [128, 1024], F32, kind="ExternalInput").ap()
    w1 = nc.dram_tensor("p_w1", [1024, 4096], F32, kind="ExternalInput").ap()
    w2 = nc.dram_tensor("p_w2", [4096, 1024], F32, kind="ExternalInput").ap()
    identb = nc.dram_tensor("c_identb", [128, 128], BF16, kind="ExternalInput").ap()
    y = nc.dram_tensor("y", [NT * 128, 1024], F32, kind="ExternalOutput").ap()
    with ExitStack() as st:
        C = Ctx()
        def sb(name, shape, dt=F32):
            t = st.enter_context(nc.sbuf_tensor(name, shape, dt)); setattr(C, name, t); return t
        def ps(name, shape, dt=F32):
            return st.enter_context(nc.psum_tensor(name, shape, dt))
        S = Sched(nc, st)
        sb("w1s", [128, 8, 4096], BF16); sb("w2s", [128, 32, 1024], BF16)
        stage = [sb("stg0", [128, 1024]), sb("stg1", [128, 1024])]
        sb("nws", [128, 1024]); sb("fws", [128, 1024]); sb("idb", [128, 128], BF16)
        C.hs = [sb("h0", [128, 1024]), sb("h1", [128, 1024])]
        sb("junk", [128, 1024]); sb("ss", [128, 1]); sb("lnv", [128, 1]); sb("rstd", [128, 1]); sb("epst", [128, 1])
        sb("hn", [128, 1024], BF16); sb("hnT", [128, 8, 128], BF16)
        rr = [sb("rr0", [128, 512]), sb("rr1", [128, 512])]
        sb("aT", [128, 32, 128], BF16)
        ho = [sb("ho0", [128, 1024]), sb("ho1", [128, 1024])]
        C.tp = ps("tp", [128, 8, 128], BF16)
        hid = [ps("hid%d" % i, [128, 4, 128]) for i in range(2)]
        ops_ = [ps("ops%d" % i, [128, 512]) for i in range(2)]
        S.op("dve", lambda e: e.memset(C.epst[:], EPS), writes=["epst"])
        S.dma("c0", C.nws[:], nw, writes=["nws"])
        S.dma("c2", C.fws[:], fwd, writes=["fws"])
        S.dma("c1", C.idb[:], identb, writes=["idb"])
        for cq in range(4):
            load_weight_bf16(S, "w1s", w1, list(range(8)), 1024, C.w1s[:, :, cq * 1024:(cq + 1) * 1024], 0, stage, 1, cq * 1024)
        load_weight_bf16(S, "w2s", w2, list(range(32)), 1024, C.w2s, 0, stage, 1, 0)
        ykeys = []
        for t in range(NT):
            b = t % 2
            h = C.hs[b]
            S.dma("hl%d" % b, h[:], x[t * 128:(t + 1) * 128, :], writes=[("h", b)])
            rmsnorm_T(S, C, h, ("h", b), C.nws, "nws")
            for g in range(8):
                hb = g % 2
                for mi in range(4):
                    m = g * 4 + mi
                    for kt in range(8):
                        S.op("pe", lambda e, m=m, mi=mi, kt=kt, hb=hb: e.matmul(
                            hid[hb][:, mi, :], C.w1s[:, kt, m * 128:(m + 1) * 128], C.hnT[:, kt, :],
                            start=(kt == 0), stop=(kt == 7)), reads=["w1s", "hnT"], writes=[("hid", hb)])
                S.op("act", lambda e, hb=hb: e.activation(rr[hb][:], hid[hb][:].rearrange("p a b -> p (a b)"), AF.Relu),
                     reads=[("hid", hb)], writes=[("rr", hb)])
                S.op("dve", lambda e, hb=hb, g=g: e.tensor_tensor(
                    C.aT[:, g * 4:(g + 1) * 4, :].rearrange("p a b -> p (a b)"), rr[hb][:], rr[hb][:], ALU.mult),
                    reads=[("rr", hb)], writes=["aT"])
            for half in range(2):
                for kt in range(32):
                    S.op("pe", lambda e, half=half, kt=kt: e.matmul(
                        ops_[half][:], C.aT[:, kt, :], C.w2s[:, kt, half * 512:(half + 1) * 512],
                        start=(kt == 0), stop=(kt == 31)), reads=["aT", "w2s"], writes=[("ops", half)])
                S.op("dve", lambda e, half=half, b=b, h=h: e.tensor_tensor(
                    ho[b][:, half * 512:(half + 1) * 512], ops_[half][:], h[:, half * 512:(half + 1) * 512], ALU.add),
                    reads=[("ops", half), ("h", b)], writes=[("ho", b)])
            if final:
                hb_ = ho[b]
                S.op("act", lambda e, hb_=hb_: e.activation(C.junk[:], hb_[:], AF.Square, scale=1.0 / 32.0, accum_out=C.ss[:]),
                     reads=[("ho", b)], writes=["junk", "ss"])
                S.op("act", lambda e: e.activation(C.lnv[:], C.ss[:], AF.Ln, bias=C.epst[:, 0:1]), reads=["ss", "epst"], writes=["lnv"])
                S.op("act", lambda e: e.activation(C.rstd[:], C.lnv[:], AF.Exp, scale=-0.5), reads=["lnv"], writes=["rstd"])
                S.op("dve", lambda e, hb_=hb_: e.scalar_tensor_tensor(hb_[:], hb_[:], C.rstd[:, 0:1], C.fws[:], ALU.mult, ALU.mult),
                     reads=[("ho", b), "rstd", "fws"], writes=[("ho", b)])
            S.dma("st%d" % b, y[t * 128:(t + 1) * 128, :], ho[b][:], reads=[("ho", b)], writes=[("y", t)])
            ykeys.append(("y", t))
        S.final_wait("sp", ykeys)
        block = st.enter_context(nc.Block())
        S.run(block)
    return nc


NT_FULL = 129
N_CORES = 8
MIXER_PARTS = ("hgrn", "gdn", "s5")


def kernel_unfused(**inputs):
    inp = {k: np.asarray(v) for k, v in inputs.items()}
    B = inp["x"].shape[0]
    NT = NT_FULL
    consts = host_consts()
    hcur = []
    for c in range(N_CORES):
        if c < B:
            hcur.append(np.concatenate([np.zeros((112, 1024), np.float32), inp["meta"].astype(np.float32), inp["x"][c]], 0))
        else:
            hcur.append(np.zeros((NT * 128, 1024), np.float32))
    nc_mix = {l: build_mixer(NT, MIXER_PARTS, debug=False, layer=l) for l in range(2)}
    nc_mlp = {False: build_mlp(NT, False), True: build_mlp(NT, True)}
    depth = inp["w_in"].shape[0]
    finalw = np.ascontiguousarray(np.broadcast_to(inp["final_norm_w"].reshape(1, 1024), (128, 1024)))
    for l in range(depth):
        lp = host_layer_params(inp, l)
        base = {"p_" + k: v.reshape(PSHAPES[k]) for k, v in lp.items()}
        for k, v in consts.items():
            base["c_" + k] = v
        ims = [dict(base, x=hcur[c]) for c in range(N_CORES)]
        res = run_bass_kernel_spmd(nc_mix[min(l, 1)], ims, core_ids=list(range(N_CORES)))
        hcur = [res.results[c]["y"] for c in range(N_CORES)]
        last = (l == depth - 1)
        mb = {"p_nmlp": lp["nmlp"], "p_finalw": finalw, "p_w1": lp["w1"], "p_w2": lp["w2"], "c_identb": consts["identb"]}
        ims = [dict(mb, x=hcur[c]) for c in range(N_CORES)]
        res = run_bass_kernel_spmd(nc_mlp[last], ims, core_ids=list(range(N_CORES)))
        hcur = [res.results[c]["y"] for c in range(N_CORES)]
    out = np.stack([hcur[c][128:] for c in range(B)], 0).astype(np.float32)
    return out


def emit_mixer_pass(nc, S, x, y, P, Cst, NT, layer, pfx):
    with ExitStack() as st:
        C = alloc_mixer(nc, st, S, pfx)
        C.gdn_ready = True
        S.barrier()
        mixer_setup(S, C, P, Cst, layer)
        ykeys = []
        for c in range(NT):
            h = C.hs[0]
            S.dma("hl0", h[:], x[c * 128:(c + 1) * 128, :], writes=[("h", 0)])
            rmsnorm_T(S, C, h, ("h", 0), None, None)
            hgrn_chunk(S, C, c == 0)
            gdn_chunk(S, C, c == 0)
            s5_chunk(S, C, c == 0)
            out_proj_store(S, C, h, ("h", 0), 0, y[c * 128:(c + 1) * 128, :], (pfx + "y", c))
            ykeys.append((pfx + "y", c))
        S.final_wait("sp", ykeys)
        S.barrier()
        block = st.enter_context(nc.Block())
        S.run(block)


def emit_mlp_pass(nc, S, x, y, P, finalw, identb, NT, final, pfx, skip_tiles=0):
    nw, w1, w2 = P["nmlp"], P["w1"], P["w2"]
    with ExitStack() as st:
        C = Ctx()
        def sb(name, shape, dt=F32):
            t = st.enter_context(nc.sbuf_tensor(pfx + name, shape, dt)); setattr(C, name, t); return t
        def ps(name, shape, dt=F32):
            return st.enter_context(nc.psum_tensor(pfx + name, shape, dt))
        S.barrier()
        sb("w1s", [128, 8, 4096], BF16); sb("w2s", [128, 32, 1024], BF16)
        stage = [sb("stg0", [128, 1024]), sb("stg1", [128, 1024])]
        sb("nws", [128, 1024]); sb("fws", [128, 1024]); sb("idb", [128, 128], BF16)
        C.hs = [sb("h0", [128, 1024]), sb("h1", [128, 1024])]
        sb("junk", [128, 1024]); sb("ss", [128, 1]); sb("lnv", [128, 1]); sb("rstd", [128, 1]); sb("epst", [128, 1])
        sb("hn", [128, 1024], BF16); sb("hnT", [128, 8, 128], BF16)
        rr = [sb("rr0", [128, 512]), sb("rr1", [128, 512])]
        sb("aT", [128, 32, 128], BF16)
        ho = [sb("ho0", [128, 1024]), sb("ho1", [128, 1024])]
        C.tp = ps("tp", [128, 8, 128], BF16)
        hid = [ps("hid%d" % i, [128, 4, 128]) for i in range(2)]
        ops_ = [ps("ops%d" % i, [128, 512]) for i in range(2)]
        S.op("dve", lambda e: e.memset(C.epst[:], EPS), writes=["epst"])
        S.dma("c0", C.nws[:], nw, writes=["nws"])
        S.dma("c2", C.fws[:], finalw, writes=["fws"])
        S.dma("c1", C.idb[:], identb, writes=["idb"])
        for cq in range(4):
            load_weight_bf16(S, "w1s", w1, list(range(8)), 1024, C.w1s[:, :, cq * 1024:(cq + 1) * 1024], 0, stage, 1, cq * 1024)
        load_weight_bf16(S, "w2s", w2, list(range(32)), 1024, C.w2s, 0, stage, 1, 0)
        ykeys = []
        for t in range(skip_tiles, NT):
            b = t % 2
            h = C.hs[b]
            S.dma("hl%d" % b, h[:], x[t * 128:(t + 1) * 128, :], writes=[("h", b)])
            rmsnorm_T(S, C, h, ("h", b), C.nws, "nws")
            for g in range(8):
                hb = g % 2
                for mi in range(4):
                    m = g * 4 + mi
                    for kt in range(8):
                        S.op("pe", lambda e, m=m, mi=mi, kt=kt, hb=hb: e.matmul(
                            hid[hb][:, mi, :], C.w1s[:, kt, m * 128:(m + 1) * 128], C.hnT[:, kt, :],
                            start=(kt == 0), stop=(kt == 7)), reads=["w1s", "hnT"], writes=[("hid", hb)])
                S.op("act", lambda e, hb=hb: e.activation(rr[hb][:], hid[hb][:].rearrange("p a b -> p (a b)"), AF.Relu),
                     reads=[("hid", hb)], writes=[("rr", hb)])
                S.op("dve", lambda e, hb=hb, g=g: e.tensor_tensor(
                    C.aT[:, g * 4:(g + 1) * 4, :].rearrange("p a b -> p (a b)"), rr[hb][:], rr[hb][:], ALU.mult),
                    reads=[("rr", hb)], writes=["aT"])
            for half in range(2):
                for kt in range(32):
                    S.op("pe", lambda e, half=half, kt=kt: e.matmul(
                        ops_[half][:], C.aT[:, kt, :], C.w2s[:, kt, half * 512:(half + 1) * 512],
                        start=(kt == 0), stop=(kt == 31)), reads=["aT", "w2s"], writes=[("ops", half)])
                S.op("dve", lambda e, half=half, b=b, h=h: e.tensor_tensor(
                    ho[b][:, half * 512:(half + 1) * 512], ops_[half][:], h[:, half * 512:(half + 1) * 512], ALU.add),
                    reads=[("ops", half), ("h", b)], writes=[("ho", b)])
            if final:
                hb_ = ho[b]
                S.op("act", lambda e, hb_=hb_: e.activation(C.junk[:], hb_[:], AF.Square, scale=1.0 / 32.0, accum_out=C.ss[:]),
                     reads=[("ho", b)], writes=["junk", "ss"])
                S.op("act", lambda e: e.activation(C.lnv[:], C.ss[:], AF.Ln, bias=C.epst[:, 0:1]), reads=["ss", "epst"], writes=["lnv"])
                S.op("act", lambda e: e.activation(C.rstd[:], C.lnv[:], AF.Exp, scale=-0.5), reads=["lnv"], writes=["rstd"])
                S.op("dve", lambda e, hb_=hb_: e.scalar_tensor_tensor(hb_[:], hb_[:], C.rstd[:, 0:1], C.fws[:], ALU.mult, ALU.mult),
                     reads=[("ho", b), "rstd", "fws"], writes=[("ho", b)])
            t2 = t - skip_tiles
            S.dma("st%d" % b, y[t2 * 128:(t2 + 1) * 128, :], ho[b][:], reads=[("ho", b)], writes=[(pfx + "y", t)])
            ykeys.append((pfx + "y", t))
        S.final_wait("sp", ykeys)
        S.barrier()
        block = st.enter_context(nc.Block())
        S.run(block)


def build_fused(NT, depth=2):
    nc = bass.Bass("TRN2", target_bir_lowering=False)
    x = nc.dram_tensor("x", [NT * 128, 1024], F32, kind="ExternalInput").ap()
    y = nc.dram_tensor("y", [(NT - 1) * 128, 1024], F32, kind="ExternalOutput").ap()
    hA = nc.dram_tensor("scr_hA", [NT * 128, 1024], F32, kind="Internal").ap()
    hB = nc.dram_tensor("scr_hB", [NT * 128, 1024], F32, kind="Internal").ap()
    finalw = nc.dram_tensor("p_finalw", [128, 1024], F32, kind="ExternalInput").ap()
    Ps = [declare_params(nc, "_l%d" % l) for l in range(depth)]
    Cst = {k: nc.dram_tensor("c_" + k, shp, dt, kind="ExternalInput").ap() for k, (shp, dt) in CSHAPES.items()}
    with ExitStack() as st:
        S = Sched(nc, st)
        src = x
        for l in range(depth):
            emit_mixer_pass(nc, S, src, hA, Ps[l], Cst, NT, min(l, 1), "m%d_" % l)
            last = (l == depth - 1)
            emit_mlp_pass(nc, S, hA, y if last else hB, Ps[l], finalw, Cst["identb"], NT, last, "f%d_" % l,
                          skip_tiles=1 if last else 0)
            src = hB
    return nc


def kernel_fused(**inputs):
    inp = {k: np.asarray(v) for k, v in inputs.items()}
    B = inp["x"].shape[0]
    NT = NT_FULL
    depth = inp["w_in"].shape[0]
    base = {}
    for l in range(depth):
        for k, v in host_layer_params(inp, l).items():
            base["p_" + k + "_l%d" % l] = v.reshape(PSHAPES[k])
    for k, v in host_consts().items():
        base["c_" + k] = v
    base["p_finalw"] = np.ascontiguousarray(np.broadcast_to(inp["final_norm_w"].reshape(1, 1024).astype(np.float32), (128, 1024)))
    ims = []
    for c in range(N_CORES):
        if c < B:
            xin = np.concatenate([np.zeros((112, 1024), np.float32), inp["meta"].astype(np.float32), inp["x"][c, :(NT - 1) * 128]], 0)
        else:
            xin = np.zeros((NT * 128, 1024), np.float32)
        ims.append(dict(base, x=xin))
    nc = build_fused(NT, depth)
    res = run_bass_kernel_spmd(nc, ims, core_ids=list(range(N_CORES)))
    return np.stack([res.results[c]["y"] for c in range(B)], 0).astype(np.float32)


kernel = kernel_fused
```

```python
from contextlib import ExitStack
import numpy as np
import concourse.bass as bass
import concourse.mybir as mybir

F32 = mybir.dt.float32
BF16 = mybir.dt.bfloat16
ALU = mybir.AluOpType
AF = mybir.ActivationFunctionType
AX = mybir.AxisListType

EPOCH = 24000
NEPOCH = 10


class Sched:
    ENGS = ("pe", "act", "dve", "pool", "sp")

    def __init__(self, nc, st):
        self.nc = nc
        self.st = st
        self.q = {e: [] for e in self.ENGS}
        self.cnt = {e: 0 for e in self.ENGS}
        self.sems = {}
        self.waited = {}
        self.lastw = {}
        self.readers = {}
        self.dmacnt = {}
        self.latest = {}
        self.h = {"pe": nc.tensor, "act": nc.scalar, "dve": nc.vector,
                  "pool": nc.gpsimd, "sp": nc.sync}

    def _sem(self, key):
        s = self.sems.get(key)
        if s is None:
            s = self.st.enter_context(self.nc.semaphore("s_%s_%s" % key))
            self.sems[key] = s
        return s

    def _wait(self, e, clock):
        if clock is None:
            return
        key, val = clock
        if key[0] == "dma":
            val = max(val, self.dmacnt[key[1]])
        if key[0] == "pe" and e == "pe":
            return
        if self.waited.get((e, key), 0) >= val:
            return
        self.waited[(e, key)] = val
        sem = self._sem(key)
        h = self.h[e]
        self.q[e].append(lambda: h.wait_ge(sem, val))

    def _deps(self, e, reads, writes):
        for r in reads:
            self._wait(e, self.lastw.get(r))
        for w in writes:
            self._wait(e, self.lastw.get(w))
            for k, v in self.readers.get(w, {}).items():
                self._wait(e, (k, v))

    def _record(self, clock, reads, writes):
        key, val = clock
        for r in reads:
            d = self.readers.setdefault(r, {})
            if d.get(key, 0) < val:
                d[key] = val
        for w in writes:
            self.lastw[w] = clock
            self.readers[w] = {}
        if self.latest.get(key, 0) < val:
            self.latest[key] = val

    def barrier(self):
        for e in self.ENGS:
            for key, val in list(self.latest.items()):
                self._wait(e, (key, val))

    @staticmethod
    def _excl(reads, writes):
        ex = [r for r in reads if (isinstance(r, str) and (r.startswith("bank") or r == "tp")) or
              (isinstance(r, tuple) and r[0] in ("hid", "ops"))]
        if ex:
            reads = [r for r in reads if r not in ex]
            writes = list(writes) + ex
        return reads, writes

    def op(self, e, fn, reads=(), writes=()):
        reads, writes = self._excl(reads, writes)
        self._deps(e, reads, writes)
        i = self.cnt[e]
        self.cnt[e] += 1
        key = (e, i // EPOCH)
        val = i % EPOCH + 1
        sem = self._sem(key)
        h = self.h[e]
        self.q[e].append(lambda: fn(h).then_inc(sem, 1))
        self._record((key, val), reads, writes)

    def dma(self, stream, out, in_, reads=(), writes=(), q="sp"):
        self._deps(q, reads, writes)
        key = ("dma", stream)
        n = self.dmacnt.get(stream, 0) + 16
        self.dmacnt[stream] = n
        sem = self._sem(key)
        h = self.h[q]
        self.q[q].append(lambda: h.dma_start(out=out, in_=in_).then_inc(sem, 16))
        self._record((key, n), reads, writes)

    def final_wait(self, e, resources):
        for r in resources:
            self._wait(e, self.lastw.get(r))

    def run(self, block):
        q = self.q
        self.q = {e: [] for e in self.ENGS}

        @block.tensor
        def _(eng):
            for f in q["pe"]:
                f()

        @block.scalar
        def _(eng):
            for f in q["act"]:
                f()

        @block.vector
        def _(eng):
            for f in q["dve"]:
                f()

        @block.gpsimd
        def _(eng):
            for f in q["pool"]:
                f()

        @block.sync
        def _(eng):
            for f in q["sp"]:
                f()


import sys, time, math
import ml_dtypes
from concourse.bass_utils import run_bass_kernel_spmd

EPS = 1e-6
TWO_PI = 2.0 * math.pi
GELU_C = 2.0 * math.sqrt(2.0 / math.pi)
NEG_BIG = 30000.0


def host_consts():
    c = {}
    c["identf"] = np.eye(128, dtype=np.float32)
    c["identb"] = np.eye(128).astype(ml_dtypes.bfloat16)
    s = np.arange(128)
    c["tri"] = (s[:, None] <= s[None, :]).astype(np.float32)
    c["gmask"] = (NEG_BIG * (s[None, :] > s[:, None])).astype(np.float32)
    c["strict"] = (s[None, :] < s[:, None]).astype(np.float32)
    c["iota"] = np.ascontiguousarray(np.broadcast_to(s.astype(np.float32), (128, 128)))
    m32 = (s[:32, None] <= s[None, :32]).astype(np.float32)
    c["mask32"] = np.ascontiguousarray(np.broadcast_to(m32[:, None, :], (32, 4, 32))).reshape(32, 128)
    rs = np.ones((64, 4, 128), np.float32)
    rs[:, :, 0::32] = 0.0
    c["hreset"] = rs.reshape(64, 512)
    c["gmask4"] = np.ascontiguousarray(np.tile(c["gmask"], (1, 4)))
    c["strict4"] = np.ascontiguousarray(np.tile(c["strict"], (1, 4)))
    del c["gmask"], c["strict"]
    return c


def host_layer_params(inp, l):
    p = {}
    bc = lambda v, n=128: np.ascontiguousarray(np.broadcast_to(np.asarray(v, np.float32).reshape(1, -1), (n, np.asarray(v).size)))
    p["nmix"] = bc(inp["norm_mix_w"][l])
    p["nmlp"] = bc(inp["norm_mlp_w"][l])
    p["nmixc"] = np.ascontiguousarray(np.asarray(inp["norm_mix_w"][l], np.float32).reshape(8, 128).T)
    p["w_in"] = np.ascontiguousarray(inp["w_in"][l])
    p["w_out"] = np.ascontiguousarray(inp["w_out"][l])
    p["w1"] = np.ascontiguousarray(inp["mlp_w1"][l])
    p["w2"] = np.ascontiguousarray(inp["mlp_w2"][l])
    p["lbl0"] = np.ascontiguousarray(inp["hgrn_lb_logits"][0].reshape(4, 64).T)
    p["lbl1"] = np.ascontiguousarray(inp["hgrn_lb_logits"][1].reshape(4, 64).T)
    p["hnw"] = np.ascontiguousarray(inp["hgrn_norm_w"][l].reshape(64, 1))
    p["convw"] = np.ascontiguousarray(inp["gdn_conv_w"][l].reshape(4, 12, 128).transpose(2, 1, 0))
    p["alog"] = bc(inp["gdn_a_log"][l])
    p["dtb"] = bc(inp["gdn_dt_bias"][l])
    p["gnw"] = np.ascontiguousarray(inp["gdn_norm_w"][l].reshape(128, 1))
    lre, lim, ls = inp["s5_lam_re"][l], inp["s5_lam_im"][l], inp["s5_log_step"][l]
    def pl(a):
        return np.ascontiguousarray(a.reshape(8, 128).T)
    p["lre_p"] = pl(lre); p["lim_p"] = pl(lim)
    p["ls_p"] = pl(np.broadcast_to(ls[:, None], (16, 64)))
    fl = lambda a: np.ascontiguousarray(np.broadcast_to(a.reshape(1, 1024), (128, 1024)))
    p["lre_f"] = fl(lre); p["lim_f"] = fl(lim)
    p["ls_f"] = fl(np.ascontiguousarray(np.broadcast_to(ls[:, None], (16, 64))))
    bre, bim = inp["s5_b_re"][l], inp["s5_b_im"][l]
    cre, cim = inp["s5_c_re"][l], inp["s5_c_im"][l]
    BTr = np.zeros((128, 8, 128), np.float32); BTi = np.zeros((128, 8, 128), np.float32)
    CTr = np.zeros((128, 8, 128), np.float32); CTi = np.zeros((128, 8, 128), np.float32)
    for g in range(16):
        j = g // 2; r0 = (g % 2) * 64; c0 = (16 * g) % 128
        BTr[c0:c0 + 16, j, r0:r0 + 64] = bre[g].T
        BTi[c0:c0 + 16, j, r0:r0 + 64] = bim[g].T
        CTr[r0:r0 + 64, j, c0:c0 + 16] = cre[g].T
        CTi[r0:r0 + 64, j, c0:c0 + 16] = cim[g].T
    p["BTr"] = BTr.reshape(128, 1024); p["BTi"] = BTi.reshape(128, 1024)
    p["CTr"] = CTr.reshape(128, 1024); p["CTi"] = CTi.reshape(128, 1024)
    p["s5d"] = np.ascontiguousarray(inp["s5_d"][l].reshape(2, 128).T)
    p["s5nw"] = np.ascontiguousarray(inp["s5_norm_w"][l].reshape(2, 128).T)
    p["glu"] = np.ascontiguousarray(inp["s5_glu_w"][l])
    return p


PSHAPES = {
    "nmix": [128, 1024], "nmixc": [128, 8], "nmlp": [128, 1024], "w_in": [1024, 3336], "w_out": [1024, 1024],
    "w1": [1024, 4096], "w2": [4096, 1024], "lbl0": [64, 4], "lbl1": [64, 4], "hnw": [64, 1],
    "convw": [128, 48], "alog": [128, 4], "dtb": [128, 4], "gnw": [128, 1],
    "lre_p": [128, 8], "lim_p": [128, 8], "ls_p": [128, 8],
    "lre_f": [128, 1024], "lim_f": [128, 1024], "ls_f": [128, 1024],
    "BTr": [128, 1024], "BTi": [128, 1024], "CTr": [128, 1024], "CTi": [128, 1024],
    "s5d": [128, 2], "s5nw": [128, 2], "glu": [256, 256],
}
CSHAPES = {"identf": ([128, 128], F32), "identb": ([128, 128], BF16), "tri": ([128, 128], F32),
           "iota": ([128, 128], F32),
           "mask32": ([32, 128], F32), "hreset": ([64, 512], F32),
           "gmask4": ([128, 512], F32), "strict4": ([128, 512], F32)}


class Ctx:
    pass


def load_weight_bf16(S, name, w_dram, kts, N, dst, dst_off, stage, kt_per, col0=0, scale=None, np_=128):
    wv = w_dram.rearrange("(kt p) n -> p kt n", p=128)
    assert kt_per * N <= 1024
    cnt = getattr(S, "_wl", 0)
    for i0 in range(0, len(kts), kt_per):
        grp = kts[i0:i0 + kt_per]
        sl = cnt % 2
        cnt += 1
        sg = stage[sl][:, 0:len(grp) * N].rearrange("p (k n) -> p k n", k=len(grp))
        S.dma("wst%d" % sl, sg, wv[:, grp[0]:grp[0] + len(grp), col0:col0 + N], writes=[("wstage", sl)])
        for i, kt in enumerate(grp):
            eng = "dve" if (cnt + i) % 2 == 0 else "pool"
            if scale is None:
                S.op(eng, lambda h, sg=sg, i=i, kt=kt: h.tensor_copy(dst[:, kt - dst_off, 0:N], sg[:, i, :]),
                     reads=[("wstage", sl)], writes=[name])
            else:
                S.op(eng, lambda h, sg=sg, i=i, kt=kt: h.tensor_scalar(dst[:, kt - dst_off, 0:N], sg[:, i, :], scale[:, kt:kt + 1], None, ALU.mult),
                     reads=[("wstage", sl), "nwcol"], writes=[name])
    S._wl = cnt


def rmsnorm_T(S, C, h, hkey, nws, nwkey):
    S.op("act", lambda e: e.activation(C.junk[:], h[:], AF.Square, scale=1.0 / 32.0, accum_out=C.ss[:]),
         reads=[hkey], writes=["junk", "hn", "ss"])
    S.op("act", lambda e: e.activation(C.lnv[:], C.ss[:], AF.Ln, bias=C.epst[:, 0:1]), reads=["ss", "epst"], writes=["lnv"])
    S.op("act", lambda e: e.activation(C.rstd[:], C.lnv[:], AF.Exp, scale=-0.5), reads=["lnv"], writes=["rstd"])
    if nws is None:
        S.op("dve", lambda e: e.tensor_scalar(C.hn[:], h[:], C.rstd[:, 0:1], None, ALU.mult), reads=[hkey, "rstd"], writes=["hn"])
    else:
        S.op("dve", lambda e: e.scalar_tensor_tensor(C.hn[:], h[:], C.rstd[:, 0:1], nws[:], ALU.mult, ALU.mult),
             reads=[hkey, "rstd", nwkey], writes=["hn"])
    for kt in range(8):
        S.op("pe", lambda e, kt=kt: e.transpose(C.tp[:, kt, :], C.hn[:, kt * 128:(kt + 1) * 128], C.idb[:]),
             reads=["hn", "idb"], writes=["tp"])
    S.op("act", lambda e: e.copy(C.hnT[:], C.tp[:]), reads=["tp"], writes=["hnT"])


def sincos_tables(S, C, ang, angkey, sin_out, cos_out, okeys, kint, tmps, tkeys):
    kf, s2, s4 = tmps
    kk, k2, k4 = tkeys
    S.op("dve", lambda e: e.tensor_scalar(kint, ang, 1.0 / TWO_PI, None, ALU.mult), reads=[angkey], writes=["kint"])
    S.op("dve", lambda e: e.tensor_copy(kf, kint), reads=["kint"], writes=[kk])
    S.op("dve", lambda e: e.scalar_tensor_tensor(kf, kf, -TWO_PI, ang, ALU.mult, ALU.add), reads=[kk, angkey], writes=[kk])
    S.op("act", lambda e: e.activation(s4, kf, AF.Sin, scale=0.25), reads=[kk], writes=[k4])
    S.op("act", lambda e: e.activation(s2, kf, AF.Sin, scale=0.5), reads=[kk], writes=[k2])
    S.op("dve", lambda e: e.tensor_tensor(s4, s4, s4, ALU.mult), reads=[k4], writes=[k4])
    S.op("dve", lambda e: e.tensor_scalar(s4, s4, -2.0, 1.0, ALU.mult, ALU.add), reads=[k4], writes=[k4])
    S.op("dve", lambda e: e.scalar_tensor_tensor(sin_out, s2, 2.0, s4, ALU.mult, ALU.mult), reads=[k2, k4], writes=[okeys[0]])
    S.op("dve", lambda e: e.tensor_tensor(s2, s2, s2, ALU.mult), reads=[k2, okeys[0]], writes=[k2])
    S.op("dve", lambda e: e.tensor_scalar(cos_out, s2, -2.0, 1.0, ALU.mult, ALU.add), reads=[k2], writes=[okeys[1]])


def s5_setup(S, C, P):
    T = C.scr
    def ld(tile_ap, name, key):
        S.dma("ld_" + key, tile_ap, P[name], writes=[key])
    ld(C.lre_p[:], "lre_p", "lre_p"); ld(C.lim_p[:], "lim_p", "lim_p"); ld(C.ls_p[:], "ls_p", "ls_p")
    ld(C.s5d[:], "s5d", "s5d"); ld(C.s5nw[:], "s5nw", "s5nw")
    S.op("act", lambda e: e.activation(C.dtp[:], C.ls_p[:], AF.Exp), reads=["ls_p"], writes=["dtp"])
    S.op("dve", lambda e: e.tensor_scalar(C.lrep[:], C.lre_p[:], -1e-4, None, ALU.min), reads=["lre_p"], writes=["lrep"])
    S.op("dve", lambda e: e.tensor_tensor(C.magp[:], C.lrep[:], C.dtp[:], ALU.mult), reads=["lrep", "dtp"], writes=["magp"])
    S.op("act", lambda e: e.activation(C.magp[:], C.magp[:], AF.Exp), reads=["magp"], writes=["magp"])
    S.op("dve", lambda e: e.tensor_tensor(C.thp[:], C.lim_p[:], C.dtp[:], ALU.mult), reads=["lim_p", "dtp"], writes=["thp"])
    ang = T[0]
    for j in range(8):
        S.op("dve", lambda e, j=j: e.tensor_scalar(ang[:, j * 128:(j + 1) * 128], C.iota[:], C.thp[:, j:j + 1], None, ALU.mult),
             reads=["iota", "thp"], writes=["scr0"])
    sincos_tables(S, C, ang[:], "scr0", C.tabS[:].rearrange("p a b -> p (a b)"), C.tabC[:].rearrange("p a b -> p (a b)"),
                  ["tabS", "tabC"], C.kint[:], [T[1][:], T[2][:], T[3][:]], ["scr1", "scr2", "scr3"])
    S.op("dve", lambda e: e.tensor_scalar(C.a128[:], C.thp[:], 128.0, None, ALU.mult), reads=["thp"], writes=["a128"])
    sincos_tables(S, C, C.a128[:], "a128", C.se[:], C.ce[:], ["se", "ce"], C.kint[:, 0:8], [C.t128[:], C.c1[:], C.c2[:]], ["t128", "c1", "c2"])
    lre, lim, ls, t3, t4, t5, t6, t7 = T[2], T[3], T[4], T[5], T[6], T[7], T[8], T[9]
    ld(lre[:], "lre_f", "scr2"); ld(lim[:], "lim_f", "scr3"); ld(ls[:], "ls_f", "scr4")
    S.op("act", lambda e: e.activation(ls[:], ls[:], AF.Exp), reads=["scr4"], writes=["scr4"])
    S.op("dve", lambda e: e.tensor_scalar(lre[:], lre[:], -1e-4, None, ALU.min), reads=["scr2"], writes=["scr2"])
    S.op("dve", lambda e: e.tensor_tensor(t3[:], lre[:], ls[:], ALU.mult), reads=["scr2", "scr4"], writes=["scr5"])
    S.op("act", lambda e: e.activation(t3[:], t3[:], AF.Exp), reads=["scr5"], writes=["scr5"])
    S.op("dve", lambda e: e.tensor_tensor(t4[:], lim[:], ls[:], ALU.mult), reads=["scr3", "scr4"], writes=["scr6"])
    sincos_tables(S, C, t4[:], "scr6", t5[:], t6[:], ["scr7", "scr8"], C.kint[:], [t7[:], T[0][:], T[1][:]], ["scr9", "scr0", "scr1"])
    S.op("dve", lambda e: e.tensor_tensor(t5[:], t5[:], t3[:], ALU.mult), reads=["scr7", "scr5"], writes=["scr7"])
    S.op("dve", lambda e: e.tensor_tensor(t6[:], t6[:], t3[:], ALU.mult), reads=["scr8", "scr5"], writes=["scr8"])
    S.op("dve", lambda e: e.tensor_scalar(t6[:], t6[:], -1.0, None, ALU.add), reads=["scr8"], writes=["scr8"])
    S.op("dve", lambda e: e.tensor_tensor(t3[:], lre[:], lre[:], ALU.mult), reads=["scr2", "scr7", "scr8"], writes=["scr5"])
    S.op("dve", lambda e: e.tensor_tensor(t4[:], lim[:], lim[:], ALU.mult), reads=["scr3"], writes=["scr6"])
    S.op("dve", lambda e: e.tensor_tensor(t3[:], t3[:], t4[:], ALU.add), reads=["scr5", "scr6"], writes=["scr5"])
    S.op("dve", lambda e: e.reciprocal(t3[:], t3[:]), reads=["scr5"], writes=["scr5"])
    S.op("dve", lambda e: e.tensor_tensor(t4[:], t6[:], lre[:], ALU.mult), reads=["scr8", "scr2", "scr6"], writes=["scr6"])
    S.op("dve", lambda e: e.tensor_tensor(t7[:], t5[:], lim[:], ALU.mult), reads=["scr7", "scr3"], writes=["scr9"])
    S.op("dve", lambda e: e.tensor_tensor(t4[:], t4[:], t7[:], ALU.add), reads=["scr6", "scr9"], writes=["scr6"])
    S.op("dve", lambda e: e.tensor_tensor(t4[:], t4[:], t3[:], ALU.mult), reads=["scr6", "scr5"], writes=["scr6"])
    S.op("dve", lambda e: e.tensor_tensor(t7[:], t5[:], lre[:], ALU.mult), reads=["scr7", "scr2", "scr6"], writes=["scr9"])
    S.op("dve", lambda e: e.tensor_tensor(ls[:], t6[:], lim[:], ALU.mult), reads=["scr8", "scr3"], writes=["scr4"])
    S.op("dve", lambda e: e.tensor_tensor(t7[:], t7[:], ls[:], ALU.subtract), reads=["scr9", "scr4"], writes=["scr9"])
    S.op("dve", lambda e: e.tensor_tensor(t7[:], t7[:], t3[:], ALU.mult), reads=["scr9", "scr5"], writes=["scr9"])
    br, bi = lre, lim
    ld(br[:], "BTr", "scr2"); ld(bi[:], "BTi", "scr3")
    flat = lambda t: t[:].rearrange("p a b -> p (a b)")
    S.op("dve", lambda e: e.tensor_tensor(t5[:], t4[:], br[:], ALU.mult), reads=["scr6", "scr2", "scr7"], writes=["scr7"])
    S.op("dve", lambda e: e.tensor_tensor(t6[:], t7[:], bi[:], ALU.mult), reads=["scr9", "scr3", "scr8"], writes=["scr8"])
    S.op("dve", lambda e: e.tensor_tensor(flat(C.BTre), t5[:], t6[:], ALU.subtract), reads=["scr7", "scr8"], writes=["BTre"])
    S.op("dve", lambda e: e.tensor_tensor(t5[:], t4[:], bi[:], ALU.mult), reads=["scr6", "scr3", "BTre"], writes=["scr7"])
    S.op("dve", lambda e: e.tensor_tensor(t6[:], t7[:], br[:], ALU.mult), reads=["scr9", "scr2", "BTre"], writes=["scr8"])
    S.op("dve", lambda e: e.tensor_tensor(flat(C.BTim), t5[:], t6[:], ALU.add), reads=["scr7", "scr8"], writes=["BTim"])
    ld(t3[:], "CTr", "scr5"); ld(t4[:], "CTi", "scr6")
    S.op("dve", lambda e: e.tensor_copy(flat(C.CTre), t3[:]), reads=["scr5"], writes=["CTre"])
    S.op("dve", lambda e: e.tensor_scalar(flat(C.CTimn), t4[:], -1.0, None, ALU.mult), reads=["scr6"], writes=["CTimn"])
    gst = T[0][:, 0:512].rearrange("p (k n) -> p k n", k=2)
    S.dma("ld_glu", gst, P["glu"].rearrange("(k p) n -> p k n", p=128), reads=["tabS", "tabC"], writes=["scr0"])
    S.op("dve", lambda e: e.tensor_copy(C.glus[:], gst), reads=["scr0"], writes=["glus"])
    S.op("dve", lambda e: e.memset(C.zst[:], 0.0), writes=["zst"])


def s5_chunk(S, C, first):
    A = C.bank
    uP = A[0]
    for m in range(2):
        for kt in range(8):
            S.op("pe", lambda e, m=m, kt=kt: e.matmul(uP[:, m, :], C.wu[:, kt, m * 128:(m + 1) * 128], C.hnT[:, kt, :],
                                                     start=(kt == 0), stop=(kt == 7)),
                 reads=["wu", "hnT"], writes=["bank0"])
    S.op("act", lambda e: e.copy(C.uF[:], uP[:, 0:2, :]), reads=["bank0"], writes=["uF"])
    S.op("dve", lambda e: e.tensor_copy(C.uB[:], uP[:, 0:2, :]), reads=["bank0"], writes=["uB"])
    for j in range(8):
        S.op("pe", lambda e, j=j: e.matmul(A[2 + j // 4][:, j % 4, :], C.BTre[:, j, :], C.uB[:, j // 4, :], start=True, stop=True),
             reads=["BTre", "uB"], writes=["bank%d" % (2 + j // 4)])
        S.op("pe", lambda e, j=j: e.matmul(A[4 + j // 4][:, j % 4, :], C.BTim[:, j, :], C.uB[:, j // 4, :], start=True, stop=True),
             reads=["BTim", "uB"], writes=["bank%d" % (4 + j // 4)])
    f4 = lambda t, hf: t[:, hf * 4:(hf + 1) * 4, :].rearrange("p a b -> p (a b)")
    fb = lambda b: b[:].rearrange("p a b -> p (a b)")
    for hf in range(2):
        bre, bim = A[2 + hf], A[4 + hf]
        kre, kim = "bank%d" % (2 + hf), "bank%d" % (4 + hf)
        S.op("dve", lambda e, hf=hf, bre=bre: e.tensor_tensor(f4(C.wre, hf), fb(bre), f4(C.tabC, hf), ALU.mult), reads=[kre, "tabC"], writes=["wre"])
        S.op("dve", lambda e, hf=hf, bim=bim: e.tensor_tensor(f4(C.t2, hf), fb(bim), f4(C.tabS, hf), ALU.mult), reads=[kim, "tabS"], writes=["t2"])
        S.op("pool", lambda e, hf=hf: e.tensor_tensor(f4(C.wre, hf), f4(C.wre, hf), f4(C.t2, hf), ALU.add), reads=["wre", "t2"], writes=["wre"])
        S.op("dve", lambda e, hf=hf, bim=bim: e.tensor_tensor(f4(C.wim, hf), fb(bim), f4(C.tabC, hf), ALU.mult), reads=[kim, "tabC"], writes=["wim"])
        S.op("dve", lambda e, hf=hf, bre=bre: e.tensor_tensor(f4(C.t2, hf), fb(bre), f4(C.tabS, hf), ALU.mult), reads=[kre, "tabS", "wre"], writes=["t2"])
        S.op("pool", lambda e, hf=hf: e.tensor_tensor(f4(C.wim, hf), f4(C.wim, hf), f4(C.t2, hf), ALU.subtract), reads=["wim", "t2"], writes=["wim"])
    zl_re, zl_im = C.zst[:, 0, :], C.zst[:, 1, :]
    S.op("dve", lambda e: e.tensor_tensor(C.c1[:], C.ce[:], zl_re, ALU.mult), reads=["ce", "zst"], writes=["c1"])
    S.op("dve", lambda e: e.tensor_tensor(C.c2[:], C.se[:], zl_im, ALU.mult), reads=["se", "zst"], writes=["c2"])
    S.op("dve", lambda e: e.tensor_tensor(C.c1[:], C.c1[:], C.c2[:], ALU.subtract), reads=["c1", "c2"], writes=["c1"])
    S.op("dve", lambda e: e.tensor_tensor(C.c3[:], C.ce[:], zl_im, ALU.mult), reads=["ce", "zst"], writes=["c3"])
    S.op("dve", lambda e: e.tensor_tensor(C.c2[:], C.se[:], zl_re, ALU.mult), reads=["se", "zst", "c1"], writes=["c2"])
    S.op("dve", lambda e: e.tensor_tensor(C.c3[:], C.c3[:], C.c2[:], ALU.add), reads=["c3", "c2"], writes=["c3"])
    for j in range(8):
        dj = C.magp[:, j:j + 1].to_broadcast([128, 128])
        S.op("dve", lambda e, j=j, dj=dj: e.tensor_tensor_scan(C.wre[:, j, :], dj, C.wre[:, j, :], C.c1[:, j:j + 1], ALU.mult, ALU.add),
             reads=["magp", "wre", "c1"], writes=["wre"])
        S.op("dve", lambda e, j=j, dj=dj: e.tensor_tensor_scan(C.wim[:, j, :], dj, C.wim[:, j, :], C.c3[:, j:j + 1], ALU.mult, ALU.add),
             reads=["magp", "wim", "c3"], writes=["wim"])
    S.op("pool", lambda e: e.tensor_copy(C.zst[:, 0, :], C.wre[:, :, 127]), reads=["wre"], writes=["zst"])
    S.op("pool", lambda e: e.tensor_copy(C.zst[:, 1, :], C.wim[:, :, 127]), reads=["wim"], writes=["zst"])
    fl = lambda t: t[:].rearrange("p a b -> p (a b)")
    tB = C.s5tmp[:]
    S.op("dve", lambda e: e.tensor_tensor(fl(C.t2), fl(C.wre), fl(C.tabC), ALU.mult), reads=["wre", "tabC"], writes=["t2"])
    S.op("dve", lambda e: e.tensor_tensor(tB, fl(C.wim), fl(C.tabS), ALU.mult), reads=["wim", "tabS"], writes=["s5tmp"])
    S.op("dve", lambda e: e.tensor_tensor(fl(C.xre), fl(C.t2), tB, ALU.subtract), reads=["t2", "s5tmp"], writes=["xre"])
    S.op("dve", lambda e: e.tensor_tensor(tB, fl(C.wre), fl(C.tabS), ALU.mult), reads=["wre", "tabS", "xre"], writes=["s5tmp"])
    S.op("dve", lambda e: e.tensor_tensor(fl(C.t2), fl(C.wim), fl(C.tabC), ALU.mult), reads=["wim", "tabC", "xre"], writes=["t2"])
    S.op("dve", lambda e: e.tensor_tensor(fl(C.xim), tB, fl(C.t2), ALU.add), reads=["t2", "s5tmp"], writes=["xim"])
    yP = A[0][:, 2:4, :]
    for m in range(2):
        for i, j in enumerate(range(4 * m, 4 * m + 4)):
            S.op("pe", lambda e, m=m, j=j, i=i: e.matmul(yP[:, m, :], C.CTre[:, j, :], C.xre[:, j, :], start=(i == 0), stop=False),
                 reads=["CTre", "xre"], writes=["bank0"])
            S.op("pe", lambda e, m=m, j=j, i=i: e.matmul(yP[:, m, :], C.CTimn[:, j, :], C.xim[:, j, :], start=False, stop=(i == 3)),
                 reads=["CTimn", "xim"], writes=["bank0"])
    f2 = lambda t: t[:].rearrange("p a b -> p (a b)")
    for m in range(2):
        S.op("dve", lambda e, m=m: e.scalar_tensor_tensor(C.yv[:, m, :], C.uF[:, m, :], C.s5d[:, m:m + 1], yP[:, m, :], ALU.mult, ALU.add),
             reads=["uF", "s5d", "bank0"], writes=["yv"])
    S.op("dve", lambda e: e.tensor_tensor(f2(C.g1), f2(C.yv), f2(C.yv), ALU.mult), reads=["yv"], writes=["g1"])
    S.op("dve", lambda e: e.tensor_scalar(f2(C.g1), f2(C.g1), 0.044715, 1.0, ALU.mult, ALU.add), reads=["g1"], writes=["g1"])
    S.op("dve", lambda e: e.tensor_tensor(f2(C.g1), f2(C.g1), f2(C.yv), ALU.mult), reads=["g1", "yv"], writes=["g1"])
    S.op("dve", lambda e: e.tensor_scalar(f2(C.g1), f2(C.g1), -GELU_C, 40.0, ALU.mult, ALU.min), reads=["g1"], writes=["g1"])
    S.op("act", lambda e: e.activation(f2(C.g1), f2(C.g1), AF.Exp), reads=["g1"], writes=["g1"])
    S.op("act", lambda e: e.activation(f2(C.g1), f2(C.g1), AF.Ln, bias=C.onet[:, 0:1]), reads=["g1", "onet"], writes=["g1"])
    S.op("act", lambda e: e.activation(f2(C.g1), f2(C.g1), AF.Exp, scale=-1.0), reads=["g1"], writes=["g1"])
    S.op("dve", lambda e: e.tensor_tensor(f2(C.zg), f2(C.yv), f2(C.g1), ALU.mult), reads=["g1", "yv"], writes=["zg"])
    S.op("dve", lambda e: e.tensor_copy(f2(C.zgb), f2(C.zg)), reads=["zg"], writes=["zgb"])
    gP = A[1]
    for m2 in range(2):
        for k in range(2):
            S.op("pe", lambda e, m2=m2, k=k: e.matmul(gP[:, m2, :], C.glus[:, k, m2 * 128:(m2 + 1) * 128], C.zgb[:, k, :],
                                                     start=(k == 0), stop=(k == 1)),
                 reads=["glus", "zgb"], writes=["bank1"])
    S.op("act", lambda e: e.activation(f2(C.g1), gP[:, 0:2, :].rearrange("p a b -> p (a b)"), AF.Exp, scale=-1.0), reads=["bank1"], writes=["g1"])
    S.op("act", lambda e: e.activation(f2(C.g1), f2(C.g1), AF.Ln, bias=C.onet[:, 0:1]), reads=["g1", "onet"], writes=["g1"])
    S.op("act", lambda e: e.activation(f2(C.g1), f2(C.g1), AF.Exp, scale=-1.0), reads=["g1"], writes=["g1"])
    S.op("dve", lambda e: e.tensor_tensor(f2(C.o5), f2(C.zg), f2(C.g1), ALU.mult), reads=["g1", "zg"], writes=["o5"])
    S.op("dve", lambda e: e.tensor_tensor(f2(C.zgb), f2(C.o5), f2(C.o5), ALU.mult), reads=["o5"], writes=["zgb"])
    sP = A[1][:, 2:4, :]
    for m in range(2):
        S.op("pe", lambda e, m=m: e.matmul(sP[:, 0, :], C.ones256b[:], C.zgb[:, m, :], start=(m == 0), stop=(m == 1)),
             reads=["ones256b", "zgb"], writes=["bank1"])
    S.op("act", lambda e: e.activation(C.rs5[:], sP[:, 0, :], AF.Ln, bias=C.epst[:, 0:1]), reads=["bank1", "epst"], writes=["rs5"])
    S.op("act", lambda e: e.activation(C.rs5[:], C.rs5[:], AF.Exp, scale=-0.5), reads=["rs5"], writes=["rs5"])
    for m in range(2):
        S.op("dve", lambda e, m=m: e.scalar_tensor_tensor(C.mixT[:, 6 + m, :], C.o5[:, m, :], C.s5nw[:, m:m + 1], C.rs5[:], ALU.mult, ALU.mult),
             reads=["o5", "s5nw", "rs5"], writes=["mixT"])


def hgrn_setup(S, C, P, layer):
    S.dma("ld_hnw", C.hnw[:], P["hnw"], writes=["hnw"])
    S.op("dve", lambda e: e.memset(C.ones64[:], 1.0 / 64.0), writes=["ones64"])
    S.op("dve", lambda e: e.memset(C.Sst[:], 0.0), writes=["Sst"])
    if layer == 0:
        S.op("dve", lambda e: e.memset(C.oml[:], 1.0), writes=["oml"])
    else:
        S.dma("ld_lbl0", C.lbl0[:], P["lbl0"], writes=["lbl0"])
        S.dma("ld_lbl1", C.lbl1[:], P["lbl1"], writes=["lbl1"])
        S.op("dve", lambda e: e.tensor_tensor(C.oml[:], C.lbl0[:], C.lbl1[:], ALU.subtract), reads=["lbl0", "lbl1"], writes=["oml"])
        S.op("act", lambda e: e.activation(C.oml[:], C.oml[:], AF.Exp), reads=["oml"], writes=["oml"])
        S.op("dve", lambda e: e.tensor_scalar(C.oml[:], C.oml[:], 1.0, None, ALU.add), reads=["oml"], writes=["oml"])
        S.op("dve", lambda e: e.reciprocal(C.oml[:], C.oml[:]), reads=["oml"], writes=["oml"])
        S.op("dve", lambda e: e.tensor_scalar(C.oml[:], C.oml[:], -1.0, 1.0, ALU.mult, ALU.add), reads=["oml"], writes=["oml"])


def hgrn_chunk(S, C, first):
    A = C.bank
    H = lambda t: t[0:64]
    fl = lambda t: t[:].rearrange("p a b -> p (a b)")
    fb = lambda b: b[0:64].rearrange("p a b -> p (a b)")
    for g in range(4):
        for h in range(4):
            c0 = g * 256 + h * 64
            for kt in range(8):
                S.op("pe", lambda e, g=g, h=h, kt=kt, c0=c0: e.matmul(A[g][0:64, h, :], C.wH[:, kt, c0:c0 + 64], C.hnT[:, kt, :],
                                                                     start=(kt == 0), stop=(kt == 7)),
                     reads=["wH", "hnT"], writes=["bank%d" % g])
    S.op("act", lambda e: e.activation(fl(C.he1), fb(A[1]), AF.Exp), reads=["bank1"], writes=["he1"])
    S.op("act", lambda e: e.activation(fl(C.he1), fl(C.he1), AF.Ln, bias=C.onet[0:64, 0:1]), reads=["he1", "onet"], writes=["he1"])
    S.op("act", lambda e: e.activation(fl(C.he1), fl(C.he1), AF.Exp, scale=-1.0), reads=["he1"], writes=["he1"])
    S.op("dve", lambda e: e.tensor_tensor(C.hk[:], C.he1[:], C.oml[:].unsqueeze(2).to_broadcast([64, 4, 128]), ALU.mult),
         reads=["he1", "oml"], writes=["hk"])
    S.op("dve", lambda e: e.tensor_scalar(fl(C.hden), fl(C.hk), -1.0, 1.0, ALU.mult, ALU.add), reads=["hk"], writes=["hden"])
    S.op("act", lambda e: e.activation(fl(C.hden), fl(C.hden), AF.Ln), reads=["hden"], writes=["hden"])
    S.op("dve", lambda e: e.tensor_tensor_scan(fl(C.hc), C.hreset[:], fl(C.hden), 0.0, ALU.mult, ALU.add), reads=["hreset", "hden"], writes=["hc"])
    S.op("act", lambda e: e.activation(fl(C.hec), fl(C.hc), AF.Exp), reads=["hc"], writes=["hec"])
    S.op("act", lambda e: e.activation(fl(C.hof), fl(C.hc), AF.Exp, scale=-1.0), reads=["hc"], writes=["hof"])
    S.op("act", lambda e: e.activation(fl(C.he1), fb(A[0]), AF.Exp, scale=-1.0), reads=["bank0"], writes=["he1"])
    S.op("act", lambda e: e.activation(fl(C.he1), fl(C.he1), AF.Ln, bias=C.onet[0:64, 0:1]), reads=["he1", "onet"], writes=["he1"])
    S.op("act", lambda e: e.activation(fl(C.he1), fl(C.he1), AF.Exp, scale=-1.0), reads=["he1"], writes=["he1"])
    S.op("dve", lambda e: e.tensor_tensor(fl(C.hqt), fb(A[0]), fl(C.he1), ALU.mult), reads=["bank0", "he1"], writes=["hqt"])
    S.op("dve", lambda e: e.tensor_tensor(fl(C.hqt), fl(C.hqt), fl(C.hec), ALU.mult), reads=["hqt", "hec"], writes=["hqt"])
    S.op("dve", lambda e: e.tensor_tensor(fl(C.hkt), fl(C.hk), fl(C.hof), ALU.mult), reads=["hk", "hof"], writes=["hkt"])
    r4 = lambda t: t[:].rearrange("p h (n c) -> p h n c", c=32)
    S.op("dve", lambda e: e.tensor_tensor(r4(C.hkh), r4(C.hkt), r4(C.hec)[:, :, :, 31:32].to_broadcast([64, 4, 4, 32]), ALU.mult),
         reads=["hkt", "hec"], writes=["hkh"])
    S.op("act", lambda e: e.copy(fl(C.hv), fb(A[2])), reads=["bank2"], writes=["hv"])
    S.op("act", lambda e: e.activation(fl(C.he1), fb(A[3]), AF.Exp, scale=-1.0), reads=["bank3"], writes=["he1"])
    S.op("act", lambda e: e.activation(fl(C.he1), fl(C.he1), AF.Ln, bias=C.onet[0:64, 0:1]), reads=["he1", "onet"], writes=["he1"])
    S.op("act", lambda e: e.activation(fl(C.he1), fl(C.he1), AF.Exp, scale=-1.0), reads=["he1"], writes=["he1"])
    S.op("dve", lambda e: e.tensor_tensor(fl(C.hgs), fb(A[3]), fl(C.he1), ALU.mult), reads=["bank3", "he1"], writes=["hc"])
    scA = A[4][0:32]
    for n in range(4):
        sl = slice(32 * n, 32 * n + 32)
        for h in range(4):
            S.op("pe", lambda e, h=h, n=n, sl=sl: e.matmul(scA[:, n, 32 * h:32 * h + 32], C.hkt[:, h, sl], C.hqt[:, h, sl], start=True, stop=True),
                 reads=["hkt", "hqt"], writes=["bank4"])
    S.op("dve", lambda e: e.tensor_tensor(C.hscm[:], scA, C.mask32[:].unsqueeze(1).to_broadcast([32, 4, 128]), ALU.mult),
         reads=["bank4", "mask32"], writes=["hscm"])
    trP = A[5][0:32]
    oP = A[2]
    dSP = A[3][0:64, 0:2, :].rearrange("p a (b c) -> p (a b) c", c=64)
    for n in range(4):
        sl = slice(32 * n, 32 * n + 32)
        for h in range(4):
            S.op("pe", lambda e, h=h, sl=sl: e.transpose(trP[:, h, 0:64], C.hkh[:, h, sl], C.identf[0:64, 0:64]),
                 reads=["hkh", "identf"], writes=["bank5"])
            S.op("pe", lambda e, h=h, sl=sl: e.transpose(trP[:, h, 64:128], C.hv[:, h, sl], C.identf[0:64, 0:64]),
                 reads=["hv", "identf"], writes=["bank5"])
        S.op("act", lambda e: e.copy(C.hkvT[:], trP), reads=["bank5"], writes=["hkvT"])
        for h in range(4):
            S.op("pe", lambda e, h=h, sl=sl, n=n: e.matmul(oP[0:64, h, sl], C.hkvT[:, h, 64:128], C.hscm[:, n, 32 * h:32 * h + 32], start=True, stop=False),
                 reads=["hkvT", "hscm"], writes=["bank2"])
            S.op("pe", lambda e, h=h, sl=sl: e.matmul(oP[0:64, h, sl], C.Sst[:, h, :], C.hqt[:, h, sl], start=False, stop=True),
                 reads=["Sst", "hqt"], writes=["bank2"])
        for h in range(4):
            S.op("pe", lambda e, h=h: e.matmul(dSP[:, h, :], C.hkvT[:, h, 0:64], C.hkvT[:, h, 64:128], start=True, stop=True),
                 reads=["hkvT"], writes=["bank3"])
        S.op("dve", lambda e, n=n: e.tensor_tensor(C.Sst[:], C.Sst[:], C.hec[:, :, 32 * n + 31:32 * n + 32].to_broadcast([64, 4, 64]), ALU.mult),
             reads=["Sst", "hec"], writes=["Sst"])
        S.op("dve", lambda e: e.tensor_tensor(C.Sst[:], C.Sst[:], dSP, ALU.add), reads=["Sst", "bank3"], writes=["Sst"])
    S.op("act", lambda e: e.copy(fl(C.hof), fb(oP)), reads=["bank2"], writes=["hof"])
    S.op("dve", lambda e: e.tensor_tensor(fl(C.he1), fl(C.hof), fl(C.hof), ALU.mult), reads=["hof"], writes=["he1"])
    msP = fb(A[4])
    S.op("pe", lambda e: e.matmul(msP, C.ones64[:], fl(C.he1), start=True, stop=True), reads=["ones64", "he1"], writes=["bank4"])
    S.op("act", lambda e: e.activation(fl(C.hden), msP, AF.Ln, bias=C.epst[0:64, 0:1]), reads=["bank4", "epst"], writes=["hden"])
    S.op("act", lambda e: e.activation(fl(C.hden), fl(C.hden), AF.Exp, scale=-0.5), reads=["hden"], writes=["hden"])
    S.op("dve", lambda e: e.scalar_tensor_tensor(fl(C.hof), fl(C.hof), C.hnw[:, 0:1], fl(C.hden), ALU.mult, ALU.mult),
         reads=["hof", "hnw", "hden"], writes=["hof"])
    S.op("dve", lambda e: e.tensor_tensor(fl(C.mixH), fl(C.hof), fl(C.hgs), ALU.mult), reads=["hof", "hc"], writes=["mixH"])


def gdn_setup(S, C, P):
    S.dma("ld_convw", C.convw[:].rearrange("p a b -> p (a b)"), P["convw"], writes=["convw"])
    S.dma("ld_alog", C.aneg[:], P["alog"], writes=["aneg"])
    S.dma("ld_dtb", C.dtb[:], P["dtb"], writes=["dtb"])
    S.dma("ld_gnw", C.gnw[:], P["gnw"], writes=["gnw"])
    S.op("act", lambda e: e.activation(C.aneg[:], C.aneg[:], AF.Exp), reads=["aneg"], writes=["aneg"])
    S.op("dve", lambda e: e.tensor_scalar(C.aneg[:], C.aneg[:], -1.0, None, ALU.mult), reads=["aneg"], writes=["aneg"])
    S.op("dve", lambda e: e.memset(C.ones128f[:], 1.0 / 128.0), writes=["ones128f"])
    S.op("dve", lambda e: e.memset(C.gx[:], 0.0), writes=["gx"])
    S.op("dve", lambda e: e.memset(C.gS[:], 0.0), writes=["gS"])


def gdn_chunk(S, C, first):
    A = C.bank
    f3 = lambda t: t[:].rearrange("p a b -> p (a b)")
    bc4 = lambda small: small.unsqueeze(2).to_broadcast([128, 4, 128])
    for g in range(4):
        for h in range(4):
            c0 = g * 512 + h * 128
            for kt in range(8):
                S.op("pe", lambda e, g=g, h=h, kt=kt, c0=c0: e.matmul(A[g][:, h, :], C.wG[:, kt, c0:c0 + 128], C.hnT[:, kt, :],
                                                                     start=(kt == 0), stop=(kt == 7)),
                     reads=["wG", "hnT"], writes=["bank%d" % g])
    abP = A[4][:, 0, 0:8]
    for kt in range(8):
        S.op("pe", lambda e, kt=kt: e.matmul(abP, C.hnT[:, kt, :], C.wab[:, kt, :], start=(kt == 0), stop=(kt == 7)),
             reads=["hnT", "wab"], writes=["bank4"])
    for g in range(3):
        S.op("act", lambda e, g=g: e.copy(C.gx[:, 4 * g:4 * g + 4, 3:131], A[g][:]), reads=["bank%d" % g], writes=["gx"])
    S.op("act", lambda e: e.activation(f3(C.gzs), f3(A[3]), AF.Exp, scale=-1.0), reads=["bank3"], writes=["gzs"])
    S.op("act", lambda e: e.activation(f3(C.gzs), f3(C.gzs), AF.Ln, bias=C.onet[:, 0:1]), reads=["gzs", "onet"], writes=["gzs"])
    S.op("act", lambda e: e.activation(f3(C.gzs), f3(C.gzs), AF.Exp, scale=-1.0), reads=["gzs"], writes=["gzs"])
    S.op("dve", lambda e: e.tensor_tensor(f3(C.gzs), f3(A[3]), f3(C.gzs), ALU.mult), reads=["bank3", "gzs"], writes=["gzs"])
    S.op("dve", lambda e: e.tensor_tensor(C.ga[:], A[4][:, 0, 0:4], C.dtb[:], ALU.add), reads=["bank4", "dtb"], writes=["ga"])
    S.op("act", lambda e: e.activation(C.ga[:], C.ga[:], AF.Exp), reads=["ga"], writes=["ga"])
    S.op("act", lambda e: e.activation(C.ga[:], C.ga[:], AF.Ln, bias=C.onet[:, 0:1]), reads=["ga", "onet"], writes=["ga"])
    S.op("dve", lambda e: e.tensor_tensor(C.gg[:], C.ga[:], C.aneg[:], ALU.mult), reads=["ga", "aneg"], writes=["gg"])
    S.op("act", lambda e: e.activation(C.gbeta[:], A[4][:, 0, 4:8], AF.Exp, scale=-1.0), reads=["bank4"], writes=["gbeta"])
    S.op("act", lambda e: e.activation(C.gbeta[:], C.gbeta[:], AF.Ln, bias=C.onet[:, 0:1]), reads=["gbeta", "onet"], writes=["gbeta"])
    S.op("act", lambda e: e.activation(C.gbeta[:], C.gbeta[:], AF.Exp, scale=-1.0), reads=["gbeta"], writes=["gbeta"])
    cP = A[4][:, 1, 0:4]
    clP = A[4][:, 2, 0:4]
    S.op("pe", lambda e: e.matmul(cP, C.tri[:], C.gg[:], start=True, stop=True), reads=["tri", "gg"], writes=["bank4"])
    S.op("pe", lambda e: e.matmul(clP, C.onesf[:], C.gg[:], start=True, stop=True), reads=["onesf", "gg"], writes=["bank4"])
    S.op("act", lambda e: e.copy(C.gc[:], cP), reads=["bank4"], writes=["gc"])
    S.op("act", lambda e: e.activation(C.gec[:], cP, AF.Exp), reads=["bank4"], writes=["gec"])
    S.op("act", lambda e: e.activation(C.gecl[:], clP, AF.Exp), reads=["bank4"], writes=["gecl"])
    S.op("dve", lambda e: e.tensor_tensor(C.gedl[:], clP, C.gc[:], ALU.subtract), reads=["bank4", "gc"], writes=["gedl"])
    S.op("act", lambda e: e.activation(C.gedl[:], C.gedl[:], AF.Exp), reads=["gedl"], writes=["gedl"])
    S.op("dve", lambda e: e.tensor_tensor(C.gbec[:], C.gbeta[:], C.gec[:], ALU.mult), reads=["gbeta", "gec"], writes=["gbec"])
    S.op("dve", lambda e: e.tensor_scalar(C.gnb[:], C.gbeta[:], -1.0, None, ALU.mult), reads=["gbeta"], writes=["gnb"])
    cw = lambda j: C.convw[:, :, j:j + 1].to_broadcast([128, 12, 128])
    S.op("pool", lambda e: e.tensor_tensor(C.gtmp[:], C.gx[:, :, 3:131], cw(3), ALU.mult), reads=["gx", "convw"], writes=["gtmp"])
    S.op("dve", lambda e: e.tensor_tensor(C.gcv[:], C.gx[:, :, 0:128], cw(0), ALU.mult), reads=["gx", "convw"], writes=["gcv"])
    for j in range(1, 3):
        S.op("dve", lambda e, j=j: e.tensor_tensor(C.gcv2[:], C.gx[:, :, j:j + 128], cw(j), ALU.mult), reads=["gx", "convw"], writes=["gXa", "gXb", "gXTa"])
        S.op("dve", lambda e: e.tensor_tensor(f3(C.gcv), f3(C.gcv), f3(C.gcv2), ALU.add), reads=["gcv", "gXa", "gXb", "gXTa"], writes=["gcv"])
    S.op("dve", lambda e: e.tensor_tensor(f3(C.gcv), f3(C.gcv), f3(C.gtmp), ALU.add), reads=["gcv", "gtmp"], writes=["gcv"])
    S.op("pool", lambda e: e.tensor_copy(C.gx[:, :, 0:3], C.gx[:, :, 128:131]), reads=["gx"], writes=["gx"])
    S.op("act", lambda e: e.activation(f3(C.gtmp), f3(C.gcv), AF.Exp, scale=-1.0), reads=["gcv"], writes=["gtmp"])
    S.op("act", lambda e: e.activation(f3(C.gtmp), f3(C.gtmp), AF.Ln, bias=C.onet[:, 0:1]), reads=["gtmp", "onet"], writes=["gtmp"])
    S.op("act", lambda e: e.activation(f3(C.gtmp), f3(C.gtmp), AF.Exp, scale=-1.0), reads=["gtmp"], writes=["gtmp"])
    S.op("dve", lambda e: e.tensor_tensor(f3(C.gcv), f3(C.gcv), f3(C.gtmp), ALU.mult), reads=["gcv", "gtmp"], writes=["gcv"])
    qk = C.gcv[:, 0:8, :]
    S.op("dve", lambda e: e.tensor_tensor(C.gtmp[:, 0:8, :], qk, qk, ALU.mult), reads=["gcv"], writes=["gtmp"])
    for i in range(2):
        S.op("pe", lambda e, i=i: e.matmul(f3(A[i]), C.onesf[:], C.gtmp[:, 4 * i:4 * i + 4, :].rearrange("p a b -> p (a b)"), start=True, stop=True),
             reads=["onesf", "gtmp"], writes=["bank%d" % i])
    for i in range(2):
        S.op("act", lambda e, i=i: e.activation(C.gtmp[:, 8 + 2 * i:10 + 2 * i, :].rearrange("p a b -> p (a b)")[:, 0:256], f3(A[i])[:, 0:256], AF.Ln, bias=C.epst[:, 0:1]),
             reads=["bank%d" % i, "epst"], writes=["gtmp"]) if False else None
    S.op("act", lambda e: e.activation(C.gtmp[:, 8:12, :].rearrange("p a b -> p (a b)"), f3(A[0]), AF.Ln, bias=C.epst[:, 0:1]), reads=["bank0", "epst"], writes=["gtmp"])
    S.op("act", lambda e: e.activation(C.gtmp[:, 8:12, :].rearrange("p a b -> p (a b)"), C.gtmp[:, 8:12, :].rearrange("p a b -> p (a b)"), AF.Exp, scale=-0.5), reads=["gtmp"], writes=["gtmp"])
    S.op("dve", lambda e: e.scalar_tensor_tensor(C.gcv[:, 0:4, :], C.gcv[:, 0:4, :], 128.0 ** -0.5, C.gtmp[:, 8:12, :], ALU.mult, ALU.mult),
         reads=["gcv", "gtmp"], writes=["gcv"])
    S.op("act", lambda e: e.activation(C.gtmp[:, 8:12, :].rearrange("p a b -> p (a b)"), f3(A[1]), AF.Ln, bias=C.epst[:, 0:1]), reads=["bank1", "epst", "gcv"], writes=["gtmp"])
    S.op("act", lambda e: e.activation(C.gtmp[:, 8:12, :].rearrange("p a b -> p (a b)"), C.gtmp[:, 8:12, :].rearrange("p a b -> p (a b)"), AF.Exp, scale=-0.5), reads=["gtmp"], writes=["gtmp"])
    S.op("dve", lambda e: e.tensor_tensor(C.gcv[:, 4:8, :], C.gcv[:, 4:8, :], C.gtmp[:, 8:12, :], ALU.mult), reads=["gcv", "gtmp"], writes=["gcv"])
    qn = lambda h: C.gcv[:, h, :]
    kn = lambda h: C.gcv[:, 4 + h, :]
    vf = lambda h: C.gcv[:, 8 + h, :]
    S.op("dve", lambda e: e.tensor_tensor(C.gdiag[:], C.identf[:].unsqueeze(1).to_broadcast([128, 4, 128]), bc4(C.gc[:]), ALU.mult),
         reads=["identf", "gc"], writes=["gdiag"])
    S.op("pe", lambda e: e.matmul(f3(A[2]), C.onesf[:], f3(C.gdiag), start=True, stop=True), reads=["onesf", "gdiag"], writes=["bank2"])
    S.op("pe", lambda e: e.matmul(f3(A[3]), C.onesf[:], f3(C.gdiag), start=True, stop=False), reads=["onesf", "gdiag"], writes=["bank3"])
    S.op("pe", lambda e: e.matmul(f3(A[3]), C.identf[:], C.gmask4[:], start=False, stop=True), reads=["identf", "gmask4"], writes=["bank3"])
    S.op("act", lambda e: e.activation(f3(C.gecb), f3(A[2]), AF.Exp), reads=["bank2"], writes=["gecb"])
    for h in range(4):
        S.op("act", lambda e, h=h: e.activation(C.gD[:, h, :], A[3][:, h, :], AF.Exp, scale=-1.0, bias=C.gc[:, h:h + 1]),
             reads=["bank3", "gc"], writes=["gD"])
    S.op("dve", lambda e: e.tensor_tensor(f3(C.gDst), f3(C.gD), C.strict4[:], ALU.mult), reads=["gD", "strict4"], writes=["gDst"])
    S.op("dve", lambda e: e.tensor_tensor(f3(C.gecb), C.gcv[:, 0:4, :].rearrange("p a b -> p (a b)"), f3(C.gecb), ALU.mult),
         reads=["gcv", "gecb"], writes=["gecb"])
    for h in range(4):
        S.op("pe", lambda e, h=h: e.matmul(A[5][:, h, :], kn(h), kn(h), start=True, stop=True), reads=["gcv"], writes=["bank5"])
        S.op("pe", lambda e, h=h: e.matmul(A[6][:, h, :], qn(h), kn(h), start=True, stop=True), reads=["gcv"], writes=["bank6"])
    S.op("dve", lambda e: e.tensor_tensor(f3(C.gdiag), f3(A[5]), f3(C.gDst), ALU.mult), reads=["bank5", "gDst"], writes=["gdiag"])
    S.op("dve", lambda e: e.tensor_tensor(C.gXTa[:], C.gdiag[:], bc4(C.gnb[:]), ALU.mult), reads=["gdiag", "gnb"], writes=["gXTa"])
    S.op("dve", lambda e: e.tensor_tensor(f3(C.gAm), f3(A[6]), f3(C.gD), ALU.mult), reads=["bank6", "gD"], writes=["gAm"])
    for h in range(4):
        S.op("pe", lambda e, h=h: e.transpose(A[0][:, h, :], C.gXTa[:, h, :], C.identf[:]), reads=["gXTa", "identf"], writes=["bank0"])
        S.op("pe", lambda e, h=h: e.transpose(A[1][:, h, :], C.gAm[:, h, :], C.identf[:]), reads=["gAm", "identf"], writes=["bank1"])
    S.op("act", lambda e: e.copy(f3(C.gXa), f3(A[0])), reads=["bank0"], writes=["gXa"])
    S.op("dve", lambda e: e.tensor_tensor(C.gR[:], A[0][:], C.identf[:].unsqueeze(1).to_broadcast([128, 4, 128]), ALU.add),
         reads=["bank0", "identf"], writes=["gR"])
    S.op("act", lambda e: e.copy(f3(C.gAT), f3(A[1])), reads=["bank1"], writes=["gAT"])
    X, XT = (C.gXa, "gXa"), (C.gXTa, "gXTa")
    Xn, XTn = (C.gXb, "gXb"), (C.gXTb, "gXTb")
    for j in range(1, 7):
        for h in range(4):
            S.op("pe", lambda e, h=h, X=X, XT=XT: e.matmul(A[5][:, h, :], X[0][:, h, :], XT[0][:, h, :], start=True, stop=True),
                 reads=[X[1], XT[1]], writes=["bank5"])
        S.op("act", lambda e, XTn=XTn: e.copy(f3(XTn[0]), f3(A[5])), reads=["bank5"], writes=[XTn[1]])
        if j < 6:
            for h in range(4):
                S.op("pe", lambda e, h=h, X=X, XT=XT: e.matmul(A[6][:, h, :], XT[0][:, h, :], X[0][:, h, :], start=True, stop=True),
                     reads=[X[1], XT[1]], writes=["bank6"])
            S.op("act", lambda e, Xn=Xn: e.copy(f3(Xn[0]), f3(A[6])), reads=["bank6"], writes=[Xn[1]])
        for h in range(4):
            S.op("pe", lambda e, h=h, XTn=XTn: e.matmul(A[0][:, h, :], XTn[0][:, h, :], C.gR[:, h, :], start=True, stop=True),
                 reads=[XTn[1], "gR"], writes=["bank0"])
        S.op("dve", lambda e: e.tensor_tensor(f3(C.gR), f3(C.gR), f3(A[0]), ALU.add), reads=["gR", "bank0"], writes=["gR"])
        X, XT, Xn, XTn = Xn, XTn, X, XT
    for h in range(4):
        S.op("pe", lambda e, h=h: e.transpose(A[1][:, h, :], kn(h), C.identf[:]), reads=["gcv", "identf"], writes=["bank1"])
        S.op("pe", lambda e, h=h: e.transpose(A[2][:, h, :], vf(h), C.identf[:]), reads=["gcv", "identf"], writes=["bank2"])
    S.op("dve", lambda e: e.tensor_tensor(C.gkbe[:], A[1][:], bc4(C.gbec[:]), ALU.mult), reads=["bank1", "gbec"], writes=["gXb"])
    S.op("dve", lambda e: e.tensor_tensor(C.gkdec[:], A[1][:], bc4(C.gedl[:]), ALU.mult), reads=["bank1", "gedl"], writes=["gXTa"])
    S.op("dve", lambda e: e.tensor_tensor(C.gvb[:], A[2][:], bc4(C.gbeta[:]), ALU.mult), reads=["bank2", "gbeta"], writes=["gXTb"])
    for h in range(4):
        S.op("pe", lambda e, h=h: e.matmul(A[3][:, h, :], C.gkbe[:, h, :], C.gR[:, h, :], start=True, stop=True), reads=["gXb", "gR"], writes=["bank3"])
        S.op("pe", lambda e, h=h: e.matmul(A[5][:, h, :], C.gR[:, h, :], C.gvb[:, h, :], start=True, stop=True), reads=["gXTb", "gR"], writes=["bank5"])
    S.op("act", lambda e: e.copy(f3(C.gwT), f3(A[3])), reads=["bank3"], writes=["gD"])
    S.op("act", lambda e: e.copy(f3(C.gu), f3(A[5])), reads=["bank5"], writes=["gAm"])
    for h in range(4):
        S.op("pe", lambda e, h=h: e.matmul(A[6][:, h, :], C.gwT[:, h, :], C.gS[:, h, :], start=True, stop=True), reads=["gD", "gS"], writes=["bank6"])
    S.op("dve", lambda e: e.tensor_tensor(f3(C.gvn), f3(C.gu), f3(A[6]), ALU.subtract), reads=["gAm", "bank6"], writes=["gDst"])
    for h in range(4):
        S.op("pe", lambda e, h=h: e.matmul(A[0][:, h, :], C.gS[:, h, :], C.gecb[:, h, :], start=True, stop=False), reads=["gS", "gecb"], writes=["bank0"])
        S.op("pe", lambda e, h=h: e.matmul(A[0][:, h, :], C.gvn[:, h, :], C.gAT[:, h, :], start=False, stop=True), reads=["gDst", "gAT"], writes=["bank0"])
    for h in range(4):
        S.op("pe", lambda e, h=h: e.matmul(A[1][:, h, :], C.gkdec[:, h, :], C.gvn[:, h, :], start=True, stop=True), reads=["gXTa", "gDst"], writes=["bank1"])
    S.op("dve", lambda e: e.tensor_tensor(C.gS[:], C.gS[:], bc4(C.gecl[:]), ALU.mult), reads=["gS", "gecl"], writes=["gS"])
    S.op("dve", lambda e: e.tensor_tensor(f3(C.gS), f3(C.gS), f3(A[1]), ALU.add), reads=["gS", "bank1"], writes=["gS"])
    S.op("act", lambda e: e.copy(f3(C.gXa), f3(A[0])), reads=["bank0"], writes=["gXa"])
    S.op("dve", lambda e: e.tensor_tensor(f3(C.gXb), f3(C.gXa), f3(C.gXa), ALU.mult), reads=["gXa"], writes=["gXb"])
    S.op("pe", lambda e: e.matmul(f3(A[2]), C.ones128f[:], f3(C.gXb), start=True, stop=True), reads=["ones128f", "gXb"], writes=["bank2"])
    S.op("act", lambda e: e.activation(f3(C.gXb), f3(A[2]), AF.Ln, bias=C.epst[:, 0:1]), reads=["bank2", "epst"], writes=["gXb"])
    S.op("act", lambda e: e.activation(f3(C.gXb), f3(C.gXb), AF.Exp, scale=-0.5), reads=["gXb"], writes=["gXb"])
    S.op("dve", lambda e: e.scalar_tensor_tensor(f3(C.gXa), f3(C.gXa), C.gnw[:, 0:1], f3(C.gXb), ALU.mult, ALU.mult), reads=["gXa", "gnw", "gXb"], writes=["gXa"])
    S.op("dve", lambda e: e.tensor_tensor(C.mixT[:, 2:6, :], C.gXa[:], C.gzs[:], ALU.mult), reads=["gXa", "gzs"], writes=["mixT"])


def alloc_mixer(nc, st, S, pfx=""):
    C = Ctx()
    def sb(name, shape, dt=F32):
        t = st.enter_context(nc.sbuf_tensor(pfx + name, shape, dt)); setattr(C, name, t); return t
    def ps(name, shape, dt=F32):
        return st.enter_context(nc.psum_tensor(pfx + name, shape, dt))
    for k, (shp, dt) in CSHAPES.items():
        sb(k, shp, dt)
    C.idb = C.identb
    sb("epst", [128, 1]); sb("onet", [128, 1]); sb("ones256b", [128, 128], BF16); sb("onesf", [128, 128])
    sb("wH", [128, 8, 1024], BF16); sb("wG", [128, 8, 2048], BF16); sb("wab", [128, 8, 8], BF16); sb("wu", [128, 8, 256], BF16)
    sb("wout", [128, 6, 1024], BF16); sb("woutH", [64, 4, 1024], BF16)
    sb("nwcol", [128, 8])
    C.hs = [sb("h0", [128, 1024])]
    sb("ss", [128, 1]); sb("lnv", [128, 1]); sb("rstd", [128, 1])
    sb("hn", [128, 1024], BF16); sb("hnT", [128, 8, 128], BF16)
    C.junk = C.hn
    sb("mixT", [128, 8, 128], BF16); sb("mixH", [64, 4, 128], BF16)
    for n in ("tabC", "tabS", "t2", "wre", "wim"):
        sb(n, [128, 8, 128])
    sb("s5tmp", [128, 1024])
    C.kint = C.s5tmp[:].bitcast(mybir.dt.int32)
    for n in ("BTre", "BTim", "CTre", "CTimn", "xre", "xim"):
        sb(n, [128, 8, 128], BF16)
    for n in ("lre_p", "lim_p", "ls_p", "dtp", "lrep", "magp", "thp", "a128", "t128", "se", "ce", "c1", "c2", "c3"):
        sb(n, [128, 8])
    sb("zst", [128, 2, 8]); sb("s5d", [128, 2]); sb("s5nw", [128, 2]); sb("glus", [128, 2, 256], BF16)
    sb("uF", [128, 2, 128]); sb("uB", [128, 2, 128], BF16)
    for n in ("yv", "g1", "zg", "o5"):
        sb(n, [128, 2, 128])
    sb("zgb", [128, 2, 128], BF16); sb("rs5", [128, 128])
    for n in ("he1", "hden", "hk", "hc", "hec", "hqt", "hkt", "hkh", "hv", "hof"):
        sb(n, [64, 4, 128])
    C.hgs = C.hc
    sb("hscm", [32, 4, 128]); sb("hkvT", [32, 4, 128]); sb("Sst", [64, 4, 64]); sb("ones64", [64, 64])
    sb("oml", [64, 4]); sb("lbl0", [64, 4]); sb("lbl1", [64, 4]); sb("hnw", [64, 1])
    sb("gx", [128, 12, 131]); sb("gcv", [128, 12, 128]); sb("gtmp", [128, 12, 128])
    for n in ("gzs", "gD", "gDst", "gAm", "gdiag", "gXTb", "gR", "gecb", "gS", "gAT"):
        sb(n, [128, 4, 128])
    sb("gXblk", [128, 12, 128])
    C.gXa = C.gXblk[:, 0:4, :]; C.gXb = C.gXblk[:, 4:8, :]; C.gXTa = C.gXblk[:, 8:12, :]
    C.gcv2 = C.gXblk
    C.gwT = C.gD; C.gvn = C.gDst; C.gu = C.gAm
    C.gkbe = C.gXb; C.gkdec = C.gXTa; C.gvb = C.gXTb
    V = lambda a: type("V", (), {"__getitem__": lambda self, k, a=a: a[k]})()
    C.stg0 = V(C.gtmp[:, 0:8, :].rearrange("p a b -> p (a b)"))
    C.stg1 = V(C.gcv[:, 0:8, :].rearrange("p a b -> p (a b)"))
    C.ho = V(C.gtmp[:, 0:8, :].rearrange("p a b -> p (a b)"))
    for n in ("ga", "gg", "gbeta", "gc", "gec", "gecl", "gedl", "gbec", "gnb", "aneg", "dtb"):
        sb(n, [128, 4])
    sb("convw", [128, 12, 4]); sb("gnw", [128, 1]); sb("ones128f", [128, 128])
    wGf = C.wG[:].rearrange("p a b -> p (a b)").bitcast(F32)
    scr = [wGf[:, i * 1024:(i + 1) * 1024] for i in range(8)]
    scr += [C.t2[:].rearrange("p a b -> p (a b)"), C.wre[:].rearrange("p a b -> p (a b)")]
    C.scr = [type("V", (), {"__getitem__": lambda self, k, a=a: a[k]})() for a in scr]
    C.tp = ps("tp", [128, 8, 128], BF16)
    C.bank = [ps("bank%d" % i, [128, 4, 128]) for i in range(7)]
    return C


def declare_params(nc, suffix=""):
    P = {}
    for k, shp in PSHAPES.items():
        P[k] = nc.dram_tensor("p_" + k + suffix, shp, F32, kind="ExternalInput").ap()
    return P


def mixer_setup(S, C, P, Cst, layer):
    for k in CSHAPES:
        S.dma("ld_c_" + k, getattr(C, k)[:], Cst[k], writes=[k])
    S.op("dve", lambda e: e.memset(C.epst[:], EPS), writes=["epst"])
    S.op("dve", lambda e: e.memset(C.onesf[:], 1.0), writes=["onesf"])
    S.op("dve", lambda e: e.memset(C.onet[:], 1.0), writes=["onet"])
    S.op("dve", lambda e: e.memset(C.ones256b[:], 1.0 / 256.0), writes=["ones256b"])
    S.op("dve", lambda e: e.memset(C.mixT[:], 0.0), writes=["mixT"])
    S.op("dve", lambda e: e.memset(C.mixH[:], 0.0), writes=["mixH"])
    s5_setup(S, C, P)
    S.barrier()
    S.dma("ld_nwcol", C.nwcol[:], P["nmixc"], writes=["nwcol"])
    stage = [C.stg0, C.stg1]
    K8 = list(range(8))
    load_weight_bf16(S, "wH", P["w_in"], K8, 1024, C.wH, 0, stage, 1, 0, C.nwcol)
    load_weight_bf16(S, "wG", P["w_in"], K8, 1024, C.wG[:, :, 0:1024], 0, stage, 1, 1024, C.nwcol)
    load_weight_bf16(S, "wG", P["w_in"], K8, 1024, C.wG[:, :, 1024:2048], 0, stage, 1, 2048, C.nwcol)
    load_weight_bf16(S, "wab", P["w_in"], K8, 8, C.wab, 0, stage, 8, 3072, C.nwcol)
    load_weight_bf16(S, "wu", P["w_in"], K8, 256, C.wu, 0, stage, 4, 3080, C.nwcol)
    load_weight_bf16(S, "wout", P["w_out"], list(range(2, 8)), 1024, C.wout, 2, stage, 1, 0)
    whv = P["w_out"][0:256, :].rearrange("(h p) n -> p h n", p=64)
    for i in range(4):
        sg = stage[i % 2][0:64, :]
        S.dma("wst%d" % (i % 2), sg, whv[:, i, :], writes=[("wstage", i % 2)])
        S.op("dve", lambda h, sg=sg, i=i: h.tensor_copy(C.woutH[:, i, :], sg), reads=[("wstage", i % 2)], writes=["woutH"])
    hgrn_setup(S, C, P, layer)
    if hasattr(C, "gdn_ready"):
        gdn_setup(S, C, P)
    S.barrier()


def out_proj_store(S, C, h, hkey, b, ydst, ykey):
    oP = [C.bank[4], C.bank[5]]
    for half in range(2):
        ob = oP[half][:].rearrange("p a b -> p (a b)")
        n = 0
        for hd in range(4):
            S.op("pe", lambda e, hd=hd, half=half, ob=ob: e.matmul(ob, C.mixH[:, hd, :], C.woutH[:, hd, half * 512:(half + 1) * 512],
                                                                  start=(hd == 0), stop=False),
                 reads=["mixH", "woutH"], writes=["bank%d" % (4 + half)])
        for ft in range(2, 8):
            S.op("pe", lambda e, ft=ft, half=half, ob=ob: e.matmul(ob, C.mixT[:, ft, :], C.wout[:, ft - 2, half * 512:(half + 1) * 512],
                                                                  start=False, stop=(ft == 7)),
                 reads=["mixT", "wout"], writes=["bank%d" % (4 + half)])
        S.op("dve", lambda e, half=half, ob=ob: e.tensor_tensor(C.ho[:, half * 512:(half + 1) * 512], ob, h[:, half * 512:(half + 1) * 512], ALU.add),
             reads=["bank%d" % (4 + half), hkey], writes=["gtmp"])
    S.dma("st0", ydst, C.ho[:], reads=["gtmp"], writes=[ykey])


def build_mixer(NT, parts=("s5",), debug=True, layer=0):
    nc = bass.Bass("TRN2", target_bir_lowering=False)
    x = nc.dram_tensor("x", [NT * 128, 1024], F32, kind="ExternalInput").ap()
    y = nc.dram_tensor("y", [NT * 128, 1024], F32, kind="ExternalOutput").ap()
    P = declare_params(nc)
    Cst = {k: nc.dram_tensor("c_" + k, shp, dt, kind="ExternalInput").ap() for k, (shp, dt) in CSHAPES.items()}
    dbg = nc.dram_tensor("dbg", [NT, 128, 1024], BF16, kind="ExternalOutput").ap() if debug else None
    dbgH = nc.dram_tensor("dbgH", [NT, 64, 512], BF16, kind="ExternalOutput").ap() if debug else None
    with ExitStack() as st:
        S = Sched(nc, st)
        C = alloc_mixer(nc, st, S)
        if "gdn" in parts:
            C.gdn_ready = True
        mixer_setup(S, C, P, Cst, layer)
        ykeys = []
        for c in range(NT):
            b = 0
            h = C.hs[b]
            S.dma("hl%d" % b, h[:], x[c * 128:(c + 1) * 128, :], writes=[("h", b)])
            rmsnorm_T(S, C, h, ("h", b), None, None)
            if "hgrn" in parts:
                hgrn_chunk(S, C, c == 0)
            if "gdn" in parts:
                gdn_chunk(S, C, c == 0)
            if "s5" in parts:
                s5_chunk(S, C, c == 0)
            if debug:
                S.dma("dbg", dbg[c], C.mixT[:].rearrange("p a b -> p (a b)"), reads=["mixT"], writes=[("dbg", c)])
                S.dma("dbgH", dbgH[c], C.mixH[:].rearrange("p a b -> p (a b)"), reads=["mixH"], writes=[("dbgH", c)])
                ykeys += [("dbg", c), ("dbgH", c)]
            out_proj_store(S, C, h, ("h", b), b, y[c * 128:(c + 1) * 128, :], ("y", c))
            ykeys.append(("y", c))
        S.final_wait("sp", ykeys)
        block = st.enter_context(nc.Block())
        S.run(block)
    return nc


def make_inmap(inp, l, xin):
    m = {"x": xin}
    for k, v in host_layer_params(inp, l).items():
        m["p_" + k] = v.reshape(PSHAPES[k])
    for k, v in host_consts().items():
        m["c_" + k] = v
    return m


def build_mlp(NT, final):
    nc = bass.Bass("TRN2", target_bir_lowering=False)
    x = nc.dram_tensor("x", [NT * 128, 1024], F32, kind="ExternalInput").ap()
    nw = nc.dram_tensor("p_nmlp", [128, 1024], F32, kind="ExternalInput").ap()
    fwd = nc.dram_tensor("p_finalw", [128, 1024], F32, kind="ExternalInput").ap()
    w1 = nc.dram_tensor("p_w1", [1024, 4096], F32, kind="ExternalInput").ap()
    w2 = nc.dram_tensor("p_w2", [4096, 1024], F32, kind="ExternalInput").ap()
    identb = nc.dram_tensor("c_identb", [128, 128], BF16, kind="ExternalInput").ap()
    y = nc.dram_tensor("y", [NT * 128, 1024], F32, kind="ExternalOutput").ap()
    with ExitStack() as st:
        C = Ctx()
        def sb(name, shape, dt=F32):
            t = st.enter_context(nc.sbuf_tensor(name, shape, dt)); setattr(C, name, t); return t
        def ps(name, shape, dt=F32):
            return st.enter_context(nc.psum_tensor(name, shape, dt))
        S = Sched(nc, st)
        sb("w1s", [128, 8, 4096], BF16); sb("w2s", [128, 32, 1024], BF16)
        stage = [sb("stg0", [128, 1024]), sb("stg1", [128, 1024])]
        sb("nws", [128, 1024]); sb("fws", [128, 1024]); sb("idb", [128, 128], BF16)
        C.hs = [sb("h0", [128, 1024]), sb("h1", [128, 1024])]
        sb("junk", [128, 1024]); sb("ss", [128, 1]); sb("lnv", [128, 1]); sb("rstd", [128, 1]); sb("epst", [128, 1])
        sb("hn", [128, 1024], BF16); sb("hnT", [128, 8, 128], BF16)
        rr = [sb("rr0", [128, 512]), sb("rr1", [128, 512])]
        sb("aT", [128, 32, 128], BF16)
        ho = [sb("ho0", [128, 1024]), sb("ho1", [128, 1024])]
        C.tp = ps("tp", [128, 8, 128], BF16)
        hid = [ps("hid%d" % i, [128, 4, 128]) for i in range(2)]
        ops_ = [ps("ops%d" % i, [128, 512]) for i in range(2)]
        S.op("dve", lambda e: e.memset(C.epst[:], EPS), writes=["epst"])
        S.dma("c0", C.nws[:], nw, writes=["nws"])
        S.dma("c2", C.fws[:], fwd, writes=["fws"])
        S.dma("c1", C.idb[:], identb, writes=["idb"])
        for cq in range(4):
            load_weight_bf16(S, "w1s", w1, list(range(8)), 1024, C.w1s[:, :, cq * 1024:(cq + 1) * 1024], 0, stage, 1, cq * 1024)
        load_weight_bf16(S, "w2s", w2, list(range(32)), 1024, C.w2s, 0, stage, 1, 0)
        ykeys = []
        for t in range(NT):
            b = t % 2
            h = C.hs[b]
            S.dma("hl%d" % b, h[:], x[t * 128:(t + 1) * 128, :], writes=[("h", b)])
            rmsnorm_T(S, C, h, ("h", b), C.nws, "nws")
            for g in range(8):
                hb = g % 2
                for mi in range(4):
                    m = g * 4 + mi
                    for kt in range(8):
                        S.op("pe", lambda e, m=m, mi=mi, kt=kt, hb=hb: e.matmul(
                            hid[hb][:, mi, :], C.w1s[:, kt, m * 128:(m + 1) * 128], C.hnT[:, kt, :],
                            start=(kt == 0), stop=(kt == 7)), reads=["w1s", "hnT"], writes=[("hid", hb)])
                S.op("act", lambda e, hb=hb: e.activation(rr[hb][:], hid[hb][:].rearrange("p a b -> p (a b)"), AF.Relu),
                     reads=[("hid", hb)], writes=[("rr", hb)])
                S.op("dve", lambda e, hb=hb, g=g: e.tensor_tensor(
                    C.aT[:, g * 4:(g + 1) * 4, :].rearrange("p a b -> p (a b)"), rr[hb][:], rr[hb][:], ALU.mult),
                    reads=[("rr", hb)], writes=["aT"])
            for half in range(2):
                for kt in range(32):
                    S.op("pe", lambda e, half=half, kt=kt: e.matmul(
                        ops_[half][:], C.aT[:, kt, :], C.w2s[:, kt, half * 512:(half + 1) * 512],
                        start=(kt == 0), stop=(kt == 31)), reads=["aT", "w2s"], writes=[("ops", half)])
                S.op("dve", lambda e, half=half, b=b, h=h: e.tensor_tensor(
                    ho[b][:, half * 512:(half + 1) * 512], ops_[half][:], h[:, half * 512:(half + 1) * 512], ALU.add),
                    reads=[("ops", half), ("h", b)], writes=[("ho", b)])
            if final:
                hb_ = ho[b]
                S.op("act", lambda e, hb_=hb_: e.activation(C.junk[:], hb_[:], AF.Square, scale=1.0 / 32.0, accum_out=C.ss[:]),
                     reads=[("ho", b)], writes=["junk", "ss"])
                S.op("act", lambda e: e.activation(C.lnv[:], C.ss[:], AF.Ln, bias=C.epst[:, 0:1]), reads=["ss", "epst"], writes=["lnv"])
                S.op("act", lambda e: e.activation(C.rstd[:], C.lnv[:], AF.Exp, scale=-0.5), reads=["lnv"], writes=["rstd"])
                S.op("dve", lambda e, hb_=hb_: e.scalar_tensor_tensor(hb_[:], hb_[:], C.rstd[:, 0:1], C.fws[:], ALU.mult, ALU.mult),
                     reads=[("ho", b), "rstd", "fws"], writes=[("ho", b)])
            S.dma("st%d" % b, y[t * 128:(t + 1) * 128, :], ho[b][:], reads=[("ho", b)], writes=[("y", t)])
            ykeys.append(("y", t))
        S.final_wait("sp", ykeys)
        block = st.enter_context(nc.Block())
        S.run(block)
    return nc


NT_FULL = 129
N_CORES = 8
MIXER_PARTS = ("hgrn", "gdn", "s5")


def kernel_unfused(**inputs):
    inp = {k: np.asarray(v) for k, v in inputs.items()}
    B = inp["x"].shape[0]
    NT = NT_FULL
    consts = host_consts()
    hcur = []
    for c in range(N_CORES):
        if c < B:
            hcur.append(np.concatenate([np.zeros((112, 1024), np.float32), inp["meta"].astype(np.float32), inp["x"][c]], 0))
        else:
            hcur.append(np.zeros((NT * 128, 1024), np.float32))
    nc_mix = {l: build_mixer(NT, MIXER_PARTS, debug=False, layer=l) for l in range(2)}
    nc_mlp = {False: build_mlp(NT, False), True: build_mlp(NT, True)}
    depth = inp["w_in"].shape[0]
    finalw = np.ascontiguousarray(np.broadcast_to(inp["final_norm_w"].reshape(1, 1024), (128, 1024)))
    for l in range(depth):
        lp = host_layer_params(inp, l)
        base = {"p_" + k: v.reshape(PSHAPES[k]) for k, v in lp.items()}
        for k, v in consts.items():
            base["c_" + k] = v
        ims = [dict(base, x=hcur[c]) for c in range(N_CORES)]
        res = run_bass_kernel_spmd(nc_mix[min(l, 1)], ims, core_ids=list(range(N_CORES)))
        hcur = [res.results[c]["y"] for c in range(N_CORES)]
        last = (l == depth - 1)
        mb = {"p_nmlp": lp["nmlp"], "p_finalw": finalw, "p_w1": lp["w1"], "p_w2": lp["w2"], "c_identb": consts["identb"]}
        ims = [dict(mb, x=hcur[c]) for c in range(N_CORES)]
        res = run_bass_kernel_spmd(nc_mlp[last], ims, core_ids=list(range(N_CORES)))
        hcur = [res.results[c]["y"] for c in range(N_CORES)]
    out = np.stack([hcur[c][128:] for c in range(B)], 0).astype(np.float32)
    return out


def emit_mixer_pass(nc, S, x, y, P, Cst, NT, layer, pfx):
    with ExitStack() as st:
        C = alloc_mixer(nc, st, S, pfx)
        C.gdn_ready = True
        S.barrier()
        mixer_setup(S, C, P, Cst, layer)
        ykeys = []
        for c in range(NT):
            h = C.hs[0]
            S.dma("hl0", h[:], x[c * 128:(c + 1) * 128, :], writes=[("h", 0)])
            rmsnorm_T(S, C, h, ("h", 0), None, None)
            hgrn_chunk(S, C, c == 0)
            gdn_chunk(S, C, c == 0)
            s5_chunk(S, C, c == 0)
            out_proj_store(S, C, h, ("h", 0), 0, y[c * 128:(c + 1) * 128, :], (pfx + "y", c))
            ykeys.append((pfx + "y", c))
        S.final_wait("sp", ykeys)
        S.barrier()
        block = st.enter_context(nc.Block())
        S.run(block)


def emit_mlp_pass(nc, S, x, y, P, finalw, identb, NT, final, pfx, skip_tiles=0):
    nw, w1, w2 = P["nmlp"], P["w1"], P["w2"]
    with ExitStack() as st:
        C = Ctx()
        def sb(name, shape, dt=F32):
            t = st.enter_context(nc.sbuf_tensor(pfx + name, shape, dt)); setattr(C, name, t); return t
        def ps(name, shape, dt=F32):
            return st.enter_context(nc.psum_tensor(pfx + name, shape, dt))
        S.barrier()
        sb("w1s", [128, 8, 4096], BF16); sb("w2s", [128, 32, 1024], BF16)
        stage = [sb("stg0", [128, 1024]), sb("stg1", [128, 1024])]
        sb("nws", [128, 1024]); sb("fws", [128, 1024]); sb("idb", [128, 128], BF16)
        C.hs = [sb("h0", [128, 1024]), sb("h1", [128, 1024])]
        sb("junk", [128, 1024]); sb("ss", [128, 1]); sb("lnv", [128, 1]); sb("rstd", [128, 1]); sb("epst", [128, 1])
        sb("hn", [128, 1024], BF16); sb("hnT", [128, 8, 128], BF16)
        rr = [sb("rr0", [128, 512]), sb("rr1", [128, 512])]
        sb("aT", [128, 32, 128], BF16)
        ho = [sb("ho0", [128, 1024]), sb("ho1", [128, 1024])]
        C.tp = ps("tp", [128, 8, 128], BF16)
        hid = [ps("hid%d" % i, [128, 4, 128]) for i in range(2)]
        ops_ = [ps("ops%d" % i, [128, 512]) for i in range(2)]
        S.op("dve", lambda e: e.memset(C.epst[:], EPS), writes=["epst"])
        S.dma("c0", C.nws[:], nw, writes=["nws"])
        S.dma("c2", C.fws[:], finalw, writes=["fws"])
        S.dma("c1", C.idb[:], identb, writes=["idb"])
        for cq in range(4):
            load_weight_bf16(S, "w1s", w1, list(range(8)), 1024, C.w1s[:, :, cq * 1024:(cq + 1) * 1024], 0, stage, 1, cq * 1024)
        load_weight_bf16(S, "w2s", w2, list(range(32)), 1024, C.w2s, 0, stage, 1, 0)
        ykeys = []
        for t in range(skip_tiles, NT):
            b = t % 2
            h = C.hs[b]
            S.dma("hl%d" % b, h[:], x[t * 128:(t + 1) * 128, :], writes=[("h", b)])
            rmsnorm_T(S, C, h, ("h", b), C.nws, "nws")
            for g in range(8):
                hb = g % 2
                for mi in range(4):
                    m = g * 4 + mi
                    for kt in range(8):
                        S.op("pe", lambda e, m=m, mi=mi, kt=kt, hb=hb: e.matmul(
                            hid[hb][:, mi, :], C.w1s[:, kt, m * 128:(m + 1) * 128], C.hnT[:, kt, :],
                            start=(kt == 0), stop=(kt == 7)), reads=["w1s", "hnT"], writes=[("hid", hb)])
                S.op("act", lambda e, hb=hb: e.activation(rr[hb][:], hid[hb][:].rearrange("p a b -> p (a b)"), AF.Relu),
                     reads=[("hid", hb)], writes=[("rr", hb)])
                S.op("dve", lambda e, hb=hb, g=g: e.tensor_tensor(
                    C.aT[:, g * 4:(g + 1) * 4, :].rearrange("p a b -> p (a b)"), rr[hb][:], rr[hb][:], ALU.mult),
                    reads=[("rr", hb)], writes=["aT"])
            for half in range(2):
                for kt in range(32):
                    S.op("pe", lambda e, half=half, kt=kt: e.matmul(
                        ops_[half][:], C.aT[:, kt, :], C.w2s[:, kt, half * 512:(half + 1) * 512],
                        start=(kt == 0), stop=(kt == 31)), reads=["aT", "w2s"], writes=[("ops", half)])
                S.op("dve", lambda e, half=half, b=b, h=h: e.tensor_tensor(
                    ho[b][:, half * 512:(half + 1) * 512], ops_[half][:], h[:, half * 512:(half + 1) * 512], ALU.add),
                    reads=[("ops", half), ("h", b)], writes=[("ho", b)])
            if final:
                hb_ = ho[b]
                S.op("act", lambda e, hb_=hb_: e.activation(C.junk[:], hb_[:], AF.Square, scale=1.0 / 32.0, accum_out=C.ss[:]),
                     reads=[("ho", b)], writes=["junk", "ss"])
                S.op("act", lambda e: e.activation(C.lnv[:], C.ss[:], AF.Ln, bias=C.epst[:, 0:1]), reads=["ss", "epst"], writes=["lnv"])
                S.op("act", lambda e: e.activation(C.rstd[:], C.lnv[:], AF.Exp, scale=-0.5), reads=["lnv"], writes=["rstd"])
                S.op("dve", lambda e, hb_=hb_: e.scalar_tensor_tensor(hb_[:], hb_[:], C.rstd[:, 0:1], C.fws[:], ALU.mult, ALU.mult),
                     reads=[("ho", b), "rstd", "fws"], writes=[("ho", b)])
            t2 = t - skip_tiles
            S.dma("st%d" % b, y[t2 * 128:(t2 + 1) * 128, :], ho[b][:], reads=[("ho", b)], writes=[(pfx + "y", t)])
            ykeys.append((pfx + "y", t))
        S.final_wait("sp", ykeys)
        S.barrier()
        block = st.enter_context(nc.Block())
        S.run(block)


def build_fused(NT, depth=2):
    nc = bass.Bass("TRN2", target_bir_lowering=False)
    x = nc.dram_tensor("x", [NT * 128, 1024], F32, kind="ExternalInput").ap()
    y = nc.dram_tensor("y", [(NT - 1) * 128, 1024], F32, kind="ExternalOutput").ap()
    hA = nc.dram_tensor("scr_hA", [NT * 128, 1024], F32, kind="Internal").ap()
    hB = nc.dram_tensor("scr_hB", [NT * 128, 1024], F32, kind="Internal").ap()
    finalw = nc.dram_tensor("p_finalw", [128, 1024], F32, kind="ExternalInput").ap()
    Ps = [declare_params(nc, "_l%d" % l) for l in range(depth)]
    Cst = {k: nc.dram_tensor("c_" + k, shp, dt, kind="ExternalInput").ap() for k, (shp, dt) in CSHAPES.items()}
    with ExitStack() as st:
        S = Sched(nc, st)
        src = x
        for l in range(depth):
            emit_mixer_pass(nc, S, src, hA, Ps[l], Cst, NT, min(l, 1), "m%d_" % l)
            last = (l == depth - 1)
            emit_mlp_pass(nc, S, hA, y if last else hB, Ps[l], finalw, Cst["identb"], NT, last, "f%d_" % l,
                          skip_tiles=1 if last else 0)
            src = hB
    return nc


def kernel_fused(**inputs):
    inp = {k: np.asarray(v) for k, v in inputs.items()}
    B = inp["x"].shape[0]
    NT = NT_FULL
    depth = inp["w_in"].shape[0]
    base = {}
    for l in range(depth):
        for k, v in host_layer_params(inp, l).items():
            base["p_" + k + "_l%d" % l] = v.reshape(PSHAPES[k])
    for k, v in host_consts().items():
        base["c_" + k] = v
    base["p_finalw"] = np.ascontiguousarray(np.broadcast_to(inp["final_norm_w"].reshape(1, 1024).astype(np.float32), (128, 1024)))
    ims = []
    for c in range(N_CORES):
        if c < B:
            xin = np.concatenate([np.zeros((112, 1024), np.float32), inp["meta"].astype(np.float32), inp["x"][c, :(NT - 1) * 128]], 0)
        else:
            xin = np.zeros((NT * 128, 1024), np.float32)
        ims.append(dict(base, x=xin))
    nc = build_fused(NT, depth)
    res = run_bass_kernel_spmd(nc, ims, core_ids=list(range(N_CORES)))
    return np.stack([res.results[c]["y"] for c in range(B)], 0).astype(np.float32)


kernel = kernel_fused
```

```python
from contextlib import ExitStack
import numpy as np
import concourse.bass as bass
import concourse.mybir as mybir

F32 = mybir.dt.float32
BF16 = mybir.dt.bfloat16
ALU = mybir.AluOpType
AF = mybir.ActivationFunctionType
AX = mybir.AxisListType

EPOCH = 24000
NEPOCH = 10


class Sched:
    ENGS = ("pe", "act", "dve", "pool", "sp")

    def __init__(self, nc, st):
        self.nc = nc
        self.st = st
        self.q = {e: [] for e in self.ENGS}
        self.cnt = {e: 0 for e in self.ENGS}
        self.sems = {}
        self.waited = {}
        self.lastw = {}
        self.readers = {}
        self.dmacnt = {}
        self.latest = {}
        self.h = {"pe": nc.tensor, "act": nc.scalar, "dve": nc.vector,
                  "pool": nc.gpsimd, "sp": nc.sync}

    def _sem(self, key):
        s = self.sems.get(key)
        if s is None:
            s = self.st.enter_context(self.nc.semaphore("s_%s_%s" % key))
            self.sems[key] = s
        return s

    def _wait(self, e, clock):
        if clock is None:
            return
        key, val = clock
        if key[0] == "dma":
            val = max(val, self.dmacnt[key[1]])
        if key[0] == "pe" and e == "pe":
            return
        if self.waited.get((e, key), 0) >= val:
            return
        self.waited[(e, key)] = val
        sem = self._sem(key)
        h = self.h[e]
        self.q[e].append(lambda: h.wait_ge(sem, val))

    def _deps(self, e, reads, writes):
        for r in reads:
            self._wait(e, self.lastw.get(r))
        for w in writes:
            self._wait(e, self.lastw.get(w))
            for k, v in self.readers.get(w, {}).items():
                self._wait(e, (k, v))

    def _record(self, clock, reads, writes):
        key, val = clock
        for r in reads:
            d = self.readers.setdefault(r, {})
            if d.get(key, 0) < val:
                d[key] = val
        for w in writes:
            self.lastw[w] = clock
            self.readers[w] = {}
        if self.latest.get(key, 0) < val:
            self.latest[key] = val

    def barrier(self):
        for e in self.ENGS:
            for key, val in list(self.latest.items()):
                self._wait(e, (key, val))

    @staticmethod
    def _excl(reads, writes):
        ex = [r for r in reads if (isinstance(r, str) and (r.startswith("bank") or r == "tp")) or
              (isinstance(r, tuple) and r[0] in ("hid", "ops"))]
        if ex:
            reads = [r for r in reads if r not in ex]
            writes = list(writes) + ex
        return reads, writes

    def op(self, e, fn, reads=(), writes=()):
        reads, writes = self._excl(reads, writes)
        self._deps(e, reads, writes)
        i = self.cnt[e]
        self.cnt[e] += 1
        key = (e, i // EPOCH)
        val = i % EPOCH + 1
        sem = self._sem(key)
        h = self.h[e]
        self.q[e].append(lambda: fn(h).then_inc(sem, 1))
        self._record((key, val), reads, writes)

    def dma(self, stream, out, in_, reads=(), writes=(), q="sp"):
        self._deps(q, reads, writes)
        key = ("dma", stream)
        n = self.dmacnt.get(stream, 0) + 16
        self.dmacnt[stream] = n
        sem = self._sem(key)
        h = self.h[q]
        self.q[q].append(lambda: h.dma_start(out=out, in_=in_).then_inc(sem, 16))
        self._record((key, n), reads, writes)

    def final_wait(self, e, resources):
        for r in resources:
            self._wait(e, self.lastw.get(r))

    def run(self, block):
        q = self.q
        self.q = {e: [] for e in self.ENGS}

        @block.tensor
        def _(eng):
            for f in q["pe"]:
                f()

        @block.scalar
        def _(eng):
            for f in q["act"]:
                f()

        @block.vector
        def _(eng):
            for f in q["dve"]:
                f()

        @block.gpsimd
        def _(eng):
            for f in q["pool"]:
                f()

        @block.sync
        def _(eng):
            for f in q["sp"]:
                f()


import sys, time, math
import ml_dtypes
from concourse.bass_utils import run_bass_kernel_spmd

EPS = 1e-6
TWO_PI = 2.0 * math.pi
GELU_C = 2.0 * math.sqrt(2.0 / math.pi)
NEG_BIG = 30000.0


def host_consts():
    c = {}
    c["identf"] = np.eye(128, dtype=np.float32)
    c["identb"] = np.eye(128).astype(ml_dtypes.bfloat16)
    s = np.arange(128)
    c["tri"] = (s[:, None] <= s[None, :]).astype(np.float32)
    c["gmask"] = (NEG_BIG * (s[None, :] > s[:, None])).astype(np.float32)
    c["strict"] = (s[None, :] < s[:, None]).astype(np.float32)
    c["iota"] = np.ascontiguousarray(np.broadcast_to(s.astype(np.float32), (128, 128)))
    m32 = (s[:32, None] <= s[None, :32]).astype(np.float32)
    c["mask32"] = np.ascontiguousarray(np.broadcast_to(m32[:, None, :], (32, 4, 32))).reshape(32, 128)
    rs = np.ones((64, 4, 128), np.float32)
    rs[:, :, 0::32] = 0.0
    c["hreset"] = rs.reshape(64, 512)
    c["gmask4"] = np.ascontiguousarray(np.tile(c["gmask"], (1, 4)))
    c["strict4"] = np.ascontiguousarray(np.tile(c["strict"], (1, 4)))
    del c["gmask"], c["strict"]
    return c


def host_layer_params(inp, l):
    p = {}
    bc = lambda v, n=128: np.ascontiguousarray(np.broadcast_to(np.asarray(v, np.float32).reshape(1, -1), (n, np.asarray(v).size)))
    p["nmix"] = bc(inp["norm_mix_w"][l])
    p["nmlp"] = bc(inp["norm_mlp_w"][l])
    p["nmixc"] = np.ascontiguousarray(np.asarray(inp["norm_mix_w"][l], np.float32).reshape(8, 128).T)
    p["w_in"] = np.ascontiguousarray(inp["w_in"][l])
    p["w_out"] = np.ascontiguousarray(inp["w_out"][l])
    p["w1"] = np.ascontiguousarray(inp["mlp_w1"][l])
    p["w2"] = np.ascontiguousarray(inp["mlp_w2"][l])
    p["lbl0"] = np.ascontiguousarray(inp["hgrn_lb_logits"][0].reshape(4, 64).T)
    p["lbl1"] = np.ascontiguousarray(inp["hgrn_lb_logits"][1].reshape(4, 64).T)
    p["hnw"] = np.ascontiguousarray(inp["hgrn_norm_w"][l].reshape(64, 1))
    p["convw"] = np.ascontiguousarray(inp["gdn_conv_w"][l].reshape(4, 12, 128).transpose(2, 1, 0))
    p["alog"] = bc(inp["gdn_a_log"][l])
    p["dtb"] = bc(inp["gdn_dt_bias"][l])
    p["gnw"] = np.ascontiguousarray(inp["gdn_norm_w"][l].reshape(128, 1))
    lre, lim, ls = inp["s5_lam_re"][l], inp["s5_lam_im"][l], inp["s5_log_step"][l]
    def pl(a):
        return np.ascontiguousarray(a.reshape(8, 128).T)
    p["lre_p"] = pl(lre); p["lim_p"] = pl(lim)
    p["ls_p"] = pl(np.broadcast_to(ls[:, None], (16, 64)))
    fl = lambda a: np.ascontiguousarray(np.broadcast_to(a.reshape(1, 1024), (128, 1024)))
    p["lre_f"] = fl(lre); p["lim_f"] = fl(lim)
    p["ls_f"] = fl(np.ascontiguousarray(np.broadcast_to(ls[:, None], (16, 64))))
    bre, bim = inp["s5_b_re"][l], inp["s5_b_im"][l]
    cre, cim = inp["s5_c_re"][l], inp["s5_c_im"][l]
    BTr = np.zeros((128, 8, 128), np.float32); BTi = np.zeros((128, 8, 128), np.float32)
    CTr = np.zeros((128, 8, 128), np.float32); CTi = np.zeros((128, 8, 128), np.float32)
    for g in range(16):
        j = g // 2; r0 = (g % 2) * 64; c0 = (16 * g) % 128
        BTr[c0:c0 + 16, j, r0:r0 + 64] = bre[g].T
        BTi[c0:c0 + 16, j, r0:r0 + 64] = bim[g].T
        CTr[r0:r0 + 64, j, c0:c0 + 16] = cre[g].T
        CTi[r0:r0 + 64, j, c0:c0 + 16] = cim[g].T
    p["BTr"] = BTr.reshape(128, 1024); p["BTi"] = BTi.reshape(128, 1024)
    p["CTr"] = CTr.reshape(128, 1024); p["CTi"] = CTi.reshape(128, 1024)
    p["s5d"] = np.ascontiguousarray(inp["s5_d"][l].reshape(2, 128).T)
    p["s5nw"] = np.ascontiguousarray(inp["s5_norm_w"][l].reshape(2, 128).T)
    p["glu"] = np.ascontiguousarray(inp["s5_glu_w"][l])
    return p


PSHAPES = {
    "nmix": [128, 1024], "nmixc": [128, 8], "nmlp": [128, 1024], "w_in": [1024, 3336], "w_out": [1024, 1024],
    "w1": [1024, 4096], "w2": [4096, 1024], "lbl0": [64, 4], "lbl1": [64, 4], "hnw": [64, 1],
    "convw": [128, 48], "alog": [128, 4], "dtb": [128, 4], "gnw": [128, 1],
    "lre_p": [128, 8], "lim_p": [128, 8], "ls_p": [128, 8],
    "lre_f": [128, 1024], "lim_f": [128, 1024], "ls_f": [128, 1024],
    "BTr": [128, 1024], "BTi": [128, 1024], "CTr": [128, 1024], "CTi": [128, 1024],
    "s5d": [128, 2], "s5nw": [128, 2], "glu": [256, 256],
}
CSHAPES = {"identf": ([128, 128], F32), "identb": ([128, 128], BF16), "tri": ([128, 128], F32),
           "iota": ([128, 128], F32),
           "mask32": ([32, 128], F32), "hreset": ([64, 512], F32),
           "gmask4": ([128, 512], F32), "strict4": ([128, 512], F32)}


class Ctx:
    pass


def load_weight_bf16(S, name, w_dram, kts, N, dst, dst_off, stage, kt_per, col0=0, scale=None, np_=128):
    wv = w_dram.rearrange("(kt p) n -> p kt n", p=128)
    assert kt_per * N <= 1024
    cnt = getattr(S, "_wl", 0)
    for i0 in range(0, len(kts), kt_per):
        grp = kts[i0:i0 + kt_per]
        sl = cnt % 2
        cnt += 1
        sg = stage[sl][:, 0:len(grp) * N].rearrange("p (k n) -> p k n", k=len(grp))
        S.dma("wst%d" % sl, sg, wv[:, grp[0]:grp[0] + len(grp), col0:col0 + N], writes=[("wstage", sl)])
        for i, kt in enumerate(grp):
            eng = "dve" if (cnt + i) % 2 == 0 else "pool"
            if scale is None:
                S.op(eng, lambda h, sg=sg, i=i, kt=kt: h.tensor_copy(dst[:, kt - dst_off, 0:N], sg[:, i, :]),
                     reads=[("wstage", sl)], writes=[name])
            else:
                S.op(eng, lambda h, sg=sg, i=i, kt=kt: h.tensor_scalar(dst[:, kt - dst_off, 0:N], sg[:, i, :], scale[:, kt:kt + 1], None, ALU.mult),
                     reads=[("wstage", sl), "nwcol"], writes=[name])
    S._wl = cnt


def rmsnorm_T(S, C, h, hkey, nws, nwkey, sfx="", hn=None, hnT=None):
    hn = C.hn if hn is None else hn
    hnT = C.hnT if hnT is None else hnT
    khn, khnT = "hn" + sfx, "hnT" + sfx
    S.op("act", lambda e: e.activation(C.junk[:], h[:], AF.Square, scale=1.0 / 32.0, accum_out=C.ss[:]),
         reads=[hkey], writes=["junk", "hn", "ss"])
    S.op("act", lambda e: e.activation(C.lnv[:], C.ss[:], AF.Ln, bias=C.epst[:, 0:1]), reads=["ss", "epst"], writes=["lnv"])
    S.op("act", lambda e: e.activation(C.rstd[:], C.lnv[:], AF.Exp, scale=-0.5), reads=["lnv"], writes=["rstd"])
    if nws is None:
        S.op("dve", lambda e: e.tensor_scalar(hn[:], h[:], C.rstd[:, 0:1], None, ALU.mult), reads=[hkey, "rstd"], writes=[khn])
    else:
        S.op("dve", lambda e: e.scalar_tensor_tensor(hn[:], h[:], C.rstd[:, 0:1], nws[:], ALU.mult, ALU.mult),
             reads=[hkey, "rstd", nwkey], writes=[khn])
    for kt in range(8):
        S.op("pe", lambda e, kt=kt: e.transpose(C.tp[:, kt, :], hn[:, kt * 128:(kt + 1) * 128], C.idb[:]),
             reads=[khn, "idb"], writes=["tp"])
    S.op("act", lambda e: e.copy(hnT[:], C.tp[:]), reads=["tp"], writes=[khnT])


def sincos_tables(S, C, ang, angkey, sin_out, cos_out, okeys, kint, tmps, tkeys):
    kf, s2, s4 = tmps
    kk, k2, k4 = tkeys
    S.op("dve", lambda e: e.tensor_scalar(kint, ang, 1.0 / TWO_PI, None, ALU.mult), reads=[angkey], writes=["kint"])
    S.op("dve", lambda e: e.tensor_copy(kf, kint), reads=["kint"], writes=[kk])
    S.op("dve", lambda e: e.scalar_tensor_tensor(kf, kf, -TWO_PI, ang, ALU.mult, ALU.add), reads=[kk, angkey], writes=[kk])
    S.op("act", lambda e: e.activation(s4, kf, AF.Sin, scale=0.25), reads=[kk], writes=[k4])
    S.op("act", lambda e: e.activation(s2, kf, AF.Sin, scale=0.5), reads=[kk], writes=[k2])
    S.op("dve", lambda e: e.tensor_tensor(s4, s4, s4, ALU.mult), reads=[k4], writes=[k4])
    S.op("dve", lambda e: e.tensor_scalar(s4, s4, -2.0, 1.0, ALU.mult, ALU.add), reads=[k4], writes=[k4])
    S.op("dve", lambda e: e.scalar_tensor_tensor(sin_out, s2, 2.0, s4, ALU.mult, ALU.mult), reads=[k2, k4], writes=[okeys[0]])
    S.op("dve", lambda e: e.tensor_tensor(s2, s2, s2, ALU.mult), reads=[k2, okeys[0]], writes=[k2])
    S.op("dve", lambda e: e.tensor_scalar(cos_out, s2, -2.0, 1.0, ALU.mult, ALU.add), reads=[k2], writes=[okeys[1]])


def s5_setup(S, C, P):
    T = C.scr
    def ld(tile_ap, name, key):
        S.dma("ld_" + key, tile_ap, P[name], writes=[key])
    ld(C.lre_p[:], "lre_p", "lre_p"); ld(C.lim_p[:], "lim_p", "lim_p"); ld(C.ls_p[:], "ls_p", "ls_p")
    ld(C.s5d[:], "s5d", "s5d"); ld(C.s5nw[:], "s5nw", "s5nw")
    S.op("act", lambda e: e.activation(C.dtp[:], C.ls_p[:], AF.Exp), reads=["ls_p"], writes=["dtp"])
    S.op("dve", lambda e: e.tensor_scalar(C.lrep[:], C.lre_p[:], -1e-4, None, ALU.min), reads=["lre_p"], writes=["lrep"])
    S.op("dve", lambda e: e.tensor_tensor(C.magp[:], C.lrep[:], C.dtp[:], ALU.mult), reads=["lrep", "dtp"], writes=["magp"])
    S.op("act", lambda e: e.activation(C.magp[:], C.magp[:], AF.Exp), reads=["magp"], writes=["magp"])
    S.op("dve", lambda e: e.tensor_tensor(C.thp[:], C.lim_p[:], C.dtp[:], ALU.mult), reads=["lim_p", "dtp"], writes=["thp"])
    ang = T[0]
    for j in range(8):
        S.op("dve", lambda e, j=j: e.tensor_scalar(ang[:, j * 128:(j + 1) * 128], C.iota[:], C.thp[:, j:j + 1], None, ALU.mult),
             reads=["iota", "thp"], writes=["scr0"])
    sincos_tables(S, C, ang[:], "scr0", C.tabS[:].rearrange("p a b -> p (a b)"), C.tabC[:].rearrange("p a b -> p (a b)"),
                  ["tabS", "tabC"], C.kint[:], [T[1][:], T[2][:], T[3][:]], ["scr1", "scr2", "scr3"])
    S.op("dve", lambda e: e.tensor_scalar(C.a128[:], C.thp[:], 128.0, None, ALU.mult), reads=["thp"], writes=["a128"])
    sincos_tables(S, C, C.a128[:], "a128", C.se[:], C.ce[:], ["se", "ce"], C.kint[:, 0:8], [C.t128[:], C.c1[:], C.c2[:]], ["t128", "c1", "c2"])
    lre, lim, ls, t3, t4, t5, t6, t7 = T[2], T[3], T[4], T[5], T[6], T[7], T[8], T[9]
    ld(lre[:], "lre_f", "scr2"); ld(lim[:], "lim_f", "scr3"); ld(ls[:], "ls_f", "scr4")
    S.op("act", lambda e: e.activation(ls[:], ls[:], AF.Exp), reads=["scr4"], writes=["scr4"])
    S.op("dve", lambda e: e.tensor_scalar(lre[:], lre[:], -1e-4, None, ALU.min), reads=["scr2"], writes=["scr2"])
    S.op("dve", lambda e: e.tensor_tensor(t3[:], lre[:], ls[:], ALU.mult), reads=["scr2", "scr4"], writes=["scr5"])
    S.op("act", lambda e: e.activation(t3[:], t3[:], AF.Exp), reads=["scr5"], writes=["scr5"])
    S.op("dve", lambda e: e.tensor_tensor(t4[:], lim[:], ls[:], ALU.mult), reads=["scr3", "scr4"], writes=["scr6"])
    sincos_tables(S, C, t4[:], "scr6", t5[:], t6[:], ["scr7", "scr8"], C.kint[:], [t7[:], T[0][:], T[1][:]], ["scr9", "scr0", "scr1"])
    S.op("dve", lambda e: e.tensor_tensor(t5[:], t5[:], t3[:], ALU.mult), reads=["scr7", "scr5"], writes=["scr7"])
    S.op("dve", lambda e: e.tensor_tensor(t6[:], t6[:], t3[:], ALU.mult), reads=["scr8", "scr5"], writes=["scr8"])
    S.op("dve", lambda e: e.tensor_scalar(t6[:], t6[:], -1.0, None, ALU.add), reads=["scr8"], writes=["scr8"])
    S.op("dve", lambda e: e.tensor_tensor(t3[:], lre[:], lre[:], ALU.mult), reads=["scr2", "scr7", "scr8"], writes=["scr5"])
    S.op("dve", lambda e: e.tensor_tensor(t4[:], lim[:], lim[:], ALU.mult), reads=["scr3"], writes=["scr6"])
    S.op("dve", lambda e: e.tensor_tensor(t3[:], t3[:], t4[:], ALU.add), reads=["scr5", "scr6"], writes=["scr5"])
    S.op("dve", lambda e: e.reciprocal(t3[:], t3[:]), reads=["scr5"], writes=["scr5"])
    S.op("dve", lambda e: e.tensor_tensor(t4[:], t6[:], lre[:], ALU.mult), reads=["scr8", "scr2", "scr6"], writes=["scr6"])
    S.op("dve", lambda e: e.tensor_tensor(t7[:], t5[:], lim[:], ALU.mult), reads=["scr7", "scr3"], writes=["scr9"])
    S.op("dve", lambda e: e.tensor_tensor(t4[:], t4[:], t7[:], ALU.add), reads=["scr6", "scr9"], writes=["scr6"])
    S.op("dve", lambda e: e.tensor_tensor(t4[:], t4[:], t3[:], ALU.mult), reads=["scr6", "scr5"], writes=["scr6"])
    S.op("dve", lambda e: e.tensor_tensor(t7[:], t5[:], lre[:], ALU.mult), reads=["scr7", "scr2", "scr6"], writes=["scr9"])
    S.op("dve", lambda e: e.tensor_tensor(ls[:], t6[:], lim[:], ALU.mult), reads=["scr8", "scr3"], writes=["scr4"])
    S.op("dve", lambda e: e.tensor_tensor(t7[:], t7[:], ls[:], ALU.subtract), reads=["scr9", "scr4"], writes=["scr9"])
    S.op("dve", lambda e: e.tensor_tensor(t7[:], t7[:], t3[:], ALU.mult), reads=["scr9", "scr5"], writes=["scr9"])
    br, bi = lre, lim
    ld(br[:], "BTr", "scr2"); ld(bi[:], "BTi", "scr3")
    flat = lambda t: t[:].rearrange("p a b -> p (a b)")
    S.op("dve", lambda e: e.tensor_tensor(t5[:], t4[:], br[:], ALU.mult), reads=["scr6", "scr2", "scr7"], writes=["scr7"])
    S.op("dve", lambda e: e.tensor_tensor(t6[:], t7[:], bi[:], ALU.mult), reads=["scr9", "scr3", "scr8"], writes=["scr8"])
    S.op("dve", lambda e: e.tensor_tensor(flat(C.BTre), t5[:], t6[:], ALU.subtract), reads=["scr7", "scr8"], writes=["BTre"])
    S.op("dve", lambda e: e.tensor_tensor(t5[:], t4[:], bi[:], ALU.mult), reads=["scr6", "scr3", "BTre"], writes=["scr7"])
    S.op("dve", lambda e: e.tensor_tensor(t6[:], t7[:], br[:], ALU.mult), reads=["scr9", "scr2", "BTre"], writes=["scr8"])
    S.op("dve", lambda e: e.tensor_tensor(flat(C.BTim), t5[:], t6[:], ALU.add), reads=["scr7", "scr8"], writes=["BTim"])
    ld(t3[:], "CTr", "scr5"); ld(t4[:], "CTi", "scr6")
    S.op("dve", lambda e: e.tensor_copy(flat(C.CTre), t3[:]), reads=["scr5"], writes=["CTre"])
    S.op("dve", lambda e: e.tensor_scalar(flat(C.CTimn), t4[:], -1.0, None, ALU.mult), reads=["scr6"], writes=["CTimn"])
    gst = T[0][:, 0:512].rearrange("p (k n) -> p k n", k=2)
    S.dma("ld_glu", gst, P["glu"].rearrange("(k p) n -> p k n", p=128), reads=["tabS", "tabC"], writes=["scr0"])
    S.op("dve", lambda e: e.tensor_copy(C.glus[:], gst), reads=["scr0"], writes=["glus"])
    S.op("dve", lambda e: e.memset(C.zst[:], 0.0), writes=["zst"])


def s5_chunk(S, C, first):
    A = C.bank
    uP = A[0]
    for m in range(2):
        for kt in range(8):
            S.op("pe", lambda e, m=m, kt=kt: e.matmul(uP[:, m, :], C.wu[:, kt, m * 128:(m + 1) * 128], C.hnT[:, kt, :],
                                                     start=(kt == 0), stop=(kt == 7)),
                 reads=["wu", "hnT"], writes=["bank0"])
    S.op("act", lambda e: e.copy(C.uF[:], uP[:, 0:2, :]), reads=["bank0"], writes=["uF"])
    S.op("dve", lambda e: e.tensor_copy(C.uB[:], uP[:, 0:2, :]), reads=["bank0"], writes=["uB"])
    for j in range(8):
        S.op("pe", lambda e, j=j: e.matmul(A[2 + j // 4][:, j % 4, :], C.BTre[:, j, :], C.uB[:, j // 4, :], start=True, stop=True),
             reads=["BTre", "uB"], writes=["bank%d" % (2 + j // 4)])
        S.op("pe", lambda e, j=j: e.matmul(A[4 + j // 4][:, j % 4, :], C.BTim[:, j, :], C.uB[:, j // 4, :], start=True, stop=True),
             reads=["BTim", "uB"], writes=["bank%d" % (4 + j // 4)])
    f4 = lambda t, hf: t[:, hf * 4:(hf + 1) * 4, :].rearrange("p a b -> p (a b)")
    fb = lambda b: b[:].rearrange("p a b -> p (a b)")
    for hf in range(2):
        bre, bim = A[2 + hf], A[4 + hf]
        kre, kim = "bank%d" % (2 + hf), "bank%d" % (4 + hf)
        S.op("dve", lambda e, hf=hf, bre=bre: e.tensor_tensor(f4(C.wre, hf), fb(bre), f4(C.tabC, hf), ALU.mult), reads=[kre, "tabC"], writes=["wre"])
        S.op("dve", lambda e, hf=hf, bim=bim: e.tensor_tensor(f4(C.t2, hf), fb(bim), f4(C.tabS, hf), ALU.mult), reads=[kim, "tabS"], writes=["t2"])
        S.op("pool", lambda e, hf=hf: e.tensor_tensor(f4(C.wre, hf), f4(C.wre, hf), f4(C.t2, hf), ALU.add), reads=["wre", "t2"], writes=["wre"])
        S.op("dve", lambda e, hf=hf, bim=bim: e.tensor_tensor(f4(C.wim, hf), fb(bim), f4(C.tabC, hf), ALU.mult), reads=[kim, "tabC"], writes=["wim"])
        S.op("dve", lambda e, hf=hf, bre=bre: e.tensor_tensor(f4(C.t2, hf), fb(bre), f4(C.tabS, hf), ALU.mult), reads=[kre, "tabS", "wre"], writes=["t2"])
        S.op("pool", lambda e, hf=hf: e.tensor_tensor(f4(C.wim, hf), f4(C.wim, hf), f4(C.t2, hf), ALU.subtract), reads=["wim", "t2"], writes=["wim"])
    zl_re, zl_im = C.zst[:, 0, :], C.zst[:, 1, :]
    S.op("dve", lambda e: e.tensor_tensor(C.c1[:], C.ce[:], zl_re, ALU.mult), reads=["ce", "zst"], writes=["c1"])
    S.op("dve", lambda e: e.tensor_tensor(C.c2[:], C.se[:], zl_im, ALU.mult), reads=["se", "zst"], writes=["c2"])
    S.op("dve", lambda e: e.tensor_tensor(C.c1[:], C.c1[:], C.c2[:], ALU.subtract), reads=["c1", "c2"], writes=["c1"])
    S.op("dve", lambda e: e.tensor_tensor(C.c3[:], C.ce[:], zl_im, ALU.mult), reads=["ce", "zst"], writes=["c3"])
    S.op("dve", lambda e: e.tensor_tensor(C.c2[:], C.se[:], zl_re, ALU.mult), reads=["se", "zst", "c1"], writes=["c2"])
    S.op("dve", lambda e: e.tensor_tensor(C.c3[:], C.c3[:], C.c2[:], ALU.add), reads=["c3", "c2"], writes=["c3"])
    for j in range(8):
        dj = C.magp[:, j:j + 1].to_broadcast([128, 128])
        S.op("dve", lambda e, j=j, dj=dj: e.tensor_tensor_scan(C.wre[:, j, :], dj, C.wre[:, j, :], C.c1[:, j:j + 1], ALU.mult, ALU.add),
             reads=["magp", "wre", "c1"], writes=["wre"])
        S.op("dve", lambda e, j=j, dj=dj: e.tensor_tensor_scan(C.wim[:, j, :], dj, C.wim[:, j, :], C.c3[:, j:j + 1], ALU.mult, ALU.add),
             reads=["magp", "wim", "c3"], writes=["wim"])
    S.op("pool", lambda e: e.tensor_copy(C.zst[:, 0, :], C.wre[:, :, 127]), reads=["wre"], writes=["zst"])
    S.op("pool", lambda e: e.tensor_copy(C.zst[:, 1, :], C.wim[:, :, 127]), reads=["wim"], writes=["zst"])
    fl = lambda t: t[:].rearrange("p a b -> p (a b)")
    tB = C.s5tmp[:]
    S.op("dve", lambda e: e.tensor_tensor(fl(C.t2), fl(C.wre), fl(C.tabC), ALU.mult), reads=["wre", "tabC"], writes=["t2"])
    S.op("dve", lambda e: e.tensor_tensor(tB, fl(C.wim), fl(C.tabS), ALU.mult), reads=["wim", "tabS"], writes=["s5tmp"])
    S.op("dve", lambda e: e.tensor_tensor(fl(C.xre), fl(C.t2), tB, ALU.subtract), reads=["t2", "s5tmp"], writes=["xre"])
    S.op("dve", lambda e: e.tensor_tensor(tB, fl(C.wre), fl(C.tabS), ALU.mult), reads=["wre", "tabS", "xre"], writes=["s5tmp"])
    S.op("dve", lambda e: e.tensor_tensor(fl(C.t2), fl(C.wim), fl(C.tabC), ALU.mult), reads=["wim", "tabC", "xre"], writes=["t2"])
    S.op("dve", lambda e: e.tensor_tensor(fl(C.xim), tB, fl(C.t2), ALU.add), reads=["t2", "s5tmp"], writes=["xim"])
    yP = A[0][:, 2:4, :]
    for m in range(2):
        for i, j in enumerate(range(4 * m, 4 * m + 4)):
            S.op("pe", lambda e, m=m, j=j, i=i: e.matmul(yP[:, m, :], C.CTre[:, j, :], C.xre[:, j, :], start=(i == 0), stop=False),
                 reads=["CTre", "xre"], writes=["bank0"])
            S.op("pe", lambda e, m=m, j=j, i=i: e.matmul(yP[:, m, :], C.CTimn[:, j, :], C.xim[:, j, :], start=False, stop=(i == 3)),
                 reads=["CTimn", "xim"], writes=["bank0"])
    f2 = lambda t: t[:].rearrange("p a b -> p (a b)")
    for m in range(2):
        S.op("dve", lambda e, m=m: e.scalar_tensor_tensor(C.yv[:, m, :], C.uF[:, m, :], C.s5d[:, m:m + 1], yP[:, m, :], ALU.mult, ALU.add),
             reads=["uF", "s5d", "bank0"], writes=["yv"])
    S.op("dve", lambda e: e.tensor_tensor(f2(C.g1), f2(C.yv), f2(C.yv), ALU.mult), reads=["yv"], writes=["g1"])
    S.op("dve", lambda e: e.tensor_scalar(f2(C.g1), f2(C.g1), 0.044715, 1.0, ALU.mult, ALU.add), reads=["g1"], writes=["g1"])
    S.op("dve", lambda e: e.tensor_tensor(f2(C.g1), f2(C.g1), f2(C.yv), ALU.mult), reads=["g1", "yv"], writes=["g1"])
    S.op("dve", lambda e: e.tensor_scalar(f2(C.g1), f2(C.g1), -GELU_C, 40.0, ALU.mult, ALU.min), reads=["g1"], writes=["g1"])
    S.op("act", lambda e: e.activation(f2(C.g1), f2(C.g1), AF.Exp), reads=["g1"], writes=["g1"])
    S.op("act", lambda e: e.activation(f2(C.g1), f2(C.g1), AF.Ln, bias=C.onet[:, 0:1]), reads=["g1", "onet"], writes=["g1"])
    S.op("act", lambda e: e.activation(f2(C.g1), f2(C.g1), AF.Exp, scale=-1.0), reads=["g1"], writes=["g1"])
    S.op("dve", lambda e: e.tensor_tensor(f2(C.zg), f2(C.yv), f2(C.g1), ALU.mult), reads=["g1", "yv"], writes=["zg"])
    S.op("dve", lambda e: e.tensor_copy(f2(C.zgb), f2(C.zg)), reads=["zg"], writes=["zgb"])
    gP = A[1]
    for m2 in range(2):
        for k in range(2):
            S.op("pe", lambda e, m2=m2, k=k: e.matmul(gP[:, m2, :], C.glus[:, k, m2 * 128:(m2 + 1) * 128], C.zgb[:, k, :],
                                                     start=(k == 0), stop=(k == 1)),
                 reads=["glus", "zgb"], writes=["bank1"])
    S.op("act", lambda e: e.activation(f2(C.g1), gP[:, 0:2, :].rearrange("p a b -> p (a b)"), AF.Exp, scale=-1.0), reads=["bank1"], writes=["g1"])
    S.op("act", lambda e: e.activation(f2(C.g1), f2(C.g1), AF.Ln, bias=C.onet[:, 0:1]), reads=["g1", "onet"], writes=["g1"])
    S.op("act", lambda e: e.activation(f2(C.g1), f2(C.g1), AF.Exp, scale=-1.0), reads=["g1"], writes=["g1"])
    S.op("dve", lambda e: e.tensor_tensor(f2(C.o5), f2(C.zg), f2(C.g1), ALU.mult), reads=["g1", "zg"], writes=["o5"])
    S.op("dve", lambda e: e.tensor_tensor(f2(C.zgb), f2(C.o5), f2(C.o5), ALU.mult), reads=["o5"], writes=["zgb"])
    sP = A[1][:, 2:4, :]
    for m in range(2):
        S.op("pe", lambda e, m=m: e.matmul(sP[:, 0, :], C.ones256b[:], C.zgb[:, m, :], start=(m == 0), stop=(m == 1)),
             reads=["ones256b", "zgb"], writes=["bank1"])
    S.op("act", lambda e: e.activation(C.rs5[:], sP[:, 0, :], AF.Ln, bias=C.epst[:, 0:1]), reads=["bank1", "epst"], writes=["rs5"])
    S.op("act", lambda e: e.activation(C.rs5[:], C.rs5[:], AF.Exp, scale=-0.5), reads=["rs5"], writes=["rs5"])
    for m in range(2):
        S.op("dve", lambda e, m=m: e.scalar_tensor_tensor(C.mixT[:, 6 + m, :], C.o5[:, m, :], C.s5nw[:, m:m + 1], C.rs5[:], ALU.mult, ALU.mult),
             reads=["o5", "s5nw", "rs5"], writes=["mixT"])


def hgrn_setup(S, C, P, layer):
    S.dma("ld_hnw", C.hnw[:], P["hnw"], writes=["hnw"])
    S.op("dve", lambda e: e.memset(C.ones64[:], 1.0 / 64.0), writes=["ones64"])
    S.op("dve", lambda e: e.memset(C.Sst[:], 0.0), writes=["Sst"])
    if layer == 0:
        S.op("dve", lambda e: e.memset(C.oml[:], 1.0), writes=["oml"])
    else:
        S.dma("ld_lbl0", C.lbl0[:], P["lbl0"], writes=["lbl0"])
        S.dma("ld_lbl1", C.lbl1[:], P["lbl1"], writes=["lbl1"])
        S.op("dve", lambda e: e.tensor_tensor(C.oml[:], C.lbl0[:], C.lbl1[:], ALU.subtract), reads=["lbl0", "lbl1"], writes=["oml"])
        S.op("act", lambda e: e.activation(C.oml[:], C.oml[:], AF.Exp), reads=["oml"], writes=["oml"])
        S.op("dve", lambda e: e.tensor_scalar(C.oml[:], C.oml[:], 1.0, None, ALU.add), reads=["oml"], writes=["oml"])
        S.op("dve", lambda e: e.reciprocal(C.oml[:], C.oml[:]), reads=["oml"], writes=["oml"])
        S.op("dve", lambda e: e.tensor_scalar(C.oml[:], C.oml[:], -1.0, 1.0, ALU.mult, ALU.add), reads=["oml"], writes=["oml"])


def hgrn_chunk(S, C, first):
    A = C.bank
    H = lambda t: t[0:64]
    fl = lambda t: t[:].rearrange("p a b -> p (a b)")
    fb = lambda b: b[0:64].rearrange("p a b -> p (a b)")
    for g in range(4):
        for h in range(4):
            c0 = g * 256 + h * 64
            for kt in range(8):
                S.op("pe", lambda e, g=g, h=h, kt=kt, c0=c0: e.matmul(A[g][0:64, h, :], C.wH[:, kt, c0:c0 + 64], C.hnT[:, kt, :],
                                                                     start=(kt == 0), stop=(kt == 7)),
                     reads=["wH", "hnT"], writes=["bank%d" % g])
    S.op("act", lambda e: e.activation(fl(C.he1), fb(A[1]), AF.Exp), reads=["bank1"], writes=["he1"])
    S.op("act", lambda e: e.activation(fl(C.he1), fl(C.he1), AF.Ln, bias=C.onet[0:64, 0:1]), reads=["he1", "onet"], writes=["he1"])
    S.op("act", lambda e: e.activation(fl(C.he1), fl(C.he1), AF.Exp, scale=-1.0), reads=["he1"], writes=["he1"])
    S.op("dve", lambda e: e.tensor_tensor(C.hk[:], C.he1[:], C.oml[:].unsqueeze(2).to_broadcast([64, 4, 128]), ALU.mult),
         reads=["he1", "oml"], writes=["hk"])
    S.op("dve", lambda e: e.tensor_scalar(fl(C.hden), fl(C.hk), -1.0, 1.0, ALU.mult, ALU.add), reads=["hk"], writes=["hden"])
    S.op("act", lambda e: e.activation(fl(C.hden), fl(C.hden), AF.Ln), reads=["hden"], writes=["hden"])
    S.op("dve", lambda e: e.tensor_tensor_scan(fl(C.hc), C.hreset[:], fl(C.hden), 0.0, ALU.mult, ALU.add), reads=["hreset", "hden"], writes=["hc"])
    S.op("act", lambda e: e.activation(fl(C.hec), fl(C.hc), AF.Exp), reads=["hc"], writes=["hec"])
    S.op("act", lambda e: e.activation(fl(C.hof), fl(C.hc), AF.Exp, scale=-1.0), reads=["hc"], writes=["hof"])
    S.op("act", lambda e: e.activation(fl(C.he1), fb(A[0]), AF.Exp, scale=-1.0), reads=["bank0"], writes=["he1"])
    S.op("act", lambda e: e.activation(fl(C.he1), fl(C.he1), AF.Ln, bias=C.onet[0:64, 0:1]), reads=["he1", "onet"], writes=["he1"])
    S.op("act", lambda e: e.activation(fl(C.he1), fl(C.he1), AF.Exp, scale=-1.0), reads=["he1"], writes=["he1"])
    S.op("dve", lambda e: e.tensor_tensor(fl(C.hqt), fb(A[0]), fl(C.he1), ALU.mult), reads=["bank0", "he1"], writes=["hqt"])
    S.op("dve", lambda e: e.tensor_tensor(fl(C.hqt), fl(C.hqt), fl(C.hec), ALU.mult), reads=["hqt", "hec"], writes=["hqt"])
    S.op("dve", lambda e: e.tensor_tensor(fl(C.hkt), fl(C.hk), fl(C.hof), ALU.mult), reads=["hk", "hof"], writes=["hkt"])
    r4 = lambda t: t[:].rearrange("p h (n c) -> p h n c", c=32)
    S.op("dve", lambda e: e.tensor_tensor(r4(C.hkh), r4(C.hkt), r4(C.hec)[:, :, :, 31:32].to_broadcast([64, 4, 4, 32]), ALU.mult),
         reads=["hkt", "hec"], writes=["hkh"])
    S.op("act", lambda e: e.copy(fl(C.hv), fb(A[2])), reads=["bank2"], writes=["hv"])
    S.op("act", lambda e: e.activation(fl(C.he1), fb(A[3]), AF.Exp, scale=-1.0), reads=["bank3"], writes=["he1"])
    S.op("act", lambda e: e.activation(fl(C.he1), fl(C.he1), AF.Ln, bias=C.onet[0:64, 0:1]), reads=["he1", "onet"], writes=["he1"])
    S.op("act", lambda e: e.activation(fl(C.he1), fl(C.he1), AF.Exp, scale=-1.0), reads=["he1"], writes=["he1"])
    S.op("dve", lambda e: e.tensor_tensor(fl(C.hgs), fb(A[3]), fl(C.he1), ALU.mult), reads=["bank3", "he1"], writes=["hc"])
    scA = A[4][0:32]
    for n in range(4):
        sl = slice(32 * n, 32 * n + 32)
        for h in range(4):
            S.op("pe", lambda e, h=h, n=n, sl=sl: e.matmul(scA[:, n, 32 * h:32 * h + 32], C.hkt[:, h, sl], C.hqt[:, h, sl], start=True, stop=True),
                 reads=["hkt", "hqt"], writes=["bank4"])
    S.op("dve", lambda e: e.tensor_tensor(C.hscm[:], scA, C.mask32[:].unsqueeze(1).to_broadcast([32, 4, 128]), ALU.mult),
         reads=["bank4", "mask32"], writes=["hscm"])
    trP = A[5][0:32]
    oP = A[2]
    dSP = A[3][0:64, 0:2, :].rearrange("p a (b c) -> p (a b) c", c=64)
    for n in range(4):
        sl = slice(32 * n, 32 * n + 32)
        for h in range(4):
            S.op("pe", lambda e, h=h, sl=sl: e.transpose(trP[:, h, 0:64], C.hkh[:, h, sl], C.identf[0:64, 0:64]),
                 reads=["hkh", "identf"], writes=["bank5"])
            S.op("pe", lambda e, h=h, sl=sl: e.transpose(trP[:, h, 64:128], C.hv[:, h, sl], C.identf[0:64, 0:64]),
                 reads=["hv", "identf"], writes=["bank5"])
        S.op("act", lambda e: e.copy(C.hkvT[:], trP), reads=["bank5"], writes=["hkvT"])
        for h in range(4):
            S.op("pe", lambda e, h=h, sl=sl, n=n: e.matmul(oP[0:64, h, sl], C.hkvT[:, h, 64:128], C.hscm[:, n, 32 * h:32 * h + 32], start=True, stop=False),
                 reads=["hkvT", "hscm"], writes=["bank2"])
            S.op("pe", lambda e, h=h, sl=sl: e.matmul(oP[0:64, h, sl], C.Sst[:, h, :], C.hqt[:, h, sl], start=False, stop=True),
                 reads=["Sst", "hqt"], writes=["bank2"])
        for h in range(4):
            S.op("pe", lambda e, h=h: e.matmul(dSP[:, h, :], C.hkvT[:, h, 0:64], C.hkvT[:, h, 64:128], start=True, stop=True),
                 reads=["hkvT"], writes=["bank3"])
        S.op("dve", lambda e, n=n: e.tensor_tensor(C.Sst[:], C.Sst[:], C.hec[:, :, 32 * n + 31:32 * n + 32].to_broadcast([64, 4, 64]), ALU.mult),
             reads=["Sst", "hec"], writes=["Sst"])
        S.op("dve", lambda e: e.tensor_tensor(C.Sst[:], C.Sst[:], dSP, ALU.add), reads=["Sst", "bank3"], writes=["Sst"])
    S.op("act", lambda e: e.copy(fl(C.hof), fb(oP)), reads=["bank2"], writes=["hof"])
    S.op("dve", lambda e: e.tensor_tensor(fl(C.he1), fl(C.hof), fl(C.hof), ALU.mult), reads=["hof"], writes=["he1"])
    msP = fb(A[4])
    S.op("pe", lambda e: e.matmul(msP, C.ones64[:], fl(C.he1), start=True, stop=True), reads=["ones64", "he1"], writes=["bank4"])
    S.op("act", lambda e: e.activation(fl(C.hden), msP, AF.Ln, bias=C.epst[0:64, 0:1]), reads=["bank4", "epst"], writes=["hden"])
    S.op("act", lambda e: e.activation(fl(C.hden), fl(C.hden), AF.Exp, scale=-0.5), reads=["hden"], writes=["hden"])
    S.op("dve", lambda e: e.scalar_tensor_tensor(fl(C.hof), fl(C.hof), C.hnw[:, 0:1], fl(C.hden), ALU.mult, ALU.mult),
         reads=["hof", "hnw", "hden"], writes=["hof"])
    S.op("dve", lambda e: e.tensor_tensor(fl(C.mixH), fl(C.hof), fl(C.hgs), ALU.mult), reads=["hof", "hc"], writes=["mixH"])


def gdn_setup(S, C, P):
    S.dma("ld_convw", C.convw[:].rearrange("p a b -> p (a b)"), P["convw"], writes=["convw"])
    S.dma("ld_alog", C.aneg[:], P["alog"], writes=["aneg"])
    S.dma("ld_dtb", C.dtb[:], P["dtb"], writes=["dtb"])
    S.dma("ld_gnw", C.gnw[:], P["gnw"], writes=["gnw"])
    S.op("act", lambda e: e.activation(C.aneg[:], C.aneg[:], AF.Exp), reads=["aneg"], writes=["aneg"])
    S.op("dve", lambda e: e.tensor_scalar(C.aneg[:], C.aneg[:], -1.0, None, ALU.mult), reads=["aneg"], writes=["aneg"])
    S.op("dve", lambda e: e.memset(C.ones128f[:], 1.0 / 128.0), writes=["ones128f"])
    S.op("dve", lambda e: e.memset(C.gx[:], 0.0), writes=["gx"])
    S.op("dve", lambda e: e.memset(C.gS[:], 0.0), writes=["gS"])


def gdn_chunk(S, C, first):
    A = C.bank
    f3 = lambda t: t[:].rearrange("p a b -> p (a b)")
    bc4 = lambda small: small.unsqueeze(2).to_broadcast([128, 4, 128])
    for g in range(4):
        for h in range(4):
            c0 = g * 512 + h * 128
            for kt in range(8):
                S.op("pe", lambda e, g=g, h=h, kt=kt, c0=c0: e.matmul(A[g][:, h, :], C.wG[:, kt, c0:c0 + 128], C.hnT[:, kt, :],
                                                                     start=(kt == 0), stop=(kt == 7)),
                     reads=["wG", "hnT"], writes=["bank%d" % g])
    abP = A[4][:, 0, 0:8]
    for kt in range(8):
        S.op("pe", lambda e, kt=kt: e.matmul(abP, C.hnT[:, kt, :], C.wab[:, kt, :], start=(kt == 0), stop=(kt == 7)),
             reads=["hnT", "wab"], writes=["bank4"])
    for g in range(3):
        S.op("act", lambda e, g=g: e.copy(C.gx[:, 4 * g:4 * g + 4, 3:131], A[g][:]), reads=["bank%d" % g], writes=["gx"])
    S.op("act", lambda e: e.activation(f3(C.gzs), f3(A[3]), AF.Exp, scale=-1.0), reads=["bank3"], writes=["gzs"])
    S.op("act", lambda e: e.activation(f3(C.gzs), f3(C.gzs), AF.Ln, bias=C.onet[:, 0:1]), reads=["gzs", "onet"], writes=["gzs"])
    S.op("act", lambda e: e.activation(f3(C.gzs), f3(C.gzs), AF.Exp, scale=-1.0), reads=["gzs"], writes=["gzs"])
    S.op("dve", lambda e: e.tensor_tensor(f3(C.gzs), f3(A[3]), f3(C.gzs), ALU.mult), reads=["bank3", "gzs"], writes=["gzs"])
    S.op("dve", lambda e: e.tensor_tensor(C.ga[:], A[4][:, 0, 0:4], C.dtb[:], ALU.add), reads=["bank4", "dtb"], writes=["ga"])
    S.op("act", lambda e: e.activation(C.ga[:], C.ga[:], AF.Exp), reads=["ga"], writes=["ga"])
    S.op("act", lambda e: e.activation(C.ga[:], C.ga[:], AF.Ln, bias=C.onet[:, 0:1]), reads=["ga", "onet"], writes=["ga"])
    S.op("dve", lambda e: e.tensor_tensor(C.gg[:], C.ga[:], C.aneg[:], ALU.mult), reads=["ga", "aneg"], writes=["gg"])
    S.op("act", lambda e: e.activation(C.gbeta[:], A[4][:, 0, 4:8], AF.Exp, scale=-1.0), reads=["bank4"], writes=["gbeta"])
    S.op("act", lambda e: e.activation(C.gbeta[:], C.gbeta[:], AF.Ln, bias=C.onet[:, 0:1]), reads=["gbeta", "onet"], writes=["gbeta"])
    S.op("act", lambda e: e.activation(C.gbeta[:], C.gbeta[:], AF.Exp, scale=-1.0), reads=["gbeta"], writes=["gbeta"])
    cP = A[4][:, 1, 0:4]
    clP = A[4][:, 2, 0:4]
    S.op("pe", lambda e: e.matmul(cP, C.tri[:], C.gg[:], start=True, stop=True), reads=["tri", "gg"], writes=["bank4"])
    S.op("pe", lambda e: e.matmul(clP, C.onesf[:], C.gg[:], start=True, stop=True), reads=["onesf", "gg"], writes=["bank4"])
    S.op("act", lambda e: e.copy(C.gc[:], cP), reads=["bank4"], writes=["gc"])
    S.op("act", lambda e: e.activation(C.gec[:], cP, AF.Exp), reads=["bank4"], writes=["gec"])
    S.op("act", lambda e: e.activation(C.gecl[:], clP, AF.Exp), reads=["bank4"], writes=["gecl"])
    S.op("dve", lambda e: e.tensor_tensor(C.gedl[:], clP, C.gc[:], ALU.subtract), reads=["bank4", "gc"], writes=["gedl"])
    S.op("act", lambda e: e.activation(C.gedl[:], C.gedl[:], AF.Exp), reads=["gedl"], writes=["gedl"])
    S.op("dve", lambda e: e.tensor_tensor(C.gbec[:], C.gbeta[:], C.gec[:], ALU.mult), reads=["gbeta", "gec"], writes=["gbec"])
    S.op("dve", lambda e: e.tensor_scalar(C.gnb[:], C.gbeta[:], -1.0, None, ALU.mult), reads=["gbeta"], writes=["gnb"])
    cw = lambda j: C.convw[:, :, j:j + 1].to_broadcast([128, 12, 128])
    S.op("pool", lambda e: e.tensor_tensor(C.gtmp[:], C.gx[:, :, 3:131], cw(3), ALU.mult), reads=["gx", "convw"], writes=["gtmp"])
    S.op("dve", lambda e: e.tensor_tensor(C.gcv[:], C.gx[:, :, 0:128], cw(0), ALU.mult), reads=["gx", "convw"], writes=["gcv"])
    for j in range(1, 3):
        S.op("dve", lambda e, j=j: e.tensor_tensor(C.gcv2[:], C.gx[:, :, j:j + 128], cw(j), ALU.mult), reads=["gx", "convw"], writes=["gXa", "gXb", "gXTa"])
        S.op("dve", lambda e: e.tensor_tensor(f3(C.gcv), f3(C.gcv), f3(C.gcv2), ALU.add), reads=["gcv", "gXa", "gXb", "gXTa"], writes=["gcv"])
    S.op("dve", lambda e: e.tensor_tensor(f3(C.gcv), f3(C.gcv), f3(C.gtmp), ALU.add), reads=["gcv", "gtmp"], writes=["gcv"])
    S.op("pool", lambda e: e.tensor_copy(C.gx[:, :, 0:3], C.gx[:, :, 128:131]), reads=["gx"], writes=["gx"])
    S.op("act", lambda e: e.activation(f3(C.gtmp), f3(C.gcv), AF.Exp, scale=-1.0), reads=["gcv"], writes=["gtmp"])
    S.op("act", lambda e: e.activation(f3(C.gtmp), f3(C.gtmp), AF.Ln, bias=C.onet[:, 0:1]), reads=["gtmp", "onet"], writes=["gtmp"])
    S.op("act", lambda e: e.activation(f3(C.gtmp), f3(C.gtmp), AF.Exp, scale=-1.0), reads=["gtmp"], writes=["gtmp"])
    S.op("dve", lambda e: e.tensor_tensor(f3(C.gcv), f3(C.gcv), f3(C.gtmp), ALU.mult), reads=["gcv", "gtmp"], writes=["gcv"])
    qk = C.gcv[:, 0:8, :]
    S.op("dve", lambda e: e.tensor_tensor(C.gtmp[:, 0:8, :], qk, qk, ALU.mult), reads=["gcv"], writes=["gtmp"])
    for i in range(2):
        S.op("pe", lambda e, i=i: e.matmul(f3(A[i]), C.onesf[:], C.gtmp[:, 4 * i:4 * i + 4, :].rearrange("p a b -> p (a b)"), start=True, stop=True),
             reads=["onesf", "gtmp"], writes=["bank%d" % i])
    for i in range(2):
        S.op("act", lambda e, i=i: e.activation(C.gtmp[:, 8 + 2 * i:10 + 2 * i, :].rearrange("p a b -> p (a b)")[:, 0:256], f3(A[i])[:, 0:256], AF.Ln, bias=C.epst[:, 0:1]),
             reads=["bank%d" % i, "epst"], writes=["gtmp"]) if False else None
    S.op("act", lambda e: e.activation(C.gtmp[:, 8:12, :].rearrange("p a b -> p (a b)"), f3(A[0]), AF.Ln, bias=C.epst[:, 0:1]), reads=["bank0", "epst"], writes=["gtmp"])
    S.op("act", lambda e: e.activation(C.gtmp[:, 8:12, :].rearrange("p a b -> p (a b)"), C.gtmp[:, 8:12, :].rearrange("p a b -> p (a b)"), AF.Exp, scale=-0.5), reads=["gtmp"], writes=["gtmp"])
    S.op("dve", lambda e: e.scalar_tensor_tensor(C.gcv[:, 0:4, :], C.gcv[:, 0:4, :], 128.0 ** -0.5, C.gtmp[:, 8:12, :], ALU.mult, ALU.mult),
         reads=["gcv", "gtmp"], writes=["gcv"])
    S.op("act", lambda e: e.activation(C.gtmp[:, 8:12, :].rearrange("p a b -> p (a b)"), f3(A[1]), AF.Ln, bias=C.epst[:, 0:1]), reads=["bank1", "epst", "gcv"], writes=["gtmp"])
    S.op("act", lambda e: e.activation(C.gtmp[:, 8:12, :].rearrange("p a b -> p (a b)"), C.gtmp[:, 8:12, :].rearrange("p a b -> p (a b)"), AF.Exp, scale=-0.5), reads=["gtmp"], writes=["gtmp"])
    S.op("dve", lambda e: e.tensor_tensor(C.gcv[:, 4:8, :], C.gcv[:, 4:8, :], C.gtmp[:, 8:12, :], ALU.mult), reads=["gcv", "gtmp"], writes=["gcv"])
    qn = lambda h: C.gcv[:, h, :]
    kn = lambda h: C.gcv[:, 4 + h, :]
    vf = lambda h: C.gcv[:, 8 + h, :]
    S.op("dve", lambda e: e.tensor_tensor(C.gdiag[:], C.identf[:].unsqueeze(1).to_broadcast([128, 4, 128]), bc4(C.gc[:]), ALU.mult),
         reads=["identf", "gc"], writes=["gdiag"])
    S.op("pe", lambda e: e.matmul(f3(A[2]), C.onesf[:], f3(C.gdiag), start=True, stop=True), reads=["onesf", "gdiag"], writes=["bank2"])
    S.op("pe", lambda e: e.matmul(f3(A[3]), C.onesf[:], f3(C.gdiag), start=True, stop=False), reads=["onesf", "gdiag"], writes=["bank3"])
    S.op("pe", lambda e: e.matmul(f3(A[3]), C.identf[:], C.gmask4[:], start=False, stop=True), reads=["identf", "gmask4"], writes=["bank3"])
    S.op("act", lambda e: e.activation(f3(C.gecb), f3(A[2]), AF.Exp), reads=["bank2"], writes=["gecb"])
    for h in range(4):
        S.op("act", lambda e, h=h: e.activation(C.gD[:, h, :], A[3][:, h, :], AF.Exp, scale=-1.0, bias=C.gc[:, h:h + 1]),
             reads=["bank3", "gc"], writes=["gD"])
    S.op("dve", lambda e: e.tensor_tensor(f3(C.gDst), f3(C.gD), C.strict4[:], ALU.mult), reads=["gD", "strict4"], writes=["gDst"])
    S.op("dve", lambda e: e.tensor_tensor(f3(C.gecb), C.gcv[:, 0:4, :].rearrange("p a b -> p (a b)"), f3(C.gecb), ALU.mult),
         reads=["gcv", "gecb"], writes=["gecb"])
    for h in range(4):
        S.op("pe", lambda e, h=h: e.matmul(A[5][:, h, :], kn(h), kn(h), start=True, stop=True), reads=["gcv"], writes=["bank5"])
        S.op("pe", lambda e, h=h: e.matmul(A[6][:, h, :], qn(h), kn(h), start=True, stop=True), reads=["gcv"], writes=["bank6"])
    S.op("dve", lambda e: e.tensor_tensor(f3(C.gdiag), f3(A[5]), f3(C.gDst), ALU.mult), reads=["bank5", "gDst"], writes=["gdiag"])
    S.op("dve", lambda e: e.tensor_tensor(C.gXTa[:], C.gdiag[:], bc4(C.gnb[:]), ALU.mult), reads=["gdiag", "gnb"], writes=["gXTa"])
    S.op("dve", lambda e: e.tensor_tensor(f3(C.gAm), f3(A[6]), f3(C.gD), ALU.mult), reads=["bank6", "gD"], writes=["gAm"])
    for h in range(4):
        S.op("pe", lambda e, h=h: e.transpose(A[0][:, h, :], C.gXTa[:, h, :], C.identf[:]), reads=["gXTa", "identf"], writes=["bank0"])
        S.op("pe", lambda e, h=h: e.transpose(A[1][:, h, :], C.gAm[:, h, :], C.identf[:]), reads=["gAm", "identf"], writes=["bank1"])
    S.op("act", lambda e: e.copy(f3(C.gXa), f3(A[0])), reads=["bank0"], writes=["gXa"])
    S.op("dve", lambda e: e.tensor_tensor(C.gR[:], A[0][:], C.identf[:].unsqueeze(1).to_broadcast([128, 4, 128]), ALU.add),
         reads=["bank0", "identf"], writes=["gR"])
    S.op("act", lambda e: e.copy(f3(C.gAT), f3(A[1])), reads=["bank1"], writes=["gAT"])
    X, XT = (C.gXa, "gXa"), (C.gXTa, "gXTa")
    Xn, XTn = (C.gXb, "gXb"), (C.gXTb, "gXTb")
    for j in range(1, 7):
        for h in range(4):
            S.op("pe", lambda e, h=h, X=X, XT=XT: e.matmul(A[5][:, h, :], X[0][:, h, :], XT[0][:, h, :], start=True, stop=True),
                 reads=[X[1], XT[1]], writes=["bank5"])
        S.op("act", lambda e, XTn=XTn: e.copy(f3(XTn[0]), f3(A[5])), reads=["bank5"], writes=[XTn[1]])
        if j < 6:
            for h in range(4):
                S.op("pe", lambda e, h=h, X=X, XT=XT: e.matmul(A[6][:, h, :], XT[0][:, h, :], X[0][:, h, :], start=True, stop=True),
                     reads=[X[1], XT[1]], writes=["bank6"])
            S.op("act", lambda e, Xn=Xn: e.copy(f3(Xn[0]), f3(A[6])), reads=["bank6"], writes=[Xn[1]])
        for h in range(4):
            S.op("pe", lambda e, h=h, XTn=XTn: e.matmul(A[0][:, h, :], XTn[0][:, h, :], C.gR[:, h, :], start=True, stop=True),
                 reads=[XTn[1], "gR"], writes=["bank0"])
        S.op("dve", lambda e: e.tensor_tensor(f3(C.gR), f3(C.gR), f3(A[0]), ALU.add), reads=["gR", "bank0"], writes=["gR"])
        X, XT, Xn, XTn = Xn, XTn, X, XT
    for h in range(4):
        S.op("pe", lambda e, h=h: e.transpose(A[1][:, h, :], kn(h), C.identf[:]), reads=["gcv", "identf"], writes=["bank1"])
        S.op("pe", lambda e, h=h: e.transpose(A[2][:, h, :], vf(h), C.identf[:]), reads=["gcv", "identf"], writes=["bank2"])
    S.op("dve", lambda e: e.tensor_tensor(C.gkbe[:], A[1][:], bc4(C.gbec[:]), ALU.mult), reads=["bank1", "gbec"], writes=["gXb"])
    S.op("dve", lambda e: e.tensor_tensor(C.gkdec[:], A[1][:], bc4(C.gedl[:]), ALU.mult), reads=["bank1", "gedl"], writes=["gXTa"])
    S.op("dve", lambda e: e.tensor_tensor(C.gvb[:], A[2][:], bc4(C.gbeta[:]), ALU.mult), reads=["bank2", "gbeta"], writes=["gXTb"])
    for h in range(4):
        S.op("pe", lambda e, h=h: e.matmul(A[3][:, h, :], C.gkbe[:, h, :], C.gR[:, h, :], start=True, stop=True), reads=["gXb", "gR"], writes=["bank3"])
        S.op("pe", lambda e, h=h: e.matmul(A[5][:, h, :], C.gR[:, h, :], C.gvb[:, h, :], start=True, stop=True), reads=["gXTb", "gR"], writes=["bank5"])
    S.op("act", lambda e: e.copy(f3(C.gwT), f3(A[3])), reads=["bank3"], writes=["gD"])
    S.op("act", lambda e: e.copy(f3(C.gu), f3(A[5])), reads=["bank5"], writes=["gAm"])
    for h in range(4):
        S.op("pe", lambda e, h=h: e.matmul(A[6][:, h, :], C.gwT[:, h, :], C.gS[:, h, :], start=True, stop=True), reads=["gD", "gS"], writes=["bank6"])
    S.op("dve", lambda e: e.tensor_tensor(f3(C.gvn), f3(C.gu), f3(A[6]), ALU.subtract), reads=["gAm", "bank6"], writes=["gDst"])
    for h in range(4):
        S.op("pe", lambda e, h=h: e.matmul(A[0][:, h, :], C.gS[:, h, :], C.gecb[:, h, :], start=True, stop=False), reads=["gS", "gecb"], writes=["bank0"])
        S.op("pe", lambda e, h=h: e.matmul(A[0][:, h, :], C.gvn[:, h, :], C.gAT[:, h, :], start=False, stop=True), reads=["gDst", "gAT"], writes=["bank0"])
    for h in range(4):
        S.op("pe", lambda e, h=h: e.matmul(A[1][:, h, :], C.gkdec[:, h, :], C.gvn[:, h, :], start=True, stop=True), reads=["gXTa", "gDst"], writes=["bank1"])
    S.op("dve", lambda e: e.tensor_tensor(C.gS[:], C.gS[:], bc4(C.gecl[:]), ALU.mult), reads=["gS", "gecl"], writes=["gS"])
    S.op("dve", lambda e: e.tensor_tensor(f3(C.gS), f3(C.gS), f3(A[1]), ALU.add), reads=["gS", "bank1"], writes=["gS"])
    S.op("act", lambda e: e.copy(f3(C.gXa), f3(A[0])), reads=["bank0"], writes=["gXa"])
    S.op("dve", lambda e: e.tensor_tensor(f3(C.gXb), f3(C.gXa), f3(C.gXa), ALU.mult), reads=["gXa"], writes=["gXb"])
    S.op("pe", lambda e: e.matmul(f3(A[2]), C.ones128f[:], f3(C.gXb), start=True, stop=True), reads=["ones128f", "gXb"], writes=["bank2"])
    S.op("act", lambda e: e.activation(f3(C.gXb), f3(A[2]), AF.Ln, bias=C.epst[:, 0:1]), reads=["bank2", "epst"], writes=["gXb"])
    S.op("act", lambda e: e.activation(f3(C.gXb), f3(C.gXb), AF.Exp, scale=-0.5), reads=["gXb"], writes=["gXb"])
    S.op("dve", lambda e: e.scalar_tensor_tensor(f3(C.gXa), f3(C.gXa), C.gnw[:, 0:1], f3(C.gXb), ALU.mult, ALU.mult), reads=["gXa", "gnw", "gXb"], writes=["gXa"])
    S.op("dve", lambda e: e.tensor_tensor(C.mixT[:, 2:6, :], C.gXa[:], C.gzs[:], ALU.mult), reads=["gXa", "gzs"], writes=["mixT"])


def alloc_mixer(nc, st, S, pfx=""):
    C = Ctx()
    def sb(name, shape, dt=F32):
        t = st.enter_context(nc.sbuf_tensor(pfx + name, shape, dt)); setattr(C, name, t); return t
    def ps(name, shape, dt=F32):
        return st.enter_context(nc.psum_tensor(pfx + name, shape, dt))
    for k, (shp, dt) in CSHAPES.items():
        sb(k, shp, dt)
    C.idb = C.identb
    sb("epst", [128, 1]); sb("onet", [128, 1]); sb("ones256b", [128, 128], BF16); sb("onesf", [128, 128])
    sb("wH", [128, 8, 1024], BF16); sb("wG", [128, 8, 2048], BF16); sb("wab", [128, 8, 8], BF16); sb("wu", [128, 8, 256], BF16)
    sb("wout", [128, 6, 1024], BF16); sb("woutH", [64, 4, 1024], BF16)
    sb("nwcol", [128, 8])
    C.hs = [sb("h0", [128, 1024])]
    sb("ss", [128, 1]); sb("lnv", [128, 1]); sb("rstd", [128, 1])
    sb("hn", [128, 1024], BF16); sb("hnT", [128, 8, 128], BF16)
    C.junk = C.hn
    sb("mixT", [128, 8, 128], BF16); sb("mixH", [64, 4, 128], BF16)
    for n in ("tabC", "tabS", "t2", "wre", "wim"):
        sb(n, [128, 8, 128])
    sb("s5tmp", [128, 1024])
    C.kint = C.s5tmp[:].bitcast(mybir.dt.int32)
    for n in ("BTre", "BTim", "CTre", "CTimn", "xre", "xim"):
        sb(n, [128, 8, 128], BF16)
    for n in ("lre_p", "lim_p", "ls_p", "dtp", "lrep", "magp", "thp", "a128", "t128", "se", "ce", "c1", "c2", "c3"):
        sb(n, [128, 8])
    sb("zst", [128, 2, 8]); sb("s5d", [128, 2]); sb("s5nw", [128, 2]); sb("glus", [128, 2, 256], BF16)
    sb("uF", [128, 2, 128]); sb("uB", [128, 2, 128], BF16)
    for n in ("yv", "g1", "zg", "o5"):
        sb(n, [128, 2, 128])
    sb("zgb", [128, 2, 128], BF16); sb("rs5", [128, 128])
    for n in ("he1", "hden", "hk", "hc", "hec", "hqt", "hkt", "hkh", "hv", "hof"):
        sb(n, [64, 4, 128])
    C.hgs = C.hc
    sb("hscm", [32, 4, 128]); sb("hkvT", [32, 4, 128]); sb("Sst", [64, 4, 64]); sb("ones64", [64, 64])
    sb("oml", [64, 4]); sb("lbl0", [64, 4]); sb("lbl1", [64, 4]); sb("hnw", [64, 1])
    sb("gx", [128, 12, 131]); sb("gcv", [128, 12, 128]); sb("gtmp", [128, 12, 128])
    for n in ("gzs", "gD", "gDst", "gAm", "gdiag", "gXTb", "gR", "gecb", "gS", "gAT"):
        sb(n, [128, 4, 128])
    sb("gXblk", [128, 12, 128])
    C.gXa = C.gXblk[:, 0:4, :]; C.gXb = C.gXblk[:, 4:8, :]; C.gXTa = C.gXblk[:, 8:12, :]
    C.gcv2 = C.gXblk
    C.gwT = C.gD; C.gvn = C.gDst; C.gu = C.gAm
    C.gkbe = C.gXb; C.gkdec = C.gXTa; C.gvb = C.gXTb
    V = lambda a: type("V", (), {"__getitem__": lambda self, k, a=a: a[k]})()
    C.stg0 = V(C.gtmp[:, 0:8, :].rearrange("p a b -> p (a b)"))
    C.stg1 = V(C.gcv[:, 0:8, :].rearrange("p a b -> p (a b)"))
    C.ho = V(C.gtmp[:, 0:8, :].rearrange("p a b -> p (a b)"))
    for n in ("ga", "gg", "gbeta", "gc", "gec", "gecl", "gedl", "gbec", "gnb", "aneg", "dtb"):
        sb(n, [128, 4])
    sb("convw", [128, 12, 4]); sb("gnw", [128, 1]); sb("ones128f", [128, 128])
    wGf = C.wG[:].rearrange("p a b -> p (a b)").bitcast(F32)
    scr = [wGf[:, i * 1024:(i + 1) * 1024] for i in range(8)]
    scr += [C.t2[:].rearrange("p a b -> p (a b)"), C.wre[:].rearrange("p a b -> p (a b)")]
    C.scr = [type("V", (), {"__getitem__": lambda self, k, a=a: a[k]})() for a in scr]
    C.tp = ps("tp", [128, 8, 128], BF16)
    C.bank = [ps("bank%d" % i, [128, 4, 128]) for i in range(7)]
    return C


def declare_params(nc, suffix=""):
    P = {}
    for k, shp in PSHAPES.items():
        P[k] = nc.dram_tensor("p_" + k + suffix, shp, F32, kind="ExternalInput").ap()
    return P


def mixer_setup(S, C, P, Cst, layer):
    for k in CSHAPES:
        S.dma("ld_c_" + k, getattr(C, k)[:], Cst[k], writes=[k])
    S.op("dve", lambda e: e.memset(C.epst[:], EPS), writes=["epst"])
    S.op("dve", lambda e: e.memset(C.onesf[:], 1.0), writes=["onesf"])
    S.op("dve", lambda e: e.memset(C.onet[:], 1.0), writes=["onet"])
    S.op("dve", lambda e: e.memset(C.ones256b[:], 1.0 / 256.0), writes=["ones256b"])
    S.op("dve", lambda e: e.memset(C.mixT[:], 0.0), writes=["mixT"])
    S.op("dve", lambda e: e.memset(C.mixH[:], 0.0), writes=["mixH"])
    s5_setup(S, C, P)
    S.barrier()
    S.dma("ld_nwcol", C.nwcol[:], P["nmixc"], writes=["nwcol"])
    stage = [C.stg0, C.stg1]
    K8 = list(range(8))
    load_weight_bf16(S, "wH", P["w_in"], K8, 1024, C.wH, 0, stage, 1, 0, C.nwcol)
    load_weight_bf16(S, "wG", P["w_in"], K8, 1024, C.wG[:, :, 0:1024], 0, stage, 1, 1024, C.nwcol)
    load_weight_bf16(S, "wG", P["w_in"], K8, 1024, C.wG[:, :, 1024:2048], 0, stage, 1, 2048, C.nwcol)
    load_weight_bf16(S, "wab", P["w_in"], K8, 8, C.wab, 0, stage, 8, 3072, C.nwcol)
    load_weight_bf16(S, "wu", P["w_in"], K8, 256, C.wu, 0, stage, 4, 3080, C.nwcol)
    load_weight_bf16(S, "wout", P["w_out"], list(range(2, 8)), 1024, C.wout, 2, stage, 1, 0)
    whv = P["w_out"][0:256, :].rearrange("(h p) n -> p h n", p=64)
    for i in range(4):
        sg = stage[i % 2][0:64, :]
        S.dma("wst%d" % (i % 2), sg, whv[:, i, :], writes=[("wstage", i % 2)])
        S.op("dve", lambda h, sg=sg, i=i: h.tensor_copy(C.woutH[:, i, :], sg), reads=[("wstage", i % 2)], writes=["woutH"])
    hgrn_setup(S, C, P, layer)
    if hasattr(C, "gdn_ready"):
        gdn_setup(S, C, P)
    S.barrier()


def out_proj_store(S, C, h, hkey, b, ydst, ykey):
    oP = [C.bank[4], C.bank[5]]
    for half in range(2):
        ob = oP[half][:].rearrange("p a b -> p (a b)")
        n = 0
        for hd in range(4):
            S.op("pe", lambda e, hd=hd, half=half, ob=ob: e.matmul(ob, C.mixH[:, hd, :], C.woutH[:, hd, half * 512:(half + 1) * 512],
                                                                  start=(hd == 0), stop=False),
                 reads=["mixH", "woutH"], writes=["bank%d" % (4 + half)])
        for ft in range(2, 8):
            S.op("pe", lambda e, ft=ft, half=half, ob=ob: e.matmul(ob, C.mixT[:, ft, :], C.wout[:, ft - 2, half * 512:(half + 1) * 512],
                                                                  start=False, stop=(ft == 7)),
                 reads=["mixT", "wout"], writes=["bank%d" % (4 + half)])
        S.op("dve", lambda e, half=half, ob=ob: e.tensor_tensor(C.ho[:, half * 512:(half + 1) * 512], ob, h[:, half * 512:(half + 1) * 512], ALU.add),
             reads=["bank%d" % (4 + half), hkey], writes=["gtmp"])
    S.dma("st0", ydst, C.ho[:], reads=["gtmp"], writes=[ykey])


def build_mixer(NT, parts=("s5",), debug=True, layer=0):
    nc = bass.Bass("TRN2", target_bir_lowering=False)
    x = nc.dram_tensor("x", [NT * 128, 1024], F32, kind="ExternalInput").ap()
    y = nc.dram_tensor("y", [NT * 128, 1024], F32, kind="ExternalOutput").ap()
    P = declare_params(nc)
    Cst = {k: nc.dram_tensor("c_" + k, shp, dt, kind="ExternalInput").ap() for k, (shp, dt) in CSHAPES.items()}
    dbg = nc.dram_tensor("dbg", [NT, 128, 1024], BF16, kind="ExternalOutput").ap() if debug else None
    dbgH = nc.dram_tensor("dbgH", [NT, 64, 512], BF16, kind="ExternalOutput").ap() if debug else None
    with ExitStack() as st:
        S = Sched(nc, st)
        C = alloc_mixer(nc, st, S)
        if "gdn" in parts:
            C.gdn_ready = True
        mixer_setup(S, C, P, Cst, layer)
        ykeys = []
        for c in range(NT):
            b = 0
            h = C.hs[b]
            S.dma("hl%d" % b, h[:], x[c * 128:(c + 1) * 128, :], writes=[("h", b)])
            rmsnorm_T(S, C, h, ("h", b), None, None)
            if "hgrn" in parts:
                hgrn_chunk(S, C, c == 0)
            if "gdn" in parts:
                gdn_chunk(S, C, c == 0)
            if "s5" in parts:
                s5_chunk(S, C, c == 0)
            if debug:
                S.dma("dbg", dbg[c], C.mixT[:].rearrange("p a b -> p (a b)"), reads=["mixT"], writes=[("dbg", c)])
                S.dma("dbgH", dbgH[c], C.mixH[:].rearrange("p a b -> p (a b)"), reads=["mixH"], writes=[("dbgH", c)])
                ykeys += [("dbg", c), ("dbgH", c)]
            out_proj_store(S, C, h, ("h", b), b, y[c * 128:(c + 1) * 128, :], ("y", c))
            ykeys.append(("y", c))
        S.final_wait("sp", ykeys)
        block = st.enter_context(nc.Block())
        S.run(block)
    return nc


def make_inmap(inp, l, xin):
    m = {"x": xin}
    for k, v in host_layer_params(inp, l).items():
        m["p_" + k] = v.reshape(PSHAPES[k])
    for k, v in host_consts().items():
        m["c_" + k] = v
    return m


def build_mlp(NT, final):
    nc = bass.Bass("TRN2", target_bir_lowering=False)
    x = nc.dram_tensor("x", [NT * 128, 1024], F32, kind="ExternalInput").ap()
    nw = nc.dram_tensor("p_nmlp", [128, 1024], F32, kind="ExternalInput").ap()
    fwd = nc.dram_tensor("p_finalw", [128, 1024], F32, kind="ExternalInput").ap()
    w1 = nc.dram_tensor("p_w1", [1024, 4096], F32, kind="ExternalInput").ap()
    w2 = nc.dram_tensor("p_w2", [4096, 1024], F32, kind="ExternalInput").ap()
    identb = nc.dram_tensor("c_identb", [128, 128], BF16, kind="ExternalInput").ap()
    y = nc.dram_tensor("y", [NT * 128, 1024], F32, kind="ExternalOutput").ap()
    with ExitStack() as st:
        C = Ctx()
        def sb(name, shape, dt=F32):
            t = st.enter_context(nc.sbuf_tensor(name, shape, dt)); setattr(C, name, t); return t
        def ps(name, shape, dt=F32):
            return st.enter_context(nc.psum_tensor(name, shape, dt))
        S = Sched(nc, st)
        sb("w1s", [128, 8, 4096], BF16); sb("w2s", [128, 32, 1024], BF16)
        stage = [sb("stg0", [128, 1024]), sb("stg1", [128, 1024])]
        sb("nws", [128, 1024]); sb("fws", [128, 1024]); sb("idb", [128, 128], BF16)
        C.hs = [sb("h0", [128, 1024]), sb("h1", [128, 1024])]
        sb("junk", [128, 1024]); sb("ss", [128, 1]); sb("lnv", [128, 1]); sb("rstd", [128, 1]); sb("epst", [128, 1])
        sb("hn", [128, 1024], BF16); sb("hnT", [128, 8, 128], BF16)
        rr = [sb("rr0", [128, 512]), sb("rr1", [128, 512])]
        sb("aT", [128, 32, 128], BF16)
        ho = [sb("ho0", [128, 1024]), sb("ho1", [128, 1024])]
        C.tp = ps("tp", [128, 8, 128], BF16)
        hid = [ps("hid%d" % i, [128, 4, 128]) for i in range(2)]
        ops_ = [ps("ops%d" % i, [128, 512]) for i in range(2)]
        S.op("dve", lambda e: e.memset(C.epst[:], EPS), writes=["epst"])
        S.dma("c0", C.nws[:], nw, writes=["nws"])
        S.dma("c2", C.fws[:], fwd, writes=["fws"])
        S.dma("c1", C.idb[:], identb, writes=["idb"])
        for cq in range(4):
            load_weight_bf16(S, "w1s", w1, list(range(8)), 1024, C.w1s[:, :, cq * 1024:(cq + 1) * 1024], 0, stage, 1, cq * 1024)
        load_weight_bf16(S, "w2s", w2, list(range(32)), 1024, C.w2s, 0, stage, 1, 0)
        ykeys = []
        for t in range(NT):
            b = t % 2
            h = C.hs[b]
            S.dma("hl%d" % b, h[:], x[t * 128:(t + 1) * 128, :], writes=[("h", b)])
            rmsnorm_T(S, C, h, ("h", b), C.nws, "nws")
            for g in range(8):
                hb = g % 2
                for mi in range(4):
                    m = g * 4 + mi
                    for kt in range(8):
                        S.op("pe", lambda e, m=m, mi=mi, kt=kt, hb=hb: e.matmul(
                            hid[hb][:, mi, :], C.w1s[:, kt, m * 128:(m + 1) * 128], C.hnT[:, kt, :],
                            start=(kt == 0), stop=(kt == 7)), reads=["w1s", "hnT"], writes=[("hid", hb)])
                S.op("act", lambda e, hb=hb: e.activation(rr[hb][:], hid[hb][:].rearrange("p a b -> p (a b)"), AF.Relu),
                     reads=[("hid", hb)], writes=[("rr", hb)])
                S.op("dve", lambda e, hb=hb, g=g: e.tensor_tensor(
                    C.aT[:, g * 4:(g + 1) * 4, :].rearrange("p a b -> p (a b)"), rr[hb][:], rr[hb][:], ALU.mult),
                    reads=[("rr", hb)], writes=["aT"])
            for half in range(2):
                for kt in range(32):
                    S.op("pe", lambda e, half=half, kt=kt: e.matmul(
                        ops_[half][:], C.aT[:, kt, :], C.w2s[:, kt, half * 512:(half + 1) * 512],
                        start=(kt == 0), stop=(kt == 31)), reads=["aT", "w2s"], writes=[("ops", half)])
                S.op("dve", lambda e, half=half, b=b, h=h: e.tensor_tensor(
                    ho[b][:, half * 512:(half + 1) * 512], ops_[half][:], h[:, half * 512:(half + 1) * 512], ALU.add),
                    reads=[("ops", half), ("h", b)], writes=[("ho", b)])
            if final:
                hb_ = ho[b]
                S.op("act", lambda e, hb_=hb_: e.activation(C.junk[:], hb_[:], AF.Square, scale=1.0 / 32.0, accum_out=C.ss[:]),
                     reads=[("ho", b)], writes=["junk", "ss"])
                S.op("act", lambda e: e.activation(C.lnv[:], C.ss[:], AF.Ln, bias=C.epst[:, 0:1]), reads=["ss", "epst"], writes=["lnv"])
                S.op("act", lambda e: e.activation(C.rstd[:], C.lnv[:], AF.Exp, scale=-0.5), reads=["lnv"], writes=["rstd"])
                S.op("dve", lambda e, hb_=hb_: e.scalar_tensor_tensor(hb_[:], hb_[:], C.rstd[:, 0:1], C.fws[:], ALU.mult, ALU.mult),
                     reads=[("ho", b), "rstd", "fws"], writes=[("ho", b)])
            S.dma("st%d" % b, y[t * 128:(t + 1) * 128, :], ho[b][:], reads=[("ho", b)], writes=[("y", t)])
            ykeys.append(("y", t))
        S.final_wait("sp", ykeys)
        block = st.enter_context(nc.Block())
        S.run(block)
    return nc


NT_FULL = 129
N_CORES = 8
MIXER_PARTS = ("hgrn", "gdn", "s5")


def kernel_unfused(**inputs):
    inp = {k: np.asarray(v) for k, v in inputs.items()}
    B = inp["x"].shape[0]
    NT = NT_FULL
    consts = host_consts()
    hcur = []
    for c in range(N_CORES):
        if c < B:
            hcur.append(np.concatenate([np.zeros((112, 1024), np.float32), inp["meta"].astype(np.float32), inp["x"][c]], 0))
        else:
            hcur.append(np.zeros((NT * 128, 1024), np.float32))
    nc_mix = {l: build_mixer(NT, MIXER_PARTS, debug=False, layer=l) for l in range(2)}
    nc_mlp = {False: build_mlp(NT, False), True: build_mlp(NT, True)}
    depth = inp["w_in"].shape[0]
    finalw = np.ascontiguousarray(np.broadcast_to(inp["final_norm_w"].reshape(1, 1024), (128, 1024)))
    for l in range(depth):
        lp = host_layer_params(inp, l)
        base = {"p_" + k: v.reshape(PSHAPES[k]) for k, v in lp.items()}
        for k, v in consts.items():
            base["c_" + k] = v
        ims = [dict(base, x=hcur[c]) for c in range(N_CORES)]
        res = run_bass_kernel_spmd(nc_mix[min(l, 1)], ims, core_ids=list(range(N_CORES)))
        hcur = [res.results[c]["y"] for c in range(N_CORES)]
        last = (l == depth - 1)
        mb = {"p_nmlp": lp["nmlp"], "p_finalw": finalw, "p_w1": lp["w1"], "p_w2": lp["w2"], "c_identb": consts["identb"]}
        ims = [dict(mb, x=hcur[c]) for c in range(N_CORES)]
        res = run_bass_kernel_spmd(nc_mlp[last], ims, core_ids=list(range(N_CORES)))
        hcur = [res.results[c]["y"] for c in range(N_CORES)]
    out = np.stack([hcur[c][128:] for c in range(B)], 0).astype(np.float32)
    return out


def emit_mixer_pass(nc, S, x, y, P, Cst, NT, layer, pfx):
    with ExitStack() as st:
        C = alloc_mixer(nc, st, S, pfx)
        C.gdn_ready = True
        S.barrier()
        mixer_setup(S, C, P, Cst, layer)
        ykeys = []
        for c in range(NT):
            h = C.hs[0]
            S.dma("hl0", h[:], x[c * 128:(c + 1) * 128, :], writes=[("h", 0)])
            rmsnorm_T(S, C, h, ("h", 0), None, None)
            hgrn_chunk(S, C, c == 0)
            gdn_chunk(S, C, c == 0)
            s5_chunk(S, C, c == 0)
            out_proj_store(S, C, h, ("h", 0), 0, y[c * 128:(c + 1) * 128, :], (pfx + "y", c))
            ykeys.append((pfx + "y", c))
        S.final_wait("sp", ykeys)
        S.barrier()
        block = st.enter_context(nc.Block())
        S.run(block)


def emit_mlp_pass(nc, S, x, y, P, finalw, identb, NT, final, pfx, skip_tiles=0):
    nw, w1, w2 = P["nmlp"], P["w1"], P["w2"]
    with ExitStack() as st:
        C = Ctx()
        def sb(name, shape, dt=F32):
            t = st.enter_context(nc.sbuf_tensor(pfx + name, shape, dt)); setattr(C, name, t); return t
        def ps(name, shape, dt=F32):
            return st.enter_context(nc.psum_tensor(pfx + name, shape, dt))
        S.barrier()
        sb("w1s", [128, 8, 4096], BF16); sb("w2s", [128, 32, 1024], BF16)
        stage = [sb("stg0", [128, 1024]), sb("stg1", [128, 1024])]
        sb("nws", [128, 1024]); sb("fws", [128, 1024]); sb("idb", [128, 128], BF16)
        C.hs = [sb("h0", [128, 1024]), sb("h1", [128, 1024])]
        sb("junk", [128, 1024]); sb("ss", [128, 1]); sb("lnv", [128, 1]); sb("rstd", [128, 1]); sb("epst", [128, 1])
        hnb = [sb("hn0", [128, 1024], BF16), sb("hn1", [128, 1024], BF16)]
        hnTb = [sb("hnT0", [128, 8, 128], BF16), sb("hnT1", [128, 8, 128], BF16)]
        C.hn = hnb[0]; C.hnT = hnTb[0]
        rr = [sb("rr0", [128, 512]), sb("rr1", [128, 512])]
        sb("aT", [128, 32, 128], BF16)
        ho = [sb("ho0", [128, 1024]), sb("ho1", [128, 1024])]
        C.tp = ps("tp", [128, 8, 128], BF16)
        hid = [ps("hid%d" % i, [128, 4, 128]) for i in range(2)]
        ops_ = [ps("ops%d" % i, [128, 512]) for i in range(2)]
        S.op("dve", lambda e: e.memset(C.epst[:], EPS), writes=["epst"])
        S.dma("c0", C.nws[:], nw, writes=["nws"])
        S.dma("c2", C.fws[:], finalw, writes=["fws"])
        S.dma("c1", C.idb[:], identb, writes=["idb"])
        for cq in range(4):
            load_weight_bf16(S, "w1s", w1, list(range(8)), 1024, C.w1s[:, :, cq * 1024:(cq + 1) * 1024], 0, stage, 1, cq * 1024)
        load_weight_bf16(S, "w2s", w2, list(range(32)), 1024, C.w2s, 0, stage, 1, 0)
        ykeys = []
        def prep(t):
            b = t % 2
            S.dma("hl%d" % b, C.hs[b][:], x[t * 128:(t + 1) * 128, :], writes=[("h", b)])
            rmsnorm_T(S, C, C.hs[b], ("h", b), C.nws, "nws", sfx=str(b), hn=hnb[b], hnT=hnTb[b])
        prep(skip_tiles)
        for t in range(skip_tiles, NT):
            b = t % 2
            h = C.hs[b]
            hnT_t = hnTb[b]
            for g in range(8):
                hb = g % 2
                for mi in range(4):
                    m = g * 4 + mi
                    for kt in range(8):
                        S.op("pe", lambda e, m=m, mi=mi, kt=kt, hb=hb, hnT_t=hnT_t: e.matmul(
                            hid[hb][:, mi, :], C.w1s[:, kt, m * 128:(m + 1) * 128], hnT_t[:, kt, :],
                            start=(kt == 0), stop=(kt == 7)), reads=["w1s", "hnT%d" % b], writes=[("hid", hb)])
                S.op("act", lambda e, hb=hb: e.activation(rr[hb][:], hid[hb][:].rearrange("p a b -> p (a b)"), AF.Relu),
                     reads=[("hid", hb)], writes=[("rr", hb)])
                S.op("dve", lambda e, hb=hb, g=g: e.tensor_tensor(
                    C.aT[:, g * 4:(g + 1) * 4, :].rearrange("p a b -> p (a b)"), rr[hb][:], rr[hb][:], ALU.mult),
                    reads=[("rr", hb)], writes=["aT"])
            if t + 1 < NT:
                prep(t + 1)
            for half in range(2):
                for kt in range(32):
                    S.op("pe", lambda e, half=half, kt=kt: e.matmul(
                        ops_[half][:], C.aT[:, kt, :], C.w2s[:, kt, half * 512:(half + 1) * 512],
                        start=(kt == 0), stop=(kt == 31)), reads=["aT", "w2s"], writes=[("ops", half)])
                S.op("dve", lambda e, half=half, b=b, h=h: e.tensor_tensor(
                    ho[b][:, half * 512:(half + 1) * 512], ops_[half][:], h[:, half * 512:(half + 1) * 512], ALU.add),
                    reads=[("ops", half), ("h", b)], writes=[("ho", b)])
            if final:
                hb_ = ho[b]
                S.op("act", lambda e, hb_=hb_: e.activation(C.junk[:], hb_[:], AF.Square, scale=1.0 / 32.0, accum_out=C.ss[:]),
                     reads=[("ho", b)], writes=["junk", "ss"])
                S.op("act", lambda e: e.activation(C.lnv[:], C.ss[:], AF.Ln, bias=C.epst[:, 0:1]), reads=["ss", "epst"], writes=["lnv"])
                S.op("act", lambda e: e.activation(C.rstd[:], C.lnv[:], AF.Exp, scale=-0.5), reads=["lnv"], writes=["rstd"])
                S.op("dve", lambda e, hb_=hb_: e.scalar_tensor_tensor(hb_[:], hb_[:], C.rstd[:, 0:1], C.fws[:], ALU.mult, ALU.mult),
                     reads=[("ho", b), "rstd", "fws"], writes=[("ho", b)])
            t2 = t - skip_tiles
            S.dma("st%d" % b, y[t2 * 128:(t2 + 1) * 128, :], ho[b][:], reads=[("ho", b)], writes=[(pfx + "y", t)])
            ykeys.append((pfx + "y", t))
        S.final_wait("sp", ykeys)
        S.barrier()
        block = st.enter_context(nc.Block())
        S.run(block)


def build_fused(NT, depth=2):
    nc = bass.Bass("TRN2", target_bir_lowering=False)
    x = nc.dram_tensor("x", [NT * 128, 1024], F32, kind="ExternalInput").ap()
    y = nc.dram_tensor("y", [(NT - 1) * 128, 1024], F32, kind="ExternalOutput").ap()
    hA = nc.dram_tensor("scr_hA", [NT * 128, 1024], F32, kind="Internal").ap()
    hB = nc.dram_tensor("scr_hB", [NT * 128, 1024], F32, kind="Internal").ap()
    finalw = nc.dram_tensor("p_finalw", [128, 1024], F32, kind="ExternalInput").ap()
    Ps = [declare_params(nc, "_l%d" % l) for l in range(depth)]
    Cst = {k: nc.dram_tensor("c_" + k, shp, dt, kind="ExternalInput").ap() for k, (shp, dt) in CSHAPES.items()}
    with ExitStack() as st:
        S = Sched(nc, st)
        src = x
        for l in range(depth):
            emit_mixer_pass(nc, S, src, hA, Ps[l], Cst, NT, min(l, 1), "m%d_" % l)
            last = (l == depth - 1)
            emit_mlp_pass(nc, S, hA, y if last else hB, Ps[l], finalw, Cst["identb"], NT, last, "f%d_" % l,
                          skip_tiles=1 if last else 0)
            src = hB
    return nc


def kernel_fused(**inputs):
    inp = {k: np.asarray(v) for k, v in inputs.items()}
    B = inp["x"].shape[0]
    NT = NT_FULL
    depth = inp["w_in"].shape[0]
    base = {}
    for l in range(depth):
        for k, v in host_layer_params(inp, l).items():
            base["p_" + k + "_l%d" % l] = v.reshape(PSHAPES[k])
    for k, v in host_consts().items():
        base["c_" + k] = v
    base["p_finalw"] = np.ascontiguousarray(np.broadcast_to(inp["final_norm_w"].reshape(1, 1024).astype(np.float32), (128, 1024)))
    ims = []
    for c in range(N_CORES):
        if c < B:
            xin = np.concatenate([np.zeros((112, 1024), np.float32), inp["meta"].astype(np.float32), inp["x"][c, :(NT - 1) * 128]], 0)
        else:
            xin = np.zeros((NT * 128, 1024), np.float32)
        ims.append(dict(base, x=xin))
    nc = build_fused(NT, depth)
    res = run_bass_kernel_spmd(nc, ims, core_ids=list(range(N_CORES)))
    return np.stack([res.results[c]["y"] for c in range(B)], 0).astype(np.float32)


kernel = kernel_fused
```
